# Optimizing a Trainium2 kernel written in Bass

```python
import math
import jax, jax.numpy as jnp
from jax import lax
import numpy as np

D_MODEL = 1024
BATCH = 4
SEQ = 4096
DEPTH = 4

GRID_W = 64
CTX_LEN = 256
D_MIX = D_MODEL
HEAD_DIM = 64
D_ATT = D_MIX // 4
D_RWKV = D_MIX // 4
D_POOL = D_MIX // 4
D_FOUR = D_MIX - D_ATT - D_RWKV - D_POOL
N_Q_HEADS = D_ATT // HEAD_DIM
N_KV_HEADS = 2
D_KV = N_KV_HEADS * HEAD_DIM
N_RWKV_HEADS = D_RWKV // HEAD_DIM
POOL_WINDOWS = (2, 4, 8, 16)
N_POOL_GROUPS = 4
POOL_GROUP = D_POOL // N_POOL_GROUPS
N_FOUR_GROUPS = 4
FOUR_GROUP = D_FOUR // N_FOUR_GROUPS
DECAY_RANK = 32
ICL_RANK = 32
GATE_RANK = 64
D_FF = 2816
Q_BLOCK = 128
ROPE_THETA = 10000.0
LN_EPS = 1e-5
QK_EPS = 1e-6
GN_EPS = 64e-5
N_SUB = 3
DEEPNORM_ALPHA = (2 * DEPTH) ** 0.25
DEEPNORM_BETA = (8 * DEPTH) ** -0.25
SPLIT_SIZES = (D_ATT, D_KV, D_KV, D_RWKV, D_RWKV, D_RWKV, D_RWKV, D_POOL, D_FOUR)
D_IN = D_ATT + 2 * D_KV + 4 * D_RWKV + D_POOL + D_FOUR

kernel_name = "hybrid_headgroup_dit_block"

F32 = jnp.float32


def _split_points():
    pts, acc = [], 0
    for s in SPLIT_SIZES[:-1]:
        acc += s
        pts.append(acc)
    return pts


def layer_norm(x, g, b):
    x32 = x.astype(F32)
    mu = jnp.mean(x32, -1, keepdims=True)
    var = jnp.mean(jnp.square(x32 - mu), -1, keepdims=True)
    return ((x32 - mu) * lax.rsqrt(var + LN_EPS) * g + b).astype(x.dtype)


def modulate(x, shift, scale):
    return x * (1 + scale) + shift


def post_norm(x, y, gate, g, b, resid_w):
    return layer_norm(DEEPNORM_ALPHA * x + resid_w * gate * y, g, b)


def swiglu(h, w_gu, w_dn):
    gate, up = jnp.split(h @ w_gu, 2, axis=-1)
    return (jax.nn.silu(gate) * up) @ w_dn


def head_rms(x, g):
    x32 = x.astype(F32)
    return (x32 * lax.rsqrt(jnp.mean(jnp.square(x32), -1, keepdims=True) + QK_EPS) * g).astype(x.dtype)


def axial_rope(T):
    n_rows = T // GRID_W
    rows = jnp.repeat(jnp.arange(n_rows), GRID_W).astype(F32)
    cols = jnp.tile(jnp.arange(GRID_W), n_rows).astype(F32)
    n_pair_axis = HEAD_DIM // 4
    inv = ROPE_THETA ** (-jnp.arange(n_pair_axis, dtype=F32) / n_pair_axis)
    ang = jnp.concatenate([rows[:, None] * inv, cols[:, None] * inv], -1)
    return jnp.cos(ang), jnp.sin(ang)


def apply_rope(x, cos, sin):
    B, T, H, dh = x.shape
    xp = x.astype(F32).reshape(B, T, H, dh // 2, 2)
    x0, x1 = xp[..., 0], xp[..., 1]
    c = cos[None, :, None, :]
    s = sin[None, :, None, :]
    out = jnp.stack([x0 * c - x1 * s, x0 * s + x1 * c], -1).reshape(B, T, H, dh)
    return out.astype(x.dtype)


def attend(q, k, v):
    B, S, Hq, dh = q.shape
    Hkv = k.shape[2]
    G = Hq // Hkv
    nb = S // Q_BLOCK
    qb = jnp.moveaxis(q.reshape(B, nb, Q_BLOCK, Hkv, G, dh), 1, 0)
    scale = dh ** -0.5

    def block(qblk):
        s = jnp.einsum('bqhgd,bkhd->bhgqk', qblk, k).astype(F32) * scale
        p = jax.nn.softmax(s, axis=-1).astype(v.dtype)
        return jnp.einsum('bhgqk,bkhd->bqhgd', p, v)

    o = lax.map(block, qb)
    return jnp.moveaxis(o, 0, 1).reshape(B, S, Hq * dh)


def attn_heads(zq, zk, zv, qg, kg):
    B, T, _ = zq.shape
    q = head_rms(zq.reshape(B, T, N_Q_HEADS, HEAD_DIM), qg)
    k = head_rms(zk.reshape(B, T, N_KV_HEADS, HEAD_DIM), kg)
    v = zv.reshape(B, T, N_KV_HEADS, HEAD_DIM)
    return q, k, v


def centred_shift(z):
    zp = jnp.pad(z, ((0, 0), (1, 1), (0, 0)))
    return 0.5 * (zp[:, :-2] + zp[:, 2:])


def rwkv_heads(t):
    B, T, _ = t.shape
    return t.reshape(B, T, N_RWKV_HEADS, HEAD_DIM)


def rwkv_prepare(z_r, z_k, z_v, z_u, p):
    mu = p['rwkv_mu']
    r = z_r + (centred_shift(z_r) - z_r) * mu[0]
    k = z_k + (centred_shift(z_k) - z_k) * mu[1]
    v = z_v + (centred_shift(z_v) - z_v) * mu[2]
    du = centred_shift(z_u) - z_u
    xw = z_u + du * mu[3]
    xa = z_u + du * mu[4]
    xg = z_u + du * mu[5]
    g = jax.nn.sigmoid(xg @ p['gate_g1']) @ p['gate_g2']
    kk = rwkv_heads(k * p['k_k']).astype(F32)
    kk = kk / jnp.maximum(jnp.sqrt(jnp.sum(jnp.square(kk), -1, keepdims=True)), 1e-12)
    dirs = []
    for d in range(2):
        w_raw = (p['decay_w0'][d] + jnp.tanh(xw @ p['decay_w1'][d]) @ p['decay_w2'][d]).astype(F32)
        decay = jnp.exp(-jnp.exp(-jax.nn.softplus(-w_raw) - 0.5))
        a = jax.nn.sigmoid(p['icl_a0'][d] + (xa @ p['icl_a1'][d]) @ p['icl_a2'][d])
        kd = k * (1 + (a - 1) * p['k_a'])
        dirs.append((rwkv_heads(decay), rwkv_heads(kd), rwkv_heads(a)))
    return rwkv_heads(r), rwkv_heads(v), kk, g, dirs


def wkv_scan(S0, r, w, k, v, a, b, reverse, emit):
    def step(S, inp):
        r_t, w_t, k_t, v_t, a_t, b_t = inp
        sa = jnp.einsum('bhvk,bhk->bhv', S, a_t)
        S = S * w_t[:, :, None, :] + sa[..., None] * b_t[:, :, None, :] + v_t[..., None] * k_t[:, :, None, :]
        y = jnp.einsum('bhvk,bhk->bhv', S, r_t) if emit else None
        return S, y

    xs = tuple(jnp.moveaxis(t.astype(F32), 1, 0) for t in (r, w, k, v, a, b))
    S_final, ys = lax.scan(step, S0, xs, reverse=reverse)
    return (jnp.moveaxis(ys, 0, 1) if emit else None), S_final


def rwkv_mix(feats, S0s, p, emit):
    r, v, kk, g, dirs = feats
    ys, bonus, finals = [], [], []
    for d, (w, kd, a) in enumerate(dirs):
        y, Sf = wkv_scan(S0s[d], r, w, kd, v, -kk, kk * a, reverse=(d == 1), emit=emit)
        finals.append(Sf)
        if emit:
            ys.append(y)
            bonus.append(jnp.sum(r * kd * p['r_k'], -1, keepdims=True).astype(F32) * v.astype(F32))
    if not emit:
        return None, finals
    B, T = r.shape[0], r.shape[1]
    y = ys[0] + ys[1]
    mu = jnp.mean(y, -1, keepdims=True)
    var = jnp.mean(jnp.square(y - mu), -1, keepdims=True)
    yn = ((y - mu) * lax.rsqrt(var + GN_EPS)).reshape(B, T, D_RWKV) * p['gn_g'] + p['gn_b']
    out = (yn + (bonus[0] + bonus[1]).reshape(B, T, D_RWKV)) * g
    return out.astype(g.dtype), finals


def centred_window_mean(z, win):
    B, T, C = z.shape
    cs = jnp.concatenate([jnp.zeros((B, 1, C), F32), jnp.cumsum(z.astype(F32), axis=1)], axis=1)
    t = jnp.arange(T)
    lo = jnp.clip(t - win // 2, 0, T)
    hi = jnp.clip(t + (win - win // 2), 0, T)
    cnt = (hi - lo).astype(F32)
    return (cs[:, hi] - cs[:, lo]) / cnt[None, :, None]


def pool_mix(z, p):
    B, T, _ = z.shape
    groups = jnp.split(z, N_POOL_GROUPS, axis=-1)
    pooled = jnp.stack([centred_window_mean(zg, w) - zg.astype(F32) for zg, w in zip(groups, POOL_WINDOWS)], axis=2)
    y = jnp.einsum('btgc,gcd->btgd', pooled.astype(z.dtype), p['pool_w']).reshape(B, T, D_POOL)
    return y * p['pool_scale']


def fourier_mix(z, p):
    B, T, _ = z.shape
    zg = z.astype(F32).reshape(B, T, N_FOUR_GROUPS, FOUR_GROUP)
    f = jnp.fft.fft2(zg, axes=(1, 3), norm='ortho').real.astype(z.dtype).reshape(B, T, D_FOUR)
    return f @ p['fourier_w']


def mixer(hl, hc, p, ctx_out):
    pts = _split_points()
    zl = jnp.split(hl @ p['w_in'], pts, axis=-1)
    zc = jnp.split(hc @ p['w_in'], pts, axis=-1)
    B = hl.shape[0]
    ql, kl, vl = attn_heads(zl[0], zl[1], zl[2], p['q_norm_g'], p['k_norm_g'])
    qc, kc, vc = attn_heads(zc[0], zc[1], zc[2], p['q_norm_g'], p['k_norm_g'])
    cos, sin = axial_rope(hl.shape[1])
    ql = apply_rope(ql, cos, sin)
    kl = apply_rope(kl, cos, sin)
    att_l = attend(ql, jnp.concatenate([kc, kl], 1), jnp.concatenate([vc, vl], 1))
    zero = jnp.zeros((B, N_RWKV_HEADS, HEAD_DIM, HEAD_DIM), F32)
    rw_c, finals = rwkv_mix(rwkv_prepare(zc[3], zc[4], zc[5], zc[6], p), (zero, zero), p, emit=ctx_out)
    rw_l, _ = rwkv_mix(rwkv_prepare(zl[3], zl[4], zl[5], zl[6], p), finals, p, emit=True)
    out_l = jnp.concatenate([att_l, rw_l, pool_mix(zl[7], p), fourier_mix(zl[8], p)], -1) @ p['w_out']
    if not ctx_out:
        return out_l, None
    att_c = attend(qc, kc, vc)
    out_c = jnp.concatenate([att_c, rw_c, pool_mix(zc[7], p), fourier_mix(zc[8], p)], -1) @ p['w_out']
    return out_l, out_c


def setup_inputs(seed: int = 0) -> dict:
    key = jax.random.key(seed)
    ks = jax.random.split(key, 40)
    L = DEPTH

    def nrm(k, shape, scale):
        return jax.random.normal(k, shape, F32) * scale

    return {
        'x': nrm(ks[0], (BATCH, SEQ, D_MODEL), 1.0),
        'c': nrm(ks[1], (BATCH, D_MODEL), 1.0),
        'ctx': nrm(ks[2], (BATCH, CTX_LEN, D_MODEL), 1.0),
        'c_ctx': nrm(ks[3], (D_MODEL,), 1.0),
        'w_mod': nrm(ks[4], (L, D_MODEL, N_SUB * 3 * D_MODEL), 0.5 * D_MODEL ** -0.5),
        'b_mod': nrm(ks[5], (L, N_SUB * 3 * D_MODEL), 0.01),
        'ln_g': 1.0 + nrm(ks[6], (L, N_SUB, D_MODEL), 0.01),
        'ln_b': nrm(ks[7], (L, N_SUB, D_MODEL), 0.01),
        'w_ffn_in': nrm(ks[8], (L, 2, D_MODEL, 2 * D_FF), D_MODEL ** -0.5),
        'w_ffn_out': nrm(ks[9], (L, 2, D_FF, D_MODEL), DEEPNORM_BETA * D_FF ** -0.5),
        'w_in': nrm(ks[10], (L, D_MODEL, D_IN), D_MODEL ** -0.5),
        'q_norm_g': 1.0 + nrm(ks[11], (L, HEAD_DIM), 0.01),
        'k_norm_g': 1.0 + nrm(ks[12], (L, HEAD_DIM), 0.01),
        'rwkv_mu': jax.random.uniform(ks[13], (L, 6, D_RWKV), F32),
        'decay_w0': jax.random.uniform(ks[14], (L, 2, D_RWKV), F32, -6.5, -1.0),
        'decay_w1': nrm(ks[15], (L, 2, D_RWKV, DECAY_RANK), 0.1 * D_RWKV ** -0.5),
        'decay_w2': nrm(ks[16], (L, 2, DECAY_RANK, D_RWKV), 0.1 * DECAY_RANK ** -0.5),
        'icl_a0': nrm(ks[17], (L, 2, D_RWKV), 0.1),
        'icl_a1': nrm(ks[18], (L, 2, D_RWKV, ICL_RANK), D_RWKV ** -0.5),
        'icl_a2': nrm(ks[19], (L, 2, ICL_RANK, D_RWKV), ICL_RANK ** -0.5),
        'gate_g1': nrm(ks[20], (L, D_RWKV, GATE_RANK), D_RWKV ** -0.5),
        'gate_g2': nrm(ks[21], (L, GATE_RANK, D_RWKV), GATE_RANK ** -0.5),
        'k_k': 0.85 + nrm(ks[22], (L, D_RWKV), 0.02),
        'k_a': 1.0 + nrm(ks[23], (L, D_RWKV), 0.02),
        'r_k': nrm(ks[24], (L, N_RWKV_HEADS, HEAD_DIM), 0.1),
        'gn_g': 1.0 + nrm(ks[25], (L, D_RWKV), 0.01),
        'gn_b': nrm(ks[26], (L, D_RWKV), 0.01),
        'pool_w': nrm(ks[27], (L, N_POOL_GROUPS, POOL_GROUP, POOL_GROUP), POOL_GROUP ** -0.5),
        'pool_scale': 1.0 + nrm(ks[28], (L, D_POOL), 0.1),
        'fourier_w': nrm(ks[29], (L, D_FOUR, D_FOUR), D_FOUR ** -0.5),
        'w_out': nrm(ks[30], (L, D_MIX, D_MODEL), DEEPNORM_BETA * D_MIX ** -0.5),
    }


def reference(x, c, ctx, c_ctx, w_mod, b_mod, ln_g, ln_b, w_ffn_in, w_ffn_out, w_in, q_norm_g, k_norm_g,
              rwkv_mu, decay_w0, decay_w1, decay_w2, icl_a0, icl_a1, icl_a2, gate_g1, gate_g2, k_k, k_a, r_k,
              gn_g, gn_b, pool_w, pool_scale, fourier_w, w_out):
    B = x.shape[0]
    xl = x
    xc = ctx
    sc = jax.nn.silu(c)
    scc = jax.nn.silu(c_ctx)
    for l in range(DEPTH):
        last = l == DEPTH - 1
        p = {
            'w_in': w_in[l], 'w_out': w_out[l], 'q_norm_g': q_norm_g[l], 'k_norm_g': k_norm_g[l],
            'rwkv_mu': rwkv_mu[l], 'decay_w0': decay_w0[l], 'decay_w1': decay_w1[l], 'decay_w2': decay_w2[l],
            'icl_a0': icl_a0[l], 'icl_a1': icl_a1[l], 'icl_a2': icl_a2[l], 'gate_g1': gate_g1[l],
            'gate_g2': gate_g2[l], 'k_k': k_k[l], 'k_a': k_a[l], 'r_k': r_k[l], 'gn_g': gn_g[l], 'gn_b': gn_b[l],
            'pool_w': pool_w[l], 'pool_scale': pool_scale[l], 'fourier_w': fourier_w[l],
        }
        ml = (sc @ w_mod[l] + b_mod[l]).reshape(B, N_SUB, 3, 1, D_MODEL)
        mc = (scc @ w_mod[l] + b_mod[l]).reshape(N_SUB, 3, 1, 1, D_MODEL)
        xl = post_norm(xl, swiglu(modulate(xl, ml[:, 0, 0], ml[:, 0, 1]), w_ffn_in[l, 0], w_ffn_out[l, 0]),
                       ml[:, 0, 2], ln_g[l, 0], ln_b[l, 0], 0.5)
        xc = post_norm(xc, swiglu(modulate(xc, mc[0, 0], mc[0, 1]), w_ffn_in[l, 0], w_ffn_out[l, 0]),
                       mc[0, 2], ln_g[l, 0], ln_b[l, 0], 0.5)
        yl, yc = mixer(modulate(xl, ml[:, 1, 0], ml[:, 1, 1]), modulate(xc, mc[1, 0], mc[1, 1]), p, not last)
        xl = post_norm(xl, yl, ml[:, 1, 2], ln_g[l, 1], ln_b[l, 1], 1.0)
        xl = post_norm(xl, swiglu(modulate(xl, ml[:, 2, 0], ml[:, 2, 1]), w_ffn_in[l, 1], w_ffn_out[l, 1]),
                       ml[:, 2, 2], ln_g[l, 2], ln_b[l, 2], 0.5)
        if not last:
            xc = post_norm(xc, yc, mc[1, 2], ln_g[l, 1], ln_b[l, 1], 1.0)
            xc = post_norm(xc, swiglu(modulate(xc, mc[2, 0], mc[2, 1]), w_ffn_in[l, 1], w_ffn_out[l, 1]),
                           mc[2, 2], ln_g[l, 2], ln_b[l, 2], 0.5)
    return xl
```

```python
import contextlib
import numpy as np
import ml_dtypes
import concourse.bass as bass
import concourse.mybir as mybir
from concourse.bass_utils import run_bass_kernel_spmd

F32 = mybir.dt.float32
BF16 = mybir.dt.bfloat16
AF = mybir.ActivationFunctionType
ALU = mybir.AluOpType
AX = mybir.AxisListType

D = 1024
SEQ = 4096
CTX = 256
NTOK = SEQ + CTX
DEPTH = 4
DFF = 2816
DIN = 2048
ALPHA = (2 * DEPTH) ** 0.25
LN_EPS = 1e-5

ENGS = ("pe", "dve", "act", "pool", "sp")


class Buf:
    __slots__ = ("w", "r", "name", "psum")

    def __init__(self, name="", psum=False):
        self.w = None
        self.r = {}
        self.name = name
        self.psum = psum


class Prog:
    def __init__(self, nc, stack, n_dma_sems=16):
        self.nc = nc
        self.streams = {e: [] for e in ENGS}
        self.count = {e: 0 for e in ENGS}
        self.seen = {e: {} for e in ENGS}
        self.sems = {}
        for e in ENGS:
            self.sems["E_" + e] = stack.enter_context(nc.semaphore("sem_" + e))
        self.dma_keys = []
        self.dma_tot = {}
        for i in range(n_dma_sems):
            k = "D_%d" % i
            self.sems[k] = stack.enter_context(nc.semaphore("semd_%d" % i))
            self.dma_keys.append(k)
            self.dma_tot[k] = 0
        self.dma_rr = 0
        self.sb_off = 16640
        self.sb_marks = []
        self.n_alloc = 0
        self.n_ops = 0

    def sb(self, shape, dtype, name=None):
        esz = 4 if dtype == F32 else 2
        per_part = esz
        for s in shape[1:]:
            per_part *= s
        per_part = (per_part + 63) // 64 * 64
        off = self.sb_off
        self.sb_off += per_part
        assert self.sb_off <= 229376, "SBUF overflow %d" % self.sb_off
        self.n_alloc += 1
        t = self.nc.alloc_sbuf_tensor_at("sb%d_%s" % (self.n_alloc, name or "t"), list(shape), dtype, offset=off)
        return t

    def mark(self):
        self.sb_marks.append(self.sb_off)

    def release(self):
        self.sb_off = self.sb_marks.pop()

    def _wait(self, eng, toks):
        own = "E_" + eng
        seen = self.seen[eng]
        for key, val in toks:
            if key == own and eng in ("pe", "sp"):
                continue
            if seen.get(key, 0) >= val:
                continue
            seen[key] = val
            self.streams[eng].append(("wait", key, val))

    @staticmethod
    def _deps(reads, writes, own=None):
        toks = []
        for b in reads:
            if b.w is not None:
                toks.append(b.w)
            if b.psum:
                toks.extend((k, v) for k, v in b.r.items() if k != own)
        for b in writes:
            if b.w is not None:
                toks.append(b.w)
            toks.extend(b.r.items())
        return toks

    @staticmethod
    def _update(tok, reads, writes):
        key, val = tok
        for b in reads:
            if b.r.get(key, 0) < val:
                b.r[key] = val
        for b in writes:
            b.w = tok
            b.r = {}

    def op(self, eng, fn, reads=(), writes=()):
        self._wait(eng, self._deps(reads, writes, "E_" + eng))
        self.count[eng] += 1
        key = "E_" + eng
        self.streams[eng].append(("op", fn, key))
        tok = (key, self.count[eng])
        self._update(tok, reads, writes)
        self.n_ops += 1
        return tok

    def dma(self, fn, reads=(), writes=(), queue="sp"):
        key = self.dma_keys[self.dma_rr % len(self.dma_keys)]
        self.dma_rr += 1
        toks = self._deps(reads, writes)
        if self.dma_tot[key] > 0:
            toks.append((key, self.dma_tot[key]))
        self._wait(queue, toks)
        self.dma_tot[key] += 16
        self.streams[queue].append(("dma", fn, key))
        tok = (key, self.dma_tot[key])
        self._update(tok, reads, writes)
        self.n_ops += 1
        return tok

    def barrier(self):
        toks = [("E_" + e, self.count[e]) for e in ENGS if self.count[e] > 0]
        toks += [(k, v) for k, v in self.dma_tot.items() if v > 0]
        for e in ENGS:
            self._wait(e, toks)

    def replay(self, name, e):
        sems = self.sems
        for item in self.streams[name]:
            if item[0] == "wait":
                e.wait_ge(sems[item[1]], item[2])
            elif item[0] == "op":
                item[1](e).then_inc(sems[item[2]], 1)
            else:
                item[1](e).then_inc(sems[item[2]], 16)


def build_program(dbg=None):
    nc = bass.Bass("TRN2", target_bir_lowering=False)
    _so = bool(dbg) and dbg.startswith("so")

    def dt(name, shape, dtype=F32, kind="ExternalInput"):
        if _so and kind == "ExternalInput" and name not in ("ident", "masks", "c2", "rwt", "etot"):
            shape = [1, 2]
        return nc.dram_tensor(name, list(shape), dtype, kind=kind)
    io = {}
    io["x"] = dt("x", [SEQ, D])
    io["ctx"] = dt("ctx", [CTX, D])
    io["c2"] = dt("c2", [2, D])
    io["w_mod"] = dt("w_mod", [DEPTH, D, 9 * D])
    io["b_mod"] = dt("b_mod", [DEPTH, 9 * D])
    io["ln_g"] = dt("ln_g", [DEPTH, 3, D])
    io["ln_b"] = dt("ln_b", [DEPTH, 3, D])
    io["w_ffn_in"] = dt("w_ffn_in", [DEPTH, 2, D, 2 * DFF])
    io["w_ffn_out"] = dt("w_ffn_out", [DEPTH, 2, DFF, D])
    io["ident"] = dt("ident", [128, 128])
    io["w_in"] = dt("w_in", [DEPTH, D, DIN])
    io["w_out"] = dt("w_out", [DEPTH, D, D])
    io["q_norm_g"] = dt("q_norm_g", [DEPTH, 64])
    io["k_norm_g"] = dt("k_norm_g", [DEPTH, 64])
    io["rope_cos"] = dt("rope_cos", [SEQ, 32])
    io["rope_sin"] = dt("rope_sin", [SEQ, 32])
    io["dftc"] = dt("dftc", [128, 128], BF16)
    io["dfts"] = dt("dfts", [128, 128], BF16)
    io["poolm"] = dt("poolm", [20, 128, 128], BF16)
    io["pool_w"] = dt("pool_w", [DEPTH, 4, 64, 64])
    io["pool_scale"] = dt("pool_scale", [DEPTH, 256])
    io["fourier_w"] = dt("fourier_w", [DEPTH, 256, 256])
    io["dft_lat"] = dt("dft_lat", [2, SEQ, SEQ], BF16)
    io["dft_ctx"] = dt("dft_ctx", [2, CTX, CTX], BF16)
    io["rwkv_mu"] = dt("rwkv_mu", [DEPTH, 6, 256])
    io["decay_w0"] = dt("decay_w0", [DEPTH, 2, 256])
    io["decay_w1"] = dt("decay_w1", [DEPTH, 2, 256, 32])
    io["decay_w2"] = dt("decay_w2", [DEPTH, 2, 32, 256])
    io["icl_a0"] = dt("icl_a0", [DEPTH, 2, 256])
    io["icl_a1"] = dt("icl_a1", [DEPTH, 2, 256, 32])
    io["icl_a2"] = dt("icl_a2", [DEPTH, 2, 32, 256])
    io["gate_g1"] = dt("gate_g1", [DEPTH, 256, 64])
    io["gate_g2"] = dt("gate_g2", [DEPTH, 64, 256])
    io["k_k"] = dt("k_k", [DEPTH, 256])
    io["k_a"] = dt("k_a", [DEPTH, 256])
    io["r_k"] = dt("r_k", [DEPTH, 4, 64])
    io["gn_g"] = dt("gn_g", [DEPTH, 256])
    io["gn_b"] = dt("gn_b", [DEPTH, 256])
    io["masks"] = dt("masks", [4, 128, 128])
    scratch_kind = "ExternalOutput" if dbg else "Internal"
    io["xs"] = dt("xs", [NTOK, D], F32, kind=scratch_kind)
    io["modv"] = dt("modv", [2, 3 * D], F32, kind=scratch_kind)
    io["cat"] = dt("cat", [D, NTOK], BF16, kind=scratch_kind)
    io["zrw"] = dt("zrw", [4, 64, 4, NTOK], F32, kind=scratch_kind)
    io["zpool"] = dt("zpool", [NTOK, 256], BF16, kind=scratch_kind)
    io["ab"] = dt("ab", [NTOK, 512], BF16, kind=scratch_kind)
    so = bool(dbg) and dbg.startswith("so")
    io["rwt"] = dt("rwt", [8, 5, 64, NTOK], F32, kind=("ExternalInput" if so else scratch_kind))
    io["etot"] = dt("etot", [64, 8, 34], F32, kind=("ExternalInput" if so else scratch_kind))
    io["yT"] = dt("yT", [8, 64, NTOK], F32, kind=scratch_kind)
    io["rwaux"] = dt("rwaux", [2, 4, 64, NTOK], F32, kind=scratch_kind)
    io["y"] = dt("y", [SEQ, D], F32, kind="ExternalOutput")

    with contextlib.ExitStack() as stack:
        P = Prog(nc, stack)
        psum = [nc.alloc_psum_tensor("ps%d" % i, [128, 512], F32) for i in range(8)]
        pbuf = [Buf("ps%d" % i, psum=True) for i in range(8)]
        P.psum = psum
        P.pbuf = pbuf
        P.io = io
        emit_all(P, dbg)
        P.barrier()
        with nc.Block() as block:
            @block.tensor
            def _(e):
                P.replay("pe", e)

            @block.vector
            def _(e):
                P.replay("dve", e)

            @block.scalar
            def _(e):
                P.replay("act", e)

            @block.gpsimd
            def _(e):
                P.replay("pool", e)

            @block.sync
            def _(e):
                P.replay("sp", e)
    return nc


def emit_all(P, dbg):
    nc = P.nc
    io = P.io
    ident = P.sb([128, 128], F32, "ident")
    b_ident = Buf("ident")
    P.dma(lambda e: e.dma_start(out=ident[:], in_=io["ident"].ap()), writes=[b_ident])
    ones = P.sb([128, 128], F32, "ones")
    b_ones = Buf("ones")
    P.op("dve", lambda e: e.memset(ones[:], 1.0), writes=[b_ones])
    craw = P.sb([128, 2, 8], F32, "craw")
    b_craw = Buf()
    for r in range(2):
        P.dma(lambda e, r=r: e.dma_start(out=craw[:, r, :], in_=io["c2"].ap()[r, :].rearrange("(k p) -> p k", p=128),
                                         allow_slow_non_contiguous=True), writes=[b_craw])
    scT = P.sb([128, 8, 2], F32, "scT")
    b_scT = Buf()
    P.op("act", lambda e: e.activation(out=scT[:].rearrange("p k r -> p r k"), in_=craw[:], func=AF.Silu),
         reads=[b_craw], writes=[b_scT])
    P.g = dict(ident=ident, b_ident=b_ident, ones=ones, b_ones=b_ones, scT=scT, b_scT=b_scT)
    P.barrier()

    if dbg and dbg.startswith("so"):
        phase_rwkv_scan(P, 0, nsteps=int(dbg[2:]))
        return
    for l in range(DEPTH):
        last = l == DEPTH - 1
        src_lat = io["x"].ap() if l == 0 else io["xs"].ap()[0:SEQ, :]
        src_ctx = io["ctx"].ap() if l == 0 else io["xs"].ap()[SEQ:NTOK, :]
        phase_ffn(P, l, 0, 0, src_lat, src_ctx, io["xs"].ap()[0:SEQ, :], io["xs"].ap()[SEQ:NTOK, :], 0.5)
        if dbg == "ffn0":
            return
        phase_mixer(P, l, dbg)
        if dbg in ("att", "mixA", "pf", "wout", "rw1", "rw2", "rw3") or (dbg and dbg.startswith("rws")):
            return
        phase_ffn(P, l, 1, 2, io["xs"].ap()[0:SEQ, :], io["xs"].ap()[SEQ:NTOK, :],
                  io["y"].ap() if last else io["xs"].ap()[0:SEQ, :], io["xs"].ap()[SEQ:NTOK, :], 0.5, skip_ctx=last)
        if dbg == "l0":
            return


def phase_mod(P, l, sub, resid_w):
    nc, io, g = P.nc, P.io, P.g
    psum, pbuf = P.psum, P.pbuf
    P.mark()
    brow = P.sb([1, 3 * D], F32, "brow")
    b_brow = Buf()
    P.dma(lambda e: e.dma_start(out=brow[:], in_=io["b_mod"].ap()[l:l + 1, sub * 3 * D:(sub + 1) * 3 * D]), writes=[b_brow])
    mrow = P.sb([2, 3 * D], F32, "mrow")
    b_mrow = Buf()
    wst = [P.sb([128, 8, 512], F32, "wmst%d" % i) for i in range(2)]
    b_wst = [Buf(), Buf()]
    for cb in range(6):
        s = cb % 2
        c0 = sub * 3 * D + cb * 512
        P.dma(lambda e, s=s, c0=c0: e.dma_start(
            out=wst[s][:], in_=io["w_mod"].ap()[l, :, c0:c0 + 512].rearrange("(k p) n -> p k n", p=128)),
            writes=[b_wst[s]])
        pb = cb % 2
        for kc in range(8):
            P.op("pe", lambda e, s=s, kc=kc, pb=pb: e.matmul(psum[pb][0:2, :], lhsT=g["scT"][:, kc, :], rhs=wst[s][:, kc, :],
                                                             start=(kc == 0), stop=False),
                 reads=[g["b_scT"], b_wst[s]], writes=[pbuf[pb]])
        P.op("pe", lambda e, cb=cb, pb=pb: e.matmul(psum[pb][0:2, :], lhsT=g["ones"][0:1, 0:2], rhs=brow[0:1, cb * 512:(cb + 1) * 512],
                                                    start=False, stop=True),
             reads=[g["b_ones"], b_brow], writes=[pbuf[pb]])
        P.op("dve", lambda e, cb=cb, pb=pb: e.tensor_copy(out=mrow[:, cb * 512:(cb + 1) * 512], in_=psum[pb][0:2, :]),
             reads=[pbuf[pb]], writes=[b_mrow])
    b_modv = Buf()
    P.dma(lambda e: e.dma_start(out=io["modv"].ap(), in_=mrow[:]), reads=[b_mrow], writes=[b_modv])
    P.release()
    shT = P.sb([128, 2, 8], F32, "shT")
    scl = P.sb([128, 2, 8], F32, "scl")
    gbc = [P.sb([128, D], F32, "gbc%d" % r) for r in range(2)]
    b_shT, b_scl, b_g = Buf(), Buf(), [Buf(), Buf()]
    for r in range(2):
        P.dma(lambda e, r=r: e.dma_start(out=shT[:, r, :], in_=io["modv"].ap()[r, 0:D].rearrange("(k p) -> p k", p=128),
                                         allow_slow_non_contiguous=True), reads=[b_modv], writes=[b_shT])
        P.dma(lambda e, r=r: e.dma_start(out=scl[:, r, :], in_=io["modv"].ap()[r, D:2 * D].rearrange("(k p) -> p k", p=128),
                                         allow_slow_non_contiguous=True), reads=[b_modv], writes=[b_scl])
        P.dma(lambda e, r=r: e.dma_start(out=gbc[r][:], in_=io["modv"].ap()[r:r + 1, 2 * D:3 * D].partition_broadcast(128)),
              reads=[b_modv], writes=[b_g[r]])
    P.op("dve", lambda e: e.tensor_scalar(out=scl[:], in0=scl[:], scalar1=1.0, scalar2=None, op0=ALU.add),
         reads=[b_scl], writes=[b_scl])
    for r in range(2):
        if resid_w != 1.0:
            P.op("pool", lambda e, r=r: e.tensor_scalar(out=gbc[r][:], in0=gbc[r][:], scalar1=float(resid_w), scalar2=None, op0=ALU.mult),
                 reads=[b_g[r]], writes=[b_g[r]])
    lg = P.sb([128, D], F32, "lng")
    lb = P.sb([128, D], F32, "lnb")
    b_lg, b_lb = Buf(), Buf()
    P.dma(lambda e: e.dma_start(out=lg[:], in_=io["ln_g"].ap()[l, sub:sub + 1, :].partition_broadcast(128)), writes=[b_lg])
    P.dma(lambda e: e.dma_start(out=lb[:], in_=io["ln_b"].ap()[l, sub:sub + 1, :].partition_broadcast(128)), writes=[b_lb])
    return dict(shT=shT, scl=scl, gbc=gbc, b_shT=b_shT, b_scl=b_scl, b_g=b_g, lg=lg, lb=lb, b_lg=b_lg, b_lb=b_lb)


def emit_postnorm(P, m, r, xt_ap, b_x, y_halves, b_ys, out_ap, b_out, work):
    u, b_u, st, b_st = work["u"], work["b_u"], work["st"], work["b_st"]
    for h in range(2):
        sl = slice(h * 512, (h + 1) * 512)
        P.op("dve", lambda e, h=h, sl=sl: e.tensor_tensor(out=u[:, sl], in0=y_halves[h], in1=m["gbc"][r][:, sl], op=ALU.mult),
             reads=[b_ys[h], m["b_g"][r]], writes=[b_u])
        P.op("dve", lambda e, sl=sl: e.scalar_tensor_tensor(out=u[:, sl], in0=xt_ap[:, sl], scalar=float(ALPHA), in1=u[:, sl],
                                                            op0=ALU.mult, op1=ALU.add),
             reads=[b_x, b_u], writes=[b_u])
        P.op("dve", lambda e, h=h, sl=sl: e.bn_stats(out=st[:, h * 6:(h + 1) * 6], in_=u[:, sl]), reads=[b_u], writes=[b_st])
    P.op("dve", lambda e: e.bn_aggr(out=st[:, 12:14], in_=st[:, 0:12]), reads=[b_st], writes=[b_st])
    P.op("dve", lambda e: e.tensor_scalar(out=st[:, 14:15], in0=st[:, 13:14], scalar1=float(LN_EPS), scalar2=None, op0=ALU.add),
         reads=[b_st], writes=[b_st])
    P.op("act", lambda e: e.activation(out=st[:, 14:15], in_=st[:, 14:15], func=AF.Sqrt), reads=[b_st], writes=[b_st])
    P.op("dve", lambda e: e.reciprocal(out=st[:, 14:15], in_=st[:, 14:15]), reads=[b_st], writes=[b_st])
    P.op("dve", lambda e: e.scalar_tensor_tensor(out=st[:, 15:16], in0=st[:, 12:13], scalar=-1.0, in1=st[:, 14:15],
                                                 op0=ALU.mult, op1=ALU.mult), reads=[b_st], writes=[b_st])
    P.op("act", lambda e: e.activation(out=u[:], in_=u[:], func=AF.Identity, scale=st[:, 14:15], bias=st[:, 15:16]),
         reads=[b_u, b_st], writes=[b_u])
    P.op("pool", lambda e: e.tensor_tensor(out=u[:], in0=u[:], in1=m["lg"][:], op=ALU.mult), reads=[b_u, m["b_lg"]], writes=[b_u])
    P.op("pool", lambda e: e.tensor_tensor(out=out_ap, in0=u[:], in1=m["lb"][:], op=ALU.add), reads=[b_u, m["b_lb"]], writes=[b_out])


def phase_ffn(P, l, f, sub, src_lat, src_ctx, dst_lat, dst_ctx, resid_w, skip_ctx=False):
    nc, io, g = P.nc, P.io, P.g
    psum, pbuf = P.psum, P.pbuf
    P.mark()
    m = phase_mod(P, l, sub, resid_w)
    wgu = P.sb([128, 8, 2 * DFF], BF16, "wgu")
    wdn = P.sb([128, 22, D], BF16, "wdn")
    b_wgu, b_wdn = Buf(), Buf()
    P.mark()
    stg = [P.sb([128, DFF], F32, "stg%d" % i) for i in range(2)]
    b_stg = [Buf(), Buf()]
    cast_engs = ["pool", "dve", "act"]
    n = 0
    for kc in range(8):
        for hf in range(2):
            s = n % 2
            P.dma(lambda e, s=s, kc=kc, hf=hf: e.dma_start(
                out=stg[s][:], in_=io["w_ffn_in"].ap()[l, f, kc * 128:(kc + 1) * 128, hf * DFF:(hf + 1) * DFF]),
                writes=[b_stg[s]])
            ce = cast_engs[n % 3]
            if ce == "act":
                P.op("act", lambda e, s=s, kc=kc, hf=hf: e.activation(out=wgu[:, kc, hf * DFF:(hf + 1) * DFF], in_=stg[s][:], func=AF.Copy),
                     reads=[b_stg[s]], writes=[b_wgu])
            else:
                P.op(ce, lambda e, s=s, kc=kc, hf=hf: e.tensor_copy(out=wgu[:, kc, hf * DFF:(hf + 1) * DFF], in_=stg[s][:]),
                     reads=[b_stg[s]], writes=[b_wgu])
            n += 1
    for pc in range(11):
        s = n % 2
        P.dma(lambda e, s=s, pc=pc: e.dma_start(
            out=stg[s][:, 0:2048].rearrange("p (c n) -> p c n", c=2),
            in_=io["w_ffn_out"].ap()[l, f, pc * 256:(pc + 1) * 256, :].rearrange("(c p) n -> p c n", p=128)),
            writes=[b_stg[s]])
        ce = cast_engs[n % 3]
        if ce == "act":
            P.op("act", lambda e, s=s, pc=pc: e.activation(out=wdn[:, 2 * pc:2 * pc + 2, :].rearrange("p c n -> p (c n)"),
                                                           in_=stg[s][:, 0:2048], func=AF.Copy),
                 reads=[b_stg[s]], writes=[b_wdn])
        else:
            P.op(ce, lambda e, s=s, pc=pc: e.tensor_copy(out=wdn[:, 2 * pc:2 * pc + 2, :].rearrange("p c n -> p (c n)"),
                                                         in_=stg[s][:, 0:2048]),
                 reads=[b_stg[s]], writes=[b_wdn])
        n += 1
    P.barrier()
    P.release()
    NB = 256
    xt = [P.sb([128, 2, D], F32, "xt%d" % i) for i in range(2)]
    b_xt = [[Buf(), Buf()], [Buf(), Buf()]]
    hT = P.sb([128, 8, NB], BF16, "hT")
    b_hT = Buf()
    aT = P.sb([128, 22, NB], BF16, "aT")
    b_aT = [Buf() for _ in range(22)]
    sg = [P.sb([128, NB], F32, "sg%d" % i) for i in range(2)]
    b_sg = [Buf(), Buf()]
    ot = [P.sb([128, D], F32, "ot%d" % i) for i in range(2)]
    b_ot = [Buf(), Buf()]
    work = dict(u=P.sb([128, D], F32, "u"), b_u=Buf(), st=P.sb([128, 16], F32, "st"), b_st=Buf())
    blocks = [(src_lat, dst_lat, i * NB, 0) for i in range(SEQ // NB)] + ([] if skip_ctx else [(src_ctx, dst_ctx, 0, 1)])
    pi = 0
    for bi, (src, dst, t0, r) in enumerate(blocks):
        xs_ = bi % 2
        for t in range(2):
            P.dma(lambda e, xs_=xs_, t=t, src=src, t0=t0: e.dma_start(out=xt[xs_][:, t, :], in_=src[t0 + t * 128:t0 + (t + 1) * 128, :]),
                  writes=[b_xt[xs_][t]])
        for kp in range(4):
            pb = pi % 8
            pi += 1
            for kk in range(2):
                kc = kp * 2 + kk
                for t in range(2):
                    P.op("pe", lambda e, pb=pb, kk=kk, t=t, kc=kc, xs_=xs_: e.transpose(
                        out=psum[pb][:, kk * 256 + t * 128: kk * 256 + (t + 1) * 128], in_=xt[xs_][:, t, kc * 128:(kc + 1) * 128],
                        identity=g["ident"][:]),
                        reads=[b_xt[xs_][t], g["b_ident"]], writes=[pbuf[pb]])
            for kk in range(2):
                kc = kp * 2 + kk
                P.op("act", lambda e, pb=pb, kk=kk, kc=kc, r=r: e.activation(
                    out=hT[:, kc, :], in_=psum[pb][:, kk * 256:(kk + 1) * 256], func=AF.Identity,
                    scale=m["scl"][:, r, kc:kc + 1], bias=m["shT"][:, r, kc:kc + 1]),
                    reads=[pbuf[pb], m["b_scl"], m["b_shT"]], writes=[b_hT])
        for i in range(22):
            pb = pi % 8
            pi += 1
            for hf in range(2):
                for kc in range(8):
                    P.op("pe", lambda e, pb=pb, hf=hf, kc=kc, i=i: e.matmul(
                        psum[pb][:, hf * 256:(hf + 1) * 256], lhsT=wgu[:, kc, hf * DFF + i * 128: hf * DFF + (i + 1) * 128],
                        rhs=hT[:, kc, :], start=(kc == 0), stop=(kc == 7)),
                        reads=[b_hT, b_wgu], writes=[pbuf[pb]])
            s = i % 2
            P.op("act", lambda e, pb=pb, s=s: e.activation(out=sg[s][:], in_=psum[pb][:, 0:256], func=AF.Silu),
                 reads=[pbuf[pb]], writes=[b_sg[s]])
            P.op("dve", lambda e, pb=pb, s=s, i=i: e.tensor_tensor(out=aT[:, i, :], in0=psum[pb][:, 256:512], in1=sg[s][:], op=ALU.mult),
                 reads=[pbuf[pb], b_sg[s]], writes=[b_aT[i]])
        for t in range(2):
            pbs = []
            for hn in range(2):
                pb = pi % 8
                pi += 1
                pbs.append(pb)
                for i in range(22):
                    P.op("pe", lambda e, pb=pb, hn=hn, i=i, t=t: e.matmul(
                        psum[pb][:, :], lhsT=aT[:, i, t * 128:(t + 1) * 128], rhs=wdn[:, i, hn * 512:(hn + 1) * 512],
                        start=(i == 0), stop=(i == 21)),
                        reads=[b_aT[i], b_wdn], writes=[pbuf[pb]])
            o = (bi * 2 + t) % 2
            emit_postnorm(P, m, r, xt[xs_][:, t, :], b_xt[xs_][t], [psum[pbs[0]][:, :], psum[pbs[1]][:, :]],
                          [pbuf[pbs[0]], pbuf[pbs[1]]], ot[o][:], b_ot[o], work)
            P.dma(lambda e, o=o, dst=dst, t0=t0, t=t: e.dma_start(out=dst[t0 + t * 128:t0 + (t + 1) * 128, :], in_=ot[o][:]),
                  reads=[b_ot[o]])
    P.barrier()
    P.release()


class Rot:
    def __init__(self, P):
        self.P = P
        self.pi = 0
        self.ei = 0

    def bank(self):
        b = self.pi % 8
        self.pi += 1
        return b

    def evac(self, out_ap, in_ap, reads, writes, eng=None):
        P = self.P
        if eng is None:
            eng = ("act", "dve")[self.ei % 2]
            self.ei += 1
        if eng == "act":
            return P.op("act", lambda e: e.activation(out=out_ap, in_=in_ap, func=AF.Copy), reads=reads, writes=writes)
        return P.op(eng, lambda e: e.tensor_copy(out=out_ap, in_=in_ap), reads=reads, writes=writes)


def emit_hT(P, rot, m, r, xt, b_xts, ntile, hT, b_hT):
    g, psum, pbuf = P.g, P.psum, P.pbuf
    for kc in range(8):
        pb = rot.bank()
        for t in range(ntile):
            P.op("pe", lambda e, pb=pb, t=t, kc=kc: e.transpose(
                out=psum[pb][:, t * 128:(t + 1) * 128], in_=xt[:, t, kc * 128:(kc + 1) * 128], identity=g["ident"][:]),
                reads=[b_xts[t], g["b_ident"]], writes=[pbuf[pb]])
        P.op("act", lambda e, pb=pb, kc=kc: e.activation(
            out=hT[:, kc, 0:ntile * 128], in_=psum[pb][:, 0:ntile * 128], func=AF.Identity,
            scale=m["scl"][:, r, kc:kc + 1], bias=m["shT"][:, r, kc:kc + 1]),
            reads=[pbuf[pb], m["b_scl"], m["b_shT"]], writes=[b_hT])


def load_cast_rows(P, dst, b_dst, src_rows_fn, nchunk, width, stg, b_stg, n0=0):
    cast_engs = ["pool", "dve", "act"]
    n = n0
    for kc in range(nchunk):
        s = n % 2
        P.dma(lambda e, s=s, kc=kc: e.dma_start(out=stg[s][:, 0:width], in_=src_rows_fn(kc)), writes=[b_stg[s]])
        ce = cast_engs[n % 3]
        if ce == "act":
            P.op("act", lambda e, s=s, kc=kc: e.activation(out=dst[:, kc, :], in_=stg[s][:, 0:width], func=AF.Copy),
                 reads=[b_stg[s]], writes=[b_dst])
        else:
            P.op(ce, lambda e, s=s, kc=kc: e.tensor_copy(out=dst[:, kc, :], in_=stg[s][:, 0:width]),
                 reads=[b_stg[s]], writes=[b_dst])
        n += 1
    return n


def phase_mixer(P, l, dbg):
    nc, io, g = P.nc, P.io, P.g
    psum, pbuf = P.psum, P.pbuf
    rot = Rot(P)
    P.mark()
    m = phase_mod(P, l, 1, 1.0)
    P.mark()
    qT = P.sb([64, 4, NTOK], BF16, "qT")
    kT = P.sb([64, 2, NTOK], BF16, "kT")
    vx = P.sb([128, 34, 2, 66], BF16, "vx")
    b_qT = [Buf() for _ in range(9)]
    b_kT, b_vx = Buf(), Buf()
    P.op("pool", lambda e: e.memset(vx[:], 1.0), writes=[b_vx])
    P.mark()
    win = P.sb([128, 8, DIN], BF16, "win")
    b_win = Buf()
    P.mark()
    stg = [P.sb([128, DIN], F32, "stg%d" % i) for i in range(2)]
    b_stg = [Buf(), Buf()]
    load_cast_rows(P, win, b_win, lambda kc: io["w_in"].ap()[l, kc * 128:(kc + 1) * 128, :], 8, DIN, stg, b_stg)
    P.barrier()
    P.release()
    gq = P.sb([128, 384], F32, "gq")
    b_gq = Buf()
    for h in range(6):
        src = io["q_norm_g"] if h < 4 else io["k_norm_g"]
        P.dma(lambda e, h=h, src=src: e.dma_start(out=gq[:, h * 64:(h + 1) * 64], in_=src.ap()[l:l + 1, :].partition_broadcast(128)),
              writes=[b_gq])
    cos_t = P.sb([128, 32, 32], F32, "cos")
    sin_t = P.sb([128, 32, 32], F32, "sin")
    b_cs = Buf()
    P.dma(lambda e: e.dma_start(out=cos_t[:], in_=io["rope_cos"].ap().rearrange("(t p) i -> p t i", p=128)), writes=[b_cs])
    P.dma(lambda e: e.dma_start(out=sin_t[:], in_=io["rope_sin"].ap().rearrange("(t p) i -> p t i", p=128)), writes=[b_cs])
    dftc = P.sb([128, 128], BF16, "dftc")
    dfts = P.sb([128, 128], BF16, "dfts")
    b_dft = Buf()
    P.dma(lambda e: e.dma_start(out=dftc[:], in_=io["dftc"].ap()), writes=[b_dft])
    P.dma(lambda e: e.dma_start(out=dfts[:], in_=io["dfts"].ap()), writes=[b_dft])

    xt = [P.sb([128, 4, D], F32, "xt%d" % i) for i in range(2)]
    b_xt = [[Buf() for _ in range(4)] for _ in range(2)]
    hT = P.sb([128, 8, 512], BF16, "hT")
    b_hT = Buf()
    zat = [P.sb([128, 512], F32, "zat%d" % i) for i in range(2)]
    b_zat = [Buf(), Buf()]
    sq = P.sb([128, 384], F32, "sq")
    qk = P.sb([128, 384], F32, "qk")
    qkr = P.sb([128, 384], F32, "qkr")
    tA = P.sb([128, 192], F32, "tA")
    tB = P.sb([128, 192], F32, "tB")
    ss = P.sb([128, 8], F32, "ss")
    b_sq, b_qk, b_qkr, b_tA, b_tB, b_ss = Buf(), Buf(), Buf(), Buf(), Buf(), Buf()
    zst = [P.sb([64, 4, 512], F32, "zst%d" % i) for i in range(2)]
    b_zst = [Buf(), Buf()]
    zp = P.sb([128, 4, 256], BF16, "zp")
    b_zp = Buf()
    zfT = P.sb([128, 2, 512], BF16, "zfT")
    b_zfT = Buf()
    abst = P.sb([128, 4, 512], BF16, "abst")
    b_abst = Buf()

    xs_ap = io["xs"].ap()
    blocks = [(i * 512, 4, 0) for i in range(8)] + [(SEQ, 2, 1)]
    for bi, (t0, ntile, r) in enumerate(blocks):
        nb = ntile * 128
        xb = bi % 2
        for t in range(ntile):
            P.dma(lambda e, xb=xb, t=t, t0=t0: e.dma_start(out=xt[xb][:, t, :], in_=xs_ap[t0 + t * 128:t0 + (t + 1) * 128, :]),
                  writes=[b_xt[xb][t]])
        emit_hT(P, rot, m, r, xt[xb], b_xt[xb], ntile, hT, b_hT)
        for t in range(ntile):
            tile_idx = (t0 // 128) + t
            tok0 = t0 + t * 128
            pb = rot.bank()
            for kc in range(8):
                P.op("pe", lambda e, pb=pb, kc=kc, t=t: e.matmul(psum[pb][:, :], lhsT=hT[:, kc, t * 128:(t + 1) * 128], rhs=win[:, kc, 0:512],
                                                                 start=(kc == 0), stop=(kc == 7)),
                     reads=[b_hT, b_win], writes=[pbuf[pb]])
            z = (bi * 4 + t) % 2
            P.op("act", lambda e, pb=pb, z=z: e.activation(out=zat[z][:], in_=psum[pb][:, :], func=AF.Copy),
                 reads=[pbuf[pb]], writes=[b_zat[z]])
            P.op("dve", lambda e, z=z: e.tensor_tensor(out=sq[:], in0=zat[z][:, 0:384], in1=zat[z][:, 0:384], op=ALU.mult),
                 reads=[b_zat[z]], writes=[b_sq])
            P.op("dve", lambda e: e.tensor_reduce(out=ss[:, 0:6], in_=sq[:].rearrange("p (h d) -> p h d", d=64), axis=AX.X, op=ALU.add),
                 reads=[b_sq], writes=[b_ss])
            P.op("dve", lambda e: e.tensor_scalar(out=ss[:, 0:6], in0=ss[:, 0:6], scalar1=1.0 / 64, scalar2=1e-6, op0=ALU.mult, op1=ALU.add),
                 reads=[b_ss], writes=[b_ss])
            P.op("act", lambda e: e.activation(out=ss[:, 0:6], in_=ss[:, 0:6], func=AF.Sqrt), reads=[b_ss], writes=[b_ss])
            P.op("dve", lambda e: e.reciprocal(out=ss[:, 0:6], in_=ss[:, 0:6]), reads=[b_ss], writes=[b_ss])
            P.op("dve", lambda e, z=z: e.tensor_tensor(out=qk[:].rearrange("p (h d) -> p h d", d=64),
                                                       in0=zat[z][:, 0:384].rearrange("p (h d) -> p h d", d=64),
                                                       in1=ss[:, 0:6].unsqueeze(2).broadcast_to([128, 6, 64]), op=ALU.mult),
                 reads=[b_zat[z], b_ss], writes=[b_qk])
            if r == 0:
                P.op("pool", lambda e: e.tensor_tensor(out=qk[:], in0=qk[:], in1=gq[:], op=ALU.mult), reads=[b_qk, b_gq], writes=[b_qk])
                v4 = lambda tl: tl[:].rearrange("p (h i two) -> p h i two", h=6, i=32, two=2)
                x0, x1 = v4(qk)[:, :, :, 0], v4(qk)[:, :, :, 1]
                o0, o1 = v4(qkr)[:, :, :, 0], v4(qkr)[:, :, :, 1]
                cb = cos_t[:, tile_idx:tile_idx + 1, :].broadcast_to([128, 6, 32])
                sb_ = sin_t[:, tile_idx:tile_idx + 1, :].broadcast_to([128, 6, 32])
                a3 = lambda tl: tl[:].rearrange("p (h i) -> p h i", h=6)
                P.op("dve", lambda e, x0=x0, cb=cb: e.tensor_tensor(out=a3(tA), in0=x0, in1=cb, op=ALU.mult), reads=[b_qk, b_cs], writes=[b_tA])
                P.op("pool", lambda e, x1=x1, sb_=sb_: e.tensor_tensor(out=a3(tB), in0=x1, in1=sb_, op=ALU.mult), reads=[b_qk, b_cs], writes=[b_tB])
                P.op("dve", lambda e, o0=o0: e.tensor_tensor(out=o0, in0=a3(tA), in1=a3(tB), op=ALU.subtract), reads=[b_tA, b_tB], writes=[b_qkr])
                P.op("pool", lambda e, x0=x0, sb_=sb_: e.tensor_tensor(out=a3(tA), in0=x0, in1=sb_, op=ALU.mult), reads=[b_qk, b_cs], writes=[b_tA])
                P.op("dve", lambda e, x1=x1, cb=cb: e.tensor_tensor(out=a3(tB), in0=x1, in1=cb, op=ALU.mult), reads=[b_qk, b_cs], writes=[b_tB])
                P.op("pool", lambda e, o1=o1: e.tensor_tensor(out=o1, in0=a3(tA), in1=a3(tB), op=ALU.add), reads=[b_tA, b_tB], writes=[b_qkr])
            else:
                P.op("pool", lambda e: e.tensor_tensor(out=qkr[:], in0=qk[:], in1=gq[:], op=ALU.mult), reads=[b_qk, b_gq], writes=[b_qkr])
            pq = rot.bank()
            for h in range(4):
                P.op("pe", lambda e, pq=pq, h=h: e.transpose(out=psum[pq][0:64, h * 128:(h + 1) * 128], in_=qkr[:, h * 64:(h + 1) * 64],
                                                             identity=g["ident"][:]),
                     reads=[b_qkr, g["b_ident"]], writes=[pbuf[pq]])
            rot.evac(qT[:, :, tok0:tok0 + 128], psum[pq][0:64, :].rearrange("p (h t) -> p h t", h=4), [pbuf[pq]], [b_qT[bi]])
            pk = rot.bank()
            for h in range(2):
                P.op("pe", lambda e, pk=pk, h=h: e.transpose(out=psum[pk][0:64, h * 128:(h + 1) * 128], in_=qkr[:, (4 + h) * 64:(5 + h) * 64],
                                                             identity=g["ident"][:]),
                     reads=[b_qkr, g["b_ident"]], writes=[pbuf[pk]])
            rot.evac(kT[:, :, tok0:tok0 + 128], psum[pk][0:64, 0:256].rearrange("p (h t) -> p h t", h=2), [pbuf[pk]], [b_kT])
            P.op("pool", lambda e, z=z, tile_idx=tile_idx: e.tensor_copy(out=vx[:, tile_idx, :, 0:64],
                                                                         in_=zat[z][:, 384:512].rearrange("p (h d) -> p h d", h=2)),
                 reads=[b_zat[z]], writes=[b_vx])
        for kind in range(4):
            zs = kind % 2
            for h in range(4):
                pb = rot.bank()
                c0 = 512 + kind * 256 + h * 64
                for kc in range(8):
                    P.op("pe", lambda e, pb=pb, kc=kc, c0=c0, nb=nb: e.matmul(psum[pb][0:64, 0:nb], lhsT=win[:, kc, c0:c0 + 64], rhs=hT[:, kc, 0:nb],
                                                                            start=(kc == 0), stop=(kc == 7)),
                         reads=[b_hT, b_win], writes=[pbuf[pb]])
                rot.evac(zst[zs][:, h, 0:nb], psum[pb][0:64, 0:nb], [pbuf[pb]], [b_zst[zs]])
            P.dma(lambda e, zs=zs, kind=kind, t0=t0, nb=nb: e.dma_start(out=io["zrw"].ap()[kind, :, :, t0:t0 + nb], in_=zst[zs][:, :, 0:nb]),
                  reads=[b_zst[zs]])
        for t in range(ntile):
            pb = rot.bank()
            for kc in range(8):
                P.op("pe", lambda e, pb=pb, kc=kc, t=t: e.matmul(psum[pb][:, 0:256], lhsT=hT[:, kc, t * 128:(t + 1) * 128], rhs=win[:, kc, 1536:1792],
                                                                 start=(kc == 0), stop=(kc == 7)),
                     reads=[b_hT, b_win], writes=[pbuf[pb]])
            rot.evac(zp[:, t, :], psum[pb][:, 0:256], [pbuf[pb]], [b_zp])
        P.dma(lambda e, t0=t0, ntile=ntile, nb=nb: e.dma_start(out=io["zpool"].ap()[t0:t0 + nb, :].rearrange("(t p) c -> p t c", p=128),
                                                               in_=zp[:, 0:ntile, :]), reads=[b_zp])
        for c in range(2):
            pb = rot.bank()
            c0 = 1792 + c * 128
            for kc in range(8):
                P.op("pe", lambda e, pb=pb, kc=kc, c0=c0, nb=nb: e.matmul(psum[pb][:, 0:nb], lhsT=win[:, kc, c0:c0 + 128], rhs=hT[:, kc, 0:nb],
                                                                        start=(kc == 0), stop=(kc == 7)),
                     reads=[b_hT, b_win], writes=[pbuf[pb]])
            rot.evac(zfT[:, c, 0:nb], psum[pb][:, 0:nb], [pbuf[pb]], [b_zfT])
        for t in range(ntile):
            pb = rot.bank()
            for ab_i, mat in enumerate((dftc, dfts)):
                for c in range(2):
                    P.op("pe", lambda e, pb=pb, ab_i=ab_i, c=c, t=t, mat=mat: e.matmul(
                        psum[pb][:, ab_i * 256 + c * 128: ab_i * 256 + (c + 1) * 128], lhsT=zfT[:, c, t * 128:(t + 1) * 128], rhs=mat[:],
                        start=True, stop=True), reads=[b_zfT, b_dft], writes=[pbuf[pb]])
            rot.evac(abst[:, t, :], psum[pb][:, :], [pbuf[pb]], [b_abst])
        P.dma(lambda e, t0=t0, ntile=ntile, nb=nb: e.dma_start(out=io["ab"].ap()[t0:t0 + nb, :].rearrange("(t p) c -> p t c", p=128),
                                                               in_=abst[:, 0:ntile, :]), reads=[b_abst])
    P.barrier()
    P.release()
    if dbg == "mixA":
        P.release()
        P.release()
        return
    P.mark()
    pT = [P.sb([128, 512], BF16, "pT%d" % i) for i in range(3)]
    b_pT = [Buf(), Buf(), Buf()]
    rden = P.sb([128, 512], F32, "rden")
    b_rden = Buf()
    osb = P.sb([64, 512], F32, "osb")
    b_osb = Buf()
    ost = [P.sb([64, 512], BF16, "ost%d" % i) for i in range(2)]
    b_ost = [Buf(), Buf()]
    n_p = 0
    n_o = 0
    cat = io["cat"].ap()
    for bi, (t0, ntile, r) in enumerate(blocks):
        nb = ntile * 128
        key_tiles = list(range(34)) if r == 0 else [32, 33]
        for h in range(4):
            kvh = h // 2
            pacc = rot.bank()
            for ki, kt in enumerate(key_tiles):
                ps_ = rot.bank()
                while ps_ == pacc:
                    ps_ = rot.bank()
                P.op("pe", lambda e, ps_=ps_, kvh=kvh, kt=kt, h=h, t0=t0, nb=nb: e.matmul(
                    psum[ps_][:, 0:nb], lhsT=kT[:, kvh, kt * 128:(kt + 1) * 128], rhs=qT[:, h, t0:t0 + nb], start=True, stop=True),
                    reads=[b_kT, b_qT[bi]], writes=[pbuf[ps_]])
                pp = n_p % 3
                n_p += 1
                P.op("act", lambda e, ps_=ps_, pp=pp, nb=nb: e.activation(out=pT[pp][:, 0:nb], in_=psum[ps_][:, 0:nb], func=AF.Exp, scale=0.125),
                     reads=[pbuf[ps_]], writes=[b_pT[pp]])
                P.op("pe", lambda e, pacc=pacc, kt=kt, kvh=kvh, pp=pp, nb=nb, ki=ki, nk=len(key_tiles): e.matmul(
                    psum[pacc][0:65, 0:nb], lhsT=vx[:, kt, kvh, 0:65], rhs=pT[pp][:, 0:nb], start=(ki == 0), stop=(ki == nk - 1)),
                    reads=[b_vx, b_pT[pp]], writes=[pbuf[pacc]])
            P.op("dve", lambda e, pacc=pacc, nb=nb: e.reciprocal(out=rden[64:65, 0:nb], in_=psum[pacc][64:65, 0:nb]),
                 reads=[pbuf[pacc]], writes=[b_rden])
            P.op("act", lambda e, pacc=pacc, nb=nb: e.activation(out=osb[:, 0:nb], in_=psum[pacc][0:64, 0:nb], func=AF.Copy),
                 reads=[pbuf[pacc]], writes=[b_osb])
            pbc = rot.bank()
            P.op("pe", lambda e, pbc=pbc, nb=nb: e.matmul(psum[pbc][0:64, 0:nb], lhsT=g["ones"][64:65, 0:64], rhs=rden[64:65, 0:nb],
                                                          start=True, stop=True),
                 reads=[g["b_ones"], b_rden], writes=[pbuf[pbc]])
            oo = n_o % 2
            n_o += 1
            P.op("dve", lambda e, pbc=pbc, oo=oo, nb=nb: e.tensor_tensor(out=ost[oo][:, 0:nb], in0=psum[pbc][0:64, 0:nb], in1=osb[:, 0:nb], op=ALU.mult),
                 reads=[pbuf[pbc], b_osb], writes=[b_ost[oo]])
            P.dma(lambda e, oo=oo, h=h, t0=t0, nb=nb: e.dma_start(out=cat[h * 64:(h + 1) * 64, t0:t0 + nb], in_=ost[oo][:, 0:nb]),
                  reads=[b_ost[oo]])
    P.barrier()
    P.release()
    P.release()
    if dbg == "att":
        P.release()
        return
    phase_rwkv_prep(P, l)
    if dbg == "rw1":
        P.release()
        return
    phase_rwkv_scan(P, l, nsteps=(int(dbg[3:]) if dbg and dbg.startswith("rws") else 34))
    if dbg and dbg.startswith("rws"):
        P.release()
        return
    phase_rwkv_fin(P, l)
    if dbg == "rw3":
        P.release()
        return
    phase_pool(P, l)
    phase_fourier(P, l)
    if dbg == "pf":
        P.release()
        return
    phase_wout(P, l, m)
    P.release()


def phase_pool(P, l):
    nc, io, g = P.nc, P.io, P.g
    psum, pbuf = P.psum, P.pbuf
    rot = Rot(P)
    P.mark()
    zp = P.sb([128, 34, 256], BF16, "zp_all")
    b_zp = Buf()
    P.dma(lambda e: e.dma_start(out=zp[:], in_=io["zpool"].ap().rearrange("(t p) c -> p t c", p=128)), writes=[b_zp])
    pm = P.sb([128, 20, 128], BF16, "poolm")
    b_pm = Buf()
    P.dma(lambda e: e.dma_start(out=pm[:], in_=io["poolm"].ap().rearrange("k s t -> s k t")), writes=[b_pm])
    pwf = P.sb([64, 4, 64], F32, "pwf")
    pw = P.sb([64, 4, 64], BF16, "pw")
    b_pwf, b_pw = Buf(), Buf()
    P.dma(lambda e: e.dma_start(out=pwf[:], in_=io["pool_w"].ap()[l].rearrange("g c d -> c g d")), writes=[b_pwf])
    P.op("dve", lambda e: e.tensor_copy(out=pw[:], in_=pwf[:]), reads=[b_pwf], writes=[b_pw])
    psc = P.sb([64, 4], F32, "psc")
    b_psc = Buf()
    P.dma(lambda e: e.dma_start(out=psc[:], in_=io["pool_scale"].ap()[l, :].rearrange("(g d) -> d g", d=64),
                                allow_slow_non_contiguous=True), writes=[b_psc])
    pooled = [P.sb([64, 512], BF16, "pooled%d" % i) for i in range(2)]
    b_pooled = [Buf(), Buf()]
    ost = [P.sb([64, 512], BF16, "post%d" % i) for i in range(2)]
    b_ost = [Buf(), Buf()]
    cat = io["cat"].ap()
    n = 0
    seqs = [(0, 32), (32, 2)]
    for (tile0, nt) in seqs:
        for j0 in range(0, nt, 4):
            ntile = min(4, nt - j0)
            nb = ntile * 128
            for gi in range(4):
                pb = rot.bank()
                for jj in range(ntile):
                    j = j0 + jj
                    terms = []
                    if j > 0:
                        terms.append((tile0 + j - 1, 0))
                    terms.append((tile0 + j, 3 if j == 0 else (4 if j == nt - 1 else 2)))
                    if j < nt - 1:
                        terms.append((tile0 + j + 1, 1))
                    for ti, (st, kind) in enumerate(terms):
                        P.op("pe", lambda e, pb=pb, jj=jj, st=st, kind=kind, gi=gi, ti=ti, nterm=len(terms): e.matmul(
                            psum[pb][0:64, jj * 128:(jj + 1) * 128], lhsT=zp[:, st, gi * 64:(gi + 1) * 64], rhs=pm[:, gi * 5 + kind, :],
                            start=(ti == 0), stop=(ti == nterm - 1)), reads=[b_zp, b_pm], writes=[pbuf[pb]])
                k = n % 2
                n += 1
                rot.evac(pooled[k][:, 0:nb], psum[pb][0:64, 0:nb], [pbuf[pb]], [b_pooled[k]])
                pb2 = rot.bank()
                P.op("pe", lambda e, pb2=pb2, gi=gi, k=k, nb=nb: e.matmul(psum[pb2][0:64, 0:nb], lhsT=pw[:, gi, :], rhs=pooled[k][:, 0:nb],
                                                                        start=True, stop=True), reads=[b_pw, b_pooled[k]], writes=[pbuf[pb2]])
                P.op("act", lambda e, pb2=pb2, gi=gi, k=k, nb=nb: e.activation(out=ost[k][:, 0:nb], in_=psum[pb2][0:64, 0:nb], func=AF.Copy,
                                                                              scale=psc[:, gi:gi + 1]),
                     reads=[pbuf[pb2], b_psc], writes=[b_ost[k]])
                tok0 = (tile0 + j0) * 128
                P.dma(lambda e, k=k, gi=gi, tok0=tok0, nb=nb: e.dma_start(out=cat[512 + gi * 64:512 + (gi + 1) * 64, tok0:tok0 + nb],
                                                                       in_=ost[k][:, 0:nb]), reads=[b_ost[k]])
    P.barrier()
    P.release()


def phase_fourier(P, l):
    nc, io, g = P.nc, P.io, P.g
    psum, pbuf = P.psum, P.pbuf
    rot = Rot(P)
    P.mark()
    ab = P.sb([128, 34, 512], BF16, "ab_all")
    b_ab = Buf()
    for q in range(2):
        P.dma(lambda e, q=q: e.dma_start(out=ab[:, q * 17:(q + 1) * 17, :],
                                         in_=io["ab"].ap()[q * 17 * 128:(q + 1) * 17 * 128, :].rearrange("(t p) c -> p t c", p=128)),
              writes=[b_ab])
    fwf = P.sb([128, 2, 256], F32, "fwf")
    fw = P.sb([128, 2, 256], BF16, "fw")
    b_fwf, b_fw = Buf(), Buf()
    P.dma(lambda e: e.dma_start(out=fwf[:], in_=io["fourier_w"].ap()[l].rearrange("(c p) d -> p c d", p=128)), writes=[b_fwf])
    P.op("dve", lambda e: e.tensor_copy(out=fw[:], in_=fwf[:]), reads=[b_fwf], writes=[b_fw])
    dm = [[P.sb([128, 32, 512], BF16, "dm%d_%d" % (i, j)) for j in range(2)] for i in range(2)]
    b_dm = [[Buf(), Buf()], [Buf(), Buf()]]
    fT = [P.sb([128, 2, 512], BF16, "fT%d" % i) for i in range(2)]
    b_fT = [[Buf(), Buf()], [Buf(), Buf()]]
    ost = [P.sb([128, 512], BF16, "fost%d" % i) for i in range(2)]
    b_ost = [Buf(), Buf()]
    cat = io["cat"].ap()
    jobs = [(0, 32, tb * 512, 512, "dft_lat") for tb in range(8)] + [(32, 2, 0, 256, "dft_ctx")]
    n_o = 0
    for ji, (tile0, nt, c0, wd, mname) in enumerate(jobs):
        bsel = ji % 2
        for cs in range(2):
            P.dma(lambda e, bsel=bsel, cs=cs, nt=nt, c0=c0, wd=wd, mname=mname: e.dma_start(
                out=dm[bsel][cs][:, 0:nt, 0:wd], in_=io[mname].ap()[cs, :, c0:c0 + wd].rearrange("(t p) n -> p t n", p=128)),
                writes=[b_dm[bsel][cs]])
        for c in range(2):
            pb = rot.bank()
            for cs in range(2):
                for t in range(nt):
                    P.op("pe", lambda e, pb=pb, cs=cs, t=t, c=c, bsel=bsel, wd=wd, tile0=tile0, nt=nt: e.matmul(
                        psum[pb][:, 0:wd], lhsT=ab[:, tile0 + t, cs * 256 + c * 128: cs * 256 + (c + 1) * 128], rhs=dm[bsel][cs][:, t, 0:wd],
                        start=(cs == 0 and t == 0), stop=(cs == 1 and t == nt - 1)),
                        reads=[b_ab, b_dm[bsel][cs]], writes=[pbuf[pb]])
            rot.evac(fT[bsel][:, c, 0:wd], psum[pb][:, 0:wd], [pbuf[pb]], [b_fT[bsel][c]])
        for dc in range(2):
            pb = rot.bank()
            for c in range(2):
                P.op("pe", lambda e, pb=pb, c=c, dc=dc, bsel=bsel, wd=wd: e.matmul(
                    psum[pb][:, 0:wd], lhsT=fw[:, c, dc * 128:(dc + 1) * 128], rhs=fT[bsel][:, c, 0:wd], start=(c == 0), stop=(c == 1)),
                    reads=[b_fw, b_fT[bsel][c]], writes=[pbuf[pb]])
            k = n_o % 2
            n_o += 1
            rot.evac(ost[k][:, 0:wd], psum[pb][:, 0:wd], [pbuf[pb]], [b_ost[k]])
            tok0 = tile0 * 128 + c0
            P.dma(lambda e, k=k, dc=dc, tok0=tok0, wd=wd: e.dma_start(out=cat[768 + dc * 128:768 + (dc + 1) * 128, tok0:tok0 + wd],
                                                                   in_=ost[k][:, 0:wd]), reads=[b_ost[k]])
    P.barrier()
    P.release()


def phase_wout(P, l, m):
    nc, io, g = P.nc, P.io, P.g
    psum, pbuf = P.psum, P.pbuf
    rot = Rot(P)
    P.mark()
    wo = P.sb([128, 8, D], BF16, "wo")
    b_wo = Buf()
    P.mark()
    stg = [P.sb([128, D], F32, "stg%d" % i) for i in range(2)]
    b_stg = [Buf(), Buf()]
    load_cast_rows(P, wo, b_wo, lambda kc: io["w_out"].ap()[l, kc * 128:(kc + 1) * 128, :], 8, D, stg, b_stg)
    P.barrier()
    P.release()
    ct = [P.sb([128, 8, 512], BF16, "ct%d" % i) for i in range(2)]
    b_ct = [Buf(), Buf()]
    xt = [P.sb([128, D], F32, "xt%d" % i) for i in range(2)]
    b_xt = [Buf(), Buf()]
    ot = [P.sb([128, D], F32, "ot%d" % i) for i in range(2)]
    b_ot = [Buf(), Buf()]
    work = dict(u=P.sb([128, D], F32, "u"), b_u=Buf(), st=P.sb([128, 16], F32, "st"), b_st=Buf())
    cat = io["cat"].ap()
    xs_ap = io["xs"].ap()
    blocks = [(i * 512, 4, 0) for i in range(8)] + [(SEQ, 2, 1)]
    n = 0
    for bi, (t0, ntile, r) in enumerate(blocks):
        nb = ntile * 128
        cb = bi % 2
        P.dma(lambda e, cb=cb, t0=t0, nb=nb: e.dma_start(out=ct[cb][:, :, 0:nb], in_=cat[:, t0:t0 + nb].rearrange("(c p) t -> p c t", p=128)),
              writes=[b_ct[cb]])
        for t in range(ntile):
            k = n % 2
            n += 1
            tok0 = t0 + t * 128
            P.dma(lambda e, k=k, tok0=tok0: e.dma_start(out=xt[k][:], in_=xs_ap[tok0:tok0 + 128, :]), writes=[b_xt[k]])
            pbs = []
            for hn in range(2):
                pb = rot.bank()
                pbs.append(pb)
                for c in range(8):
                    P.op("pe", lambda e, pb=pb, c=c, hn=hn, cb=cb, t=t: e.matmul(
                        psum[pb][:, :], lhsT=ct[cb][:, c, t * 128:(t + 1) * 128], rhs=wo[:, c, hn * 512:(hn + 1) * 512],
                        start=(c == 0), stop=(c == 7)), reads=[b_ct[cb], b_wo], writes=[pbuf[pb]])
            emit_postnorm(P, m, r, xt[k][:], b_xt[k], [psum[pbs[0]][:, :], psum[pbs[1]][:, :]], [pbuf[pbs[0]], pbuf[pbs[1]]],
                          ot[k][:], b_ot[k], work)
            P.dma(lambda e, k=k, tok0=tok0: e.dma_start(out=xs_ap[tok0:tok0 + 128, :], in_=ot[k][:]), reads=[b_ot[k]])
    P.barrier()
    P.release()


LOGDECAY_SCALE = -0.6065306597126334
GN_EPS = 64e-5
CHUNK_ORDER = {0: [32, 33] + list(range(32)), 1: [33, 32] + list(range(31, -1, -1))}
RW_SEQS = [(i * 512, 512, i == 0, i == 7) for i in range(8)] + [(SEQ, 256, True, True)]


def col_param(P, src_ap_1d, name, n=4):
    t = P.sb([64, n], F32, name)
    b = Buf()
    P.dma(lambda e: e.dma_start(out=t[:], in_=src_ap_1d.rearrange("(h d) -> d h", d=64), allow_slow_non_contiguous=True), writes=[b])
    return t, b


def load_halo(P, buf_ap_fn, b_buf, src_fn, t0, nb, first, last):
    lo = 0 if not first else 1
    hi = nb + 2 if not last else nb + 1
    if first:
        P.op("pool", lambda e: e.memset(buf_ap_fn(0, 1), 0.0), writes=[b_buf])
    if last:
        P.op("pool", lambda e: e.memset(buf_ap_fn(nb + 1, nb + 2), 0.0), writes=[b_buf])
    P.dma(lambda e: e.dma_start(out=buf_ap_fn(lo, hi), in_=src_fn(t0 - 1 + lo, t0 - 1 + hi)), writes=[b_buf])


def phase_rwkv_prep(P, l):
    nc, io, g = P.nc, P.io, P.g
    psum, pbuf = P.psum, P.pbuf
    rot = Rot(P)
    P.mark()
    mu = [col_param(P, io["rwkv_mu"].ap()[l, i, :], "mu%d" % i) for i in range(6)]
    w0 = [col_param(P, io["decay_w0"].ap()[l, d, :], "w0%d" % d) for d in range(2)]
    a0 = [col_param(P, io["icl_a0"].ap()[l, d, :], "a0%d" % d) for d in range(2)]
    k_k = col_param(P, io["k_k"].ap()[l, :], "k_k")
    k_a = col_param(P, io["k_a"].ap()[l, :], "k_a")
    r_k = col_param(P, io["r_k"].ap()[l].rearrange("h d -> (h d)"), "r_k")
    hm, om = [], []
    for i in range(3):
        t1 = P.sb([64, 4], F32, "hm%d" % i)
        t2 = P.sb([64, 4], F32, "om%d" % i)
        b1, b2 = Buf(), Buf()
        P.op("dve", lambda e, i=i, t1=t1: e.tensor_scalar(out=t1[:], in0=mu[i][0][:], scalar1=0.5, scalar2=None, op0=ALU.mult),
             reads=[mu[i][1]], writes=[b1])
        P.op("dve", lambda e, i=i, t2=t2: e.tensor_scalar(out=t2[:], in0=mu[i][0][:], scalar1=-1.0, scalar2=1.0, op0=ALU.mult, op1=ALU.add),
             reads=[mu[i][1]], writes=[b2])
        hm.append((t1, b1))
        om.append((t2, b2))
    omka = P.sb([64, 4], F32, "omka")
    b_omka = Buf()
    P.op("dve", lambda e: e.tensor_scalar(out=omka[:], in0=k_a[0][:], scalar1=-1.0, scalar2=1.0, op0=ALU.mult, op1=ALU.add),
         reads=[k_a[1]], writes=[b_omka])

    def lora_in(src, name, rank):
        tf = P.sb([64, 4, rank], F32, name + "f")
        tb = P.sb([64, 4, rank], BF16, name)
        bf_, bb = Buf(), Buf()
        P.dma(lambda e: e.dma_start(out=tf[:], in_=src.rearrange("(h d) r -> d h r", d=64)), writes=[bf_])
        P.op("dve", lambda e: e.tensor_copy(out=tb[:], in_=tf[:]), reads=[bf_], writes=[bb])
        return tb, bb

    def lora_out(src, name, rank):
        tf = P.sb([rank, 256], F32, name + "f")
        tb = P.sb([rank, 256], BF16, name)
        bf_, bb = Buf(), Buf()
        P.dma(lambda e: e.dma_start(out=tf[:], in_=src), writes=[bf_])
        P.op("dve", lambda e: e.tensor_copy(out=tb[:], in_=tf[:]), reads=[bf_], writes=[bb])
        return tb, bb

    W1 = [lora_in(io["decay_w1"].ap()[l, d], "W1_%d" % d, 32) for d in range(2)]
    A1 = [lora_in(io["icl_a1"].ap()[l, d], "A1_%d" % d, 32) for d in range(2)]
    G1 = lora_in(io["gate_g1"].ap()[l], "G1", 64)
    W2 = [lora_out(io["decay_w2"].ap()[l, d], "W2_%d" % d, 32) for d in range(2)]
    A2 = [lora_out(io["icl_a2"].ap()[l, d], "A2_%d" % d, 32) for d in range(2)]
    G2 = lora_out(io["gate_g2"].ap()[l], "G2", 64)
    tw = P.sb([32, 2, NTOK], BF16, "tw")
    ta = P.sb([32, 2, NTOK], BF16, "ta")
    tg = P.sb([64, NTOK], BF16, "tg")
    b_tw, b_ta, b_tg = [Buf(), Buf()], [Buf(), Buf()], Buf()
    etot = P.sb([64, 8, 34], F32, "etot")
    b_etot = Buf()
    zrw = io["zrw"].ap()
    P.mark()
    zu = [P.sb([64, 4, 514], F32, "zu%d" % i) for i in range(2)]
    b_zu = [Buf(), Buf()]
    ssum = P.sb([64, 4, 512], F32, "ssum")
    du = P.sb([64, 4, 512], F32, "du")
    b_ssum, b_du = Buf(), Buf()
    xq = [P.sb([64, 4, 512], BF16, "xq%d" % i) for i in range(3)]
    b_xq = [Buf(), Buf(), Buf()]
    for bi, (t0, nb, first, last) in enumerate(RW_SEQS):
        z = bi % 2
        load_halo(P, lambda a, b, z=z: zu[z][:, :, a:b], b_zu[z], lambda a, b: zrw[3, :, :, a:b], t0, nb, first, last)
        P.op("dve", lambda e, z=z, nb=nb: e.tensor_tensor(out=ssum[:, :, 0:nb], in0=zu[z][:, :, 0:nb], in1=zu[z][:, :, 2:nb + 2], op=ALU.add),
             reads=[b_zu[z]], writes=[b_ssum])
        P.op("dve", lambda e, z=z, nb=nb: e.scalar_tensor_tensor(out=du[:, :, 0:nb], in0=ssum[:, :, 0:nb], scalar=0.5, in1=zu[z][:, :, 1:nb + 1],
                                                                 op0=ALU.mult, op1=ALU.subtract), reads=[b_ssum, b_zu[z]], writes=[b_du])
        for j in range(3):
            for h in range(4):
                eng = "dve" if (j * 4 + h) % 2 == 0 else "pool"
                if eng == "dve":
                    P.op("dve", lambda e, j=j, h=h, z=z, nb=nb: e.scalar_tensor_tensor(
                        out=xq[j][:, h, 0:nb], in0=du[:, h, 0:nb], scalar=mu[3 + j][0][:, h:h + 1], in1=zu[z][:, h, 1:nb + 1],
                        op0=ALU.mult, op1=ALU.add), reads=[b_du, b_zu[z], mu[3 + j][1]], writes=[b_xq[j]])
                else:
                    P.op("pool", lambda e, j=j, h=h, nb=nb: e.tensor_scalar(
                        out=xq[j][:, h, 0:nb], in0=du[:, h, 0:nb], scalar1=mu[3 + j][0][:, h:h + 1], scalar2=None, op0=ALU.mult),
                        reads=[b_du, mu[3 + j][1]], writes=[b_xq[j]])
                    P.op("pool", lambda e, j=j, h=h, z=z, nb=nb: e.tensor_tensor(
                        out=xq[j][:, h, 0:nb], in0=xq[j][:, h, 0:nb], in1=zu[z][:, h, 1:nb + 1], op=ALU.add),
                        reads=[b_xq[j], b_zu[z]], writes=[b_xq[j]])
        jobs = [(0, W1[0], 32, tw[:, 0, t0:t0 + nb], b_tw[0], AF.Tanh), (0, W1[1], 32, tw[:, 1, t0:t0 + nb], b_tw[1], AF.Tanh),
                (1, A1[0], 32, ta[:, 0, t0:t0 + nb], b_ta[0], AF.Copy), (1, A1[1], 32, ta[:, 1, t0:t0 + nb], b_ta[1], AF.Copy),
                (2, G1, 64, tg[:, t0:t0 + nb], b_tg, AF.Sigmoid)]
        for (j, wt, rank, dst, b_dst, fn) in jobs:
            pb = rot.bank()
            for h in range(4):
                P.op("pe", lambda e, pb=pb, h=h, j=j, wt=wt, rank=rank, nb=nb: e.matmul(
                    psum[pb][0:rank, 0:nb], lhsT=wt[0][:, h, :], rhs=xq[j][:, h, 0:nb], start=(h == 0), stop=(h == 3)),
                    reads=[wt[1], b_xq[j]], writes=[pbuf[pb]])
            P.op("act", lambda e, pb=pb, rank=rank, nb=nb, dst=dst, fn=fn: e.activation(out=dst, in_=psum[pb][0:rank, 0:nb], func=fn),
                 reads=[pbuf[pb]], writes=[b_dst])
    P.barrier()
    P.release()
    P.mark()
    z3 = [P.sb([64, 3, 514], F32, "z3_%d" % i) for i in range(2)]
    b_z3 = [Buf(), Buf()]
    s3 = P.sb([64, 3, 512], F32, "s3")
    b_s3 = Buf()
    rk = P.sb([64, 2, 512], F32, "rk")
    b_r, b_k = Buf(), Buf()
    Fs = [[P.sb([64, 5, 512], F32, "F%d_%d" % (i, d)) for d in range(2)] for i in range(2)]
    b_F = [[Buf(), Buf()], [Buf(), Buf()]]
    kkr = P.sb([64, 512], F32, "kkr")
    kk = P.sb([64, 512], F32, "kk")
    sqt = P.sb([64, 512], F32, "sqt")
    nrm = P.sb([64, 512], F32, "nrm")
    b_kkr, b_kk, b_sqt, b_nrm = Buf(), Buf(), Buf(), Buf()
    lw = P.sb([64, 512], F32, "lw")
    cl = P.sb([64, 512], F32, "cl")
    ci = P.sb([64, 512], F32, "ci")
    cml = P.sb([64, 512], F32, "cml")
    einc = P.sb([64, 512], F32, "einc")
    eexc = P.sb([64, 512], F32, "eexc")
    einv = P.sb([64, 512], F32, "einv")
    av = P.sb([64, 512], F32, "av")
    tt = P.sb([64, 512], F32, "tt")
    kd = P.sb([64, 512], F32, "kd")
    kds = P.sb([64, 512], F32, "kds")
    tmpb = P.sb([64, 512], F32, "tmpb")
    tot = P.sb([64, 4], F32, "tot")
    b_lw, b_cl, b_ci, b_cml, b_einc, b_eexc, b_einv, b_av, b_tt, b_kd, b_kds, b_tmpb, b_tot = [Buf() for _ in range(13)]
    aux = [P.sb([64, 2, 512], F32, "aux%d" % i) for i in range(2)]
    b_aux = [Buf(), Buf()]
    rwt = io["rwt"].ap()
    def head_block(h, t0, nb, first, last, n_it):
        if True:
            hs = slice(h, h + 1)
            nch = nb // 128
            z = n_it % 2
            fi = n_it % 2
            n_it += 1
            for i in range(3):
                load_halo(P, lambda a, b, z=z, i=i: z3[z][:, i, a:b], b_z3[z], lambda a, b, i=i: zrw[i, :, h, a:b], t0, nb, first, last)
            P.op("dve", lambda e, z=z, nb=nb: e.tensor_tensor(out=s3[:, :, 0:nb], in0=z3[z][:, :, 0:nb], in1=z3[z][:, :, 2:nb + 2], op=ALU.add),
                 reads=[b_z3[z]], writes=[b_s3])
            for i in range(3):
                P.op("pool", lambda e, i=i, nb=nb: e.tensor_scalar(out=s3[:, i, 0:nb], in0=s3[:, i, 0:nb], scalar1=hm[i][0][:, hs], scalar2=None,
                                                                   op0=ALU.mult), reads=[b_s3, hm[i][1]], writes=[b_s3])
            dsts = [(rk[:, 0, 0:nb], b_r), (rk[:, 1, 0:nb], b_k), (Fs[fi][0][:, 4, 0:nb], b_F[fi][0])]
            for i in range(3):
                P.op("dve", lambda e, i=i, z=z, nb=nb, dst=dsts[i][0]: e.scalar_tensor_tensor(
                    out=dst, in0=z3[z][:, i, 1:nb + 1], scalar=om[i][0][:, hs], in1=s3[:, i, 0:nb], op0=ALU.mult, op1=ALU.add),
                    reads=[b_z3[z], b_s3, om[i][1]], writes=[dsts[i][1]])
            r_ap, k_ap, v_ap = rk[:, 0, 0:nb], rk[:, 1, 0:nb], Fs[fi][0][:, 4, 0:nb]
            b_v = b_F[fi][0]
            P.op("pool", lambda e, nb=nb, fi=fi, v_ap=v_ap: e.tensor_copy(out=Fs[fi][1][:, 4, 0:nb], in_=v_ap), reads=[b_v], writes=[b_F[fi][1]])
            P.op("dve", lambda e, nb=nb, k_ap=k_ap: e.tensor_scalar(out=kkr[:, 0:nb], in0=k_ap, scalar1=k_k[0][:, hs], scalar2=None, op0=ALU.mult),
                 reads=[b_k, k_k[1]], writes=[b_kkr])
            P.op("pool", lambda e, nb=nb: e.tensor_tensor(out=sqt[:, 0:nb], in0=kkr[:, 0:nb], in1=kkr[:, 0:nb], op=ALU.mult), reads=[b_kkr], writes=[b_sqt])
            pb = rot.bank()
            P.op("pe", lambda e, pb=pb, nb=nb: e.matmul(psum[pb][0:64, 0:nb], lhsT=g["ones"][0:64, 0:64], rhs=sqt[:, 0:nb], start=True, stop=True),
                 reads=[g["b_ones"], b_sqt], writes=[pbuf[pb]])
            P.op("act", lambda e, pb=pb, nb=nb: e.activation(out=nrm[:, 0:nb], in_=psum[pb][0:64, 0:nb], func=AF.Sqrt), reads=[pbuf[pb]], writes=[b_nrm])
            P.op("dve", lambda e, nb=nb: e.tensor_scalar(out=nrm[:, 0:nb], in0=nrm[:, 0:nb], scalar1=1e-12, scalar2=None, op0=ALU.max),
                 reads=[b_nrm], writes=[b_nrm])
            P.op("dve", lambda e, nb=nb: e.reciprocal(out=nrm[:, 0:nb], in_=nrm[:, 0:nb]), reads=[b_nrm], writes=[b_nrm])
            P.op("dve", lambda e, nb=nb: e.tensor_tensor(out=kk[:, 0:nb], in0=kkr[:, 0:nb], in1=nrm[:, 0:nb], op=ALU.mult),
                 reads=[b_kkr, b_nrm], writes=[b_kk])
            def dir_part(d):
                s_id = h * 2 + d
                F = Fs[fi][d]
                bF = b_F[fi][d]
                pb = rot.bank()
                P.op("pe", lambda e, pb=pb, d=d, nb=nb: e.matmul(psum[pb][0:64, 0:nb], lhsT=W2[d][0][:, h * 64:(h + 1) * 64], rhs=tw[:, d, t0:t0 + nb],
                                                                 start=True, stop=True), reads=[W2[d][1], b_tw[d]], writes=[pbuf[pb]])
                P.op("act", lambda e, pb=pb, d=d, nb=nb: e.activation(out=lw[:, 0:nb], in_=psum[pb][0:64, 0:nb], func=AF.Sigmoid, bias=w0[d][0][:, hs]),
                     reads=[pbuf[pb], w0[d][1]], writes=[b_lw])
                P.op("pool", lambda e, nb=nb: e.tensor_scalar(out=lw[:, 0:nb], in0=lw[:, 0:nb], scalar1=LOGDECAY_SCALE, scalar2=None, op0=ALU.mult),
                     reads=[b_lw], writes=[b_lw])
                for j in range(nch):
                    P.op("dve", lambda e, j=j: e.tensor_tensor_scan(out=cl[:, j * 128:(j + 1) * 128], data0=g["ones"][0:64, 0:128],
                                                                   data1=lw[:, j * 128:(j + 1) * 128], initial=0.0, op0=ALU.mult, op1=ALU.add),
                         reads=[b_lw, g["b_ones"]], writes=[b_cl])
                clv = cl[:, 0:nb].rearrange("p (c j) -> p c j", j=128)
                P.op("dve", lambda e, nch=nch, clv=clv: e.tensor_copy(out=tot[:, 0:nch], in_=clv[:, :, 127]), reads=[b_cl], writes=[b_tot])
                c0 = t0 // 128
                P.op("act", lambda e, nch=nch, s_id=s_id, c0=c0: e.activation(out=etot[:, s_id, c0:c0 + nch], in_=tot[:, 0:nch], func=AF.Exp),
                     reads=[b_tot], writes=[b_etot])
                if d == 0:
                    ci_ap, b_cix = cl, b_cl
                else:
                    P.op("dve", lambda e, nch=nch, nb=nb, clv=clv: e.tensor_tensor(
                        out=ci[:, 0:nb].rearrange("p (c j) -> p c j", j=128), in0=tot[:, 0:nch].unsqueeze(2).broadcast_to([64, nch, 128]),
                        in1=clv, op=ALU.subtract), reads=[b_tot, b_cl], writes=[b_ci])
                    P.op("pool", lambda e, nb=nb: e.tensor_tensor(out=ci[:, 0:nb], in0=ci[:, 0:nb], in1=lw[:, 0:nb], op=ALU.add),
                         reads=[b_ci, b_lw], writes=[b_ci])
                    ci_ap, b_cix = ci, b_ci
                P.op("pool", lambda e, nb=nb, ci_ap=ci_ap: e.tensor_tensor(out=cml[:, 0:nb], in0=ci_ap[:, 0:nb], in1=lw[:, 0:nb], op=ALU.subtract),
                     reads=[b_cix, b_lw], writes=[b_cml])
                P.op("act", lambda e, nb=nb, ci_ap=ci_ap: e.activation(out=einc[:, 0:nb], in_=ci_ap[:, 0:nb], func=AF.Exp), reads=[b_cix], writes=[b_einc])
                P.op("act", lambda e, nb=nb, ci_ap=ci_ap: e.activation(out=einv[:, 0:nb], in_=ci_ap[:, 0:nb], func=AF.Exp, scale=-1.0),
                     reads=[b_cix], writes=[b_einv])
                P.op("act", lambda e, nb=nb: e.activation(out=eexc[:, 0:nb], in_=cml[:, 0:nb], func=AF.Exp), reads=[b_cml], writes=[b_eexc])
                pb = rot.bank()
                P.op("pe", lambda e, pb=pb, d=d, nb=nb: e.matmul(psum[pb][0:64, 0:nb], lhsT=A2[d][0][:, h * 64:(h + 1) * 64], rhs=ta[:, d, t0:t0 + nb],
                                                                 start=True, stop=True), reads=[A2[d][1], b_ta[d]], writes=[pbuf[pb]])
                P.op("act", lambda e, pb=pb, d=d, nb=nb: e.activation(out=av[:, 0:nb], in_=psum[pb][0:64, 0:nb], func=AF.Sigmoid, bias=a0[d][0][:, hs]),
                     reads=[pbuf[pb], a0[d][1]], writes=[b_av])
                P.op("dve", lambda e, nb=nb: e.tensor_scalar(out=tt[:, 0:nb], in0=av[:, 0:nb], scalar1=k_a[0][:, hs], scalar2=omka[:, hs],
                                                             op0=ALU.mult, op1=ALU.add), reads=[b_av, k_a[1], b_omka], writes=[b_tt])
                P.op("dve", lambda e, nb=nb, k_ap=k_ap: e.tensor_tensor(out=kd[:, 0:nb], in0=k_ap, in1=tt[:, 0:nb], op=ALU.mult),
                     reads=[b_k, b_tt], writes=[b_kd])
                if d == 0:
                    P.op("pool", lambda e, nb=nb: e.tensor_copy(out=kds[:, 0:nb], in_=kd[:, 0:nb]), reads=[b_kd], writes=[b_kds])
                else:
                    P.op("pool", lambda e, nb=nb: e.tensor_tensor(out=kds[:, 0:nb], in0=kds[:, 0:nb], in1=kd[:, 0:nb], op=ALU.add),
                         reads=[b_kd, b_kds], writes=[b_kds])
                P.op("dve", lambda e, nb=nb, F=F: e.scalar_tensor_tensor(out=F[:, 0, 0:nb], in0=kk[:, 0:nb], scalar=-1.0, in1=eexc[:, 0:nb],
                                                                         op0=ALU.mult, op1=ALU.mult), reads=[b_kk, b_eexc], writes=[bF])
                P.op("pool", lambda e, nb=nb: e.tensor_tensor(out=tmpb[:, 0:nb], in0=kk[:, 0:nb], in1=av[:, 0:nb], op=ALU.mult),
                     reads=[b_kk, b_av], writes=[b_tmpb])
                P.op("dve", lambda e, nb=nb, F=F: e.tensor_tensor(out=F[:, 1, 0:nb], in0=tmpb[:, 0:nb], in1=einv[:, 0:nb], op=ALU.mult),
                     reads=[b_tmpb, b_einv], writes=[bF])
                P.op("pool", lambda e, nb=nb, F=F: e.tensor_tensor(out=F[:, 2, 0:nb], in0=kd[:, 0:nb], in1=einv[:, 0:nb], op=ALU.mult),
                     reads=[b_kd, b_einv], writes=[bF])
                P.op("dve", lambda e, nb=nb, F=F, r_ap=r_ap: e.tensor_tensor(out=F[:, 3, 0:nb], in0=r_ap, in1=einc[:, 0:nb], op=ALU.mult),
                     reads=[b_r, b_einc], writes=[bF])
                P.dma(lambda e, F=F, s_id=s_id, nb=nb: e.dma_start(out=rwt[s_id].rearrange("f d t -> d f t")[:, :, t0:t0 + nb], in_=F[:, :, 0:nb]),
                      reads=[bF])
            dir_part(0)
            dir_part(1)
            ax = aux[n_it % 2]
            b_ax = b_aux[n_it % 2]
            pb = rot.bank()
            P.op("pe", lambda e, pb=pb, nb=nb: e.matmul(psum[pb][0:64, 0:nb], lhsT=G2[0][:, h * 64:(h + 1) * 64], rhs=tg[:, t0:t0 + nb],
                                                        start=True, stop=True), reads=[G2[1], b_tg], writes=[pbuf[pb]])
            P.op("act", lambda e, pb=pb, nb=nb, ax=ax: e.activation(out=ax[:, 0, 0:nb], in_=psum[pb][0:64, 0:nb], func=AF.Copy),
                 reads=[pbuf[pb]], writes=[b_ax])
            P.op("dve", lambda e, nb=nb, r_ap=r_ap: e.scalar_tensor_tensor(out=tmpb[:, 0:nb], in0=r_ap, scalar=r_k[0][:, hs], in1=kds[:, 0:nb],
                                                                           op0=ALU.mult, op1=ALU.mult), reads=[b_r, b_kds, r_k[1]], writes=[b_tmpb])
            pb = rot.bank()
            P.op("pe", lambda e, pb=pb, nb=nb: e.matmul(psum[pb][0:64, 0:nb], lhsT=g["ones"][0:64, 0:64], rhs=tmpb[:, 0:nb], start=True, stop=True),
                 reads=[g["b_ones"], b_tmpb], writes=[pbuf[pb]])
            P.op("dve", lambda e, pb=pb, nb=nb, ax=ax, v_ap=v_ap: e.tensor_tensor(out=ax[:, 1, 0:nb], in0=psum[pb][0:64, 0:nb], in1=v_ap, op=ALU.mult),
                 reads=[pbuf[pb], b_v], writes=[b_ax])
            P.dma(lambda e, ax=ax, nb=nb: e.dma_start(out=io["rwaux"].ap()[:, h, :, t0:t0 + nb].rearrange("a d t -> d a t"), in_=ax[:, :, 0:nb]),
                  reads=[b_ax])
    n_it = 0
    for h in range(4):
        for (t0, nb, first, last) in RW_SEQS:
            head_block(h, t0, nb, first, last, n_it)
            n_it += 1
    P.dma(lambda e: e.dma_start(out=io["etot"].ap(), in_=etot[:]), reads=[b_etot])
    P.barrier()
    P.release()
    P.release()


class RwStream:
    pass


def phase_rwkv_scan(P, l, nsteps=34):
    nc, io, g = P.nc, P.io, P.g
    psum = P.psum
    P.mark()
    slot_ap = [psum[i // 2][:, (i % 2) * 256:(i % 2 + 1) * 256] for i in range(16)]
    bank_b = [Buf(psum=True) for _ in range(8)]
    slot_b = [bank_b[i // 2] for i in range(16)]
    masks = P.sb([128, 4, 128], F32, "masks")
    b_masks = Buf()
    P.dma(lambda e: e.dma_start(out=masks[:], in_=io["masks"].ap().rearrange("m p f -> p m f")), writes=[b_masks])
    identb = P.sb([64, 64], BF16, "identb")
    b_identb = Buf()
    P.op("dve", lambda e: e.tensor_copy(out=identb[:], in_=g["ident"][0:64, 0:64]), reads=[g["b_ident"]], writes=[b_identb])
    etot = P.sb([64, 8, 34], F32, "etot2")
    b_etot = Buf()
    P.dma(lambda e: e.dma_start(out=etot[:], in_=io["etot"].ap()), writes=[b_etot])
    rwt = io["rwt"].ap()
    yT = io["yT"].ap()
    ident = g["ident"]

    streams = []
    for s_id in range(8):
        S = RwStream()
        S.id = s_id
        S.d = s_id % 2
        S.F = [P.sb([128, 5, 128], F32, "F%d_%d" % (s_id, i)) for i in range(2)]
        S.b_F = [Buf(), Buf()]
        for i in range(2):
            P.op("pool", lambda e, S=S, i=i: e.memset(S.F[i][64:128, :, :], 0.0), writes=[S.b_F[i]])
        S.Fb = P.sb([64, 5, 128], BF16, "Fb%d" % s_id)
        S.b_Fb = Buf()
        S.Atok = P.sb([128, 64], F32, "Atok%d" % s_id)
        S.b_Atok = Buf()
        S.BKV = P.sb([128, 3, 64], BF16, "BKV%d" % s_id)
        S.b_BKV = Buf()
        S.L = [P.sb([128, 128], F32, "L%d_%d" % (s_id, i)) for i in range(2)]
        S.Q = [P.sb([128, 128], F32, "Q%d_%d" % (s_id, i)) for i in range(2)]
        S.Z = [P.sb([128, 128], F32, "Z%d_%d" % (s_id, i)) for i in range(2)]
        S.b_L, S.b_Q, S.b_Z = [Buf(), Buf()], [Buf(), Buf()], [Buf(), Buf()]
        S.LakT = P.sb([128, 128], BF16, "LakT%d" % s_id)
        S.MrbT = P.sb([128, 128], BF16, "MrbT%d" % s_id)
        S.MrkT = P.sb([128, 128], BF16, "MrkT%d" % s_id)
        S.b_LakT, S.b_MrbT, S.b_MrkT = Buf(), Buf(), Buf()
        S.W = P.sb([128, 64], BF16, "W%d" % s_id)
        S.WT = P.sb([64, 128], BF16, "WT%d" % s_id)
        S.X = P.sb([128, 64], F32, "X%d" % s_id)
        S.U0 = P.sb([128, 64], F32, "U0%d" % s_id)
        S.U0b = P.sb([128, 64], BF16, "U0b%d" % s_id)
        S.GT = P.sb([64, 64], BF16, "GT%d" % s_id)
        S.DE = P.sb([64, 64], F32, "DE%d" % s_id)
        S.Ub = P.sb([128, 64], BF16, "Ub%d" % s_id)
        S.b_W, S.b_WT, S.b_X, S.b_U0, S.b_U0b, S.b_GT, S.b_DE, S.b_Ub = [Buf() for _ in range(8)]
        S.H = [P.sb([64, 64], BF16, "H%d_%d" % (s_id, i)) for i in range(2)]
        S.b_H = [Buf(), Buf()]
        S.ys = [P.sb([64, 128], F32, "ys%d_%d" % (s_id, i)) for i in range(2)]
        S.b_ys = [Buf(), Buf()]
        S.slot = 0
        S.ei = s_id
        S.mL, S.mQ, S.mM = (0, 1, 3) if S.d == 0 else (1, 0, 2)
        P.op("pool", lambda e, S=S: e.memset(S.H[0][:], 0.0), writes=[S.b_H[0]])
        streams.append(S)

    def next_slot(S):
        i = 2 * S.id + (S.slot % 2)
        S.slot += 1
        return slot_ap[i], slot_b[i]

    def ev_eng(S):
        S.ei += 1
        return ("act", "dve")[S.ei % 2]

    def load(S, step):
        c = CHUNK_ORDER[S.d][step]
        fb = step % 2
        P.dma(lambda e: e.dma_start(out=S.F[fb][0:64, :, :], in_=rwt[S.id].rearrange("f d t -> d f t")[:, :, c * 128:(c + 1) * 128]),
              writes=[S.b_F[fb]])

    def stage_prep(S, step):
        fb = step % 2
        F, bF = S.F[fb], S.b_F[fb]
        P.op("pool", lambda e: e.tensor_copy(out=S.Fb[:], in_=F[0:64, :, :]), reads=[bF], writes=[S.b_Fb])
        ps, pb = next_slot(S)
        for i, fidx in enumerate((0, 1, 2, 4)):
            P.op("pe", lambda e, i=i, fidx=fidx: e.matmul(ps[:, i * 64:(i + 1) * 64], lhsT=F[:, fidx, :], rhs=ident[:, 0:64],
                                                          start=True, stop=True),
                 reads=[bF, g["b_ident"]], writes=[pb])
        P.op("act", lambda e: e.activation(out=S.Atok[:], in_=ps[:, 0:64], func=AF.Copy), reads=[pb], writes=[S.b_Atok])
        P.op("dve", lambda e: e.tensor_copy(out=S.BKV[:], in_=ps[:, 64:256].rearrange("p (a d) -> p a d", a=3)), reads=[pb], writes=[S.b_BKV])

    def stage_scores(S, step):
        fb = step % 2
        F, bF = S.F[fb], S.b_F[fb]
        ps, pb = next_slot(S)
        P.op("pe", lambda e: e.matmul(ps[:, 0:128], lhsT=F[:, 0, :], rhs=F[:, 1, :], start=True, stop=True), reads=[bF], writes=[pb])
        P.op("pe", lambda e: e.matmul(ps[:, 128:256], lhsT=F[:, 1, :], rhs=F[:, 0, :], start=True, stop=True), reads=[bF], writes=[pb])
        P.op("dve", lambda e: e.tensor_tensor(out=S.L[0][:], in0=ps[:, 0:128], in1=masks[:, S.mL, :], op=ALU.mult),
             reads=[pb, b_masks], writes=[S.b_L[0]])
        P.op("dve", lambda e: e.tensor_tensor(out=S.Q[0][:], in0=ps[:, 128:256], in1=masks[:, S.mQ, :], op=ALU.mult),
             reads=[pb, b_masks], writes=[S.b_Q[0]])
        P.op("pool", lambda e: e.tensor_tensor(out=S.Z[0][:], in0=S.Q[0][:], in1=ident[:], op=ALU.add),
             reads=[S.b_Q[0], g["b_ident"]], writes=[S.b_Z[0]])
        ps, pb = next_slot(S)
        P.op("pe", lambda e: e.matmul(ps[:, 0:128], lhsT=S.Fb[:, 2, :], rhs=S.Fb[:, 0, :], start=True, stop=True), reads=[S.b_Fb], writes=[pb])
        P.op("pe", lambda e: e.matmul(ps[:, 128:256], lhsT=S.Fb[:, 1, :], rhs=S.Fb[:, 3, :], start=True, stop=True), reads=[S.b_Fb], writes=[pb])
        P.op("dve", lambda e: e.tensor_tensor(out=S.LakT[:], in0=ps[:, 0:128], in1=masks[:, S.mQ, :], op=ALU.mult),
             reads=[pb, b_masks], writes=[S.b_LakT])
        P.op("dve", lambda e: e.tensor_tensor(out=S.MrbT[:], in0=ps[:, 128:256], in1=masks[:, S.mM, :], op=ALU.mult),
             reads=[pb, b_masks], writes=[S.b_MrbT])
        ps, pb = next_slot(S)
        P.op("pe", lambda e: e.matmul(ps[:, 0:128], lhsT=S.Fb[:, 2, :], rhs=S.Fb[:, 3, :], start=True, stop=True), reads=[S.b_Fb], writes=[pb])
        P.op("dve", lambda e: e.tensor_tensor(out=S.MrkT[:], in0=ps[:, 0:128], in1=masks[:, S.mM, :], op=ALU.mult),
             reads=[pb, b_masks], writes=[S.b_MrkT])

    def stage_double(S, lvl):
        a, b = (lvl - 1) % 2, lvl % 2
        last = lvl == 6
        ps, pb = next_slot(S)
        P.op("pe", lambda e: e.matmul(ps[:, 0:128], lhsT=S.Q[a][:], rhs=S.L[a][:], start=True, stop=True),
             reads=[S.b_Q[a], S.b_L[a]], writes=[pb])
        if not last:
            P.op("pe", lambda e: e.matmul(ps[:, 128:256], lhsT=S.L[a][:], rhs=S.Q[a][:], start=True, stop=True),
                 reads=[S.b_Q[a], S.b_L[a]], writes=[pb])
        P.op("act", lambda e: e.activation(out=S.L[b][:], in_=ps[:, 0:128], func=AF.Copy), reads=[pb], writes=[S.b_L[b]])
        if not last:
            P.op("dve", lambda e: e.tensor_copy(out=S.Q[b][:], in_=ps[:, 128:256]), reads=[pb], writes=[S.b_Q[b]])
        ps2, pb2 = next_slot(S)
        P.op("pe", lambda e: e.matmul(ps2[:, 0:128], lhsT=S.L[b][:], rhs=S.Z[a][:], start=True, stop=True),
             reads=[S.b_L[b], S.b_Z[a]], writes=[pb2])
        P.op("dve", lambda e: e.tensor_tensor(out=S.Z[b][:], in0=ps2[:, 0:128], in1=S.Z[a][:], op=ALU.add),
             reads=[pb2, S.b_Z[a]], writes=[S.b_Z[b]])

    def stage_wux(S):
        Z, bZ = S.Z[0], S.b_Z[0]
        ps, pb = next_slot(S)
        P.op("pe", lambda e: e.matmul(ps[:, 0:64], lhsT=Z[:], rhs=S.Atok[:], start=True, stop=True), reads=[bZ, S.b_Atok], writes=[pb])
        P.op("pe", lambda e: e.matmul(ps[0:64, 64:192], lhsT=S.Atok[:], rhs=Z[:], start=True, stop=True), reads=[bZ, S.b_Atok], writes=[pb])
        P.op("pe", lambda e: e.matmul(ps[:, 192:256], lhsT=S.LakT[:], rhs=S.BKV[:, 2, :], start=True, stop=True),
             reads=[S.b_LakT, S.b_BKV], writes=[pb])
        P.op("act", lambda e: e.activation(out=S.W[:], in_=ps[:, 0:64], func=AF.Copy), reads=[pb], writes=[S.b_W])
        P.op("dve", lambda e: e.tensor_copy(out=S.WT[:], in_=ps[0:64, 64:192]), reads=[pb], writes=[S.b_WT])
        P.op("dve", lambda e: e.tensor_copy(out=S.X[:], in_=ps[:, 192:256]), reads=[pb], writes=[S.b_X])
        ps2, pb2 = next_slot(S)
        P.op("pe", lambda e: e.matmul(ps2[:, 0:64], lhsT=Z[:], rhs=S.X[:], start=True, stop=True), reads=[bZ, S.b_X], writes=[pb2])
        P.op("act", lambda e: e.activation(out=S.U0[:], in_=ps2[:, 0:64], func=AF.Copy), reads=[pb2], writes=[S.b_U0])
        P.op("dve", lambda e: e.tensor_copy(out=S.U0b[:], in_=ps2[:, 0:64]), reads=[pb2], writes=[S.b_U0b])

    def stage_gd(S, step):
        c = CHUNK_ORDER[S.d][step]
        ps, pb = next_slot(S)
        P.op("pe", lambda e: e.matmul(ps[0:64, 0:64], lhsT=S.W[:], rhs=S.BKV[:, 0, :], start=True, stop=False),
             reads=[S.b_W, S.b_BKV], writes=[pb])
        P.op("pe", lambda e: e.matmul(ps[0:64, 0:64], lhsT=identb[:], rhs=identb[:], start=False, stop=True), reads=[b_identb], writes=[pb])
        P.op("pe", lambda e: e.matmul(ps[0:64, 64:128], lhsT=S.BKV[:, 0, :], rhs=S.U0b[:], start=True, stop=False),
             reads=[S.b_BKV, S.b_U0b], writes=[pb])
        P.op("pe", lambda e: e.matmul(ps[0:64, 64:128], lhsT=S.BKV[:, 1, :], rhs=S.BKV[:, 2, :], start=False, stop=True),
             reads=[S.b_BKV], writes=[pb])
        P.op("dve", lambda e: e.tensor_copy(out=S.GT[:], in_=ps[0:64, 0:64]), reads=[pb], writes=[S.b_GT])
        P.op("act", lambda e: e.activation(out=S.DE[:], in_=ps[0:64, 64:128], func=AF.Copy, scale=etot[:, S.id, c:c + 1]),
             reads=[pb, b_etot], writes=[S.b_DE])

    def stage_state(S, step):
        c = CHUNK_ORDER[S.d][step]
        fb = step % 2
        hi, ho = step % 2, (step + 1) % 2
        H, bH = S.H[hi], S.b_H[hi]
        ps, pb = next_slot(S)
        P.op("pe", lambda e: e.matmul(ps[:, 0:64], lhsT=S.WT[:], rhs=H[:], start=True, stop=True), reads=[S.b_WT, bH], writes=[pb])
        P.op("dve", lambda e: e.tensor_tensor(out=S.Ub[:], in0=ps[:, 0:64], in1=S.U0[:], op=ALU.add), reads=[pb, S.b_U0], writes=[S.b_Ub])
        ps2, pb2 = next_slot(S)
        P.op("pe", lambda e: e.matmul(ps2[0:64, 0:128], lhsT=H[:], rhs=S.Fb[:, 3, :], start=True, stop=False), reads=[bH, S.b_Fb], writes=[pb2])
        P.op("pe", lambda e: e.matmul(ps2[0:64, 0:128], lhsT=S.Ub[:], rhs=S.MrbT[:], start=False, stop=False),
             reads=[S.b_Ub, S.b_MrbT], writes=[pb2])
        P.op("pe", lambda e: e.matmul(ps2[0:64, 0:128], lhsT=S.BKV[:, 2, :], rhs=S.MrkT[:], start=False, stop=True),
             reads=[S.b_BKV, S.b_MrkT], writes=[pb2])
        P.op("pe", lambda e: e.matmul(ps2[0:64, 128:192], lhsT=S.GT[:], rhs=H[:], start=True, stop=True), reads=[S.b_GT, bH], writes=[pb2])
        yb = step % 2
        P.op("act", lambda e: e.activation(out=S.ys[yb][:], in_=ps2[0:64, 0:128], func=AF.Copy), reads=[pb2], writes=[S.b_ys[yb]])
        P.op("dve", lambda e: e.scalar_tensor_tensor(out=S.H[ho][:], in0=ps2[0:64, 128:192], scalar=etot[:, S.id, c:c + 1], in1=S.DE[:],
                                                     op0=ALU.mult, op1=ALU.add), reads=[pb2, b_etot, S.b_DE], writes=[S.b_H[ho]])
        P.dma(lambda e: e.dma_start(out=yT[S.id, :, c * 128:(c + 1) * 128], in_=S.ys[yb][:]), reads=[S.b_ys[yb]])

    for S in streams:
        load(S, 0)
    import os
    stop = int(os.environ.get("SO_STAGE", "99"))
    for step in range(nsteps):
        if step + 1 < nsteps:
            for S in streams:
                load(S, step + 1)
        for S in streams:
            stage_prep(S, step)
        if stop <= 0:
            continue
        for S in streams:
            stage_scores(S, step)
        if stop <= 1:
            continue
        for lvl in range(1, 7):
            for S in streams:
                stage_double(S, lvl)
        if stop <= 2:
            continue
        for S in streams:
            stage_wux(S)
        if stop <= 3:
            continue
        for S in streams:
            stage_gd(S, step)
        if stop <= 4:
            continue
        for S in streams:
            stage_state(S, step)
    P.barrier()
    P.release()


def phase_rwkv_fin(P, l):
    nc, io, g = P.nc, P.io, P.g
    psum, pbuf = P.psum, P.pbuf
    rot = Rot(P)
    P.mark()
    gn_g = col_param(P, io["gn_g"].ap()[l, :], "gn_g")
    gn_b = col_param(P, io["gn_b"].ap()[l, :], "gn_b")
    od = P.sb([64, 64], F32, "onesdiv")
    b_od = Buf()
    P.op("dve", lambda e: e.memset(od[:], 1.0 / 64), writes=[b_od])
    y2 = [P.sb([64, 2, 512], F32, "y2_%d" % i) for i in range(2)]
    ax = [P.sb([64, 2, 512], F32, "ax_%d" % i) for i in range(2)]
    b_y2, b_ax = [Buf(), Buf()], [Buf(), Buf()]
    y = P.sb([64, 512], F32, "y")
    yc = P.sb([64, 512], F32, "yc")
    sq = P.sb([64, 512], F32, "sq")
    sd = P.sb([64, 512], F32, "sd")
    b_y, b_yc, b_sq, b_sd = Buf(), Buf(), Buf(), Buf()
    ost = [P.sb([64, 512], BF16, "rost%d" % i) for i in range(2)]
    b_ost = [Buf(), Buf()]
    yT = io["yT"].ap()
    cat = io["cat"].ap()

    def blk(h, t0, nb, n):
        k = n % 2
        hs = slice(h, h + 1)
        P.dma(lambda e: e.dma_start(out=y2[k][:, :, 0:nb], in_=yT[2 * h:2 * h + 2, :, t0:t0 + nb].rearrange("s d t -> d s t")), writes=[b_y2[k]])
        P.dma(lambda e: e.dma_start(out=ax[k][:, :, 0:nb], in_=io["rwaux"].ap()[:, h, :, t0:t0 + nb].rearrange("a d t -> d a t")), writes=[b_ax[k]])
        P.op("dve", lambda e: e.tensor_tensor(out=y[:, 0:nb], in0=y2[k][:, 0, 0:nb], in1=y2[k][:, 1, 0:nb], op=ALU.add), reads=[b_y2[k]], writes=[b_y])
        pb = rot.bank()
        P.op("pe", lambda e: e.matmul(psum[pb][0:64, 0:nb], lhsT=od[:], rhs=y[:, 0:nb], start=True, stop=True), reads=[b_od, b_y], writes=[pbuf[pb]])
        P.op("dve", lambda e: e.tensor_tensor(out=yc[:, 0:nb], in0=y[:, 0:nb], in1=psum[pb][0:64, 0:nb], op=ALU.subtract),
             reads=[b_y, pbuf[pb]], writes=[b_yc])
        P.op("pool", lambda e: e.tensor_tensor(out=sq[:, 0:nb], in0=yc[:, 0:nb], in1=yc[:, 0:nb], op=ALU.mult), reads=[b_yc], writes=[b_sq])
        pb2 = rot.bank()
        P.op("pe", lambda e: e.matmul(psum[pb2][0:64, 0:nb], lhsT=od[:], rhs=sq[:, 0:nb], start=True, stop=True), reads=[b_od, b_sq], writes=[pbuf[pb2]])
        P.op("dve", lambda e: e.tensor_scalar(out=sd[:, 0:nb], in0=psum[pb2][0:64, 0:nb], scalar1=float(GN_EPS), scalar2=None, op0=ALU.add),
             reads=[pbuf[pb2]], writes=[b_sd])
        P.op("act", lambda e: e.activation(out=sd[:, 0:nb], in_=sd[:, 0:nb], func=AF.Sqrt), reads=[b_sd], writes=[b_sd])
        P.op("dve", lambda e: e.reciprocal(out=sd[:, 0:nb], in_=sd[:, 0:nb]), reads=[b_sd], writes=[b_sd])
        P.op("dve", lambda e: e.tensor_tensor(out=yc[:, 0:nb], in0=yc[:, 0:nb], in1=sd[:, 0:nb], op=ALU.mult), reads=[b_yc, b_sd], writes=[b_yc])
        P.op("pool", lambda e: e.tensor_scalar(out=yc[:, 0:nb], in0=yc[:, 0:nb], scalar1=gn_g[0][:, hs], scalar2=gn_b[0][:, hs],
                                               op0=ALU.mult, op1=ALU.add), reads=[b_yc, gn_g[1], gn_b[1]], writes=[b_yc])
        P.op("dve", lambda e: e.tensor_tensor(out=yc[:, 0:nb], in0=yc[:, 0:nb], in1=ax[k][:, 1, 0:nb], op=ALU.add), reads=[b_yc, b_ax[k]], writes=[b_yc])
        P.op("pool", lambda e: e.tensor_tensor(out=ost[k][:, 0:nb], in0=yc[:, 0:nb], in1=ax[k][:, 0, 0:nb], op=ALU.mult),
             reads=[b_yc, b_ax[k]], writes=[b_ost[k]])
        P.dma(lambda e: e.dma_start(out=cat[256 + h * 64:256 + (h + 1) * 64, t0:t0 + nb], in_=ost[k][:, 0:nb]), reads=[b_ost[k]])

    n = 0
    for h in range(4):
        for (t0, nb, first, last) in RW_SEQS:
            blk(h, t0, nb, n)
            n += 1
    P.barrier()
    P.release()


_NC_CACHE = {}


def _get_nc(dbg=None):
    if dbg not in _NC_CACHE:
        _NC_CACHE[dbg] = build_program(dbg)
    return _NC_CACHE[dbg]


_CONST = {}


def _constants():
    if _CONST:
        return _CONST
    bf = ml_dtypes.bfloat16
    t = np.arange(SEQ)
    rows = (t // 64).astype(np.float64)
    cols = (t % 64).astype(np.float64)
    inv = 10000.0 ** (-np.arange(16, dtype=np.float64) / 16)
    ang = np.concatenate([rows[:, None] * inv, cols[:, None] * inv], -1)
    _CONST["rope_cos"] = np.cos(ang).astype(np.float32)
    _CONST["rope_sin"] = np.sin(ang).astype(np.float32)
    c = np.arange(64)
    th = 2 * np.pi * np.outer(c, c) / 64
    z = np.zeros((64, 64))
    _CONST["dftc"] = np.block([[np.cos(th), z], [z, np.cos(th)]]).astype(bf)
    _CONST["dfts"] = np.block([[np.sin(th), z], [z, np.sin(th)]]).astype(bf)
    pi_, fi_ = np.arange(128)[:, None], np.arange(128)[None, :]
    _CONST["masks"] = np.stack([fi_ < pi_, fi_ > pi_, fi_ <= pi_, fi_ >= pi_], 0).astype(np.float32)
    pm = np.zeros((4, 5, 128, 128))
    for gi, win in enumerate((2, 4, 8, 16)):
        T3 = 384
        tt = np.arange(T3)
        lo = np.clip(tt - win // 2, 0, T3)
        hi = np.clip(tt + (win - win // 2), 0, T3)
        ss_ = np.arange(T3)[:, None]
        M = ((ss_ >= lo[None, :]) & (ss_ < hi[None, :])) / (hi - lo)[None, :].astype(np.float64) - np.eye(T3)
        pm[gi, 0] = M[0:128, 128:256]
        pm[gi, 1] = M[128:256, 0:128]
        pm[gi, 2] = M[128:256, 128:256]
        pm[gi, 3] = M[0:128, 0:128]
        pm[gi, 4] = M[256:384, 256:384]
    _CONST["poolm"] = pm.reshape(20, 128, 128).astype(bf)
    for nm, T in (("dft_lat", SEQ), ("dft_ctx", CTX)):
        k = np.arange(T)
        kk = (np.outer(k, k) % T).astype(np.float64)
        ang = 2 * np.pi * kk / T
        sc = 1.0 / np.sqrt(T * 64.0)
        _CONST[nm] = np.stack([np.cos(ang) * sc, -np.sin(ang) * sc], 0).astype(bf)
    return _CONST


def make_in_maps(inputs):
    f32 = lambda a: np.ascontiguousarray(np.asarray(a, dtype=np.float32))
    shared = {k: f32(inputs[k]) for k in ("w_mod", "b_mod", "ln_g", "ln_b", "w_ffn_in", "w_ffn_out", "w_in", "w_out",
                                          "q_norm_g", "k_norm_g", "pool_w", "pool_scale", "fourier_w", "rwkv_mu", "decay_w0", "decay_w1", "decay_w2",
                                          "icl_a0", "icl_a1", "icl_a2", "gate_g1", "gate_g2", "k_k", "k_a", "r_k", "gn_g", "gn_b")}
    shared["ident"] = np.eye(128, dtype=np.float32)
    shared.update(_constants())
    maps = []
    for core in range(8):
        b = core // 2
        m = dict(shared)
        m["x"] = f32(inputs["x"][b])
        m["ctx"] = f32(inputs["ctx"][b])
        m["c2"] = f32(np.stack([np.asarray(inputs["c"])[b], np.asarray(inputs["c_ctx"])], 0))
        maps.append(m)
    return maps


def kernel(**inputs):
    nc = _get_nc(None)
    res = run_bass_kernel_spmd(nc, make_in_maps(inputs), core_ids=list(range(8)))
    out = np.zeros((4, SEQ, D), np.float32)
    hf = SEQ // 2
    for core in range(8):
        b, j = core // 2, core % 2
        out[b, j * hf:(j + 1) * hf] = np.asarray(res.results[core]["y"])[j * hf:(j + 1) * hf]
    return out
```

```python
import contextlib
import numpy as np
import ml_dtypes
import concourse.bass as bass
import concourse.mybir as mybir
from concourse.bass_utils import run_bass_kernel_spmd

F32 = mybir.dt.float32
BF16 = mybir.dt.bfloat16
AF = mybir.ActivationFunctionType
ALU = mybir.AluOpType
AX = mybir.AxisListType

D = 1024
SEQ = 4096
CTX = 256
NTOK = SEQ + CTX
DEPTH = 4
DFF = 2816
DIN = 2048
ALPHA = (2 * DEPTH) ** 0.25
LN_EPS = 1e-5

ENGS = ("pe", "dve", "act", "pool", "sp")


class Buf:
    __slots__ = ("w", "r", "name", "psum")

    def __init__(self, name="", psum=False):
        self.w = None
        self.r = {}
        self.name = name
        self.psum = psum


class Prog:
    def __init__(self, nc, stack, n_dma_sems=16):
        self.nc = nc
        self.streams = {e: [] for e in ENGS}
        self.count = {e: 0 for e in ENGS}
        self.seen = {e: {} for e in ENGS}
        self.sems = {}
        for e in ENGS:
            self.sems["E_" + e] = stack.enter_context(nc.semaphore("sem_" + e))
        self.dma_keys = []
        self.dma_tot = {}
        for i in range(n_dma_sems):
            k = "D_%d" % i
            self.sems[k] = stack.enter_context(nc.semaphore("semd_%d" % i))
            self.dma_keys.append(k)
            self.dma_tot[k] = 0
        self.dma_rr = 0
        self.sb_off = 16640
        self.sb_marks = []
        self.n_alloc = 0
        self.n_ops = 0

    def sb(self, shape, dtype, name=None):
        esz = 4 if dtype == F32 else 2
        per_part = esz
        for s in shape[1:]:
            per_part *= s
        per_part = (per_part + 63) // 64 * 64
        off = self.sb_off
        self.sb_off += per_part
        assert self.sb_off <= 229376, "SBUF overflow %d" % self.sb_off
        self.n_alloc += 1
        t = self.nc.alloc_sbuf_tensor_at("sb%d_%s" % (self.n_alloc, name or "t"), list(shape), dtype, offset=off)
        return t

    def mark(self):
        self.sb_marks.append(self.sb_off)

    def release(self):
        self.sb_off = self.sb_marks.pop()

    def _wait(self, eng, toks):
        own = "E_" + eng
        seen = self.seen[eng]
        for key, val in toks:
            if key == own and eng in ("pe", "sp"):
                continue
            if seen.get(key, 0) >= val:
                continue
            seen[key] = val
            self.streams[eng].append(("wait", key, val))

    @staticmethod
    def _deps(reads, writes, own=None):
        toks = []
        for b in reads:
            if b.w is not None:
                toks.append(b.w)
            if b.psum:
                toks.extend((k, v) for k, v in b.r.items() if k != own)
        for b in writes:
            if b.w is not None:
                toks.append(b.w)
            toks.extend(b.r.items())
        return toks

    @staticmethod
    def _update(tok, reads, writes):
        key, val = tok
        for b in reads:
            if b.r.get(key, 0) < val:
                b.r[key] = val
        for b in writes:
            b.w = tok
            b.r = {}

    def op(self, eng, fn, reads=(), writes=()):
        self._wait(eng, self._deps(reads, writes, "E_" + eng))
        self.count[eng] += 1
        key = "E_" + eng
        self.streams[eng].append(("op", fn, key))
        tok = (key, self.count[eng])
        self._update(tok, reads, writes)
        self.n_ops += 1
        return tok

    def dma(self, fn, reads=(), writes=(), queue="sp"):
        key = self.dma_keys[self.dma_rr % len(self.dma_keys)]
        self.dma_rr += 1
        toks = self._deps(reads, writes)
        if self.dma_tot[key] > 0:
            toks.append((key, self.dma_tot[key]))
        self._wait(queue, toks)
        self.dma_tot[key] += 16
        self.streams[queue].append(("dma", fn, key))
        tok = (key, self.dma_tot[key])
        self._update(tok, reads, writes)
        self.n_ops += 1
        return tok

    def barrier(self):
        toks = [("E_" + e, self.count[e]) for e in ENGS if self.count[e] > 0]
        toks += [(k, v) for k, v in self.dma_tot.items() if v > 0]
        for e in ENGS:
            self._wait(e, toks)

    def replay(self, name, e):
        sems = self.sems
        for item in self.streams[name]:
            if item[0] == "wait":
                e.wait_ge(sems[item[1]], item[2])
            elif item[0] == "op":
                item[1](e).then_inc(sems[item[2]], 1)
            else:
                item[1](e).then_inc(sems[item[2]], 16)


def build_program(dbg=None):
    nc = bass.Bass("TRN2", target_bir_lowering=False)
    _so = bool(dbg) and dbg.startswith("so")

    def dt(name, shape, dtype=F32, kind="ExternalInput"):
        if _so and kind == "ExternalInput" and name not in ("ident", "masks", "c2", "rwt", "etot"):
            shape = [1, 2]
        return nc.dram_tensor(name, list(shape), dtype, kind=kind)
    io = {}
    io["x"] = dt("x", [SEQ, D])
    io["ctx"] = dt("ctx", [CTX, D])
    io["c2"] = dt("c2", [2, D])
    io["w_mod"] = dt("w_mod", [DEPTH, D, 9 * D])
    io["b_mod"] = dt("b_mod", [DEPTH, 9 * D])
    io["ln_g"] = dt("ln_g", [DEPTH, 3, D])
    io["ln_b"] = dt("ln_b", [DEPTH, 3, D])
    io["w_ffn_in"] = dt("w_ffn_in", [DEPTH, 2, D, 2 * DFF])
    io["w_ffn_out"] = dt("w_ffn_out", [DEPTH, 2, DFF, D])
    io["ident"] = dt("ident", [128, 128])
    io["w_in"] = dt("w_in", [DEPTH, D, DIN])
    io["w_out"] = dt("w_out", [DEPTH, D, D])
    io["q_norm_g"] = dt("q_norm_g", [DEPTH, 64])
    io["k_norm_g"] = dt("k_norm_g", [DEPTH, 64])
    io["rope_cos"] = dt("rope_cos", [SEQ, 32])
    io["rope_sin"] = dt("rope_sin", [SEQ, 32])
    io["dftc"] = dt("dftc", [128, 128], BF16)
    io["dfts"] = dt("dfts", [128, 128], BF16)
    io["poolm"] = dt("poolm", [20, 128, 128], BF16)
    io["pool_w"] = dt("pool_w", [DEPTH, 4, 64, 64])
    io["pool_scale"] = dt("pool_scale", [DEPTH, 256])
    io["fourier_w"] = dt("fourier_w", [DEPTH, 256, 256])
    io["dft_lat"] = dt("dft_lat", [2, SEQ, SEQ], BF16)
    io["dft_ctx"] = dt("dft_ctx", [2, CTX, CTX], BF16)
    io["rwkv_mu"] = dt("rwkv_mu", [DEPTH, 6, 256])
    io["decay_w0"] = dt("decay_w0", [DEPTH, 2, 256])
    io["decay_w1"] = dt("decay_w1", [DEPTH, 2, 256, 32])
    io["decay_w2"] = dt("decay_w2", [DEPTH, 2, 32, 256])
    io["icl_a0"] = dt("icl_a0", [DEPTH, 2, 256])
    io["icl_a1"] = dt("icl_a1", [DEPTH, 2, 256, 32])
    io["icl_a2"] = dt("icl_a2", [DEPTH, 2, 32, 256])
    io["gate_g1"] = dt("gate_g1", [DEPTH, 256, 64])
    io["gate_g2"] = dt("gate_g2", [DEPTH, 64, 256])
    io["k_k"] = dt("k_k", [DEPTH, 256])
    io["k_a"] = dt("k_a", [DEPTH, 256])
    io["r_k"] = dt("r_k", [DEPTH, 4, 64])
    io["gn_g"] = dt("gn_g", [DEPTH, 256])
    io["gn_b"] = dt("gn_b", [DEPTH, 256])
    io["masks"] = dt("masks", [5, 128, 128])
    scratch_kind = "ExternalOutput" if dbg else "Internal"
    io["xs"] = dt("xs", [NTOK, D], F32, kind=scratch_kind)
    io["modv"] = dt("modv", [2, 3 * D], F32, kind=scratch_kind)
    io["cat"] = dt("cat", [D, NTOK], BF16, kind=scratch_kind)
    io["zrw"] = dt("zrw", [4, 64, 4, NTOK], F32, kind=scratch_kind)
    io["zpool"] = dt("zpool", [NTOK, 256], BF16, kind=scratch_kind)
    io["ab"] = dt("ab", [NTOK, 512], BF16, kind=scratch_kind)
    so = bool(dbg) and dbg.startswith("so")
    io["rwt"] = dt("rwt", [8, 5, 64, NTOK], F32, kind=("ExternalInput" if so else scratch_kind))
    io["etot"] = dt("etot", [64, 8, 34], F32, kind=("ExternalInput" if so else scratch_kind))
    io["yT"] = dt("yT", [8, 64, NTOK], F32, kind=scratch_kind)
    io["rwaux"] = dt("rwaux", [2, 4, 64, NTOK], F32, kind=scratch_kind)
    io["y"] = dt("y", [SEQ, D], F32, kind="ExternalOutput")

    with contextlib.ExitStack() as stack:
        P = Prog(nc, stack)
        psum = [nc.alloc_psum_tensor("ps%d" % i, [128, 512], F32) for i in range(8)]
        pbuf = [Buf("ps%d" % i, psum=True) for i in range(8)]
        P.psum = psum
        P.pbuf = pbuf
        P.io = io
        emit_all(P, dbg)
        P.barrier()
        with nc.Block() as block:
            @block.tensor
            def _(e):
                P.replay("pe", e)

            @block.vector
            def _(e):
                P.replay("dve", e)

            @block.scalar
            def _(e):
                P.replay("act", e)

            @block.gpsimd
            def _(e):
                P.replay("pool", e)

            @block.sync
            def _(e):
                P.replay("sp", e)
    return nc


def emit_all(P, dbg):
    nc = P.nc
    io = P.io
    ident = P.sb([128, 128], F32, "ident")
    b_ident = Buf("ident")
    P.dma(lambda e: e.dma_start(out=ident[:], in_=io["ident"].ap()), writes=[b_ident])
    ones = P.sb([128, 128], F32, "ones")
    b_ones = Buf("ones")
    P.op("dve", lambda e: e.memset(ones[:], 1.0), writes=[b_ones])
    craw = P.sb([128, 2, 8], F32, "craw")
    b_craw = Buf()
    for r in range(2):
        P.dma(lambda e, r=r: e.dma_start(out=craw[:, r, :], in_=io["c2"].ap()[r, :].rearrange("(k p) -> p k", p=128),
                                         allow_slow_non_contiguous=True), writes=[b_craw])
    scT = P.sb([128, 8, 2], F32, "scT")
    b_scT = Buf()
    P.op("act", lambda e: e.activation(out=scT[:].rearrange("p k r -> p r k"), in_=craw[:], func=AF.Silu),
         reads=[b_craw], writes=[b_scT])
    P.g = dict(ident=ident, b_ident=b_ident, ones=ones, b_ones=b_ones, scT=scT, b_scT=b_scT)
    P.barrier()

    if dbg and dbg.startswith("so"):
        phase_rwkv_scan(P, 0, nsteps=int(dbg[2:]))
        return
    for l in range(DEPTH):
        last = l == DEPTH - 1
        src_lat = io["x"].ap() if l == 0 else io["xs"].ap()[0:SEQ, :]
        src_ctx = io["ctx"].ap() if l == 0 else io["xs"].ap()[SEQ:NTOK, :]
        phase_ffn(P, l, 0, 0, src_lat, src_ctx, io["xs"].ap()[0:SEQ, :], io["xs"].ap()[SEQ:NTOK, :], 0.5)
        if dbg == "ffn0":
            return
        phase_mixer(P, l, dbg)
        if dbg in ("att", "mixA", "pf", "wout", "rw1", "rw2", "rw3") or (dbg and dbg.startswith("rws")):
            return
        phase_ffn(P, l, 1, 2, io["xs"].ap()[0:SEQ, :], io["xs"].ap()[SEQ:NTOK, :],
                  io["y"].ap() if last else io["xs"].ap()[0:SEQ, :], io["xs"].ap()[SEQ:NTOK, :], 0.5, skip_ctx=last)
        if dbg == "l0":
            return


def phase_mod(P, l, sub, resid_w):
    nc, io, g = P.nc, P.io, P.g
    psum, pbuf = P.psum, P.pbuf
    P.mark()
    brow = P.sb([1, 3 * D], F32, "brow")
    b_brow = Buf()
    P.dma(lambda e: e.dma_start(out=brow[:], in_=io["b_mod"].ap()[l:l + 1, sub * 3 * D:(sub + 1) * 3 * D]), writes=[b_brow])
    mrow = P.sb([2, 3 * D], F32, "mrow")
    b_mrow = Buf()
    wst = [P.sb([128, 8, 512], F32, "wmst%d" % i) for i in range(2)]
    b_wst = [Buf(), Buf()]
    for cb in range(6):
        s = cb % 2
        c0 = sub * 3 * D + cb * 512
        P.dma(lambda e, s=s, c0=c0: e.dma_start(
            out=wst[s][:], in_=io["w_mod"].ap()[l, :, c0:c0 + 512].rearrange("(k p) n -> p k n", p=128)),
            writes=[b_wst[s]])
        pb = cb % 2
        for kc in range(8):
            P.op("pe", lambda e, s=s, kc=kc, pb=pb: e.matmul(psum[pb][0:2, :], lhsT=g["scT"][:, kc, :], rhs=wst[s][:, kc, :],
                                                             start=(kc == 0), stop=False),
                 reads=[g["b_scT"], b_wst[s]], writes=[pbuf[pb]])
        P.op("pe", lambda e, cb=cb, pb=pb: e.matmul(psum[pb][0:2, :], lhsT=g["ones"][0:1, 0:2], rhs=brow[0:1, cb * 512:(cb + 1) * 512],
                                                    start=False, stop=True),
             reads=[g["b_ones"], b_brow], writes=[pbuf[pb]])
        P.op("dve", lambda e, cb=cb, pb=pb: e.tensor_copy(out=mrow[:, cb * 512:(cb + 1) * 512], in_=psum[pb][0:2, :]),
             reads=[pbuf[pb]], writes=[b_mrow])
    b_modv = Buf()
    P.dma(lambda e: e.dma_start(out=io["modv"].ap(), in_=mrow[:]), reads=[b_mrow], writes=[b_modv])
    P.release()
    shT = P.sb([128, 2, 8], F32, "shT")
    scl = P.sb([128, 2, 8], F32, "scl")
    gbc = [P.sb([128, D], F32, "gbc%d" % r) for r in range(2)]
    b_shT, b_scl, b_g = Buf(), Buf(), [Buf(), Buf()]
    for r in range(2):
        P.dma(lambda e, r=r: e.dma_start(out=shT[:, r, :], in_=io["modv"].ap()[r, 0:D].rearrange("(k p) -> p k", p=128),
                                         allow_slow_non_contiguous=True), reads=[b_modv], writes=[b_shT])
        P.dma(lambda e, r=r: e.dma_start(out=scl[:, r, :], in_=io["modv"].ap()[r, D:2 * D].rearrange("(k p) -> p k", p=128),
                                         allow_slow_non_contiguous=True), reads=[b_modv], writes=[b_scl])
        P.dma(lambda e, r=r: e.dma_start(out=gbc[r][:], in_=io["modv"].ap()[r:r + 1, 2 * D:3 * D].partition_broadcast(128)),
              reads=[b_modv], writes=[b_g[r]])
    P.op("dve", lambda e: e.tensor_scalar(out=scl[:], in0=scl[:], scalar1=1.0, scalar2=None, op0=ALU.add),
         reads=[b_scl], writes=[b_scl])
    for r in range(2):
        if resid_w != 1.0:
            P.op("pool", lambda e, r=r: e.tensor_scalar(out=gbc[r][:], in0=gbc[r][:], scalar1=float(resid_w), scalar2=None, op0=ALU.mult),
                 reads=[b_g[r]], writes=[b_g[r]])
    lg = P.sb([128, D], F32, "lng")
    lb = P.sb([128, D], F32, "lnb")
    b_lg, b_lb = Buf(), Buf()
    P.dma(lambda e: e.dma_start(out=lg[:], in_=io["ln_g"].ap()[l, sub:sub + 1, :].partition_broadcast(128)), writes=[b_lg])
    P.dma(lambda e: e.dma_start(out=lb[:], in_=io["ln_b"].ap()[l, sub:sub + 1, :].partition_broadcast(128)), writes=[b_lb])
    return dict(shT=shT, scl=scl, gbc=gbc, b_shT=b_shT, b_scl=b_scl, b_g=b_g, lg=lg, lb=lb, b_lg=b_lg, b_lb=b_lb)


def emit_postnorm(P, m, r, xt_ap, b_x, y_halves, b_ys, out_ap, b_out, work):
    u, b_u, st, b_st = work["u"], work["b_u"], work["st"], work["b_st"]
    for h in range(2):
        sl = slice(h * 512, (h + 1) * 512)
        P.op("dve", lambda e, h=h, sl=sl: e.tensor_tensor(out=u[:, sl], in0=y_halves[h], in1=m["gbc"][r][:, sl], op=ALU.mult),
             reads=[b_ys[h], m["b_g"][r]], writes=[b_u])
        P.op("dve", lambda e, sl=sl: e.scalar_tensor_tensor(out=u[:, sl], in0=xt_ap[:, sl], scalar=float(ALPHA), in1=u[:, sl],
                                                            op0=ALU.mult, op1=ALU.add),
             reads=[b_x, b_u], writes=[b_u])
        P.op("dve", lambda e, h=h, sl=sl: e.bn_stats(out=st[:, h * 6:(h + 1) * 6], in_=u[:, sl]), reads=[b_u], writes=[b_st])
    P.op("dve", lambda e: e.bn_aggr(out=st[:, 12:14], in_=st[:, 0:12]), reads=[b_st], writes=[b_st])
    P.op("dve", lambda e: e.tensor_scalar(out=st[:, 14:15], in0=st[:, 13:14], scalar1=float(LN_EPS), scalar2=None, op0=ALU.add),
         reads=[b_st], writes=[b_st])
    P.op("act", lambda e: e.activation(out=st[:, 14:15], in_=st[:, 14:15], func=AF.Sqrt), reads=[b_st], writes=[b_st])
    P.op("dve", lambda e: e.reciprocal(out=st[:, 14:15], in_=st[:, 14:15]), reads=[b_st], writes=[b_st])
    P.op("dve", lambda e: e.scalar_tensor_tensor(out=st[:, 15:16], in0=st[:, 12:13], scalar=-1.0, in1=st[:, 14:15],
                                                 op0=ALU.mult, op1=ALU.mult), reads=[b_st], writes=[b_st])
    P.op("act", lambda e: e.activation(out=u[:], in_=u[:], func=AF.Identity, scale=st[:, 14:15], bias=st[:, 15:16]),
         reads=[b_u, b_st], writes=[b_u])
    P.op("pool", lambda e: e.tensor_tensor(out=u[:], in0=u[:], in1=m["lg"][:], op=ALU.mult), reads=[b_u, m["b_lg"]], writes=[b_u])
    P.op("pool", lambda e: e.tensor_tensor(out=out_ap, in0=u[:], in1=m["lb"][:], op=ALU.add), reads=[b_u, m["b_lb"]], writes=[b_out])


def phase_ffn(P, l, f, sub, src_lat, src_ctx, dst_lat, dst_ctx, resid_w, skip_ctx=False):
    nc, io, g = P.nc, P.io, P.g
    psum, pbuf = P.psum, P.pbuf
    P.mark()
    m = phase_mod(P, l, sub, resid_w)
    wgu = P.sb([128, 8, 2 * DFF], BF16, "wgu")
    wdn = P.sb([128, 22, D], BF16, "wdn")
    b_wgu, b_wdn = Buf(), Buf()
    P.mark()
    stg = [P.sb([128, DFF], F32, "stg%d" % i) for i in range(2)]
    b_stg = [Buf(), Buf()]
    cast_engs = ["pool", "dve", "act"]
    n = 0
    for kc in range(8):
        for hf in range(2):
            s = n % 2
            P.dma(lambda e, s=s, kc=kc, hf=hf: e.dma_start(
                out=stg[s][:], in_=io["w_ffn_in"].ap()[l, f, kc * 128:(kc + 1) * 128, hf * DFF:(hf + 1) * DFF]),
                writes=[b_stg[s]])
            ce = cast_engs[n % 3]
            if ce == "act":
                P.op("act", lambda e, s=s, kc=kc, hf=hf: e.activation(out=wgu[:, kc, hf * DFF:(hf + 1) * DFF], in_=stg[s][:], func=AF.Copy),
                     reads=[b_stg[s]], writes=[b_wgu])
            else:
                P.op(ce, lambda e, s=s, kc=kc, hf=hf: e.tensor_copy(out=wgu[:, kc, hf * DFF:(hf + 1) * DFF], in_=stg[s][:]),
                     reads=[b_stg[s]], writes=[b_wgu])
            n += 1
    for pc in range(11):
        s = n % 2
        P.dma(lambda e, s=s, pc=pc: e.dma_start(
            out=stg[s][:, 0:2048].rearrange("p (c n) -> p c n", c=2),
            in_=io["w_ffn_out"].ap()[l, f, pc * 256:(pc + 1) * 256, :].rearrange("(c p) n -> p c n", p=128)),
            writes=[b_stg[s]])
        ce = cast_engs[n % 3]
        if ce == "act":
            P.op("act", lambda e, s=s, pc=pc: e.activation(out=wdn[:, 2 * pc:2 * pc + 2, :].rearrange("p c n -> p (c n)"),
                                                           in_=stg[s][:, 0:2048], func=AF.Copy),
                 reads=[b_stg[s]], writes=[b_wdn])
        else:
            P.op(ce, lambda e, s=s, pc=pc: e.tensor_copy(out=wdn[:, 2 * pc:2 * pc + 2, :].rearrange("p c n -> p (c n)"),
                                                         in_=stg[s][:, 0:2048]),
                 reads=[b_stg[s]], writes=[b_wdn])
        n += 1
    P.barrier()
    P.release()
    NB = 256
    xt = [P.sb([128, 2, D], F32, "xt%d" % i) for i in range(2)]
    b_xt = [[Buf(), Buf()], [Buf(), Buf()]]
    hT = P.sb([128, 8, NB], BF16, "hT")
    b_hT = Buf()
    aT = P.sb([128, 22, NB], BF16, "aT")
    b_aT = [Buf() for _ in range(22)]
    sg = [P.sb([128, NB], F32, "sg%d" % i) for i in range(2)]
    b_sg = [Buf(), Buf()]
    ot = [P.sb([128, D], F32, "ot%d" % i) for i in range(2)]
    b_ot = [Buf(), Buf()]
    work = dict(u=P.sb([128, D], F32, "u"), b_u=Buf(), st=P.sb([128, 16], F32, "st"), b_st=Buf())
    blocks = [(src_lat, dst_lat, i * NB, 0) for i in range(SEQ // NB)] + ([] if skip_ctx else [(src_ctx, dst_ctx, 0, 1)])
    pi = 0
    for bi, (src, dst, t0, r) in enumerate(blocks):
        xs_ = bi % 2
        for t in range(2):
            P.dma(lambda e, xs_=xs_, t=t, src=src, t0=t0: e.dma_start(out=xt[xs_][:, t, :], in_=src[t0 + t * 128:t0 + (t + 1) * 128, :]),
                  writes=[b_xt[xs_][t]])
        for kp in range(4):
            pb = pi % 8
            pi += 1
            for kk in range(2):
                kc = kp * 2 + kk
                for t in range(2):
                    P.op("pe", lambda e, pb=pb, kk=kk, t=t, kc=kc, xs_=xs_: e.transpose(
                        out=psum[pb][:, kk * 256 + t * 128: kk * 256 + (t + 1) * 128], in_=xt[xs_][:, t, kc * 128:(kc + 1) * 128],
                        identity=g["ident"][:]),
                        reads=[b_xt[xs_][t], g["b_ident"]], writes=[pbuf[pb]])
            for kk in range(2):
                kc = kp * 2 + kk
                P.op("act", lambda e, pb=pb, kk=kk, kc=kc, r=r: e.activation(
                    out=hT[:, kc, :], in_=psum[pb][:, kk * 256:(kk + 1) * 256], func=AF.Identity,
                    scale=m["scl"][:, r, kc:kc + 1], bias=m["shT"][:, r, kc:kc + 1]),
                    reads=[pbuf[pb], m["b_scl"], m["b_shT"]], writes=[b_hT])
        for i in range(22):
            pb = pi % 8
            pi += 1
            for hf in range(2):
                for kc in range(8):
                    P.op("pe", lambda e, pb=pb, hf=hf, kc=kc, i=i: e.matmul(
                        psum[pb][:, hf * 256:(hf + 1) * 256], lhsT=wgu[:, kc, hf * DFF + i * 128: hf * DFF + (i + 1) * 128],
                        rhs=hT[:, kc, :], start=(kc == 0), stop=(kc == 7)),
                        reads=[b_hT, b_wgu], writes=[pbuf[pb]])
            s = i % 2
            P.op("act", lambda e, pb=pb, s=s: e.activation(out=sg[s][:], in_=psum[pb][:, 0:256], func=AF.Silu),
                 reads=[pbuf[pb]], writes=[b_sg[s]])
            P.op("dve", lambda e, pb=pb, s=s, i=i: e.tensor_tensor(out=aT[:, i, :], in0=psum[pb][:, 256:512], in1=sg[s][:], op=ALU.mult),
                 reads=[pbuf[pb], b_sg[s]], writes=[b_aT[i]])
        for t in range(2):
            pbs = []
            for hn in range(2):
                pb = pi % 8
                pi += 1
                pbs.append(pb)
                for i in range(22):
                    P.op("pe", lambda e, pb=pb, hn=hn, i=i, t=t: e.matmul(
                        psum[pb][:, :], lhsT=aT[:, i, t * 128:(t + 1) * 128], rhs=wdn[:, i, hn * 512:(hn + 1) * 512],
                        start=(i == 0), stop=(i == 21)),
                        reads=[b_aT[i], b_wdn], writes=[pbuf[pb]])
            o = (bi * 2 + t) % 2
            emit_postnorm(P, m, r, xt[xs_][:, t, :], b_xt[xs_][t], [psum[pbs[0]][:, :], psum[pbs[1]][:, :]],
                          [pbuf[pbs[0]], pbuf[pbs[1]]], ot[o][:], b_ot[o], work)
            P.dma(lambda e, o=o, dst=dst, t0=t0, t=t: e.dma_start(out=dst[t0 + t * 128:t0 + (t + 1) * 128, :], in_=ot[o][:]),
                  reads=[b_ot[o]])
    P.barrier()
    P.release()


class Rot:
    def __init__(self, P):
        self.P = P
        self.pi = 0
        self.ei = 0

    def bank(self):
        b = self.pi % 8
        self.pi += 1
        return b

    def evac(self, out_ap, in_ap, reads, writes, eng=None):
        P = self.P
        if eng is None:
            eng = ("act", "dve")[self.ei % 2]
            self.ei += 1
        if eng == "act":
            return P.op("act", lambda e: e.activation(out=out_ap, in_=in_ap, func=AF.Copy), reads=reads, writes=writes)
        return P.op(eng, lambda e: e.tensor_copy(out=out_ap, in_=in_ap), reads=reads, writes=writes)


def emit_hT(P, rot, m, r, xt, b_xts, ntile, hT, b_hT):
    g, psum, pbuf = P.g, P.psum, P.pbuf
    for kc in range(8):
        pb = rot.bank()
        for t in range(ntile):
            P.op("pe", lambda e, pb=pb, t=t, kc=kc: e.transpose(
                out=psum[pb][:, t * 128:(t + 1) * 128], in_=xt[:, t, kc * 128:(kc + 1) * 128], identity=g["ident"][:]),
                reads=[b_xts[t], g["b_ident"]], writes=[pbuf[pb]])
        P.op("act", lambda e, pb=pb, kc=kc: e.activation(
            out=hT[:, kc, 0:ntile * 128], in_=psum[pb][:, 0:ntile * 128], func=AF.Identity,
            scale=m["scl"][:, r, kc:kc + 1], bias=m["shT"][:, r, kc:kc + 1]),
            reads=[pbuf[pb], m["b_scl"], m["b_shT"]], writes=[b_hT])


def load_cast_rows(P, dst, b_dst, src_rows_fn, nchunk, width, stg, b_stg, n0=0):
    cast_engs = ["pool", "dve", "act"]
    n = n0
    for kc in range(nchunk):
        s = n % 2
        P.dma(lambda e, s=s, kc=kc: e.dma_start(out=stg[s][:, 0:width], in_=src_rows_fn(kc)), writes=[b_stg[s]])
        ce = cast_engs[n % 3]
        if ce == "act":
            P.op("act", lambda e, s=s, kc=kc: e.activation(out=dst[:, kc, :], in_=stg[s][:, 0:width], func=AF.Copy),
                 reads=[b_stg[s]], writes=[b_dst])
        else:
            P.op(ce, lambda e, s=s, kc=kc: e.tensor_copy(out=dst[:, kc, :], in_=stg[s][:, 0:width]),
                 reads=[b_stg[s]], writes=[b_dst])
        n += 1
    return n


def phase_mixer(P, l, dbg):
    nc, io, g = P.nc, P.io, P.g
    psum, pbuf = P.psum, P.pbuf
    rot = Rot(P)
    P.mark()
    m = phase_mod(P, l, 1, 1.0)
    P.mark()
    qT = P.sb([64, 4, NTOK], BF16, "qT")
    kT = P.sb([64, 2, NTOK], BF16, "kT")
    vx = P.sb([128, 34, 2, 66], BF16, "vx")
    b_qT = [Buf() for _ in range(9)]
    b_kT, b_vx = Buf(), Buf()
    P.op("pool", lambda e: e.memset(vx[:], 1.0), writes=[b_vx])
    P.mark()
    win = P.sb([128, 8, DIN], BF16, "win")
    b_win = Buf()
    P.mark()
    stg = [P.sb([128, DIN], F32, "stg%d" % i) for i in range(2)]
    b_stg = [Buf(), Buf()]
    load_cast_rows(P, win, b_win, lambda kc: io["w_in"].ap()[l, kc * 128:(kc + 1) * 128, :], 8, DIN, stg, b_stg)
    P.barrier()
    P.release()
    gq = P.sb([128, 384], F32, "gq")
    b_gq = Buf()
    for h in range(6):
        src = io["q_norm_g"] if h < 4 else io["k_norm_g"]
        P.dma(lambda e, h=h, src=src: e.dma_start(out=gq[:, h * 64:(h + 1) * 64], in_=src.ap()[l:l + 1, :].partition_broadcast(128)),
              writes=[b_gq])
    cos_t = P.sb([128, 32, 32], F32, "cos")
    sin_t = P.sb([128, 32, 32], F32, "sin")
    b_cs = Buf()
    P.dma(lambda e: e.dma_start(out=cos_t[:], in_=io["rope_cos"].ap().rearrange("(t p) i -> p t i", p=128)), writes=[b_cs])
    P.dma(lambda e: e.dma_start(out=sin_t[:], in_=io["rope_sin"].ap().rearrange("(t p) i -> p t i", p=128)), writes=[b_cs])
    dftc = P.sb([128, 128], BF16, "dftc")
    dfts = P.sb([128, 128], BF16, "dfts")
    b_dft = Buf()
    P.dma(lambda e: e.dma_start(out=dftc[:], in_=io["dftc"].ap()), writes=[b_dft])
    P.dma(lambda e: e.dma_start(out=dfts[:], in_=io["dfts"].ap()), writes=[b_dft])

    xt = [P.sb([128, 4, D], F32, "xt%d" % i) for i in range(2)]
    b_xt = [[Buf() for _ in range(4)] for _ in range(2)]
    hT = P.sb([128, 8, 512], BF16, "hT")
    b_hT = Buf()
    zat = [P.sb([128, 512], F32, "zat%d" % i) for i in range(2)]
    b_zat = [Buf(), Buf()]
    sq = P.sb([128, 384], F32, "sq")
    qk = P.sb([128, 384], F32, "qk")
    qkr = P.sb([128, 384], F32, "qkr")
    tA = P.sb([128, 192], F32, "tA")
    tB = P.sb([128, 192], F32, "tB")
    ss = P.sb([128, 8], F32, "ss")
    b_sq, b_qk, b_qkr, b_tA, b_tB, b_ss = Buf(), Buf(), Buf(), Buf(), Buf(), Buf()
    zst = [P.sb([64, 4, 512], F32, "zst%d" % i) for i in range(2)]
    b_zst = [Buf(), Buf()]
    zp = P.sb([128, 4, 256], BF16, "zp")
    b_zp = Buf()
    zfT = P.sb([128, 2, 512], BF16, "zfT")
    b_zfT = Buf()
    abst = P.sb([128, 4, 512], BF16, "abst")
    b_abst = Buf()

    xs_ap = io["xs"].ap()
    blocks = [(i * 512, 4, 0) for i in range(8)] + [(SEQ, 2, 1)]
    for bi, (t0, ntile, r) in enumerate(blocks):
        nb = ntile * 128
        xb = bi % 2
        for t in range(ntile):
            P.dma(lambda e, xb=xb, t=t, t0=t0: e.dma_start(out=xt[xb][:, t, :], in_=xs_ap[t0 + t * 128:t0 + (t + 1) * 128, :]),
                  writes=[b_xt[xb][t]])
        emit_hT(P, rot, m, r, xt[xb], b_xt[xb], ntile, hT, b_hT)
        for t in range(ntile):
            tile_idx = (t0 // 128) + t
            tok0 = t0 + t * 128
            pb = rot.bank()
            for kc in range(8):
                P.op("pe", lambda e, pb=pb, kc=kc, t=t: e.matmul(psum[pb][:, :], lhsT=hT[:, kc, t * 128:(t + 1) * 128], rhs=win[:, kc, 0:512],
                                                                 start=(kc == 0), stop=(kc == 7)),
                     reads=[b_hT, b_win], writes=[pbuf[pb]])
            z = (bi * 4 + t) % 2
            P.op("act", lambda e, pb=pb, z=z: e.activation(out=zat[z][:], in_=psum[pb][:, :], func=AF.Copy),
                 reads=[pbuf[pb]], writes=[b_zat[z]])
            P.op("dve", lambda e, z=z: e.tensor_tensor(out=sq[:], in0=zat[z][:, 0:384], in1=zat[z][:, 0:384], op=ALU.mult),
                 reads=[b_zat[z]], writes=[b_sq])
            P.op("dve", lambda e: e.tensor_reduce(out=ss[:, 0:6], in_=sq[:].rearrange("p (h d) -> p h d", d=64), axis=AX.X, op=ALU.add),
                 reads=[b_sq], writes=[b_ss])
            P.op("dve", lambda e: e.tensor_scalar(out=ss[:, 0:6], in0=ss[:, 0:6], scalar1=1.0 / 64, scalar2=1e-6, op0=ALU.mult, op1=ALU.add),
                 reads=[b_ss], writes=[b_ss])
            P.op("act", lambda e: e.activation(out=ss[:, 0:6], in_=ss[:, 0:6], func=AF.Sqrt), reads=[b_ss], writes=[b_ss])
            P.op("dve", lambda e: e.reciprocal(out=ss[:, 0:6], in_=ss[:, 0:6]), reads=[b_ss], writes=[b_ss])
            P.op("dve", lambda e, z=z: e.tensor_tensor(out=qk[:].rearrange("p (h d) -> p h d", d=64),
                                                       in0=zat[z][:, 0:384].rearrange("p (h d) -> p h d", d=64),
                                                       in1=ss[:, 0:6].unsqueeze(2).broadcast_to([128, 6, 64]), op=ALU.mult),
                 reads=[b_zat[z], b_ss], writes=[b_qk])
            if r == 0:
                P.op("pool", lambda e: e.tensor_tensor(out=qk[:], in0=qk[:], in1=gq[:], op=ALU.mult), reads=[b_qk, b_gq], writes=[b_qk])
                v4 = lambda tl: tl[:].rearrange("p (h i two) -> p h i two", h=6, i=32, two=2)
                x0, x1 = v4(qk)[:, :, :, 0], v4(qk)[:, :, :, 1]
                o0, o1 = v4(qkr)[:, :, :, 0], v4(qkr)[:, :, :, 1]
                cb = cos_t[:, tile_idx:tile_idx + 1, :].broadcast_to([128, 6, 32])
                sb_ = sin_t[:, tile_idx:tile_idx + 1, :].broadcast_to([128, 6, 32])
                a3 = lambda tl: tl[:].rearrange("p (h i) -> p h i", h=6)
                P.op("dve", lambda e, x0=x0, cb=cb: e.tensor_tensor(out=a3(tA), in0=x0, in1=cb, op=ALU.mult), reads=[b_qk, b_cs], writes=[b_tA])
                P.op("pool", lambda e, x1=x1, sb_=sb_: e.tensor_tensor(out=a3(tB), in0=x1, in1=sb_, op=ALU.mult), reads=[b_qk, b_cs], writes=[b_tB])
                P.op("dve", lambda e, o0=o0: e.tensor_tensor(out=o0, in0=a3(tA), in1=a3(tB), op=ALU.subtract), reads=[b_tA, b_tB], writes=[b_qkr])
                P.op("pool", lambda e, x0=x0, sb_=sb_: e.tensor_tensor(out=a3(tA), in0=x0, in1=sb_, op=ALU.mult), reads=[b_qk, b_cs], writes=[b_tA])
                P.op("dve", lambda e, x1=x1, cb=cb: e.tensor_tensor(out=a3(tB), in0=x1, in1=cb, op=ALU.mult), reads=[b_qk, b_cs], writes=[b_tB])
                P.op("pool", lambda e, o1=o1: e.tensor_tensor(out=o1, in0=a3(tA), in1=a3(tB), op=ALU.add), reads=[b_tA, b_tB], writes=[b_qkr])
            else:
                P.op("pool", lambda e: e.tensor_tensor(out=qkr[:], in0=qk[:], in1=gq[:], op=ALU.mult), reads=[b_qk, b_gq], writes=[b_qkr])
            pq = rot.bank()
            for h in range(4):
                P.op("pe", lambda e, pq=pq, h=h: e.transpose(out=psum[pq][0:64, h * 128:(h + 1) * 128], in_=qkr[:, h * 64:(h + 1) * 64],
                                                             identity=g["ident"][:]),
                     reads=[b_qkr, g["b_ident"]], writes=[pbuf[pq]])
            rot.evac(qT[:, :, tok0:tok0 + 128], psum[pq][0:64, :].rearrange("p (h t) -> p h t", h=4), [pbuf[pq]], [b_qT[bi]])
            pk = rot.bank()
            for h in range(2):
                P.op("pe", lambda e, pk=pk, h=h: e.transpose(out=psum[pk][0:64, h * 128:(h + 1) * 128], in_=qkr[:, (4 + h) * 64:(5 + h) * 64],
                                                             identity=g["ident"][:]),
                     reads=[b_qkr, g["b_ident"]], writes=[pbuf[pk]])
            rot.evac(kT[:, :, tok0:tok0 + 128], psum[pk][0:64, 0:256].rearrange("p (h t) -> p h t", h=2), [pbuf[pk]], [b_kT])
            P.op("pool", lambda e, z=z, tile_idx=tile_idx: e.tensor_copy(out=vx[:, tile_idx, :, 0:64],
                                                                         in_=zat[z][:, 384:512].rearrange("p (h d) -> p h d", h=2)),
                 reads=[b_zat[z]], writes=[b_vx])
        for kind in range(4):
            zs = kind % 2
            for h in range(4):
                pb = rot.bank()
                c0 = 512 + kind * 256 + h * 64
                for kc in range(8):
                    P.op("pe", lambda e, pb=pb, kc=kc, c0=c0, nb=nb: e.matmul(psum[pb][0:64, 0:nb], lhsT=win[:, kc, c0:c0 + 64], rhs=hT[:, kc, 0:nb],
                                                                            start=(kc == 0), stop=(kc == 7)),
                         reads=[b_hT, b_win], writes=[pbuf[pb]])
                rot.evac(zst[zs][:, h, 0:nb], psum[pb][0:64, 0:nb], [pbuf[pb]], [b_zst[zs]])
            P.dma(lambda e, zs=zs, kind=kind, t0=t0, nb=nb: e.dma_start(out=io["zrw"].ap()[kind, :, :, t0:t0 + nb], in_=zst[zs][:, :, 0:nb]),
                  reads=[b_zst[zs]])
        for t in range(ntile):
            pb = rot.bank()
            for kc in range(8):
                P.op("pe", lambda e, pb=pb, kc=kc, t=t: e.matmul(psum[pb][:, 0:256], lhsT=hT[:, kc, t * 128:(t + 1) * 128], rhs=win[:, kc, 1536:1792],
                                                                 start=(kc == 0), stop=(kc == 7)),
                     reads=[b_hT, b_win], writes=[pbuf[pb]])
            rot.evac(zp[:, t, :], psum[pb][:, 0:256], [pbuf[pb]], [b_zp])
        P.dma(lambda e, t0=t0, ntile=ntile, nb=nb: e.dma_start(out=io["zpool"].ap()[t0:t0 + nb, :].rearrange("(t p) c -> p t c", p=128),
                                                               in_=zp[:, 0:ntile, :]), reads=[b_zp])
        for c in range(2):
            pb = rot.bank()
            c0 = 1792 + c * 128
            for kc in range(8):
                P.op("pe", lambda e, pb=pb, kc=kc, c0=c0, nb=nb: e.matmul(psum[pb][:, 0:nb], lhsT=win[:, kc, c0:c0 + 128], rhs=hT[:, kc, 0:nb],
                                                                        start=(kc == 0), stop=(kc == 7)),
                     reads=[b_hT, b_win], writes=[pbuf[pb]])
            rot.evac(zfT[:, c, 0:nb], psum[pb][:, 0:nb], [pbuf[pb]], [b_zfT])
        for t in range(ntile):
            pb = rot.bank()
            for ab_i, mat in enumerate((dftc, dfts)):
                for c in range(2):
                    P.op("pe", lambda e, pb=pb, ab_i=ab_i, c=c, t=t, mat=mat: e.matmul(
                        psum[pb][:, ab_i * 256 + c * 128: ab_i * 256 + (c + 1) * 128], lhsT=zfT[:, c, t * 128:(t + 1) * 128], rhs=mat[:],
                        start=True, stop=True), reads=[b_zfT, b_dft], writes=[pbuf[pb]])
            rot.evac(abst[:, t, :], psum[pb][:, :], [pbuf[pb]], [b_abst])
        P.dma(lambda e, t0=t0, ntile=ntile, nb=nb: e.dma_start(out=io["ab"].ap()[t0:t0 + nb, :].rearrange("(t p) c -> p t c", p=128),
                                                               in_=abst[:, 0:ntile, :]), reads=[b_abst])
    P.barrier()
    P.release()
    if dbg == "mixA":
        P.release()
        P.release()
        return
    P.mark()
    pT = [P.sb([128, 512], BF16, "pT%d" % i) for i in range(3)]
    b_pT = [Buf(), Buf(), Buf()]
    rden = P.sb([128, 512], F32, "rden")
    b_rden = Buf()
    osb = P.sb([64, 512], F32, "osb")
    b_osb = Buf()
    ost = [P.sb([64, 512], BF16, "ost%d" % i) for i in range(2)]
    b_ost = [Buf(), Buf()]
    n_p = 0
    n_o = 0
    cat = io["cat"].ap()
    for bi, (t0, ntile, r) in enumerate(blocks):
        nb = ntile * 128
        key_tiles = list(range(34)) if r == 0 else [32, 33]
        for h in range(4):
            kvh = h // 2
            pacc = rot.bank()
            nk = len(key_tiles)
            pend = {}

            def qk_exp(ki, pacc=pacc, kvh=kvh, h=h, t0=t0, nb=nb, bi=bi):
                nonlocal n_p
                kt = key_tiles[ki]
                ps_ = rot.bank()
                while ps_ == pacc:
                    ps_ = rot.bank()
                P.op("pe", lambda e: e.matmul(psum[ps_][:, 0:nb], lhsT=kT[:, kvh, kt * 128:(kt + 1) * 128], rhs=qT[:, h, t0:t0 + nb],
                                              start=True, stop=True), reads=[b_kT, b_qT[bi]], writes=[pbuf[ps_]])
                pp = n_p % 3
                n_p += 1
                P.op("act", lambda e: e.activation(out=pT[pp][:, 0:nb], in_=psum[ps_][:, 0:nb], func=AF.Exp, scale=0.125),
                     reads=[pbuf[ps_]], writes=[b_pT[pp]])
                pend[ki] = pp

            def pv(ki, pacc=pacc, kvh=kvh, nb=nb, nk=nk):
                kt = key_tiles[ki]
                pp = pend.pop(ki)
                P.op("pe", lambda e: e.matmul(psum[pacc][0:65, 0:nb], lhsT=vx[:, kt, kvh, 0:65], rhs=pT[pp][:, 0:nb],
                                              start=(ki == 0), stop=(ki == nk - 1)), reads=[b_vx, b_pT[pp]], writes=[pbuf[pacc]])

            LOOK = 2
            for ki in range(min(LOOK, nk)):
                qk_exp(ki)
            for ki in range(nk):
                pv(ki)
                if ki + LOOK < nk:
                    qk_exp(ki + LOOK)
            P.op("dve", lambda e, pacc=pacc, nb=nb: e.reciprocal(out=rden[64:65, 0:nb], in_=psum[pacc][64:65, 0:nb]),
                 reads=[pbuf[pacc]], writes=[b_rden])
            P.op("act", lambda e, pacc=pacc, nb=nb: e.activation(out=osb[:, 0:nb], in_=psum[pacc][0:64, 0:nb], func=AF.Copy),
                 reads=[pbuf[pacc]], writes=[b_osb])
            pbc = rot.bank()
            P.op("pe", lambda e, pbc=pbc, nb=nb: e.matmul(psum[pbc][0:64, 0:nb], lhsT=g["ones"][64:65, 0:64], rhs=rden[64:65, 0:nb],
                                                          start=True, stop=True),
                 reads=[g["b_ones"], b_rden], writes=[pbuf[pbc]])
            oo = n_o % 2
            n_o += 1
            P.op("dve", lambda e, pbc=pbc, oo=oo, nb=nb: e.tensor_tensor(out=ost[oo][:, 0:nb], in0=psum[pbc][0:64, 0:nb], in1=osb[:, 0:nb], op=ALU.mult),
                 reads=[pbuf[pbc], b_osb], writes=[b_ost[oo]])
            P.dma(lambda e, oo=oo, h=h, t0=t0, nb=nb: e.dma_start(out=cat[h * 64:(h + 1) * 64, t0:t0 + nb], in_=ost[oo][:, 0:nb]),
                  reads=[b_ost[oo]])
    P.barrier()
    P.release()
    P.release()
    if dbg == "att":
        P.release()
        return
    phase_rwkv_prep(P, l)
    if dbg == "rw1":
        P.release()
        return
    phase_rwkv_scan(P, l, nsteps=(int(dbg[3:]) if dbg and dbg.startswith("rws") else 34))
    if dbg and dbg.startswith("rws"):
        P.release()
        return
    phase_rwkv_fin(P, l)
    if dbg == "rw3":
        P.release()
        return
    phase_pool(P, l)
    phase_fourier(P, l)
    if dbg == "pf":
        P.release()
        return
    phase_wout(P, l, m)
    P.release()


def phase_pool(P, l):
    nc, io, g = P.nc, P.io, P.g
    psum, pbuf = P.psum, P.pbuf
    rot = Rot(P)
    P.mark()
    zp = P.sb([128, 34, 256], BF16, "zp_all")
    b_zp = Buf()
    P.dma(lambda e: e.dma_start(out=zp[:], in_=io["zpool"].ap().rearrange("(t p) c -> p t c", p=128)), writes=[b_zp])
    pm = P.sb([128, 20, 128], BF16, "poolm")
    b_pm = Buf()
    P.dma(lambda e: e.dma_start(out=pm[:], in_=io["poolm"].ap().rearrange("k s t -> s k t")), writes=[b_pm])
    pwf = P.sb([64, 4, 64], F32, "pwf")
    pw = P.sb([64, 4, 64], BF16, "pw")
    b_pwf, b_pw = Buf(), Buf()
    P.dma(lambda e: e.dma_start(out=pwf[:], in_=io["pool_w"].ap()[l].rearrange("g c d -> c g d")), writes=[b_pwf])
    P.op("dve", lambda e: e.tensor_copy(out=pw[:], in_=pwf[:]), reads=[b_pwf], writes=[b_pw])
    psc = P.sb([64, 4], F32, "psc")
    b_psc = Buf()
    P.dma(lambda e: e.dma_start(out=psc[:], in_=io["pool_scale"].ap()[l, :].rearrange("(g d) -> d g", d=64),
                                allow_slow_non_contiguous=True), writes=[b_psc])
    pooled = [P.sb([64, 512], BF16, "pooled%d" % i) for i in range(2)]
    b_pooled = [Buf(), Buf()]
    ost = [P.sb([64, 512], BF16, "post%d" % i) for i in range(2)]
    b_ost = [Buf(), Buf()]
    cat = io["cat"].ap()
    n = 0
    seqs = [(0, 32), (32, 2)]
    for (tile0, nt) in seqs:
        for j0 in range(0, nt, 4):
            ntile = min(4, nt - j0)
            nb = ntile * 128
            for gi in range(4):
                pb = rot.bank()
                for jj in range(ntile):
                    j = j0 + jj
                    terms = []
                    if j > 0:
                        terms.append((tile0 + j - 1, 0))
                    terms.append((tile0 + j, 3 if j == 0 else (4 if j == nt - 1 else 2)))
                    if j < nt - 1:
                        terms.append((tile0 + j + 1, 1))
                    for ti, (st, kind) in enumerate(terms):
                        P.op("pe", lambda e, pb=pb, jj=jj, st=st, kind=kind, gi=gi, ti=ti, nterm=len(terms): e.matmul(
                            psum[pb][0:64, jj * 128:(jj + 1) * 128], lhsT=zp[:, st, gi * 64:(gi + 1) * 64], rhs=pm[:, gi * 5 + kind, :],
                            start=(ti == 0), stop=(ti == nterm - 1)), reads=[b_zp, b_pm], writes=[pbuf[pb]])
                k = n % 2
                n += 1
                rot.evac(pooled[k][:, 0:nb], psum[pb][0:64, 0:nb], [pbuf[pb]], [b_pooled[k]])
                pb2 = rot.bank()
                P.op("pe", lambda e, pb2=pb2, gi=gi, k=k, nb=nb: e.matmul(psum[pb2][0:64, 0:nb], lhsT=pw[:, gi, :], rhs=pooled[k][:, 0:nb],
                                                                        start=True, stop=True), reads=[b_pw, b_pooled[k]], writes=[pbuf[pb2]])
                P.op("act", lambda e, pb2=pb2, gi=gi, k=k, nb=nb: e.activation(out=ost[k][:, 0:nb], in_=psum[pb2][0:64, 0:nb], func=AF.Copy,
                                                                              scale=psc[:, gi:gi + 1]),
                     reads=[pbuf[pb2], b_psc], writes=[b_ost[k]])
                tok0 = (tile0 + j0) * 128
                P.dma(lambda e, k=k, gi=gi, tok0=tok0, nb=nb: e.dma_start(out=cat[512 + gi * 64:512 + (gi + 1) * 64, tok0:tok0 + nb],
                                                                       in_=ost[k][:, 0:nb]), reads=[b_ost[k]])
    P.barrier()
    P.release()


def phase_fourier(P, l):
    nc, io, g = P.nc, P.io, P.g
    psum, pbuf = P.psum, P.pbuf
    rot = Rot(P)
    P.mark()
    ab = P.sb([128, 34, 512], BF16, "ab_all")
    b_ab = Buf()
    for q in range(2):
        P.dma(lambda e, q=q: e.dma_start(out=ab[:, q * 17:(q + 1) * 17, :],
                                         in_=io["ab"].ap()[q * 17 * 128:(q + 1) * 17 * 128, :].rearrange("(t p) c -> p t c", p=128)),
              writes=[b_ab])
    fwf = P.sb([128, 2, 256], F32, "fwf")
    fw = P.sb([128, 2, 256], BF16, "fw")
    b_fwf, b_fw = Buf(), Buf()
    P.dma(lambda e: e.dma_start(out=fwf[:], in_=io["fourier_w"].ap()[l].rearrange("(c p) d -> p c d", p=128)), writes=[b_fwf])
    P.op("dve", lambda e: e.tensor_copy(out=fw[:], in_=fwf[:]), reads=[b_fwf], writes=[b_fw])
    dm = [[P.sb([128, 32, 512], BF16, "dm%d_%d" % (i, j)) for j in range(2)] for i in range(2)]
    b_dm = [[Buf(), Buf()], [Buf(), Buf()]]
    fT = [P.sb([128, 2, 512], BF16, "fT%d" % i) for i in range(2)]
    b_fT = [[Buf(), Buf()], [Buf(), Buf()]]
    ost = [P.sb([128, 512], BF16, "fost%d" % i) for i in range(2)]
    b_ost = [Buf(), Buf()]
    cat = io["cat"].ap()
    jobs = [(0, 32, tb * 512, 512, "dft_lat") for tb in range(8)] + [(32, 2, 0, 256, "dft_ctx")]
    n_o = 0
    for ji, (tile0, nt, c0, wd, mname) in enumerate(jobs):
        bsel = ji % 2
        for cs in range(2):
            P.dma(lambda e, bsel=bsel, cs=cs, nt=nt, c0=c0, wd=wd, mname=mname: e.dma_start(
                out=dm[bsel][cs][:, 0:nt, 0:wd], in_=io[mname].ap()[cs, :, c0:c0 + wd].rearrange("(t p) n -> p t n", p=128)),
                writes=[b_dm[bsel][cs]])
        for c in range(2):
            pb = rot.bank()
            for cs in range(2):
                for t in range(nt):
                    P.op("pe", lambda e, pb=pb, cs=cs, t=t, c=c, bsel=bsel, wd=wd, tile0=tile0, nt=nt: e.matmul(
                        psum[pb][:, 0:wd], lhsT=ab[:, tile0 + t, cs * 256 + c * 128: cs * 256 + (c + 1) * 128], rhs=dm[bsel][cs][:, t, 0:wd],
                        start=(cs == 0 and t == 0), stop=(cs == 1 and t == nt - 1)),
                        reads=[b_ab, b_dm[bsel][cs]], writes=[pbuf[pb]])
            rot.evac(fT[bsel][:, c, 0:wd], psum[pb][:, 0:wd], [pbuf[pb]], [b_fT[bsel][c]])
        for dc in range(2):
            pb = rot.bank()
            for c in range(2):
                P.op("pe", lambda e, pb=pb, c=c, dc=dc, bsel=bsel, wd=wd: e.matmul(
                    psum[pb][:, 0:wd], lhsT=fw[:, c, dc * 128:(dc + 1) * 128], rhs=fT[bsel][:, c, 0:wd], start=(c == 0), stop=(c == 1)),
                    reads=[b_fw, b_fT[bsel][c]], writes=[pbuf[pb]])
            k = n_o % 2
            n_o += 1
            rot.evac(ost[k][:, 0:wd], psum[pb][:, 0:wd], [pbuf[pb]], [b_ost[k]])
            tok0 = tile0 * 128 + c0
            P.dma(lambda e, k=k, dc=dc, tok0=tok0, wd=wd: e.dma_start(out=cat[768 + dc * 128:768 + (dc + 1) * 128, tok0:tok0 + wd],
                                                                   in_=ost[k][:, 0:wd]), reads=[b_ost[k]])
    P.barrier()
    P.release()


def phase_wout(P, l, m):
    nc, io, g = P.nc, P.io, P.g
    psum, pbuf = P.psum, P.pbuf
    rot = Rot(P)
    P.mark()
    wo = P.sb([128, 8, D], BF16, "wo")
    b_wo = Buf()
    P.mark()
    stg = [P.sb([128, D], F32, "stg%d" % i) for i in range(2)]
    b_stg = [Buf(), Buf()]
    load_cast_rows(P, wo, b_wo, lambda kc: io["w_out"].ap()[l, kc * 128:(kc + 1) * 128, :], 8, D, stg, b_stg)
    P.barrier()
    P.release()
    ct = [P.sb([128, 8, 512], BF16, "ct%d" % i) for i in range(2)]
    b_ct = [Buf(), Buf()]
    xt = [P.sb([128, D], F32, "xt%d" % i) for i in range(2)]
    b_xt = [Buf(), Buf()]
    ot = [P.sb([128, D], F32, "ot%d" % i) for i in range(2)]
    b_ot = [Buf(), Buf()]
    work = dict(u=P.sb([128, D], F32, "u"), b_u=Buf(), st=P.sb([128, 16], F32, "st"), b_st=Buf())
    cat = io["cat"].ap()
    xs_ap = io["xs"].ap()
    blocks = [(i * 512, 4, 0) for i in range(8)] + [(SEQ, 2, 1)]
    n = 0
    for bi, (t0, ntile, r) in enumerate(blocks):
        nb = ntile * 128
        cb = bi % 2
        P.dma(lambda e, cb=cb, t0=t0, nb=nb: e.dma_start(out=ct[cb][:, :, 0:nb], in_=cat[:, t0:t0 + nb].rearrange("(c p) t -> p c t", p=128)),
              writes=[b_ct[cb]])
        for t in range(ntile):
            k = n % 2
            n += 1
            tok0 = t0 + t * 128
            P.dma(lambda e, k=k, tok0=tok0: e.dma_start(out=xt[k][:], in_=xs_ap[tok0:tok0 + 128, :]), writes=[b_xt[k]])
            pbs = []
            for hn in range(2):
                pb = rot.bank()
                pbs.append(pb)
                for c in range(8):
                    P.op("pe", lambda e, pb=pb, c=c, hn=hn, cb=cb, t=t: e.matmul(
                        psum[pb][:, :], lhsT=ct[cb][:, c, t * 128:(t + 1) * 128], rhs=wo[:, c, hn * 512:(hn + 1) * 512],
                        start=(c == 0), stop=(c == 7)), reads=[b_ct[cb], b_wo], writes=[pbuf[pb]])
            emit_postnorm(P, m, r, xt[k][:], b_xt[k], [psum[pbs[0]][:, :], psum[pbs[1]][:, :]], [pbuf[pbs[0]], pbuf[pbs[1]]],
                          ot[k][:], b_ot[k], work)
            P.dma(lambda e, k=k, tok0=tok0: e.dma_start(out=xs_ap[tok0:tok0 + 128, :], in_=ot[k][:]), reads=[b_ot[k]])
    P.barrier()
    P.release()


LOGDECAY_SCALE = -0.6065306597126334
GN_EPS = 64e-5
CHUNK_ORDER = {0: [32, 33] + list(range(32)), 1: [33, 32] + list(range(31, -1, -1))}
RW_SEQS = [(i * 512, 512, i == 0, i == 7) for i in range(8)] + [(SEQ, 256, True, True)]


def col_param(P, src_ap_1d, name, n=4):
    t = P.sb([64, n], F32, name)
    b = Buf()
    P.dma(lambda e: e.dma_start(out=t[:], in_=src_ap_1d.rearrange("(h d) -> d h", d=64), allow_slow_non_contiguous=True), writes=[b])
    return t, b


def load_halo(P, buf_ap_fn, b_buf, src_fn, t0, nb, first, last):
    lo = 0 if not first else 1
    hi = nb + 2 if not last else nb + 1
    if first:
        P.op("pool", lambda e: e.memset(buf_ap_fn(0, 1), 0.0), writes=[b_buf])
    if last:
        P.op("pool", lambda e: e.memset(buf_ap_fn(nb + 1, nb + 2), 0.0), writes=[b_buf])
    P.dma(lambda e: e.dma_start(out=buf_ap_fn(lo, hi), in_=src_fn(t0 - 1 + lo, t0 - 1 + hi)), writes=[b_buf])


def phase_rwkv_prep(P, l):
    nc, io, g = P.nc, P.io, P.g
    psum, pbuf = P.psum, P.pbuf
    rot = Rot(P)
    P.mark()
    mu = [col_param(P, io["rwkv_mu"].ap()[l, i, :], "mu%d" % i) for i in range(6)]
    w0 = [col_param(P, io["decay_w0"].ap()[l, d, :], "w0%d" % d) for d in range(2)]
    a0 = [col_param(P, io["icl_a0"].ap()[l, d, :], "a0%d" % d) for d in range(2)]
    k_k = col_param(P, io["k_k"].ap()[l, :], "k_k")
    k_a = col_param(P, io["k_a"].ap()[l, :], "k_a")
    r_k = col_param(P, io["r_k"].ap()[l].rearrange("h d -> (h d)"), "r_k")
    hm, om = [], []
    for i in range(3):
        t1 = P.sb([64, 4], F32, "hm%d" % i)
        t2 = P.sb([64, 4], F32, "om%d" % i)
        b1, b2 = Buf(), Buf()
        P.op("dve", lambda e, i=i, t1=t1: e.tensor_scalar(out=t1[:], in0=mu[i][0][:], scalar1=0.5, scalar2=None, op0=ALU.mult),
             reads=[mu[i][1]], writes=[b1])
        P.op("dve", lambda e, i=i, t2=t2: e.tensor_scalar(out=t2[:], in0=mu[i][0][:], scalar1=-1.0, scalar2=1.0, op0=ALU.mult, op1=ALU.add),
             reads=[mu[i][1]], writes=[b2])
        hm.append((t1, b1))
        om.append((t2, b2))
    omka = P.sb([64, 4], F32, "omka")
    b_omka = Buf()
    P.op("dve", lambda e: e.tensor_scalar(out=omka[:], in0=k_a[0][:], scalar1=-1.0, scalar2=1.0, op0=ALU.mult, op1=ALU.add),
         reads=[k_a[1]], writes=[b_omka])

    def lora_in(src, name, rank):
        tf = P.sb([64, 4, rank], F32, name + "f")
        tb = P.sb([64, 4, rank], BF16, name)
        bf_, bb = Buf(), Buf()
        P.dma(lambda e: e.dma_start(out=tf[:], in_=src.rearrange("(h d) r -> d h r", d=64)), writes=[bf_])
        P.op("dve", lambda e: e.tensor_copy(out=tb[:], in_=tf[:]), reads=[bf_], writes=[bb])
        return tb, bb

    def lora_out(src, name, rank):
        tf = P.sb([rank, 256], F32, name + "f")
        tb = P.sb([rank, 256], BF16, name)
        bf_, bb = Buf(), Buf()
        P.dma(lambda e: e.dma_start(out=tf[:], in_=src), writes=[bf_])
        P.op("dve", lambda e: e.tensor_copy(out=tb[:], in_=tf[:]), reads=[bf_], writes=[bb])
        return tb, bb

    W1 = [lora_in(io["decay_w1"].ap()[l, d], "W1_%d" % d, 32) for d in range(2)]
    A1 = [lora_in(io["icl_a1"].ap()[l, d], "A1_%d" % d, 32) for d in range(2)]
    G1 = lora_in(io["gate_g1"].ap()[l], "G1", 64)
    W2 = [lora_out(io["decay_w2"].ap()[l, d], "W2_%d" % d, 32) for d in range(2)]
    A2 = [lora_out(io["icl_a2"].ap()[l, d], "A2_%d" % d, 32) for d in range(2)]
    G2 = lora_out(io["gate_g2"].ap()[l], "G2", 64)
    tw = P.sb([32, 2, NTOK], BF16, "tw")
    ta = P.sb([32, 2, NTOK], BF16, "ta")
    tg = P.sb([64, NTOK], BF16, "tg")
    b_tw, b_ta, b_tg = [Buf(), Buf()], [Buf(), Buf()], Buf()
    etot = P.sb([64, 8, 34], F32, "etot")
    b_etot = Buf()
    zrw = io["zrw"].ap()
    P.mark()
    zu = [P.sb([64, 4, 514], F32, "zu%d" % i) for i in range(2)]
    b_zu = [Buf(), Buf()]
    ssum = P.sb([64, 4, 512], F32, "ssum")
    du = P.sb([64, 4, 512], F32, "du")
    b_ssum, b_du = Buf(), Buf()
    xq = [P.sb([64, 4, 512], BF16, "xq%d" % i) for i in range(3)]
    b_xq = [Buf(), Buf(), Buf()]
    for bi, (t0, nb, first, last) in enumerate(RW_SEQS):
        z = bi % 2
        load_halo(P, lambda a, b, z=z: zu[z][:, :, a:b], b_zu[z], lambda a, b: zrw[3, :, :, a:b], t0, nb, first, last)
        P.op("dve", lambda e, z=z, nb=nb: e.tensor_tensor(out=ssum[:, :, 0:nb], in0=zu[z][:, :, 0:nb], in1=zu[z][:, :, 2:nb + 2], op=ALU.add),
             reads=[b_zu[z]], writes=[b_ssum])
        P.op("dve", lambda e, z=z, nb=nb: e.scalar_tensor_tensor(out=du[:, :, 0:nb], in0=ssum[:, :, 0:nb], scalar=0.5, in1=zu[z][:, :, 1:nb + 1],
                                                                 op0=ALU.mult, op1=ALU.subtract), reads=[b_ssum, b_zu[z]], writes=[b_du])
        for j in range(3):
            for h in range(4):
                eng = "dve" if (j * 4 + h) % 2 == 0 else "pool"
                if eng == "dve":
                    P.op("dve", lambda e, j=j, h=h, z=z, nb=nb: e.scalar_tensor_tensor(
                        out=xq[j][:, h, 0:nb], in0=du[:, h, 0:nb], scalar=mu[3 + j][0][:, h:h + 1], in1=zu[z][:, h, 1:nb + 1],
                        op0=ALU.mult, op1=ALU.add), reads=[b_du, b_zu[z], mu[3 + j][1]], writes=[b_xq[j]])
                else:
                    P.op("pool", lambda e, j=j, h=h, nb=nb: e.tensor_scalar(
                        out=xq[j][:, h, 0:nb], in0=du[:, h, 0:nb], scalar1=mu[3 + j][0][:, h:h + 1], scalar2=None, op0=ALU.mult),
                        reads=[b_du, mu[3 + j][1]], writes=[b_xq[j]])
                    P.op("pool", lambda e, j=j, h=h, z=z, nb=nb: e.tensor_tensor(
                        out=xq[j][:, h, 0:nb], in0=xq[j][:, h, 0:nb], in1=zu[z][:, h, 1:nb + 1], op=ALU.add),
                        reads=[b_xq[j], b_zu[z]], writes=[b_xq[j]])
        jobs = [(0, W1[0], 32, tw[:, 0, t0:t0 + nb], b_tw[0], AF.Tanh), (0, W1[1], 32, tw[:, 1, t0:t0 + nb], b_tw[1], AF.Tanh),
                (1, A1[0], 32, ta[:, 0, t0:t0 + nb], b_ta[0], AF.Copy), (1, A1[1], 32, ta[:, 1, t0:t0 + nb], b_ta[1], AF.Copy),
                (2, G1, 64, tg[:, t0:t0 + nb], b_tg, AF.Sigmoid)]
        for (j, wt, rank, dst, b_dst, fn) in jobs:
            pb = rot.bank()
            for h in range(4):
                P.op("pe", lambda e, pb=pb, h=h, j=j, wt=wt, rank=rank, nb=nb: e.matmul(
                    psum[pb][0:rank, 0:nb], lhsT=wt[0][:, h, :], rhs=xq[j][:, h, 0:nb], start=(h == 0), stop=(h == 3)),
                    reads=[wt[1], b_xq[j]], writes=[pbuf[pb]])
            P.op("act", lambda e, pb=pb, rank=rank, nb=nb, dst=dst, fn=fn: e.activation(out=dst, in_=psum[pb][0:rank, 0:nb], func=fn),
                 reads=[pbuf[pb]], writes=[b_dst])
    P.barrier()
    P.release()
    P.mark()
    z3 = [P.sb([64, 3, 514], F32, "z3_%d" % i) for i in range(2)]
    b_z3 = [Buf(), Buf()]
    s3 = P.sb([64, 3, 512], F32, "s3")
    b_s3 = Buf()
    rk = P.sb([64, 2, 512], F32, "rk")
    b_r, b_k = Buf(), Buf()
    Fs = [[P.sb([64, 5, 512], F32, "F%d_%d" % (i, d)) for d in range(2)] for i in range(2)]
    b_F = [[Buf(), Buf()], [Buf(), Buf()]]
    kkr = P.sb([64, 512], F32, "kkr")
    kk = P.sb([64, 512], F32, "kk")
    sqt = P.sb([64, 512], F32, "sqt")
    nrm = P.sb([64, 512], F32, "nrm")
    b_kkr, b_kk, b_sqt, b_nrm = Buf(), Buf(), Buf(), Buf()
    lw = P.sb([64, 512], F32, "lw")
    cl = P.sb([64, 512], F32, "cl")
    ci = P.sb([64, 512], F32, "ci")
    cml = P.sb([64, 512], F32, "cml")
    einc = P.sb([64, 512], F32, "einc")
    eexc = P.sb([64, 512], F32, "eexc")
    einv = P.sb([64, 512], F32, "einv")
    av = P.sb([64, 512], F32, "av")
    tt = P.sb([64, 512], F32, "tt")
    kd = P.sb([64, 512], F32, "kd")
    kds = P.sb([64, 512], F32, "kds")
    tmpb = P.sb([64, 512], F32, "tmpb")
    tot = P.sb([64, 4], F32, "tot")
    b_lw, b_cl, b_ci, b_cml, b_einc, b_eexc, b_einv, b_av, b_tt, b_kd, b_kds, b_tmpb, b_tot = [Buf() for _ in range(13)]
    aux = [P.sb([64, 2, 512], F32, "aux%d" % i) for i in range(2)]
    b_aux = [Buf(), Buf()]
    rwt = io["rwt"].ap()
    def head_block(h, t0, nb, first, last, n_it):
        if True:
            hs = slice(h, h + 1)
            nch = nb // 128
            z = n_it % 2
            fi = n_it % 2
            n_it += 1
            for i in range(3):
                load_halo(P, lambda a, b, z=z, i=i: z3[z][:, i, a:b], b_z3[z], lambda a, b, i=i: zrw[i, :, h, a:b], t0, nb, first, last)
            P.op("dve", lambda e, z=z, nb=nb: e.tensor_tensor(out=s3[:, :, 0:nb], in0=z3[z][:, :, 0:nb], in1=z3[z][:, :, 2:nb + 2], op=ALU.add),
                 reads=[b_z3[z]], writes=[b_s3])
            for i in range(3):
                P.op("act", lambda e, i=i, nb=nb: e.activation(out=s3[:, i, 0:nb], in_=s3[:, i, 0:nb], func=AF.Copy, scale=hm[i][0][:, hs]),
                     reads=[b_s3, hm[i][1]], writes=[b_s3])
            dsts = [(rk[:, 0, 0:nb], b_r), (rk[:, 1, 0:nb], b_k), (Fs[fi][0][:, 4, 0:nb], b_F[fi][0])]
            for i in range(3):
                P.op("dve", lambda e, i=i, z=z, nb=nb, dst=dsts[i][0]: e.scalar_tensor_tensor(
                    out=dst, in0=z3[z][:, i, 1:nb + 1], scalar=om[i][0][:, hs], in1=s3[:, i, 0:nb], op0=ALU.mult, op1=ALU.add),
                    reads=[b_z3[z], b_s3, om[i][1]], writes=[dsts[i][1]])
            r_ap, k_ap, v_ap = rk[:, 0, 0:nb], rk[:, 1, 0:nb], Fs[fi][0][:, 4, 0:nb]
            b_v = b_F[fi][0]
            P.op("act", lambda e, nb=nb, fi=fi, v_ap=v_ap: e.activation(out=Fs[fi][1][:, 4, 0:nb], in_=v_ap, func=AF.Copy), reads=[b_v], writes=[b_F[fi][1]])
            P.op("dve", lambda e, nb=nb, k_ap=k_ap: e.tensor_scalar(out=kkr[:, 0:nb], in0=k_ap, scalar1=k_k[0][:, hs], scalar2=None, op0=ALU.mult),
                 reads=[b_k, k_k[1]], writes=[b_kkr])
            P.op("act", lambda e, nb=nb: e.activation(out=sqt[:, 0:nb], in_=kkr[:, 0:nb], func=AF.Square), reads=[b_kkr], writes=[b_sqt])
            pb = rot.bank()
            P.op("pe", lambda e, pb=pb, nb=nb: e.matmul(psum[pb][0:64, 0:nb], lhsT=g["ones"][0:64, 0:64], rhs=sqt[:, 0:nb], start=True, stop=True),
                 reads=[g["b_ones"], b_sqt], writes=[pbuf[pb]])
            P.op("act", lambda e, pb=pb, nb=nb: e.activation(out=nrm[:, 0:nb], in_=psum[pb][0:64, 0:nb], func=AF.Sqrt), reads=[pbuf[pb]], writes=[b_nrm])
            P.op("dve", lambda e, nb=nb: e.tensor_scalar(out=nrm[:, 0:nb], in0=nrm[:, 0:nb], scalar1=1e-12, scalar2=None, op0=ALU.max),
                 reads=[b_nrm], writes=[b_nrm])
            P.op("dve", lambda e, nb=nb: e.reciprocal(out=nrm[:, 0:nb], in_=nrm[:, 0:nb]), reads=[b_nrm], writes=[b_nrm])
            P.op("dve", lambda e, nb=nb: e.tensor_tensor(out=kk[:, 0:nb], in0=kkr[:, 0:nb], in1=nrm[:, 0:nb], op=ALU.mult),
                 reads=[b_kkr, b_nrm], writes=[b_kk])
            def dir_part(d):
                s_id = h * 2 + d
                F = Fs[fi][d]
                bF = b_F[fi][d]
                pb = rot.bank()
                P.op("pe", lambda e, pb=pb, d=d, nb=nb: e.matmul(psum[pb][0:64, 0:nb], lhsT=W2[d][0][:, h * 64:(h + 1) * 64], rhs=tw[:, d, t0:t0 + nb],
                                                                 start=True, stop=True), reads=[W2[d][1], b_tw[d]], writes=[pbuf[pb]])
                P.op("act", lambda e, pb=pb, d=d, nb=nb: e.activation(out=lw[:, 0:nb], in_=psum[pb][0:64, 0:nb], func=AF.Sigmoid, bias=w0[d][0][:, hs]),
                     reads=[pbuf[pb], w0[d][1]], writes=[b_lw])
                P.op("act", lambda e, nb=nb: e.activation(out=lw[:, 0:nb], in_=lw[:, 0:nb], func=AF.Copy, scale=LOGDECAY_SCALE),
                     reads=[b_lw], writes=[b_lw])
                for j in range(nch):
                    P.op("dve", lambda e, j=j: e.tensor_tensor_scan(out=cl[:, j * 128:(j + 1) * 128], data0=g["ones"][0:64, 0:128],
                                                                   data1=lw[:, j * 128:(j + 1) * 128], initial=0.0, op0=ALU.mult, op1=ALU.add),
                         reads=[b_lw, g["b_ones"]], writes=[b_cl])
                clv = cl[:, 0:nb].rearrange("p (c j) -> p c j", j=128)
                P.op("dve", lambda e, nch=nch, clv=clv: e.tensor_copy(out=tot[:, 0:nch], in_=clv[:, :, 127]), reads=[b_cl], writes=[b_tot])
                c0 = t0 // 128
                P.op("act", lambda e, nch=nch, s_id=s_id, c0=c0: e.activation(out=etot[:, s_id, c0:c0 + nch], in_=tot[:, 0:nch], func=AF.Exp),
                     reads=[b_tot], writes=[b_etot])
                if d == 0:
                    ci_ap, b_cix = cl, b_cl
                else:
                    P.op("dve", lambda e, nch=nch, nb=nb, clv=clv: e.tensor_tensor(
                        out=ci[:, 0:nb].rearrange("p (c j) -> p c j", j=128), in0=tot[:, 0:nch].unsqueeze(2).broadcast_to([64, nch, 128]),
                        in1=clv, op=ALU.subtract), reads=[b_tot, b_cl], writes=[b_ci])
                    P.op("dve", lambda e, nb=nb: e.tensor_tensor(out=ci[:, 0:nb], in0=ci[:, 0:nb], in1=lw[:, 0:nb], op=ALU.add),
                         reads=[b_ci, b_lw], writes=[b_ci])
                    ci_ap, b_cix = ci, b_ci
                P.op("pool", lambda e, nb=nb, ci_ap=ci_ap: e.tensor_tensor(out=cml[:, 0:nb], in0=ci_ap[:, 0:nb], in1=lw[:, 0:nb], op=ALU.subtract),
                     reads=[b_cix, b_lw], writes=[b_cml])
                P.op("act", lambda e, nb=nb, ci_ap=ci_ap: e.activation(out=einc[:, 0:nb], in_=ci_ap[:, 0:nb], func=AF.Exp), reads=[b_cix], writes=[b_einc])
                P.op("act", lambda e, nb=nb, ci_ap=ci_ap: e.activation(out=einv[:, 0:nb], in_=ci_ap[:, 0:nb], func=AF.Exp, scale=-1.0),
                     reads=[b_cix], writes=[b_einv])
                P.op("act", lambda e, nb=nb: e.activation(out=eexc[:, 0:nb], in_=cml[:, 0:nb], func=AF.Exp), reads=[b_cml], writes=[b_eexc])
                pb = rot.bank()
                P.op("pe", lambda e, pb=pb, d=d, nb=nb: e.matmul(psum[pb][0:64, 0:nb], lhsT=A2[d][0][:, h * 64:(h + 1) * 64], rhs=ta[:, d, t0:t0 + nb],
                                                                 start=True, stop=True), reads=[A2[d][1], b_ta[d]], writes=[pbuf[pb]])
                P.op("act", lambda e, pb=pb, d=d, nb=nb: e.activation(out=av[:, 0:nb], in_=psum[pb][0:64, 0:nb], func=AF.Sigmoid, bias=a0[d][0][:, hs]),
                     reads=[pbuf[pb], a0[d][1]], writes=[b_av])
                P.op("dve", lambda e, nb=nb: e.tensor_scalar(out=tt[:, 0:nb], in0=av[:, 0:nb], scalar1=k_a[0][:, hs], scalar2=omka[:, hs],
                                                             op0=ALU.mult, op1=ALU.add), reads=[b_av, k_a[1], b_omka], writes=[b_tt])
                P.op("dve", lambda e, nb=nb, k_ap=k_ap: e.tensor_tensor(out=kd[:, 0:nb], in0=k_ap, in1=tt[:, 0:nb], op=ALU.mult),
                     reads=[b_k, b_tt], writes=[b_kd])
                if d == 0:
                    P.op("pool", lambda e, nb=nb: e.tensor_copy(out=kds[:, 0:nb], in_=kd[:, 0:nb]), reads=[b_kd], writes=[b_kds])
                else:
                    P.op("pool", lambda e, nb=nb: e.tensor_tensor(out=kds[:, 0:nb], in0=kds[:, 0:nb], in1=kd[:, 0:nb], op=ALU.add),
                         reads=[b_kd, b_kds], writes=[b_kds])
                P.op("dve", lambda e, nb=nb, F=F: e.scalar_tensor_tensor(out=F[:, 0, 0:nb], in0=kk[:, 0:nb], scalar=-1.0, in1=eexc[:, 0:nb],
                                                                         op0=ALU.mult, op1=ALU.mult), reads=[b_kk, b_eexc], writes=[bF])
                P.op("pool", lambda e, nb=nb: e.tensor_tensor(out=tmpb[:, 0:nb], in0=kk[:, 0:nb], in1=av[:, 0:nb], op=ALU.mult),
                     reads=[b_kk, b_av], writes=[b_tmpb])
                P.op("dve", lambda e, nb=nb, F=F: e.tensor_tensor(out=F[:, 1, 0:nb], in0=tmpb[:, 0:nb], in1=einv[:, 0:nb], op=ALU.mult),
                     reads=[b_tmpb, b_einv], writes=[bF])
                P.op("pool", lambda e, nb=nb, F=F: e.tensor_tensor(out=F[:, 2, 0:nb], in0=kd[:, 0:nb], in1=einv[:, 0:nb], op=ALU.mult),
                     reads=[b_kd, b_einv], writes=[bF])
                P.op("dve", lambda e, nb=nb, F=F, r_ap=r_ap: e.tensor_tensor(out=F[:, 3, 0:nb], in0=r_ap, in1=einc[:, 0:nb], op=ALU.mult),
                     reads=[b_r, b_einc], writes=[bF])
                P.dma(lambda e, F=F, s_id=s_id, nb=nb: e.dma_start(out=rwt[s_id].rearrange("f d t -> d f t")[:, :, t0:t0 + nb], in_=F[:, :, 0:nb]),
                      reads=[bF])
            dir_part(0)
            dir_part(1)
            ax = aux[n_it % 2]
            b_ax = b_aux[n_it % 2]
            pb = rot.bank()
            P.op("pe", lambda e, pb=pb, nb=nb: e.matmul(psum[pb][0:64, 0:nb], lhsT=G2[0][:, h * 64:(h + 1) * 64], rhs=tg[:, t0:t0 + nb],
                                                        start=True, stop=True), reads=[G2[1], b_tg], writes=[pbuf[pb]])
            P.op("act", lambda e, pb=pb, nb=nb, ax=ax: e.activation(out=ax[:, 0, 0:nb], in_=psum[pb][0:64, 0:nb], func=AF.Copy),
                 reads=[pbuf[pb]], writes=[b_ax])
            P.op("dve", lambda e, nb=nb, r_ap=r_ap: e.scalar_tensor_tensor(out=tmpb[:, 0:nb], in0=r_ap, scalar=r_k[0][:, hs], in1=kds[:, 0:nb],
                                                                           op0=ALU.mult, op1=ALU.mult), reads=[b_r, b_kds, r_k[1]], writes=[b_tmpb])
            pb = rot.bank()
            P.op("pe", lambda e, pb=pb, nb=nb: e.matmul(psum[pb][0:64, 0:nb], lhsT=g["ones"][0:64, 0:64], rhs=tmpb[:, 0:nb], start=True, stop=True),
                 reads=[g["b_ones"], b_tmpb], writes=[pbuf[pb]])
            P.op("dve", lambda e, pb=pb, nb=nb, ax=ax, v_ap=v_ap: e.tensor_tensor(out=ax[:, 1, 0:nb], in0=psum[pb][0:64, 0:nb], in1=v_ap, op=ALU.mult),
                 reads=[pbuf[pb], b_v], writes=[b_ax])
            P.dma(lambda e, ax=ax, nb=nb: e.dma_start(out=io["rwaux"].ap()[:, h, :, t0:t0 + nb].rearrange("a d t -> d a t"), in_=ax[:, :, 0:nb]),
                  reads=[b_ax])
    n_it = 0
    for h in range(4):
        for (t0, nb, first, last) in RW_SEQS:
            head_block(h, t0, nb, first, last, n_it)
            n_it += 1
    P.dma(lambda e: e.dma_start(out=io["etot"].ap(), in_=etot[:]), reads=[b_etot])
    P.barrier()
    P.release()
    P.release()


class RwStream:
    pass


def phase_rwkv_scan(P, l, nsteps=34):
    nc, io, g = P.nc, P.io, P.g
    psum = P.psum
    P.mark()
    slot_ap = [psum[i // 2][:, (i % 2) * 256:(i % 2 + 1) * 256] for i in range(16)]
    bank_b = [Buf(psum=True) for _ in range(8)]
    slot_b = [bank_b[i // 2] for i in range(16)]
    masks = P.sb([128, 5, 128], F32, "masks")
    b_masks = Buf()
    P.dma(lambda e: e.dma_start(out=masks[:], in_=io["masks"].ap().rearrange("m p f -> p m f")), writes=[b_masks])
    identb = P.sb([64, 64], BF16, "identb")
    b_identb = Buf()
    P.op("dve", lambda e: e.tensor_copy(out=identb[:], in_=g["ident"][0:64, 0:64]), reads=[g["b_ident"]], writes=[b_identb])
    etot = P.sb([64, 8, 34], F32, "etot2")
    b_etot = Buf()
    P.dma(lambda e: e.dma_start(out=etot[:], in_=io["etot"].ap()), writes=[b_etot])
    rwt = io["rwt"].ap()
    yT = io["yT"].ap()
    ident = g["ident"]

    streams = []
    for s_id in range(8):
        S = RwStream()
        S.id = s_id
        S.d = s_id % 2
        S.F = [P.sb([128, 5, 128], F32, "F%d_%d" % (s_id, i)) for i in range(2)]
        S.b_F = [Buf(), Buf()]
        for i in range(2):
            P.op("pool", lambda e, S=S, i=i: e.memset(S.F[i][64:128, :, :], 0.0), writes=[S.b_F[i]])
        S.Fb = P.sb([64, 5, 128], BF16, "Fb%d" % s_id)
        S.b_Fb = Buf()
        S.Atok = P.sb([128, 64], F32, "Atok%d" % s_id)
        S.b_Atok = Buf()
        S.BKV = P.sb([128, 3, 64], BF16, "BKV%d" % s_id)
        S.b_BKV = Buf()
        S.LQ = [P.sb([128, 256], F32, "LQ%d_%d" % (s_id, i)) for i in range(2)]
        S.b_LQ = [Buf(), Buf()]
        S.Z = [P.sb([128, 128], F32, "Z%d_%d" % (s_id, i)) for i in range(2)]
        S.b_Z = [Buf(), Buf()]
        S.LakT = P.sb([128, 128], BF16, "LakT%d" % s_id)
        S.MrbT = P.sb([128, 128], BF16, "MrbT%d" % s_id)
        S.MrkT = P.sb([128, 128], BF16, "MrkT%d" % s_id)
        S.b_LakT, S.b_MrbT, S.b_MrkT = Buf(), Buf(), Buf()
        S.W = P.sb([128, 64], BF16, "W%d" % s_id)
        S.WT = P.sb([64, 128], BF16, "WT%d" % s_id)
        S.X = P.sb([128, 64], F32, "X%d" % s_id)
        S.U0 = P.sb([128, 64], F32, "U0%d" % s_id)
        S.U0b = P.sb([128, 64], BF16, "U0b%d" % s_id)
        S.GT = P.sb([64, 64], BF16, "GT%d" % s_id)
        S.DE = P.sb([64, 64], F32, "DE%d" % s_id)
        S.Ub = P.sb([128, 64], BF16, "Ub%d" % s_id)
        S.b_W, S.b_WT, S.b_X, S.b_U0, S.b_U0b, S.b_GT, S.b_DE, S.b_Ub = [Buf() for _ in range(8)]
        S.H = [P.sb([64, 64], BF16, "H%d_%d" % (s_id, i)) for i in range(2)]
        S.b_H = [Buf(), Buf()]
        S.ys = [P.sb([64, 128], F32, "ys%d_%d" % (s_id, i)) for i in range(2)]
        S.b_ys = [Buf(), Buf()]
        S.slot = 0
        S.ei = s_id
        S.mLQ, S.mQ, S.mM = (0, 1, 4) if S.d == 0 else (1, 2, 3)
        P.op("pool", lambda e, S=S: e.memset(S.H[0][:], 0.0), writes=[S.b_H[0]])
        streams.append(S)

    def next_slot(S):
        i = 2 * S.id + (S.slot % 2)
        S.slot += 1
        return slot_ap[i], slot_b[i]

    def ev_eng(S):
        S.ei += 1
        return ("act", "dve")[S.ei % 2]

    def load(S, step):
        c = CHUNK_ORDER[S.d][step]
        fb = step % 2
        P.dma(lambda e: e.dma_start(out=S.F[fb][0:64, :, :], in_=rwt[S.id].rearrange("f d t -> d f t")[:, :, c * 128:(c + 1) * 128]),
              writes=[S.b_F[fb]])

    def stage_prep(S, step):
        fb = step % 2
        F, bF = S.F[fb], S.b_F[fb]
        P.op("pool", lambda e: e.tensor_copy(out=S.Fb[:], in_=F[0:64, :, :]), reads=[bF], writes=[S.b_Fb])
        ps, pb = next_slot(S)
        for i, fidx in enumerate((0, 1, 2, 4)):
            P.op("pe", lambda e, i=i, fidx=fidx: e.matmul(ps[:, i * 64:(i + 1) * 64], lhsT=F[:, fidx, :], rhs=ident[:, 0:64],
                                                          start=True, stop=True),
                 reads=[bF, g["b_ident"]], writes=[pb])
        yield
        P.op("act", lambda e: e.activation(out=S.Atok[:], in_=ps[:, 0:64], func=AF.Copy), reads=[pb], writes=[S.b_Atok])
        P.op("dve", lambda e: e.tensor_copy(out=S.BKV[:], in_=ps[:, 64:256].rearrange("p (a d) -> p a d", a=3)), reads=[pb], writes=[S.b_BKV])
        yield

    def stage_scores(S, step):
        fb = step % 2
        F, bF = S.F[fb], S.b_F[fb]
        ps, pb = next_slot(S)
        P.op("pe", lambda e: e.matmul(ps[:, 0:128], lhsT=F[:, 0, :], rhs=F[:, 1, :], start=True, stop=True), reads=[bF], writes=[pb])
        P.op("pe", lambda e: e.matmul(ps[:, 128:256], lhsT=F[:, 1, :], rhs=F[:, 0, :], start=True, stop=True), reads=[bF], writes=[pb])
        yield
        P.op("dve", lambda e: e.tensor_tensor(out=S.LQ[0][:].rearrange("p (a f) -> p a f", a=2), in0=ps[:, 0:256].rearrange("p (a f) -> p a f", a=2),
                                              in1=masks[:, S.mLQ:S.mLQ + 2, :], op=ALU.mult), reads=[pb, b_masks], writes=[S.b_LQ[0]])
        P.op("pool", lambda e: e.tensor_tensor(out=S.Z[0][:], in0=S.LQ[0][:, 128:256], in1=ident[:], op=ALU.add),
             reads=[S.b_LQ[0], g["b_ident"]], writes=[S.b_Z[0]])
        yield
        ps2, pb2 = next_slot(S)
        P.op("pe", lambda e: e.matmul(ps2[:, 0:128], lhsT=S.Fb[:, 2, :], rhs=S.Fb[:, 0, :], start=True, stop=True), reads=[S.b_Fb], writes=[pb2])
        P.op("pe", lambda e: e.matmul(ps2[:, 128:256], lhsT=S.Fb[:, 1, :], rhs=S.Fb[:, 3, :], start=True, stop=True), reads=[S.b_Fb], writes=[pb2])
        yield
        P.op("dve", lambda e: e.tensor_tensor(out=S.LakT[:], in0=ps2[:, 0:128], in1=masks[:, S.mQ, :], op=ALU.mult),
             reads=[pb2, b_masks], writes=[S.b_LakT])
        P.op("dve", lambda e: e.tensor_tensor(out=S.MrbT[:], in0=ps2[:, 128:256], in1=masks[:, S.mM, :], op=ALU.mult),
             reads=[pb2, b_masks], writes=[S.b_MrbT])
        yield
        ps3, pb3 = next_slot(S)
        P.op("pe", lambda e: e.matmul(ps3[:, 0:128], lhsT=S.Fb[:, 2, :], rhs=S.Fb[:, 3, :], start=True, stop=True), reads=[S.b_Fb], writes=[pb3])
        yield
        P.op("dve", lambda e: e.tensor_tensor(out=S.MrkT[:], in0=ps3[:, 0:128], in1=masks[:, S.mM, :], op=ALU.mult),
             reads=[pb3, b_masks], writes=[S.b_MrkT])
        yield

    def stage_double(S, lvl):
        a, b = (lvl - 1) % 2, lvl % 2
        last = lvl == 6
        ps, pb = next_slot(S)
        La, Qa = S.LQ[a][:, 0:128], S.LQ[a][:, 128:256]
        P.op("pe", lambda e: e.matmul(ps[:, 0:128], lhsT=Qa, rhs=La, start=True, stop=True), reads=[S.b_LQ[a]], writes=[pb])
        if not last:
            P.op("pe", lambda e: e.matmul(ps[:, 128:256], lhsT=La, rhs=Qa, start=True, stop=True), reads=[S.b_LQ[a]], writes=[pb])
        yield
        w = 128 if last else 256
        eng = ev_eng(S)
        if eng == "act":
            P.op("act", lambda e: e.activation(out=S.LQ[b][:, 0:w], in_=ps[:, 0:w], func=AF.Copy), reads=[pb], writes=[S.b_LQ[b]])
        else:
            P.op("dve", lambda e: e.tensor_copy(out=S.LQ[b][:, 0:w], in_=ps[:, 0:w]), reads=[pb], writes=[S.b_LQ[b]])
        yield
        ps2, pb2 = next_slot(S)
        P.op("pe", lambda e: e.matmul(ps2[:, 0:128], lhsT=S.LQ[b][:, 0:128], rhs=S.Z[a][:], start=True, stop=True),
             reads=[S.b_LQ[b], S.b_Z[a]], writes=[pb2])
        yield
        P.op("dve", lambda e: e.tensor_tensor(out=S.Z[b][:], in0=ps2[:, 0:128], in1=S.Z[a][:], op=ALU.add),
             reads=[pb2, S.b_Z[a]], writes=[S.b_Z[b]])
        yield

    def stage_wux(S):
        Z, bZ = S.Z[0], S.b_Z[0]
        ps, pb = next_slot(S)
        P.op("pe", lambda e: e.matmul(ps[:, 0:64], lhsT=Z[:], rhs=S.Atok[:], start=True, stop=True), reads=[bZ, S.b_Atok], writes=[pb])
        P.op("pe", lambda e: e.matmul(ps[0:64, 64:192], lhsT=S.Atok[:], rhs=Z[:], start=True, stop=True), reads=[bZ, S.b_Atok], writes=[pb])
        P.op("pe", lambda e: e.matmul(ps[:, 192:256], lhsT=S.LakT[:], rhs=S.BKV[:, 2, :], start=True, stop=True),
             reads=[S.b_LakT, S.b_BKV], writes=[pb])
        yield
        P.op("dve", lambda e: e.tensor_copy(out=S.X[:], in_=ps[:, 192:256]), reads=[pb], writes=[S.b_X])
        P.op("act", lambda e: e.activation(out=S.W[:], in_=ps[:, 0:64], func=AF.Copy), reads=[pb], writes=[S.b_W])
        P.op("act", lambda e: e.activation(out=S.WT[:], in_=ps[0:64, 64:192], func=AF.Copy), reads=[pb], writes=[S.b_WT])
        yield
        ps2, pb2 = next_slot(S)
        P.op("pe", lambda e: e.matmul(ps2[:, 0:64], lhsT=Z[:], rhs=S.X[:], start=True, stop=True), reads=[bZ, S.b_X], writes=[pb2])
        yield
        P.op("act", lambda e: e.activation(out=S.U0[:], in_=ps2[:, 0:64], func=AF.Copy), reads=[pb2], writes=[S.b_U0])
        P.op("act", lambda e: e.activation(out=S.U0b[:], in_=ps2[:, 0:64], func=AF.Copy), reads=[pb2], writes=[S.b_U0b])
        yield

    def stage_gd(S, step):
        c = CHUNK_ORDER[S.d][step]
        ps, pb = next_slot(S)
        P.op("pe", lambda e: e.matmul(ps[0:64, 0:64], lhsT=S.W[:], rhs=S.BKV[:, 0, :], start=True, stop=False),
             reads=[S.b_W, S.b_BKV], writes=[pb])
        P.op("pe", lambda e: e.matmul(ps[0:64, 0:64], lhsT=identb[:], rhs=identb[:], start=False, stop=True), reads=[b_identb], writes=[pb])
        P.op("pe", lambda e: e.matmul(ps[0:64, 64:128], lhsT=S.BKV[:, 0, :], rhs=S.U0b[:], start=True, stop=False),
             reads=[S.b_BKV, S.b_U0b], writes=[pb])
        P.op("pe", lambda e: e.matmul(ps[0:64, 64:128], lhsT=S.BKV[:, 1, :], rhs=S.BKV[:, 2, :], start=False, stop=True),
             reads=[S.b_BKV], writes=[pb])
        yield
        P.op("act", lambda e: e.activation(out=S.GT[:], in_=ps[0:64, 0:64], func=AF.Copy), reads=[pb], writes=[S.b_GT])
        P.op("act", lambda e: e.activation(out=S.DE[:], in_=ps[0:64, 64:128], func=AF.Copy, scale=etot[:, S.id, c:c + 1]),
             reads=[pb, b_etot], writes=[S.b_DE])
        yield

    def stage_state(S, step):
        c = CHUNK_ORDER[S.d][step]
        hi, ho = step % 2, (step + 1) % 2
        H, bH = S.H[hi], S.b_H[hi]
        ps, pb = next_slot(S)
        P.op("pe", lambda e: e.matmul(ps[:, 0:64], lhsT=S.WT[:], rhs=H[:], start=True, stop=True), reads=[S.b_WT, bH], writes=[pb])
        yield
        P.op("dve", lambda e: e.tensor_tensor(out=S.Ub[:], in0=ps[:, 0:64], in1=S.U0[:], op=ALU.add), reads=[pb, S.b_U0], writes=[S.b_Ub])
        yield
        ps2, pb2 = next_slot(S)
        P.op("pe", lambda e: e.matmul(ps2[0:64, 0:128], lhsT=H[:], rhs=S.Fb[:, 3, :], start=True, stop=False), reads=[bH, S.b_Fb], writes=[pb2])
        P.op("pe", lambda e: e.matmul(ps2[0:64, 0:128], lhsT=S.Ub[:], rhs=S.MrbT[:], start=False, stop=False),
             reads=[S.b_Ub, S.b_MrbT], writes=[pb2])
        P.op("pe", lambda e: e.matmul(ps2[0:64, 0:128], lhsT=S.BKV[:, 2, :], rhs=S.MrkT[:], start=False, stop=True),
             reads=[S.b_BKV, S.b_MrkT], writes=[pb2])
        P.op("pe", lambda e: e.matmul(ps2[0:64, 128:192], lhsT=S.GT[:], rhs=H[:], start=True, stop=True), reads=[S.b_GT, bH], writes=[pb2])
        yield
        yb = step % 2
        P.op("dve", lambda e: e.scalar_tensor_tensor(out=S.H[ho][:], in0=ps2[0:64, 128:192], scalar=etot[:, S.id, c:c + 1], in1=S.DE[:],
                                                     op0=ALU.mult, op1=ALU.add), reads=[pb2, b_etot, S.b_DE], writes=[S.b_H[ho]])
        P.op("dve", lambda e: e.tensor_copy(out=S.ys[yb][:], in_=ps2[0:64, 0:128]), reads=[pb2], writes=[S.b_ys[yb]])
        P.dma(lambda e: e.dma_start(out=yT[S.id, :, c * 128:(c + 1) * 128], in_=S.ys[yb][:]), reads=[S.b_ys[yb]])
        yield

    def lockstep(gens):
        gens = list(gens)
        while gens:
            for gg in list(gens):
                try:
                    next(gg)
                except StopIteration:
                    gens.remove(gg)

    for S in streams:
        load(S, 0)
    for step in range(nsteps):
        if step + 1 < nsteps:
            for S in streams:
                load(S, step + 1)
        lockstep(stage_prep(S, step) for S in streams)
        lockstep(stage_scores(S, step) for S in streams)
        for lvl in range(1, 7):
            lockstep(stage_double(S, lvl) for S in streams)
        lockstep(stage_wux(S) for S in streams)
        lockstep(stage_gd(S, step) for S in streams)
        lockstep(stage_state(S, step) for S in streams)
    P.barrier()
    P.release()


def phase_rwkv_fin(P, l):
    nc, io, g = P.nc, P.io, P.g
    psum, pbuf = P.psum, P.pbuf
    rot = Rot(P)
    P.mark()
    gn_g = col_param(P, io["gn_g"].ap()[l, :], "gn_g")
    gn_b = col_param(P, io["gn_b"].ap()[l, :], "gn_b")
    od = P.sb([64, 64], F32, "onesdiv")
    b_od = Buf()
    P.op("dve", lambda e: e.memset(od[:], 1.0 / 64), writes=[b_od])
    y2 = [P.sb([64, 2, 512], F32, "y2_%d" % i) for i in range(2)]
    ax = [P.sb([64, 2, 512], F32, "ax_%d" % i) for i in range(2)]
    b_y2, b_ax = [Buf(), Buf()], [Buf(), Buf()]
    y = P.sb([64, 512], F32, "y")
    yc = P.sb([64, 512], F32, "yc")
    sq = P.sb([64, 512], F32, "sq")
    sd = P.sb([64, 512], F32, "sd")
    b_y, b_yc, b_sq, b_sd = Buf(), Buf(), Buf(), Buf()
    ost = [P.sb([64, 512], BF16, "rost%d" % i) for i in range(2)]
    b_ost = [Buf(), Buf()]
    yT = io["yT"].ap()
    cat = io["cat"].ap()

    def blk(h, t0, nb, n):
        k = n % 2
        hs = slice(h, h + 1)
        P.dma(lambda e: e.dma_start(out=y2[k][:, :, 0:nb], in_=yT[2 * h:2 * h + 2, :, t0:t0 + nb].rearrange("s d t -> d s t")), writes=[b_y2[k]])
        P.dma(lambda e: e.dma_start(out=ax[k][:, :, 0:nb], in_=io["rwaux"].ap()[:, h, :, t0:t0 + nb].rearrange("a d t -> d a t")), writes=[b_ax[k]])
        P.op("dve", lambda e: e.tensor_tensor(out=y[:, 0:nb], in0=y2[k][:, 0, 0:nb], in1=y2[k][:, 1, 0:nb], op=ALU.add), reads=[b_y2[k]], writes=[b_y])
        pb = rot.bank()
        P.op("pe", lambda e: e.matmul(psum[pb][0:64, 0:nb], lhsT=od[:], rhs=y[:, 0:nb], start=True, stop=True), reads=[b_od, b_y], writes=[pbuf[pb]])
        P.op("dve", lambda e: e.tensor_tensor(out=yc[:, 0:nb], in0=y[:, 0:nb], in1=psum[pb][0:64, 0:nb], op=ALU.subtract),
             reads=[b_y, pbuf[pb]], writes=[b_yc])
        P.op("pool", lambda e: e.tensor_tensor(out=sq[:, 0:nb], in0=yc[:, 0:nb], in1=yc[:, 0:nb], op=ALU.mult), reads=[b_yc], writes=[b_sq])
        pb2 = rot.bank()
        P.op("pe", lambda e: e.matmul(psum[pb2][0:64, 0:nb], lhsT=od[:], rhs=sq[:, 0:nb], start=True, stop=True), reads=[b_od, b_sq], writes=[pbuf[pb2]])
        P.op("dve", lambda e: e.tensor_scalar(out=sd[:, 0:nb], in0=psum[pb2][0:64, 0:nb], scalar1=float(GN_EPS), scalar2=None, op0=ALU.add),
             reads=[pbuf[pb2]], writes=[b_sd])
        P.op("act", lambda e: e.activation(out=sd[:, 0:nb], in_=sd[:, 0:nb], func=AF.Sqrt), reads=[b_sd], writes=[b_sd])
        P.op("dve", lambda e: e.reciprocal(out=sd[:, 0:nb], in_=sd[:, 0:nb]), reads=[b_sd], writes=[b_sd])
        P.op("dve", lambda e: e.tensor_tensor(out=yc[:, 0:nb], in0=yc[:, 0:nb], in1=sd[:, 0:nb], op=ALU.mult), reads=[b_yc, b_sd], writes=[b_yc])
        P.op("pool", lambda e: e.tensor_scalar(out=yc[:, 0:nb], in0=yc[:, 0:nb], scalar1=gn_g[0][:, hs], scalar2=gn_b[0][:, hs],
                                               op0=ALU.mult, op1=ALU.add), reads=[b_yc, gn_g[1], gn_b[1]], writes=[b_yc])
        P.op("dve", lambda e: e.tensor_tensor(out=yc[:, 0:nb], in0=yc[:, 0:nb], in1=ax[k][:, 1, 0:nb], op=ALU.add), reads=[b_yc, b_ax[k]], writes=[b_yc])
        P.op("pool", lambda e: e.tensor_tensor(out=ost[k][:, 0:nb], in0=yc[:, 0:nb], in1=ax[k][:, 0, 0:nb], op=ALU.mult),
             reads=[b_yc, b_ax[k]], writes=[b_ost[k]])
        P.dma(lambda e: e.dma_start(out=cat[256 + h * 64:256 + (h + 1) * 64, t0:t0 + nb], in_=ost[k][:, 0:nb]), reads=[b_ost[k]])

    n = 0
    for h in range(4):
        for (t0, nb, first, last) in RW_SEQS:
            blk(h, t0, nb, n)
            n += 1
    P.barrier()
    P.release()


_NC_CACHE = {}


def _get_nc(dbg=None):
    if dbg not in _NC_CACHE:
        _NC_CACHE[dbg] = build_program(dbg)
    return _NC_CACHE[dbg]


_CONST = {}


def _constants():
    if _CONST:
        return _CONST
    bf = ml_dtypes.bfloat16
    t = np.arange(SEQ)
    rows = (t // 64).astype(np.float64)
    cols = (t % 64).astype(np.float64)
    inv = 10000.0 ** (-np.arange(16, dtype=np.float64) / 16)
    ang = np.concatenate([rows[:, None] * inv, cols[:, None] * inv], -1)
    _CONST["rope_cos"] = np.cos(ang).astype(np.float32)
    _CONST["rope_sin"] = np.sin(ang).astype(np.float32)
    c = np.arange(64)
    th = 2 * np.pi * np.outer(c, c) / 64
    z = np.zeros((64, 64))
    _CONST["dftc"] = np.block([[np.cos(th), z], [z, np.cos(th)]]).astype(bf)
    _CONST["dfts"] = np.block([[np.sin(th), z], [z, np.sin(th)]]).astype(bf)
    pi_, fi_ = np.arange(128)[:, None], np.arange(128)[None, :]
    _CONST["masks"] = np.stack([fi_ < pi_, fi_ > pi_, fi_ < pi_, fi_ <= pi_, fi_ >= pi_], 0).astype(np.float32)
    pm = np.zeros((4, 5, 128, 128))
    for gi, win in enumerate((2, 4, 8, 16)):
        T3 = 384
        tt = np.arange(T3)
        lo = np.clip(tt - win // 2, 0, T3)
        hi = np.clip(tt + (win - win // 2), 0, T3)
        ss_ = np.arange(T3)[:, None]
        M = ((ss_ >= lo[None, :]) & (ss_ < hi[None, :])) / (hi - lo)[None, :].astype(np.float64) - np.eye(T3)
        pm[gi, 0] = M[0:128, 128:256]
        pm[gi, 1] = M[128:256, 0:128]
        pm[gi, 2] = M[128:256, 128:256]
        pm[gi, 3] = M[0:128, 0:128]
        pm[gi, 4] = M[256:384, 256:384]
    _CONST["poolm"] = pm.reshape(20, 128, 128).astype(bf)
    for nm, T in (("dft_lat", SEQ), ("dft_ctx", CTX)):
        k = np.arange(T)
        kk = (np.outer(k, k) % T).astype(np.float64)
        ang = 2 * np.pi * kk / T
        sc = 1.0 / np.sqrt(T * 64.0)
        _CONST[nm] = np.stack([np.cos(ang) * sc, -np.sin(ang) * sc], 0).astype(bf)
    return _CONST


def make_in_maps(inputs):
    f32 = lambda a: np.ascontiguousarray(np.asarray(a, dtype=np.float32))
    shared = {k: f32(inputs[k]) for k in ("w_mod", "b_mod", "ln_g", "ln_b", "w_ffn_in", "w_ffn_out", "w_in", "w_out",
                                          "q_norm_g", "k_norm_g", "pool_w", "pool_scale", "fourier_w", "rwkv_mu", "decay_w0", "decay_w1", "decay_w2",
                                          "icl_a0", "icl_a1", "icl_a2", "gate_g1", "gate_g2", "k_k", "k_a", "r_k", "gn_g", "gn_b")}
    shared["ident"] = np.eye(128, dtype=np.float32)
    shared.update(_constants())
    maps = []
    for core in range(8):
        b = core // 2
        m = dict(shared)
        m["x"] = f32(inputs["x"][b])
        m["ctx"] = f32(inputs["ctx"][b])
        m["c2"] = f32(np.stack([np.asarray(inputs["c"])[b], np.asarray(inputs["c_ctx"])], 0))
        maps.append(m)
    return maps


def kernel(**inputs):
    nc = _get_nc(None)
    res = run_bass_kernel_spmd(nc, make_in_maps(inputs), core_ids=list(range(8)))
    out = np.zeros((4, SEQ, D), np.float32)
    hf = SEQ // 2
    for core in range(8):
        b, j = core // 2, core % 2
        out[b, j * hf:(j + 1) * hf] = np.asarray(res.results[core]["y"])[j * hf:(j + 1) * hf]
    return out
```

```python
import contextlib
import numpy as np
import ml_dtypes
import concourse.bass as bass
import concourse.mybir as mybir
from concourse.bass_utils import run_bass_kernel_spmd

F32 = mybir.dt.float32
BF16 = mybir.dt.bfloat16
AF = mybir.ActivationFunctionType
ALU = mybir.AluOpType
AX = mybir.AxisListType

D = 1024
SEQ = 4096
CTX = 256
NTOK = SEQ + CTX
DEPTH = 4
DFF = 2816
DIN = 2048
ALPHA = (2 * DEPTH) ** 0.25
LN_EPS = 1e-5

ENGS = ("pe", "dve", "act", "pool", "sp")


class Buf:
    __slots__ = ("w", "r", "name", "psum")

    def __init__(self, name="", psum=False):
        self.w = None
        self.r = {}
        self.name = name
        self.psum = psum


class Prog:
    def __init__(self, nc, stack, n_dma_sems=16):
        self.nc = nc
        self.streams = {e: [] for e in ENGS}
        self.count = {e: 0 for e in ENGS}
        self.seen = {e: {} for e in ENGS}
        self.sems = {}
        for e in ENGS:
            self.sems["E_" + e] = stack.enter_context(nc.semaphore("sem_" + e))
        self.dma_keys = []
        self.dma_tot = {}
        for i in range(n_dma_sems):
            k = "D_%d" % i
            self.sems[k] = stack.enter_context(nc.semaphore("semd_%d" % i))
            self.dma_keys.append(k)
            self.dma_tot[k] = 0
        self.dma_rr = 0
        self.sb_off = 16640
        self.sb_marks = []
        self.n_alloc = 0
        self.n_ops = 0

    def sb(self, shape, dtype, name=None):
        esz = 4 if dtype == F32 else 2
        per_part = esz
        for s in shape[1:]:
            per_part *= s
        per_part = (per_part + 63) // 64 * 64
        off = self.sb_off
        self.sb_off += per_part
        assert self.sb_off <= 229376, "SBUF overflow %d" % self.sb_off
        self.n_alloc += 1
        t = self.nc.alloc_sbuf_tensor_at("sb%d_%s" % (self.n_alloc, name or "t"), list(shape), dtype, offset=off)
        return t

    def mark(self):
        self.sb_marks.append(self.sb_off)

    def release(self):
        self.sb_off = self.sb_marks.pop()

    def _wait(self, eng, toks):
        own = "E_" + eng
        seen = self.seen[eng]
        for key, val in toks:
            if key == own and eng in ("pe", "sp"):
                continue
            if seen.get(key, 0) >= val:
                continue
            seen[key] = val
            self.streams[eng].append(("wait", key, val))

    @staticmethod
    def _deps(reads, writes, own=None):
        toks = []
        for b in reads:
            if b.w is not None:
                toks.append(b.w)
            if b.psum:
                toks.extend((k, v) for k, v in b.r.items() if k != own)
        for b in writes:
            if b.w is not None:
                toks.append(b.w)
            toks.extend(b.r.items())
        return toks

    @staticmethod
    def _update(tok, reads, writes):
        key, val = tok
        for b in reads:
            if b.r.get(key, 0) < val:
                b.r[key] = val
        for b in writes:
            b.w = tok
            b.r = {}

    def op(self, eng, fn, reads=(), writes=()):
        self._wait(eng, self._deps(reads, writes, "E_" + eng))
        self.count[eng] += 1
        key = "E_" + eng
        self.streams[eng].append(("op", fn, key))
        tok = (key, self.count[eng])
        self._update(tok, reads, writes)
        self.n_ops += 1
        return tok

    def dma(self, fn, reads=(), writes=(), queue="sp"):
        key = self.dma_keys[self.dma_rr % len(self.dma_keys)]
        self.dma_rr += 1
        toks = self._deps(reads, writes)
        if self.dma_tot[key] > 0:
            toks.append((key, self.dma_tot[key]))
        self._wait(queue, toks)
        self.dma_tot[key] += 16
        self.streams[queue].append(("dma", fn, key))
        tok = (key, self.dma_tot[key])
        self._update(tok, reads, writes)
        self.n_ops += 1
        return tok

    def barrier(self):
        toks = [("E_" + e, self.count[e]) for e in ENGS if self.count[e] > 0]
        toks += [(k, v) for k, v in self.dma_tot.items() if v > 0]
        for e in ENGS:
            self._wait(e, toks)

    def replay(self, name, e):
        sems = self.sems
        for item in self.streams[name]:
            if item[0] == "wait":
                e.wait_ge(sems[item[1]], item[2])
            elif item[0] == "op":
                item[1](e).then_inc(sems[item[2]], 1)
            else:
                item[1](e).then_inc(sems[item[2]], 16)


def build_program(dbg=None):
    nc = bass.Bass("TRN2", target_bir_lowering=False)
    _so = bool(dbg) and dbg.startswith("so")

    def dt(name, shape, dtype=F32, kind="ExternalInput"):
        if _so and kind == "ExternalInput" and name not in ("ident", "masks", "c2", "rwt", "etot"):
            shape = [1, 2]
        return nc.dram_tensor(name, list(shape), dtype, kind=kind)
    io = {}
    io["x"] = dt("x", [SEQ, D])
    io["ctx"] = dt("ctx", [CTX, D])
    io["c2"] = dt("c2", [2, D])
    io["w_mod"] = dt("w_mod", [DEPTH, D, 9 * D])
    io["b_mod"] = dt("b_mod", [DEPTH, 9 * D])
    io["ln_g"] = dt("ln_g", [DEPTH, 3, D])
    io["ln_b"] = dt("ln_b", [DEPTH, 3, D])
    io["w_ffn_in"] = dt("w_ffn_in", [DEPTH, 2, D, 2 * DFF])
    io["w_ffn_out"] = dt("w_ffn_out", [DEPTH, 2, DFF, D])
    io["ident"] = dt("ident", [128, 128])
    io["w_in"] = dt("w_in", [DEPTH, D, DIN])
    io["w_out"] = dt("w_out", [DEPTH, D, D])
    io["q_norm_g"] = dt("q_norm_g", [DEPTH, 64])
    io["k_norm_g"] = dt("k_norm_g", [DEPTH, 64])
    io["rope_cos"] = dt("rope_cos", [SEQ, 32])
    io["rope_sin"] = dt("rope_sin", [SEQ, 32])
    io["dftc"] = dt("dftc", [128, 128], BF16)
    io["dfts"] = dt("dfts", [128, 128], BF16)
    io["poolm"] = dt("poolm", [20, 128, 128], BF16)
    io["pool_w"] = dt("pool_w", [DEPTH, 4, 64, 64])
    io["pool_scale"] = dt("pool_scale", [DEPTH, 256])
    io["fourier_w"] = dt("fourier_w", [DEPTH, 256, 256])
    io["dft_lat"] = dt("dft_lat", [2, SEQ, SEQ], BF16)
    io["dft_ctx"] = dt("dft_ctx", [2, CTX, CTX], BF16)
    io["rwkv_mu"] = dt("rwkv_mu", [DEPTH, 6, 256])
    io["decay_w0"] = dt("decay_w0", [DEPTH, 2, 256])
    io["decay_w1"] = dt("decay_w1", [DEPTH, 2, 256, 32])
    io["decay_w2"] = dt("decay_w2", [DEPTH, 2, 32, 256])
    io["icl_a0"] = dt("icl_a0", [DEPTH, 2, 256])
    io["icl_a1"] = dt("icl_a1", [DEPTH, 2, 256, 32])
    io["icl_a2"] = dt("icl_a2", [DEPTH, 2, 32, 256])
    io["gate_g1"] = dt("gate_g1", [DEPTH, 256, 64])
    io["gate_g2"] = dt("gate_g2", [DEPTH, 64, 256])
    io["k_k"] = dt("k_k", [DEPTH, 256])
    io["k_a"] = dt("k_a", [DEPTH, 256])
    io["r_k"] = dt("r_k", [DEPTH, 4, 64])
    io["gn_g"] = dt("gn_g", [DEPTH, 256])
    io["gn_b"] = dt("gn_b", [DEPTH, 256])
    io["masks"] = dt("masks", [5, 128, 128])
    scratch_kind = "ExternalOutput" if dbg else "Internal"
    io["xs"] = dt("xs", [NTOK, D], F32, kind=scratch_kind)
    io["modv"] = dt("modv", [2, 3 * D], F32, kind=scratch_kind)
    io["cat"] = dt("cat", [D, NTOK], BF16, kind=scratch_kind)
    io["zrw"] = dt("zrw", [4, 64, 4, NTOK], F32, kind=scratch_kind)
    io["zpool"] = dt("zpool", [NTOK, 256], BF16, kind=scratch_kind)
    io["ab"] = dt("ab", [NTOK, 512], BF16, kind=scratch_kind)
    so = bool(dbg) and dbg.startswith("so")
    io["rwt"] = dt("rwt", [8, 5, 64, NTOK], F32, kind=("ExternalInput" if so else scratch_kind))
    io["etot"] = dt("etot", [64, 8, 34], F32, kind=("ExternalInput" if so else scratch_kind))
    io["yT"] = dt("yT", [8, 64, NTOK], F32, kind=scratch_kind)
    io["rwaux"] = dt("rwaux", [2, 4, 64, NTOK], F32, kind=scratch_kind)
    io["y"] = dt("y", [SEQ, D], F32, kind="ExternalOutput")

    with contextlib.ExitStack() as stack:
        P = Prog(nc, stack)
        psum = [nc.alloc_psum_tensor("ps%d" % i, [128, 512], F32) for i in range(8)]
        pbuf = [Buf("ps%d" % i, psum=True) for i in range(8)]
        P.psum = psum
        P.pbuf = pbuf
        P.io = io
        emit_all(P, dbg)
        P.barrier()
        with nc.Block() as block:
            @block.tensor
            def _(e):
                P.replay("pe", e)

            @block.vector
            def _(e):
                P.replay("dve", e)

            @block.scalar
            def _(e):
                P.replay("act", e)

            @block.gpsimd
            def _(e):
                P.replay("pool", e)

            @block.sync
            def _(e):
                P.replay("sp", e)
    return nc


def emit_all(P, dbg):
    nc = P.nc
    io = P.io
    ident = P.sb([128, 128], F32, "ident")
    b_ident = Buf("ident")
    P.dma(lambda e: e.dma_start(out=ident[:], in_=io["ident"].ap()), writes=[b_ident])
    ones = P.sb([128, 128], F32, "ones")
    b_ones = Buf("ones")
    P.op("dve", lambda e: e.memset(ones[:], 1.0), writes=[b_ones])
    craw = P.sb([128, 2, 8], F32, "craw")
    b_craw = Buf()
    for r in range(2):
        P.dma(lambda e, r=r: e.dma_start(out=craw[:, r, :], in_=io["c2"].ap()[r, :].rearrange("(k p) -> p k", p=128),
                                         allow_slow_non_contiguous=True), writes=[b_craw])
    scT = P.sb([128, 8, 2], F32, "scT")
    b_scT = Buf()
    P.op("act", lambda e: e.activation(out=scT[:].rearrange("p k r -> p r k"), in_=craw[:], func=AF.Silu),
         reads=[b_craw], writes=[b_scT])
    P.g = dict(ident=ident, b_ident=b_ident, ones=ones, b_ones=b_ones, scT=scT, b_scT=b_scT)
    P.barrier()

    if dbg and dbg.startswith("so"):
        phase_rwkv_scan(P, 0, nsteps=int(dbg[2:]))
        return
    for l in range(DEPTH):
        last = l == DEPTH - 1
        src_lat = io["x"].ap() if l == 0 else io["xs"].ap()[0:SEQ, :]
        src_ctx = io["ctx"].ap() if l == 0 else io["xs"].ap()[SEQ:NTOK, :]
        phase_ffn(P, l, 0, 0, src_lat, src_ctx, io["xs"].ap()[0:SEQ, :], io["xs"].ap()[SEQ:NTOK, :], 0.5)
        if dbg == "ffn0":
            return
        phase_mixer(P, l, dbg)
        if dbg in ("att", "mixA", "pf", "wout", "rw1", "rw2", "rw3") or (dbg and dbg.startswith("rws")):
            return
        phase_ffn(P, l, 1, 2, io["xs"].ap()[0:SEQ, :], io["xs"].ap()[SEQ:NTOK, :],
                  io["y"].ap() if last else io["xs"].ap()[0:SEQ, :], io["xs"].ap()[SEQ:NTOK, :], 0.5, skip_ctx=last)
        if dbg == "l0":
            return


def phase_mod(P, l, sub, resid_w):
    nc, io, g = P.nc, P.io, P.g
    psum, pbuf = P.psum, P.pbuf
    P.mark()
    brow = P.sb([1, 3 * D], F32, "brow")
    b_brow = Buf()
    P.dma(lambda e: e.dma_start(out=brow[:], in_=io["b_mod"].ap()[l:l + 1, sub * 3 * D:(sub + 1) * 3 * D]), writes=[b_brow])
    mrow = P.sb([2, 3 * D], F32, "mrow")
    b_mrow = Buf()
    wst = [P.sb([128, 8, 512], F32, "wmst%d" % i) for i in range(2)]
    b_wst = [Buf(), Buf()]
    for cb in range(6):
        s = cb % 2
        c0 = sub * 3 * D + cb * 512
        P.dma(lambda e, s=s, c0=c0: e.dma_start(
            out=wst[s][:], in_=io["w_mod"].ap()[l, :, c0:c0 + 512].rearrange("(k p) n -> p k n", p=128)),
            writes=[b_wst[s]])
        pb = cb % 2
        for kc in range(8):
            P.op("pe", lambda e, s=s, kc=kc, pb=pb: e.matmul(psum[pb][0:2, :], lhsT=g["scT"][:, kc, :], rhs=wst[s][:, kc, :],
                                                             start=(kc == 0), stop=False),
                 reads=[g["b_scT"], b_wst[s]], writes=[pbuf[pb]])
        P.op("pe", lambda e, cb=cb, pb=pb: e.matmul(psum[pb][0:2, :], lhsT=g["ones"][0:1, 0:2], rhs=brow[0:1, cb * 512:(cb + 1) * 512],
                                                    start=False, stop=True),
             reads=[g["b_ones"], b_brow], writes=[pbuf[pb]])
        P.op("dve", lambda e, cb=cb, pb=pb: e.tensor_copy(out=mrow[:, cb * 512:(cb + 1) * 512], in_=psum[pb][0:2, :]),
             reads=[pbuf[pb]], writes=[b_mrow])
    b_modv = Buf()
    P.dma(lambda e: e.dma_start(out=io["modv"].ap(), in_=mrow[:]), reads=[b_mrow], writes=[b_modv])
    P.release()
    shT = P.sb([128, 2, 8], F32, "shT")
    scl = P.sb([128, 2, 8], F32, "scl")
    gbc = [P.sb([128, D], F32, "gbc%d" % r) for r in range(2)]
    b_shT, b_scl, b_g = Buf(), Buf(), [Buf(), Buf()]
    for r in range(2):
        P.dma(lambda e, r=r: e.dma_start(out=shT[:, r, :], in_=io["modv"].ap()[r, 0:D].rearrange("(k p) -> p k", p=128),
                                         allow_slow_non_contiguous=True), reads=[b_modv], writes=[b_shT])
        P.dma(lambda e, r=r: e.dma_start(out=scl[:, r, :], in_=io["modv"].ap()[r, D:2 * D].rearrange("(k p) -> p k", p=128),
                                         allow_slow_non_contiguous=True), reads=[b_modv], writes=[b_scl])
        P.dma(lambda e, r=r: e.dma_start(out=gbc[r][:], in_=io["modv"].ap()[r:r + 1, 2 * D:3 * D].partition_broadcast(128)),
              reads=[b_modv], writes=[b_g[r]])
    P.op("dve", lambda e: e.tensor_scalar(out=scl[:], in0=scl[:], scalar1=1.0, scalar2=None, op0=ALU.add),
         reads=[b_scl], writes=[b_scl])
    for r in range(2):
        if resid_w != 1.0:
            P.op("pool", lambda e, r=r: e.tensor_scalar(out=gbc[r][:], in0=gbc[r][:], scalar1=float(resid_w), scalar2=None, op0=ALU.mult),
                 reads=[b_g[r]], writes=[b_g[r]])
    lg = P.sb([128, D], F32, "lng")
    lb = P.sb([128, D], F32, "lnb")
    b_lg, b_lb = Buf(), Buf()
    P.dma(lambda e: e.dma_start(out=lg[:], in_=io["ln_g"].ap()[l, sub:sub + 1, :].partition_broadcast(128)), writes=[b_lg])
    P.dma(lambda e: e.dma_start(out=lb[:], in_=io["ln_b"].ap()[l, sub:sub + 1, :].partition_broadcast(128)), writes=[b_lb])
    return dict(shT=shT, scl=scl, gbc=gbc, b_shT=b_shT, b_scl=b_scl, b_g=b_g, lg=lg, lb=lb, b_lg=b_lg, b_lb=b_lb)


def emit_postnorm(P, m, r, xt_ap, b_x, y_halves, b_ys, out_ap, b_out, work):
    u, b_u, st, b_st = work["u"], work["b_u"], work["st"], work["b_st"]
    for h in range(2):
        sl = slice(h * 512, (h + 1) * 512)
        P.op("dve", lambda e, h=h, sl=sl: e.tensor_tensor(out=u[:, sl], in0=y_halves[h], in1=m["gbc"][r][:, sl], op=ALU.mult),
             reads=[b_ys[h], m["b_g"][r]], writes=[b_u])
        P.op("dve", lambda e, sl=sl: e.scalar_tensor_tensor(out=u[:, sl], in0=xt_ap[:, sl], scalar=float(ALPHA), in1=u[:, sl],
                                                            op0=ALU.mult, op1=ALU.add),
             reads=[b_x, b_u], writes=[b_u])
        P.op("dve", lambda e, h=h, sl=sl: e.bn_stats(out=st[:, h * 6:(h + 1) * 6], in_=u[:, sl]), reads=[b_u], writes=[b_st])
    P.op("dve", lambda e: e.bn_aggr(out=st[:, 12:14], in_=st[:, 0:12]), reads=[b_st], writes=[b_st])
    P.op("dve", lambda e: e.tensor_scalar(out=st[:, 14:15], in0=st[:, 13:14], scalar1=float(LN_EPS), scalar2=None, op0=ALU.add),
         reads=[b_st], writes=[b_st])
    P.op("act", lambda e: e.activation(out=st[:, 14:15], in_=st[:, 14:15], func=AF.Sqrt), reads=[b_st], writes=[b_st])
    P.op("dve", lambda e: e.reciprocal(out=st[:, 14:15], in_=st[:, 14:15]), reads=[b_st], writes=[b_st])
    P.op("dve", lambda e: e.scalar_tensor_tensor(out=st[:, 15:16], in0=st[:, 12:13], scalar=-1.0, in1=st[:, 14:15],
                                                 op0=ALU.mult, op1=ALU.mult), reads=[b_st], writes=[b_st])
    P.op("act", lambda e: e.activation(out=u[:], in_=u[:], func=AF.Identity, scale=st[:, 14:15], bias=st[:, 15:16]),
         reads=[b_u, b_st], writes=[b_u])
    P.op("pool", lambda e: e.tensor_tensor(out=u[:], in0=u[:], in1=m["lg"][:], op=ALU.mult), reads=[b_u, m["b_lg"]], writes=[b_u])
    P.op("pool", lambda e: e.tensor_tensor(out=out_ap, in0=u[:], in1=m["lb"][:], op=ALU.add), reads=[b_u, m["b_lb"]], writes=[b_out])


def phase_ffn(P, l, f, sub, src_lat, src_ctx, dst_lat, dst_ctx, resid_w, skip_ctx=False):
    nc, io, g = P.nc, P.io, P.g
    psum, pbuf = P.psum, P.pbuf
    P.mark()
    m = phase_mod(P, l, sub, resid_w)
    wgu = P.sb([128, 8, 2 * DFF], BF16, "wgu")
    wdn = P.sb([128, 22, D], BF16, "wdn")
    b_wgu, b_wdn = Buf(), Buf()
    P.mark()
    stg = [P.sb([128, DFF], F32, "stg%d" % i) for i in range(2)]
    b_stg = [Buf(), Buf()]
    cast_engs = ["pool", "dve", "act"]
    n = 0
    for kc in range(8):
        for hf in range(2):
            s = n % 2
            P.dma(lambda e, s=s, kc=kc, hf=hf: e.dma_start(
                out=stg[s][:], in_=io["w_ffn_in"].ap()[l, f, kc * 128:(kc + 1) * 128, hf * DFF:(hf + 1) * DFF]),
                writes=[b_stg[s]])
            ce = cast_engs[n % 3]
            if ce == "act":
                P.op("act", lambda e, s=s, kc=kc, hf=hf: e.activation(out=wgu[:, kc, hf * DFF:(hf + 1) * DFF], in_=stg[s][:], func=AF.Copy),
                     reads=[b_stg[s]], writes=[b_wgu])
            else:
                P.op(ce, lambda e, s=s, kc=kc, hf=hf: e.tensor_copy(out=wgu[:, kc, hf * DFF:(hf + 1) * DFF], in_=stg[s][:]),
                     reads=[b_stg[s]], writes=[b_wgu])
            n += 1
    for pc in range(11):
        s = n % 2
        P.dma(lambda e, s=s, pc=pc: e.dma_start(
            out=stg[s][:, 0:2048].rearrange("p (c n) -> p c n", c=2),
            in_=io["w_ffn_out"].ap()[l, f, pc * 256:(pc + 1) * 256, :].rearrange("(c p) n -> p c n", p=128)),
            writes=[b_stg[s]])
        ce = cast_engs[n % 3]
        if ce == "act":
            P.op("act", lambda e, s=s, pc=pc: e.activation(out=wdn[:, 2 * pc:2 * pc + 2, :].rearrange("p c n -> p (c n)"),
                                                           in_=stg[s][:, 0:2048], func=AF.Copy),
                 reads=[b_stg[s]], writes=[b_wdn])
        else:
            P.op(ce, lambda e, s=s, pc=pc: e.tensor_copy(out=wdn[:, 2 * pc:2 * pc + 2, :].rearrange("p c n -> p (c n)"),
                                                         in_=stg[s][:, 0:2048]),
                 reads=[b_stg[s]], writes=[b_wdn])
        n += 1
    P.barrier()
    P.release()
    NB = 256
    xt = [P.sb([128, 2, D], F32, "xt%d" % i) for i in range(2)]
    b_xt = [[Buf(), Buf()], [Buf(), Buf()]]
    hT = P.sb([128, 8, NB], BF16, "hT")
    b_hT = Buf()
    aT = P.sb([128, 22, NB], BF16, "aT")
    b_aT = [Buf() for _ in range(22)]
    sg = [P.sb([128, NB], F32, "sg%d" % i) for i in range(2)]
    b_sg = [Buf(), Buf()]
    ot = [P.sb([128, D], F32, "ot%d" % i) for i in range(2)]
    b_ot = [Buf(), Buf()]
    work = dict(u=P.sb([128, D], F32, "u"), b_u=Buf(), st=P.sb([128, 16], F32, "st"), b_st=Buf())
    blocks = [(src_lat, dst_lat, i * NB, 0) for i in range(SEQ // NB)] + ([] if skip_ctx else [(src_ctx, dst_ctx, 0, 1)])
    pi = 0
    for bi, (src, dst, t0, r) in enumerate(blocks):
        xs_ = bi % 2
        for t in range(2):
            P.dma(lambda e, xs_=xs_, t=t, src=src, t0=t0: e.dma_start(out=xt[xs_][:, t, :], in_=src[t0 + t * 128:t0 + (t + 1) * 128, :]),
                  writes=[b_xt[xs_][t]])
        for kp in range(4):
            pb = pi % 8
            pi += 1
            for kk in range(2):
                kc = kp * 2 + kk
                for t in range(2):
                    P.op("pe", lambda e, pb=pb, kk=kk, t=t, kc=kc, xs_=xs_: e.transpose(
                        out=psum[pb][:, kk * 256 + t * 128: kk * 256 + (t + 1) * 128], in_=xt[xs_][:, t, kc * 128:(kc + 1) * 128],
                        identity=g["ident"][:]),
                        reads=[b_xt[xs_][t], g["b_ident"]], writes=[pbuf[pb]])
            for kk in range(2):
                kc = kp * 2 + kk
                P.op("act", lambda e, pb=pb, kk=kk, kc=kc, r=r: e.activation(
                    out=hT[:, kc, :], in_=psum[pb][:, kk * 256:(kk + 1) * 256], func=AF.Identity,
                    scale=m["scl"][:, r, kc:kc + 1], bias=m["shT"][:, r, kc:kc + 1]),
                    reads=[pbuf[pb], m["b_scl"], m["b_shT"]], writes=[b_hT])
        for i in range(22):
            pb = pi % 8
            pi += 1
            for hf in range(2):
                for kc in range(8):
                    P.op("pe", lambda e, pb=pb, hf=hf, kc=kc, i=i: e.matmul(
                        psum[pb][:, hf * 256:(hf + 1) * 256], lhsT=wgu[:, kc, hf * DFF + i * 128: hf * DFF + (i + 1) * 128],
                        rhs=hT[:, kc, :], start=(kc == 0), stop=(kc == 7)),
                        reads=[b_hT, b_wgu], writes=[pbuf[pb]])
            s = i % 2
            P.op("act", lambda e, pb=pb, s=s: e.activation(out=sg[s][:], in_=psum[pb][:, 0:256], func=AF.Silu),
                 reads=[pbuf[pb]], writes=[b_sg[s]])
            P.op("dve", lambda e, pb=pb, s=s, i=i: e.tensor_tensor(out=aT[:, i, :], in0=psum[pb][:, 256:512], in1=sg[s][:], op=ALU.mult),
                 reads=[pbuf[pb], b_sg[s]], writes=[b_aT[i]])
        for t in range(2):
            pbs = []
            for hn in range(2):
                pb = pi % 8
                pi += 1
                pbs.append(pb)
                for i in range(22):
                    P.op("pe", lambda e, pb=pb, hn=hn, i=i, t=t: e.matmul(
                        psum[pb][:, :], lhsT=aT[:, i, t * 128:(t + 1) * 128], rhs=wdn[:, i, hn * 512:(hn + 1) * 512],
                        start=(i == 0), stop=(i == 21)),
                        reads=[b_aT[i], b_wdn], writes=[pbuf[pb]])
            o = (bi * 2 + t) % 2
            emit_postnorm(P, m, r, xt[xs_][:, t, :], b_xt[xs_][t], [psum[pbs[0]][:, :], psum[pbs[1]][:, :]],
                          [pbuf[pbs[0]], pbuf[pbs[1]]], ot[o][:], b_ot[o], work)
            P.dma(lambda e, o=o, dst=dst, t0=t0, t=t: e.dma_start(out=dst[t0 + t * 128:t0 + (t + 1) * 128, :], in_=ot[o][:]),
                  reads=[b_ot[o]])
    P.barrier()
    P.release()


class Rot:
    def __init__(self, P):
        self.P = P
        self.pi = 0
        self.ei = 0

    def bank(self):
        b = self.pi % 8
        self.pi += 1
        return b

    def evac(self, out_ap, in_ap, reads, writes, eng=None):
        P = self.P
        if eng is None:
            eng = ("act", "dve")[self.ei % 2]
            self.ei += 1
        if eng == "act":
            return P.op("act", lambda e: e.activation(out=out_ap, in_=in_ap, func=AF.Copy), reads=reads, writes=writes)
        return P.op(eng, lambda e: e.tensor_copy(out=out_ap, in_=in_ap), reads=reads, writes=writes)


def emit_hT(P, rot, m, r, xt, b_xts, ntile, hT, b_hT):
    g, psum, pbuf = P.g, P.psum, P.pbuf
    for kc in range(8):
        pb = rot.bank()
        for t in range(ntile):
            P.op("pe", lambda e, pb=pb, t=t, kc=kc: e.transpose(
                out=psum[pb][:, t * 128:(t + 1) * 128], in_=xt[:, t, kc * 128:(kc + 1) * 128], identity=g["ident"][:]),
                reads=[b_xts[t], g["b_ident"]], writes=[pbuf[pb]])
        P.op("act", lambda e, pb=pb, kc=kc: e.activation(
            out=hT[:, kc, 0:ntile * 128], in_=psum[pb][:, 0:ntile * 128], func=AF.Identity,
            scale=m["scl"][:, r, kc:kc + 1], bias=m["shT"][:, r, kc:kc + 1]),
            reads=[pbuf[pb], m["b_scl"], m["b_shT"]], writes=[b_hT])


def load_cast_rows(P, dst, b_dst, src_rows_fn, nchunk, width, stg, b_stg, n0=0):
    cast_engs = ["pool", "dve", "act"]
    n = n0
    for kc in range(nchunk):
        s = n % 2
        P.dma(lambda e, s=s, kc=kc: e.dma_start(out=stg[s][:, 0:width], in_=src_rows_fn(kc)), writes=[b_stg[s]])
        ce = cast_engs[n % 3]
        if ce == "act":
            P.op("act", lambda e, s=s, kc=kc: e.activation(out=dst[:, kc, :], in_=stg[s][:, 0:width], func=AF.Copy),
                 reads=[b_stg[s]], writes=[b_dst])
        else:
            P.op(ce, lambda e, s=s, kc=kc: e.tensor_copy(out=dst[:, kc, :], in_=stg[s][:, 0:width]),
                 reads=[b_stg[s]], writes=[b_dst])
        n += 1
    return n


def phase_mixer(P, l, dbg):
    nc, io, g = P.nc, P.io, P.g
    psum, pbuf = P.psum, P.pbuf
    rot = Rot(P)
    P.mark()
    m = phase_mod(P, l, 1, 1.0)
    P.mark()
    qT = P.sb([64, 4, NTOK], BF16, "qT")
    kT = P.sb([64, 2, NTOK], BF16, "kT")
    vx = P.sb([128, 34, 2, 66], BF16, "vx")
    b_qT = [Buf() for _ in range(9)]
    b_kT, b_vx = Buf(), Buf()
    P.op("pool", lambda e: e.memset(vx[:], 1.0), writes=[b_vx])
    P.mark()
    win = P.sb([128, 8, DIN], BF16, "win")
    b_win = Buf()
    P.mark()
    stg = [P.sb([128, DIN], F32, "stg%d" % i) for i in range(2)]
    b_stg = [Buf(), Buf()]
    load_cast_rows(P, win, b_win, lambda kc: io["w_in"].ap()[l, kc * 128:(kc + 1) * 128, :], 8, DIN, stg, b_stg)
    P.barrier()
    P.release()
    gq = P.sb([128, 384], F32, "gq")
    b_gq = Buf()
    for h in range(6):
        src = io["q_norm_g"] if h < 4 else io["k_norm_g"]
        P.dma(lambda e, h=h, src=src: e.dma_start(out=gq[:, h * 64:(h + 1) * 64], in_=src.ap()[l:l + 1, :].partition_broadcast(128)),
              writes=[b_gq])
    cos_t = P.sb([128, 32, 32], F32, "cos")
    sin_t = P.sb([128, 32, 32], F32, "sin")
    b_cs = Buf()
    P.dma(lambda e: e.dma_start(out=cos_t[:], in_=io["rope_cos"].ap().rearrange("(t p) i -> p t i", p=128)), writes=[b_cs])
    P.dma(lambda e: e.dma_start(out=sin_t[:], in_=io["rope_sin"].ap().rearrange("(t p) i -> p t i", p=128)), writes=[b_cs])
    dftc = P.sb([128, 128], BF16, "dftc")
    dfts = P.sb([128, 128], BF16, "dfts")
    b_dft = Buf()
    P.dma(lambda e: e.dma_start(out=dftc[:], in_=io["dftc"].ap()), writes=[b_dft])
    P.dma(lambda e: e.dma_start(out=dfts[:], in_=io["dfts"].ap()), writes=[b_dft])

    xt = [P.sb([128, 4, D], F32, "xt%d" % i) for i in range(2)]
    b_xt = [[Buf() for _ in range(4)] for _ in range(2)]
    hT = P.sb([128, 8, 512], BF16, "hT")
    b_hT = Buf()
    zat = [P.sb([128, 512], F32, "zat%d" % i) for i in range(2)]
    b_zat = [Buf(), Buf()]
    sq = P.sb([128, 384], F32, "sq")
    qk = P.sb([128, 384], F32, "qk")
    qkr = P.sb([128, 384], F32, "qkr")
    tA = P.sb([128, 192], F32, "tA")
    tB = P.sb([128, 192], F32, "tB")
    ss = P.sb([128, 8], F32, "ss")
    b_sq, b_qk, b_qkr, b_tA, b_tB, b_ss = Buf(), Buf(), Buf(), Buf(), Buf(), Buf()
    zst = [P.sb([64, 4, 512], F32, "zst%d" % i) for i in range(2)]
    b_zst = [Buf(), Buf()]
    zp = P.sb([128, 4, 256], BF16, "zp")
    b_zp = Buf()
    zfT = P.sb([128, 2, 512], BF16, "zfT")
    b_zfT = Buf()
    abst = P.sb([128, 4, 512], BF16, "abst")
    b_abst = Buf()

    xs_ap = io["xs"].ap()
    blocks = [(i * 512, 4, 0) for i in range(8)] + [(SEQ, 2, 1)]
    for bi, (t0, ntile, r) in enumerate(blocks):
        nb = ntile * 128
        xb = bi % 2
        for t in range(ntile):
            P.dma(lambda e, xb=xb, t=t, t0=t0: e.dma_start(out=xt[xb][:, t, :], in_=xs_ap[t0 + t * 128:t0 + (t + 1) * 128, :]),
                  writes=[b_xt[xb][t]])
        emit_hT(P, rot, m, r, xt[xb], b_xt[xb], ntile, hT, b_hT)
        for t in range(ntile):
            tile_idx = (t0 // 128) + t
            tok0 = t0 + t * 128
            pb = rot.bank()
            for kc in range(8):
                P.op("pe", lambda e, pb=pb, kc=kc, t=t: e.matmul(psum[pb][:, :], lhsT=hT[:, kc, t * 128:(t + 1) * 128], rhs=win[:, kc, 0:512],
                                                                 start=(kc == 0), stop=(kc == 7)),
                     reads=[b_hT, b_win], writes=[pbuf[pb]])
            z = (bi * 4 + t) % 2
            P.op("act", lambda e, pb=pb, z=z: e.activation(out=zat[z][:], in_=psum[pb][:, :], func=AF.Copy),
                 reads=[pbuf[pb]], writes=[b_zat[z]])
            P.op("dve", lambda e, z=z: e.tensor_tensor(out=sq[:], in0=zat[z][:, 0:384], in1=zat[z][:, 0:384], op=ALU.mult),
                 reads=[b_zat[z]], writes=[b_sq])
            P.op("dve", lambda e: e.tensor_reduce(out=ss[:, 0:6], in_=sq[:].rearrange("p (h d) -> p h d", d=64), axis=AX.X, op=ALU.add),
                 reads=[b_sq], writes=[b_ss])
            P.op("dve", lambda e: e.tensor_scalar(out=ss[:, 0:6], in0=ss[:, 0:6], scalar1=1.0 / 64, scalar2=1e-6, op0=ALU.mult, op1=ALU.add),
                 reads=[b_ss], writes=[b_ss])
            P.op("act", lambda e: e.activation(out=ss[:, 0:6], in_=ss[:, 0:6], func=AF.Sqrt), reads=[b_ss], writes=[b_ss])
            P.op("dve", lambda e: e.reciprocal(out=ss[:, 0:6], in_=ss[:, 0:6]), reads=[b_ss], writes=[b_ss])
            P.op("dve", lambda e, z=z: e.tensor_tensor(out=qk[:].rearrange("p (h d) -> p h d", d=64),
                                                       in0=zat[z][:, 0:384].rearrange("p (h d) -> p h d", d=64),
                                                       in1=ss[:, 0:6].unsqueeze(2).broadcast_to([128, 6, 64]), op=ALU.mult),
                 reads=[b_zat[z], b_ss], writes=[b_qk])
            if r == 0:
                P.op("pool", lambda e: e.tensor_tensor(out=qk[:], in0=qk[:], in1=gq[:], op=ALU.mult), reads=[b_qk, b_gq], writes=[b_qk])
                v4 = lambda tl: tl[:].rearrange("p (h i two) -> p h i two", h=6, i=32, two=2)
                x0, x1 = v4(qk)[:, :, :, 0], v4(qk)[:, :, :, 1]
                o0, o1 = v4(qkr)[:, :, :, 0], v4(qkr)[:, :, :, 1]
                cb = cos_t[:, tile_idx:tile_idx + 1, :].broadcast_to([128, 6, 32])
                sb_ = sin_t[:, tile_idx:tile_idx + 1, :].broadcast_to([128, 6, 32])
                a3 = lambda tl: tl[:].rearrange("p (h i) -> p h i", h=6)
                P.op("dve", lambda e, x0=x0, cb=cb: e.tensor_tensor(out=a3(tA), in0=x0, in1=cb, op=ALU.mult), reads=[b_qk, b_cs], writes=[b_tA])
                P.op("pool", lambda e, x1=x1, sb_=sb_: e.tensor_tensor(out=a3(tB), in0=x1, in1=sb_, op=ALU.mult), reads=[b_qk, b_cs], writes=[b_tB])
                P.op("dve", lambda e, o0=o0: e.tensor_tensor(out=o0, in0=a3(tA), in1=a3(tB), op=ALU.subtract), reads=[b_tA, b_tB], writes=[b_qkr])
                P.op("pool", lambda e, x0=x0, sb_=sb_: e.tensor_tensor(out=a3(tA), in0=x0, in1=sb_, op=ALU.mult), reads=[b_qk, b_cs], writes=[b_tA])
                P.op("dve", lambda e, x1=x1, cb=cb: e.tensor_tensor(out=a3(tB), in0=x1, in1=cb, op=ALU.mult), reads=[b_qk, b_cs], writes=[b_tB])
                P.op("pool", lambda e, o1=o1: e.tensor_tensor(out=o1, in0=a3(tA), in1=a3(tB), op=ALU.add), reads=[b_tA, b_tB], writes=[b_qkr])
            else:
                P.op("pool", lambda e: e.tensor_tensor(out=qkr[:], in0=qk[:], in1=gq[:], op=ALU.mult), reads=[b_qk, b_gq], writes=[b_qkr])
            pq = rot.bank()
            for h in range(4):
                P.op("pe", lambda e, pq=pq, h=h: e.transpose(out=psum[pq][0:64, h * 128:(h + 1) * 128], in_=qkr[:, h * 64:(h + 1) * 64],
                                                             identity=g["ident"][:]),
                     reads=[b_qkr, g["b_ident"]], writes=[pbuf[pq]])
            rot.evac(qT[:, :, tok0:tok0 + 128], psum[pq][0:64, :].rearrange("p (h t) -> p h t", h=4), [pbuf[pq]], [b_qT[bi]])
            pk = rot.bank()
            for h in range(2):
                P.op("pe", lambda e, pk=pk, h=h: e.transpose(out=psum[pk][0:64, h * 128:(h + 1) * 128], in_=qkr[:, (4 + h) * 64:(5 + h) * 64],
                                                             identity=g["ident"][:]),
                     reads=[b_qkr, g["b_ident"]], writes=[pbuf[pk]])
            rot.evac(kT[:, :, tok0:tok0 + 128], psum[pk][0:64, 0:256].rearrange("p (h t) -> p h t", h=2), [pbuf[pk]], [b_kT])
            P.op("pool", lambda e, z=z, tile_idx=tile_idx: e.tensor_copy(out=vx[:, tile_idx, :, 0:64],
                                                                         in_=zat[z][:, 384:512].rearrange("p (h d) -> p h d", h=2)),
                 reads=[b_zat[z]], writes=[b_vx])
        for kind in range(4):
            zs = kind % 2
            for h in range(4):
                pb = rot.bank()
                c0 = 512 + kind * 256 + h * 64
                for kc in range(8):
                    P.op("pe", lambda e, pb=pb, kc=kc, c0=c0, nb=nb: e.matmul(psum[pb][0:64, 0:nb], lhsT=win[:, kc, c0:c0 + 64], rhs=hT[:, kc, 0:nb],
                                                                            start=(kc == 0), stop=(kc == 7)),
                         reads=[b_hT, b_win], writes=[pbuf[pb]])
                rot.evac(zst[zs][:, h, 0:nb], psum[pb][0:64, 0:nb], [pbuf[pb]], [b_zst[zs]])
            P.dma(lambda e, zs=zs, kind=kind, t0=t0, nb=nb: e.dma_start(out=io["zrw"].ap()[kind, :, :, t0:t0 + nb], in_=zst[zs][:, :, 0:nb]),
                  reads=[b_zst[zs]])
        for t in range(ntile):
            pb = rot.bank()
            for kc in range(8):
                P.op("pe", lambda e, pb=pb, kc=kc, t=t: e.matmul(psum[pb][:, 0:256], lhsT=hT[:, kc, t * 128:(t + 1) * 128], rhs=win[:, kc, 1536:1792],
                                                                 start=(kc == 0), stop=(kc == 7)),
                     reads=[b_hT, b_win], writes=[pbuf[pb]])
            rot.evac(zp[:, t, :], psum[pb][:, 0:256], [pbuf[pb]], [b_zp])
        P.dma(lambda e, t0=t0, ntile=ntile, nb=nb: e.dma_start(out=io["zpool"].ap()[t0:t0 + nb, :].rearrange("(t p) c -> p t c", p=128),
                                                               in_=zp[:, 0:ntile, :]), reads=[b_zp])
        for c in range(2):
            pb = rot.bank()
            c0 = 1792 + c * 128
            for kc in range(8):
                P.op("pe", lambda e, pb=pb, kc=kc, c0=c0, nb=nb: e.matmul(psum[pb][:, 0:nb], lhsT=win[:, kc, c0:c0 + 128], rhs=hT[:, kc, 0:nb],
                                                                        start=(kc == 0), stop=(kc == 7)),
                     reads=[b_hT, b_win], writes=[pbuf[pb]])
            rot.evac(zfT[:, c, 0:nb], psum[pb][:, 0:nb], [pbuf[pb]], [b_zfT])
        for t in range(ntile):
            pb = rot.bank()
            for ab_i, mat in enumerate((dftc, dfts)):
                for c in range(2):
                    P.op("pe", lambda e, pb=pb, ab_i=ab_i, c=c, t=t, mat=mat: e.matmul(
                        psum[pb][:, ab_i * 256 + c * 128: ab_i * 256 + (c + 1) * 128], lhsT=zfT[:, c, t * 128:(t + 1) * 128], rhs=mat[:],
                        start=True, stop=True), reads=[b_zfT, b_dft], writes=[pbuf[pb]])
            rot.evac(abst[:, t, :], psum[pb][:, :], [pbuf[pb]], [b_abst])
        P.dma(lambda e, t0=t0, ntile=ntile, nb=nb: e.dma_start(out=io["ab"].ap()[t0:t0 + nb, :].rearrange("(t p) c -> p t c", p=128),
                                                               in_=abst[:, 0:ntile, :]), reads=[b_abst])
    P.barrier()
    P.release()
    if dbg == "mixA":
        P.release()
        P.release()
        return
    P.mark()
    pT = [P.sb([128, 512], BF16, "pT%d" % i) for i in range(3)]
    b_pT = [Buf(), Buf(), Buf()]
    rden = P.sb([128, 512], F32, "rden")
    b_rden = Buf()
    osb = P.sb([64, 512], F32, "osb")
    b_osb = Buf()
    ost = [P.sb([64, 512], BF16, "ost%d" % i) for i in range(2)]
    b_ost = [Buf(), Buf()]
    n_p = 0
    n_o = 0
    cat = io["cat"].ap()
    busy = set()
    pending_tail = [None]
    for bi, (t0, ntile, r) in enumerate(blocks):
        nb = ntile * 128
        key_tiles = list(range(34)) if r == 0 else [32, 33]
        for h in range(4):
            kvh = h // 2
            pacc = rot.bank()
            while pacc in busy:
                pacc = rot.bank()
            nk = len(key_tiles)
            pend = {}

            def qk_exp(ki, pacc=pacc, kvh=kvh, h=h, t0=t0, nb=nb, bi=bi):
                nonlocal n_p
                kt = key_tiles[ki]
                ps_ = rot.bank()
                while ps_ == pacc or ps_ in busy:
                    ps_ = rot.bank()
                P.op("pe", lambda e: e.matmul(psum[ps_][:, 0:nb], lhsT=kT[:, kvh, kt * 128:(kt + 1) * 128], rhs=qT[:, h, t0:t0 + nb],
                                              start=True, stop=True), reads=[b_kT, b_qT[bi]], writes=[pbuf[ps_]])
                pp = n_p % 3
                n_p += 1
                P.op("act", lambda e: e.activation(out=pT[pp][:, 0:nb], in_=psum[ps_][:, 0:nb], func=AF.Exp, scale=0.125),
                     reads=[pbuf[ps_]], writes=[b_pT[pp]])
                pend[ki] = pp

            def pv(ki, pacc=pacc, kvh=kvh, nb=nb, nk=nk):
                kt = key_tiles[ki]
                pp = pend.pop(ki)
                P.op("pe", lambda e: e.matmul(psum[pacc][0:65, 0:nb], lhsT=vx[:, kt, kvh, 0:65], rhs=pT[pp][:, 0:nb],
                                              start=(ki == 0), stop=(ki == nk - 1)), reads=[b_vx, b_pT[pp]], writes=[pbuf[pacc]])

            def tail(pacc=pacc, nb=nb, h=h, t0=t0):
                nonlocal n_o
                P.op("dve", lambda e: e.reciprocal(out=rden[64:65, 0:nb], in_=psum[pacc][64:65, 0:nb]), reads=[pbuf[pacc]], writes=[b_rden])
                P.op("act", lambda e: e.activation(out=osb[:, 0:nb], in_=psum[pacc][0:64, 0:nb], func=AF.Copy), reads=[pbuf[pacc]], writes=[b_osb])
                pbc = rot.bank()
                while pbc == pacc or pbc in busy:
                    pbc = rot.bank()
                P.op("pe", lambda e: e.matmul(psum[pbc][0:64, 0:nb], lhsT=g["ones"][64:65, 0:64], rhs=rden[64:65, 0:nb], start=True, stop=True),
                     reads=[g["b_ones"], b_rden], writes=[pbuf[pbc]])
                oo = n_o % 2
                n_o += 1
                P.op("dve", lambda e: e.tensor_tensor(out=ost[oo][:, 0:nb], in0=psum[pbc][0:64, 0:nb], in1=osb[:, 0:nb], op=ALU.mult),
                     reads=[pbuf[pbc], b_osb], writes=[b_ost[oo]])
                P.dma(lambda e: e.dma_start(out=cat[h * 64:(h + 1) * 64, t0:t0 + nb], in_=ost[oo][:, 0:nb]), reads=[b_ost[oo]])

            LOOK = 2
            for ki in range(min(LOOK, nk)):
                qk_exp(ki)
            if pending_tail[0] is not None:
                pending_tail[0]()
                busy.clear()
            for ki in range(nk):
                pv(ki)
                if ki + LOOK < nk:
                    qk_exp(ki + LOOK)
            pending_tail[0] = tail
            busy.add(pacc)
    pending_tail[0]()
    P.barrier()
    P.release()
    P.release()
    if dbg == "att":
        P.release()
        return
    phase_rwkv_prep(P, l)
    if dbg == "rw1":
        P.release()
        return
    phase_rwkv_scan(P, l, nsteps=(int(dbg[3:]) if dbg and dbg.startswith("rws") else 34))
    if dbg and dbg.startswith("rws"):
        P.release()
        return
    phase_rwkv_fin(P, l)
    if dbg == "rw3":
        P.release()
        return
    phase_pool(P, l)
    phase_fourier(P, l)
    if dbg == "pf":
        P.release()
        return
    phase_wout(P, l, m)
    P.release()


def phase_pool(P, l):
    nc, io, g = P.nc, P.io, P.g
    psum, pbuf = P.psum, P.pbuf
    rot = Rot(P)
    P.mark()
    zp = P.sb([128, 34, 256], BF16, "zp_all")
    b_zp = Buf()
    P.dma(lambda e: e.dma_start(out=zp[:], in_=io["zpool"].ap().rearrange("(t p) c -> p t c", p=128)), writes=[b_zp])
    pm = P.sb([128, 20, 128], BF16, "poolm")
    b_pm = Buf()
    P.dma(lambda e: e.dma_start(out=pm[:], in_=io["poolm"].ap().rearrange("k s t -> s k t")), writes=[b_pm])
    pwf = P.sb([64, 4, 64], F32, "pwf")
    pw = P.sb([64, 4, 64], BF16, "pw")
    b_pwf, b_pw = Buf(), Buf()
    P.dma(lambda e: e.dma_start(out=pwf[:], in_=io["pool_w"].ap()[l].rearrange("g c d -> c g d")), writes=[b_pwf])
    P.op("dve", lambda e: e.tensor_copy(out=pw[:], in_=pwf[:]), reads=[b_pwf], writes=[b_pw])
    psc = P.sb([64, 4], F32, "psc")
    b_psc = Buf()
    P.dma(lambda e: e.dma_start(out=psc[:], in_=io["pool_scale"].ap()[l, :].rearrange("(g d) -> d g", d=64),
                                allow_slow_non_contiguous=True), writes=[b_psc])
    pooled = [P.sb([64, 512], BF16, "pooled%d" % i) for i in range(2)]
    b_pooled = [Buf(), Buf()]
    ost = [P.sb([64, 512], BF16, "post%d" % i) for i in range(2)]
    b_ost = [Buf(), Buf()]
    cat = io["cat"].ap()
    n = 0
    seqs = [(0, 32), (32, 2)]
    for (tile0, nt) in seqs:
        for j0 in range(0, nt, 4):
            ntile = min(4, nt - j0)
            nb = ntile * 128
            for gi in range(4):
                pb = rot.bank()
                for jj in range(ntile):
                    j = j0 + jj
                    terms = []
                    if j > 0:
                        terms.append((tile0 + j - 1, 0))
                    terms.append((tile0 + j, 3 if j == 0 else (4 if j == nt - 1 else 2)))
                    if j < nt - 1:
                        terms.append((tile0 + j + 1, 1))
                    for ti, (st, kind) in enumerate(terms):
                        P.op("pe", lambda e, pb=pb, jj=jj, st=st, kind=kind, gi=gi, ti=ti, nterm=len(terms): e.matmul(
                            psum[pb][0:64, jj * 128:(jj + 1) * 128], lhsT=zp[:, st, gi * 64:(gi + 1) * 64], rhs=pm[:, gi * 5 + kind, :],
                            start=(ti == 0), stop=(ti == nterm - 1)), reads=[b_zp, b_pm], writes=[pbuf[pb]])
                k = n % 2
                n += 1
                rot.evac(pooled[k][:, 0:nb], psum[pb][0:64, 0:nb], [pbuf[pb]], [b_pooled[k]])
                pb2 = rot.bank()
                P.op("pe", lambda e, pb2=pb2, gi=gi, k=k, nb=nb: e.matmul(psum[pb2][0:64, 0:nb], lhsT=pw[:, gi, :], rhs=pooled[k][:, 0:nb],
                                                                        start=True, stop=True), reads=[b_pw, b_pooled[k]], writes=[pbuf[pb2]])
                P.op("act", lambda e, pb2=pb2, gi=gi, k=k, nb=nb: e.activation(out=ost[k][:, 0:nb], in_=psum[pb2][0:64, 0:nb], func=AF.Copy,
                                                                              scale=psc[:, gi:gi + 1]),
                     reads=[pbuf[pb2], b_psc], writes=[b_ost[k]])
                tok0 = (tile0 + j0) * 128
                P.dma(lambda e, k=k, gi=gi, tok0=tok0, nb=nb: e.dma_start(out=cat[512 + gi * 64:512 + (gi + 1) * 64, tok0:tok0 + nb],
                                                                       in_=ost[k][:, 0:nb]), reads=[b_ost[k]])
    P.barrier()
    P.release()


def phase_fourier(P, l):
    nc, io, g = P.nc, P.io, P.g
    psum, pbuf = P.psum, P.pbuf
    rot = Rot(P)
    P.mark()
    ab = P.sb([128, 34, 512], BF16, "ab_all")
    b_ab = Buf()
    for q in range(2):
        P.dma(lambda e, q=q: e.dma_start(out=ab[:, q * 17:(q + 1) * 17, :],
                                         in_=io["ab"].ap()[q * 17 * 128:(q + 1) * 17 * 128, :].rearrange("(t p) c -> p t c", p=128)),
              writes=[b_ab])
    fwf = P.sb([128, 2, 256], F32, "fwf")
    fw = P.sb([128, 2, 256], BF16, "fw")
    b_fwf, b_fw = Buf(), Buf()
    P.dma(lambda e: e.dma_start(out=fwf[:], in_=io["fourier_w"].ap()[l].rearrange("(c p) d -> p c d", p=128)), writes=[b_fwf])
    P.op("dve", lambda e: e.tensor_copy(out=fw[:], in_=fwf[:]), reads=[b_fwf], writes=[b_fw])
    dm = [[P.sb([128, 32, 512], BF16, "dm%d_%d" % (i, j)) for j in range(2)] for i in range(2)]
    b_dm = [[Buf(), Buf()], [Buf(), Buf()]]
    fT = [P.sb([128, 2, 512], BF16, "fT%d" % i) for i in range(2)]
    b_fT = [[Buf(), Buf()], [Buf(), Buf()]]
    ost = [P.sb([128, 512], BF16, "fost%d" % i) for i in range(2)]
    b_ost = [Buf(), Buf()]
    cat = io["cat"].ap()
    jobs = [(0, 32, tb * 512, 512, "dft_lat") for tb in range(8)] + [(32, 2, 0, 256, "dft_ctx")]
    n_o = 0
    for ji, (tile0, nt, c0, wd, mname) in enumerate(jobs):
        bsel = ji % 2
        for cs in range(2):
            P.dma(lambda e, bsel=bsel, cs=cs, nt=nt, c0=c0, wd=wd, mname=mname: e.dma_start(
                out=dm[bsel][cs][:, 0:nt, 0:wd], in_=io[mname].ap()[cs, :, c0:c0 + wd].rearrange("(t p) n -> p t n", p=128)),
                writes=[b_dm[bsel][cs]])
        for c in range(2):
            pb = rot.bank()
            for cs in range(2):
                for t in range(nt):
                    P.op("pe", lambda e, pb=pb, cs=cs, t=t, c=c, bsel=bsel, wd=wd, tile0=tile0, nt=nt: e.matmul(
                        psum[pb][:, 0:wd], lhsT=ab[:, tile0 + t, cs * 256 + c * 128: cs * 256 + (c + 1) * 128], rhs=dm[bsel][cs][:, t, 0:wd],
                        start=(cs == 0 and t == 0), stop=(cs == 1 and t == nt - 1)),
                        reads=[b_ab, b_dm[bsel][cs]], writes=[pbuf[pb]])
            rot.evac(fT[bsel][:, c, 0:wd], psum[pb][:, 0:wd], [pbuf[pb]], [b_fT[bsel][c]])
        for dc in range(2):
            pb = rot.bank()
            for c in range(2):
                P.op("pe", lambda e, pb=pb, c=c, dc=dc, bsel=bsel, wd=wd: e.matmul(
                    psum[pb][:, 0:wd], lhsT=fw[:, c, dc * 128:(dc + 1) * 128], rhs=fT[bsel][:, c, 0:wd], start=(c == 0), stop=(c == 1)),
                    reads=[b_fw, b_fT[bsel][c]], writes=[pbuf[pb]])
            k = n_o % 2
            n_o += 1
            rot.evac(ost[k][:, 0:wd], psum[pb][:, 0:wd], [pbuf[pb]], [b_ost[k]])
            tok0 = tile0 * 128 + c0
            P.dma(lambda e, k=k, dc=dc, tok0=tok0, wd=wd: e.dma_start(out=cat[768 + dc * 128:768 + (dc + 1) * 128, tok0:tok0 + wd],
                                                                   in_=ost[k][:, 0:wd]), reads=[b_ost[k]])
    P.barrier()
    P.release()


def phase_wout(P, l, m):
    nc, io, g = P.nc, P.io, P.g
    psum, pbuf = P.psum, P.pbuf
    rot = Rot(P)
    P.mark()
    wo = P.sb([128, 8, D], BF16, "wo")
    b_wo = Buf()
    P.mark()
    stg = [P.sb([128, D], F32, "stg%d" % i) for i in range(2)]
    b_stg = [Buf(), Buf()]
    load_cast_rows(P, wo, b_wo, lambda kc: io["w_out"].ap()[l, kc * 128:(kc + 1) * 128, :], 8, D, stg, b_stg)
    P.barrier()
    P.release()
    ct = [P.sb([128, 8, 512], BF16, "ct%d" % i) for i in range(2)]
    b_ct = [Buf(), Buf()]
    xt = [P.sb([128, D], F32, "xt%d" % i) for i in range(2)]
    b_xt = [Buf(), Buf()]
    ot = [P.sb([128, D], F32, "ot%d" % i) for i in range(2)]
    b_ot = [Buf(), Buf()]
    work = dict(u=P.sb([128, D], F32, "u"), b_u=Buf(), st=P.sb([128, 16], F32, "st"), b_st=Buf())
    cat = io["cat"].ap()
    xs_ap = io["xs"].ap()
    blocks = [(i * 512, 4, 0) for i in range(8)] + [(SEQ, 2, 1)]
    n = 0
    for bi, (t0, ntile, r) in enumerate(blocks):
        nb = ntile * 128
        cb = bi % 2
        P.dma(lambda e, cb=cb, t0=t0, nb=nb: e.dma_start(out=ct[cb][:, :, 0:nb], in_=cat[:, t0:t0 + nb].rearrange("(c p) t -> p c t", p=128)),
              writes=[b_ct[cb]])
        for t in range(ntile):
            k = n % 2
            n += 1
            tok0 = t0 + t * 128
            P.dma(lambda e, k=k, tok0=tok0: e.dma_start(out=xt[k][:], in_=xs_ap[tok0:tok0 + 128, :]), writes=[b_xt[k]])
            pbs = []
            for hn in range(2):
                pb = rot.bank()
                pbs.append(pb)
                for c in range(8):
                    P.op("pe", lambda e, pb=pb, c=c, hn=hn, cb=cb, t=t: e.matmul(
                        psum[pb][:, :], lhsT=ct[cb][:, c, t * 128:(t + 1) * 128], rhs=wo[:, c, hn * 512:(hn + 1) * 512],
                        start=(c == 0), stop=(c == 7)), reads=[b_ct[cb], b_wo], writes=[pbuf[pb]])
            emit_postnorm(P, m, r, xt[k][:], b_xt[k], [psum[pbs[0]][:, :], psum[pbs[1]][:, :]], [pbuf[pbs[0]], pbuf[pbs[1]]],
                          ot[k][:], b_ot[k], work)
            P.dma(lambda e, k=k, tok0=tok0: e.dma_start(out=xs_ap[tok0:tok0 + 128, :], in_=ot[k][:]), reads=[b_ot[k]])
    P.barrier()
    P.release()


LOGDECAY_SCALE = -0.6065306597126334
GN_EPS = 64e-5
CHUNK_ORDER = {0: [32, 33] + list(range(32)), 1: [33, 32] + list(range(31, -1, -1))}
RW_SEQS = [(i * 512, 512, i == 0, i == 7) for i in range(8)] + [(SEQ, 256, True, True)]


def col_param(P, src_ap_1d, name, n=4):
    t = P.sb([64, n], F32, name)
    b = Buf()
    P.dma(lambda e: e.dma_start(out=t[:], in_=src_ap_1d.rearrange("(h d) -> d h", d=64), allow_slow_non_contiguous=True), writes=[b])
    return t, b


def load_halo(P, buf_ap_fn, b_buf, src_fn, t0, nb, first, last):
    lo = 0 if not first else 1
    hi = nb + 2 if not last else nb + 1
    if first:
        P.op("pool", lambda e: e.memset(buf_ap_fn(0, 1), 0.0), writes=[b_buf])
    if last:
        P.op("pool", lambda e: e.memset(buf_ap_fn(nb + 1, nb + 2), 0.0), writes=[b_buf])
    P.dma(lambda e: e.dma_start(out=buf_ap_fn(lo, hi), in_=src_fn(t0 - 1 + lo, t0 - 1 + hi)), writes=[b_buf])


def phase_rwkv_prep(P, l):
    nc, io, g = P.nc, P.io, P.g
    psum, pbuf = P.psum, P.pbuf
    rot = Rot(P)
    P.mark()
    mu = [col_param(P, io["rwkv_mu"].ap()[l, i, :], "mu%d" % i) for i in range(6)]
    w0 = [col_param(P, io["decay_w0"].ap()[l, d, :], "w0%d" % d) for d in range(2)]
    a0 = [col_param(P, io["icl_a0"].ap()[l, d, :], "a0%d" % d) for d in range(2)]
    k_k = col_param(P, io["k_k"].ap()[l, :], "k_k")
    k_a = col_param(P, io["k_a"].ap()[l, :], "k_a")
    r_k = col_param(P, io["r_k"].ap()[l].rearrange("h d -> (h d)"), "r_k")
    hm, om = [], []
    for i in range(3):
        t1 = P.sb([64, 4], F32, "hm%d" % i)
        t2 = P.sb([64, 4], F32, "om%d" % i)
        b1, b2 = Buf(), Buf()
        P.op("dve", lambda e, i=i, t1=t1: e.tensor_scalar(out=t1[:], in0=mu[i][0][:], scalar1=0.5, scalar2=None, op0=ALU.mult),
             reads=[mu[i][1]], writes=[b1])
        P.op("dve", lambda e, i=i, t2=t2: e.tensor_scalar(out=t2[:], in0=mu[i][0][:], scalar1=-1.0, scalar2=1.0, op0=ALU.mult, op1=ALU.add),
             reads=[mu[i][1]], writes=[b2])
        hm.append((t1, b1))
        om.append((t2, b2))
    omka = P.sb([64, 4], F32, "omka")
    b_omka = Buf()
    P.op("dve", lambda e: e.tensor_scalar(out=omka[:], in0=k_a[0][:], scalar1=-1.0, scalar2=1.0, op0=ALU.mult, op1=ALU.add),
         reads=[k_a[1]], writes=[b_omka])

    def lora_in(src, name, rank):
        tf = P.sb([64, 4, rank], F32, name + "f")
        tb = P.sb([64, 4, rank], BF16, name)
        bf_, bb = Buf(), Buf()
        P.dma(lambda e: e.dma_start(out=tf[:], in_=src.rearrange("(h d) r -> d h r", d=64)), writes=[bf_])
        P.op("dve", lambda e: e.tensor_copy(out=tb[:], in_=tf[:]), reads=[bf_], writes=[bb])
        return tb, bb

    def lora_out(src, name, rank):
        tf = P.sb([rank, 256], F32, name + "f")
        tb = P.sb([rank, 256], BF16, name)
        bf_, bb = Buf(), Buf()
        P.dma(lambda e: e.dma_start(out=tf[:], in_=src), writes=[bf_])
        P.op("dve", lambda e: e.tensor_copy(out=tb[:], in_=tf[:]), reads=[bf_], writes=[bb])
        return tb, bb

    W1 = [lora_in(io["decay_w1"].ap()[l, d], "W1_%d" % d, 32) for d in range(2)]
    A1 = [lora_in(io["icl_a1"].ap()[l, d], "A1_%d" % d, 32) for d in range(2)]
    G1 = lora_in(io["gate_g1"].ap()[l], "G1", 64)
    W2 = [lora_out(io["decay_w2"].ap()[l, d], "W2_%d" % d, 32) for d in range(2)]
    A2 = [lora_out(io["icl_a2"].ap()[l, d], "A2_%d" % d, 32) for d in range(2)]
    G2 = lora_out(io["gate_g2"].ap()[l], "G2", 64)
    tw = P.sb([32, 2, NTOK], BF16, "tw")
    ta = P.sb([32, 2, NTOK], BF16, "ta")
    tg = P.sb([64, NTOK], BF16, "tg")
    b_tw, b_ta, b_tg = [Buf(), Buf()], [Buf(), Buf()], Buf()
    etot = P.sb([64, 8, 34], F32, "etot")
    b_etot = Buf()
    zrw = io["zrw"].ap()
    P.mark()
    zu = [P.sb([64, 4, 514], F32, "zu%d" % i) for i in range(2)]
    b_zu = [Buf(), Buf()]
    ssum = P.sb([64, 4, 512], F32, "ssum")
    du = P.sb([64, 4, 512], F32, "du")
    b_ssum, b_du = Buf(), Buf()
    xq = [P.sb([64, 4, 512], BF16, "xq%d" % i) for i in range(3)]
    b_xq = [Buf(), Buf(), Buf()]
    for bi, (t0, nb, first, last) in enumerate(RW_SEQS):
        z = bi % 2
        load_halo(P, lambda a, b, z=z: zu[z][:, :, a:b], b_zu[z], lambda a, b: zrw[3, :, :, a:b], t0, nb, first, last)
        P.op("dve", lambda e, z=z, nb=nb: e.tensor_tensor(out=ssum[:, :, 0:nb], in0=zu[z][:, :, 0:nb], in1=zu[z][:, :, 2:nb + 2], op=ALU.add),
             reads=[b_zu[z]], writes=[b_ssum])
        P.op("dve", lambda e, z=z, nb=nb: e.scalar_tensor_tensor(out=du[:, :, 0:nb], in0=ssum[:, :, 0:nb], scalar=0.5, in1=zu[z][:, :, 1:nb + 1],
                                                                 op0=ALU.mult, op1=ALU.subtract), reads=[b_ssum, b_zu[z]], writes=[b_du])
        for j in range(3):
            for h in range(4):
                eng = "dve" if (j * 4 + h) % 2 == 0 else "pool"
                if eng == "dve":
                    P.op("dve", lambda e, j=j, h=h, z=z, nb=nb: e.scalar_tensor_tensor(
                        out=xq[j][:, h, 0:nb], in0=du[:, h, 0:nb], scalar=mu[3 + j][0][:, h:h + 1], in1=zu[z][:, h, 1:nb + 1],
                        op0=ALU.mult, op1=ALU.add), reads=[b_du, b_zu[z], mu[3 + j][1]], writes=[b_xq[j]])
                else:
                    P.op("pool", lambda e, j=j, h=h, nb=nb: e.tensor_scalar(
                        out=xq[j][:, h, 0:nb], in0=du[:, h, 0:nb], scalar1=mu[3 + j][0][:, h:h + 1], scalar2=None, op0=ALU.mult),
                        reads=[b_du, mu[3 + j][1]], writes=[b_xq[j]])
                    P.op("pool", lambda e, j=j, h=h, z=z, nb=nb: e.tensor_tensor(
                        out=xq[j][:, h, 0:nb], in0=xq[j][:, h, 0:nb], in1=zu[z][:, h, 1:nb + 1], op=ALU.add),
                        reads=[b_xq[j], b_zu[z]], writes=[b_xq[j]])
        jobs = [(0, W1[0], 32, tw[:, 0, t0:t0 + nb], b_tw[0], AF.Tanh), (0, W1[1], 32, tw[:, 1, t0:t0 + nb], b_tw[1], AF.Tanh),
                (1, A1[0], 32, ta[:, 0, t0:t0 + nb], b_ta[0], AF.Copy), (1, A1[1], 32, ta[:, 1, t0:t0 + nb], b_ta[1], AF.Copy),
                (2, G1, 64, tg[:, t0:t0 + nb], b_tg, AF.Sigmoid)]
        for (j, wt, rank, dst, b_dst, fn) in jobs:
            pb = rot.bank()
            for h in range(4):
                P.op("pe", lambda e, pb=pb, h=h, j=j, wt=wt, rank=rank, nb=nb: e.matmul(
                    psum[pb][0:rank, 0:nb], lhsT=wt[0][:, h, :], rhs=xq[j][:, h, 0:nb], start=(h == 0), stop=(h == 3)),
                    reads=[wt[1], b_xq[j]], writes=[pbuf[pb]])
            P.op("act", lambda e, pb=pb, rank=rank, nb=nb, dst=dst, fn=fn: e.activation(out=dst, in_=psum[pb][0:rank, 0:nb], func=fn),
                 reads=[pbuf[pb]], writes=[b_dst])
    P.barrier()
    P.release()
    P.mark()
    z3 = [P.sb([64, 3, 514], F32, "z3_%d" % i) for i in range(2)]
    b_z3 = [Buf(), Buf()]
    s3 = P.sb([64, 3, 512], F32, "s3")
    b_s3 = Buf()
    rk = P.sb([64, 2, 512], F32, "rk")
    b_r, b_k = Buf(), Buf()
    Fs = [[P.sb([64, 5, 512], F32, "F%d_%d" % (i, d)) for d in range(2)] for i in range(2)]
    b_F = [[Buf(), Buf()], [Buf(), Buf()]]
    kkr = P.sb([64, 512], F32, "kkr")
    kk = P.sb([64, 512], F32, "kk")
    sqt = P.sb([64, 512], F32, "sqt")
    nrm = P.sb([64, 512], F32, "nrm")
    b_kkr, b_kk, b_sqt, b_nrm = Buf(), Buf(), Buf(), Buf()
    TD = []
    for d_ in range(2):
        td = {}
        for nm in ("lw", "cl", "ci", "cml", "einc", "eexc", "einv", "av", "tt", "kd", "tmpb"):
            td[nm] = P.sb([64, 512], F32, "%s%d" % (nm, d_))
            td["b_" + nm] = Buf()
        td["tot"] = P.sb([64, 4], F32, "tot%d" % d_)
        td["b_tot"] = Buf()
        TD.append(td)
    kds = P.sb([64, 512], F32, "kds")
    tmpb = P.sb([64, 512], F32, "tmpb")
    b_kds, b_tmpb = Buf(), Buf()
    aux = [P.sb([64, 2, 512], F32, "aux%d" % i) for i in range(2)]
    b_aux = [Buf(), Buf()]
    rwt = io["rwt"].ap()
    def head_block(h, t0, nb, first, last, n_it):
        if True:
            hs = slice(h, h + 1)
            nch = nb // 128
            z = n_it % 2
            fi = n_it % 2
            n_it += 1
            for i in range(3):
                load_halo(P, lambda a, b, z=z, i=i: z3[z][:, i, a:b], b_z3[z], lambda a, b, i=i: zrw[i, :, h, a:b], t0, nb, first, last)
            P.op("dve", lambda e, z=z, nb=nb: e.tensor_tensor(out=s3[:, :, 0:nb], in0=z3[z][:, :, 0:nb], in1=z3[z][:, :, 2:nb + 2], op=ALU.add),
                 reads=[b_z3[z]], writes=[b_s3])
            for i in range(3):
                P.op("act", lambda e, i=i, nb=nb: e.activation(out=s3[:, i, 0:nb], in_=s3[:, i, 0:nb], func=AF.Copy, scale=hm[i][0][:, hs]),
                     reads=[b_s3, hm[i][1]], writes=[b_s3])
            dsts = [(rk[:, 0, 0:nb], b_r), (rk[:, 1, 0:nb], b_k), (Fs[fi][0][:, 4, 0:nb], b_F[fi][0])]
            for i in range(3):
                P.op("dve", lambda e, i=i, z=z, nb=nb, dst=dsts[i][0]: e.scalar_tensor_tensor(
                    out=dst, in0=z3[z][:, i, 1:nb + 1], scalar=om[i][0][:, hs], in1=s3[:, i, 0:nb], op0=ALU.mult, op1=ALU.add),
                    reads=[b_z3[z], b_s3, om[i][1]], writes=[dsts[i][1]])
            r_ap, k_ap, v_ap = rk[:, 0, 0:nb], rk[:, 1, 0:nb], Fs[fi][0][:, 4, 0:nb]
            b_v = b_F[fi][0]
            P.op("act", lambda e, nb=nb, fi=fi, v_ap=v_ap: e.activation(out=Fs[fi][1][:, 4, 0:nb], in_=v_ap, func=AF.Copy), reads=[b_v], writes=[b_F[fi][1]])
            P.op("dve", lambda e, nb=nb, k_ap=k_ap: e.tensor_scalar(out=kkr[:, 0:nb], in0=k_ap, scalar1=k_k[0][:, hs], scalar2=None, op0=ALU.mult),
                 reads=[b_k, k_k[1]], writes=[b_kkr])
            P.op("act", lambda e, nb=nb: e.activation(out=sqt[:, 0:nb], in_=kkr[:, 0:nb], func=AF.Square), reads=[b_kkr], writes=[b_sqt])
            pb = rot.bank()
            P.op("pe", lambda e, pb=pb, nb=nb: e.matmul(psum[pb][0:64, 0:nb], lhsT=g["ones"][0:64, 0:64], rhs=sqt[:, 0:nb], start=True, stop=True),
                 reads=[g["b_ones"], b_sqt], writes=[pbuf[pb]])
            P.op("act", lambda e, pb=pb, nb=nb: e.activation(out=nrm[:, 0:nb], in_=psum[pb][0:64, 0:nb], func=AF.Sqrt), reads=[pbuf[pb]], writes=[b_nrm])
            P.op("dve", lambda e, nb=nb: e.tensor_scalar(out=nrm[:, 0:nb], in0=nrm[:, 0:nb], scalar1=1e-12, scalar2=None, op0=ALU.max),
                 reads=[b_nrm], writes=[b_nrm])
            P.op("dve", lambda e, nb=nb: e.reciprocal(out=nrm[:, 0:nb], in_=nrm[:, 0:nb]), reads=[b_nrm], writes=[b_nrm])
            P.op("dve", lambda e, nb=nb: e.tensor_tensor(out=kk[:, 0:nb], in0=kkr[:, 0:nb], in1=nrm[:, 0:nb], op=ALU.mult),
                 reads=[b_kkr, b_nrm], writes=[b_kk])
            def dir_part(d):
                s_id = h * 2 + d
                F = Fs[fi][d]
                bF = b_F[fi][d]
                T = TD[d]
                lw, cl, ci, cml, einc, eexc, einv, av, tt, kd, tmpd, tot = (T[k] for k in ("lw", "cl", "ci", "cml", "einc", "eexc", "einv", "av", "tt", "kd", "tmpb", "tot"))
                b_lw, b_cl, b_ci, b_cml, b_einc, b_eexc, b_einv, b_av, b_tt, b_kd, b_tmpd, b_tot = (
                    T["b_" + k] for k in ("lw", "cl", "ci", "cml", "einc", "eexc", "einv", "av", "tt", "kd", "tmpb", "tot"))
                pb = rot.bank()
                P.op("pe", lambda e: e.matmul(psum[pb][0:64, 0:nb], lhsT=W2[d][0][:, h * 64:(h + 1) * 64], rhs=tw[:, d, t0:t0 + nb],
                                              start=True, stop=True), reads=[W2[d][1], b_tw[d]], writes=[pbuf[pb]])
                pb2 = rot.bank()
                P.op("pe", lambda e: e.matmul(psum[pb2][0:64, 0:nb], lhsT=A2[d][0][:, h * 64:(h + 1) * 64], rhs=ta[:, d, t0:t0 + nb],
                                              start=True, stop=True), reads=[A2[d][1], b_ta[d]], writes=[pbuf[pb2]])
                yield
                P.op("act", lambda e: e.activation(out=lw[:, 0:nb], in_=psum[pb][0:64, 0:nb], func=AF.Sigmoid, bias=w0[d][0][:, hs]),
                     reads=[pbuf[pb], w0[d][1]], writes=[b_lw])
                P.op("act", lambda e: e.activation(out=av[:, 0:nb], in_=psum[pb2][0:64, 0:nb], func=AF.Sigmoid, bias=a0[d][0][:, hs]),
                     reads=[pbuf[pb2], a0[d][1]], writes=[b_av])
                yield
                P.op("act", lambda e: e.activation(out=lw[:, 0:nb], in_=lw[:, 0:nb], func=AF.Copy, scale=LOGDECAY_SCALE),
                     reads=[b_lw], writes=[b_lw])
                P.op("pool", lambda e: e.tensor_tensor(out=tmpd[:, 0:nb], in0=kk[:, 0:nb], in1=av[:, 0:nb], op=ALU.mult),
                     reads=[b_kk, b_av], writes=[b_tmpd])
                yield
                for j in range(nch):
                    P.op("dve", lambda e, j=j: e.tensor_tensor_scan(out=cl[:, j * 128:(j + 1) * 128], data0=g["ones"][0:64, 0:128],
                                                                   data1=lw[:, j * 128:(j + 1) * 128], initial=0.0, op0=ALU.mult, op1=ALU.add),
                         reads=[b_lw, g["b_ones"]], writes=[b_cl])
                P.op("pool", lambda e: e.tensor_scalar(out=tt[:, 0:nb], in0=av[:, 0:nb], scalar1=k_a[0][:, hs], scalar2=omka[:, hs],
                                                       op0=ALU.mult, op1=ALU.add), reads=[b_av, k_a[1], b_omka], writes=[b_tt])
                yield
                clv = cl[:, 0:nb].rearrange("p (c j) -> p c j", j=128)
                P.op("dve", lambda e: e.tensor_copy(out=tot[:, 0:nch], in_=clv[:, :, 127]), reads=[b_cl], writes=[b_tot])
                P.op("pool", lambda e: e.tensor_tensor(out=kd[:, 0:nb], in0=k_ap, in1=tt[:, 0:nb], op=ALU.mult),
                     reads=[b_k, b_tt], writes=[b_kd])
                yield
                c0 = t0 // 128
                P.op("act", lambda e: e.activation(out=etot[:, s_id, c0:c0 + nch], in_=tot[:, 0:nch], func=AF.Exp),
                     reads=[b_tot], writes=[b_etot])
                if d == 0:
                    ci_ap, b_cix = cl, b_cl
                else:
                    P.op("dve", lambda e: e.tensor_tensor(
                        out=ci[:, 0:nb].rearrange("p (c j) -> p c j", j=128), in0=tot[:, 0:nch].unsqueeze(2).broadcast_to([64, nch, 128]),
                        in1=clv, op=ALU.subtract), reads=[b_tot, b_cl], writes=[b_ci])
                    P.op("dve", lambda e: e.tensor_tensor(out=ci[:, 0:nb], in0=ci[:, 0:nb], in1=lw[:, 0:nb], op=ALU.add),
                         reads=[b_ci, b_lw], writes=[b_ci])
                    ci_ap, b_cix = ci, b_ci
                yield
                P.op("dve", lambda e: e.tensor_tensor(out=cml[:, 0:nb], in0=ci_ap[:, 0:nb], in1=lw[:, 0:nb], op=ALU.subtract),
                     reads=[b_cix, b_lw], writes=[b_cml])
                P.op("act", lambda e: e.activation(out=einc[:, 0:nb], in_=ci_ap[:, 0:nb], func=AF.Exp), reads=[b_cix], writes=[b_einc])
                yield
                P.op("act", lambda e: e.activation(out=einv[:, 0:nb], in_=ci_ap[:, 0:nb], func=AF.Exp, scale=-1.0),
                     reads=[b_cix], writes=[b_einv])
                P.op("dve", lambda e: e.tensor_tensor(out=F[:, 3, 0:nb], in0=r_ap, in1=einc[:, 0:nb], op=ALU.mult),
                     reads=[b_r, b_einc], writes=[bF])
                yield
                P.op("act", lambda e: e.activation(out=eexc[:, 0:nb], in_=cml[:, 0:nb], func=AF.Exp), reads=[b_cml], writes=[b_eexc])
                P.op("dve", lambda e: e.tensor_tensor(out=F[:, 1, 0:nb], in0=tmpd[:, 0:nb], in1=einv[:, 0:nb], op=ALU.mult),
                     reads=[b_tmpd, b_einv], writes=[bF])
                yield
                P.op("pool", lambda e: e.tensor_tensor(out=F[:, 2, 0:nb], in0=kd[:, 0:nb], in1=einv[:, 0:nb], op=ALU.mult),
                     reads=[b_kd, b_einv], writes=[bF])
                P.op("dve", lambda e: e.scalar_tensor_tensor(out=F[:, 0, 0:nb], in0=kk[:, 0:nb], scalar=-1.0, in1=eexc[:, 0:nb],
                                                             op0=ALU.mult, op1=ALU.mult), reads=[b_kk, b_eexc], writes=[bF])
                yield
                P.dma(lambda e: e.dma_start(out=rwt[s_id].rearrange("f d t -> d f t")[:, :, t0:t0 + nb], in_=F[:, :, 0:nb]), reads=[bF])

            lockstep([dir_part(0), dir_part(1)])
            P.op("pool", lambda e: e.tensor_tensor(out=kds[:, 0:nb], in0=TD[0]["kd"][:, 0:nb], in1=TD[1]["kd"][:, 0:nb], op=ALU.add),
                 reads=[TD[0]["b_kd"], TD[1]["b_kd"]], writes=[b_kds])
            ax = aux[n_it % 2]
            b_ax = b_aux[n_it % 2]
            pb = rot.bank()
            P.op("pe", lambda e, pb=pb, nb=nb: e.matmul(psum[pb][0:64, 0:nb], lhsT=G2[0][:, h * 64:(h + 1) * 64], rhs=tg[:, t0:t0 + nb],
                                                        start=True, stop=True), reads=[G2[1], b_tg], writes=[pbuf[pb]])
            P.op("act", lambda e, pb=pb, nb=nb, ax=ax: e.activation(out=ax[:, 0, 0:nb], in_=psum[pb][0:64, 0:nb], func=AF.Copy),
                 reads=[pbuf[pb]], writes=[b_ax])
            P.op("dve", lambda e, nb=nb, r_ap=r_ap: e.scalar_tensor_tensor(out=tmpb[:, 0:nb], in0=r_ap, scalar=r_k[0][:, hs], in1=kds[:, 0:nb],
                                                                           op0=ALU.mult, op1=ALU.mult), reads=[b_r, b_kds, r_k[1]], writes=[b_tmpb])
            pb = rot.bank()
            P.op("pe", lambda e, pb=pb, nb=nb: e.matmul(psum[pb][0:64, 0:nb], lhsT=g["ones"][0:64, 0:64], rhs=tmpb[:, 0:nb], start=True, stop=True),
                 reads=[g["b_ones"], b_tmpb], writes=[pbuf[pb]])
            P.op("dve", lambda e, pb=pb, nb=nb, ax=ax, v_ap=v_ap: e.tensor_tensor(out=ax[:, 1, 0:nb], in0=psum[pb][0:64, 0:nb], in1=v_ap, op=ALU.mult),
                 reads=[pbuf[pb], b_v], writes=[b_ax])
            P.dma(lambda e, ax=ax, nb=nb: e.dma_start(out=io["rwaux"].ap()[:, h, :, t0:t0 + nb].rearrange("a d t -> d a t"), in_=ax[:, :, 0:nb]),
                  reads=[b_ax])
    n_it = 0
    for h in range(4):
        for (t0, nb, first, last) in RW_SEQS:
            head_block(h, t0, nb, first, last, n_it)
            n_it += 1
    P.dma(lambda e: e.dma_start(out=io["etot"].ap(), in_=etot[:]), reads=[b_etot])
    P.barrier()
    P.release()
    P.release()


class RwStream:
    pass


def lockstep(gens):
    gens = list(gens)
    while gens:
        for gg in list(gens):
            try:
                next(gg)
            except StopIteration:
                gens.remove(gg)


def phase_rwkv_scan(P, l, nsteps=34):
    nc, io, g = P.nc, P.io, P.g
    psum = P.psum
    P.mark()
    slot_ap = [psum[i // 2][:, (i % 2) * 256:(i % 2 + 1) * 256] for i in range(16)]
    bank_b = [Buf(psum=True) for _ in range(8)]
    slot_b = [bank_b[i // 2] for i in range(16)]
    masks = P.sb([128, 5, 128], F32, "masks")
    b_masks = Buf()
    P.dma(lambda e: e.dma_start(out=masks[:], in_=io["masks"].ap().rearrange("m p f -> p m f")), writes=[b_masks])
    identb = P.sb([64, 64], BF16, "identb")
    b_identb = Buf()
    P.op("dve", lambda e: e.tensor_copy(out=identb[:], in_=g["ident"][0:64, 0:64]), reads=[g["b_ident"]], writes=[b_identb])
    etot = P.sb([64, 8, 34], F32, "etot2")
    b_etot = Buf()
    P.dma(lambda e: e.dma_start(out=etot[:], in_=io["etot"].ap()), writes=[b_etot])
    rwt = io["rwt"].ap()
    yT = io["yT"].ap()
    ident = g["ident"]

    streams = []
    for s_id in range(8):
        S = RwStream()
        S.id = s_id
        S.d = s_id % 2
        S.F = [P.sb([128, 5, 128], F32, "F%d_%d" % (s_id, i)) for i in range(2)]
        S.b_F = [Buf(), Buf()]
        for i in range(2):
            P.op("pool", lambda e, S=S, i=i: e.memset(S.F[i][64:128, :, :], 0.0), writes=[S.b_F[i]])
        S.Fb = P.sb([64, 5, 128], BF16, "Fb%d" % s_id)
        S.b_Fb = Buf()
        S.Atok = P.sb([128, 64], F32, "Atok%d" % s_id)
        S.b_Atok = Buf()
        S.BKV = P.sb([128, 3, 64], BF16, "BKV%d" % s_id)
        S.b_BKV = Buf()
        S.LQ = [P.sb([128, 256], F32, "LQ%d_%d" % (s_id, i)) for i in range(2)]
        S.b_LQ = [Buf(), Buf()]
        S.Z = [P.sb([128, 128], F32, "Z%d_%d" % (s_id, i)) for i in range(2)]
        S.b_Z = [Buf(), Buf()]
        S.LakT = P.sb([128, 128], BF16, "LakT%d" % s_id)
        S.MrbT = P.sb([128, 128], BF16, "MrbT%d" % s_id)
        S.MrkT = P.sb([128, 128], BF16, "MrkT%d" % s_id)
        S.b_LakT, S.b_MrbT, S.b_MrkT = Buf(), Buf(), Buf()
        S.W = P.sb([128, 64], BF16, "W%d" % s_id)
        S.WT = P.sb([64, 128], BF16, "WT%d" % s_id)
        S.X = P.sb([128, 64], F32, "X%d" % s_id)
        S.U0 = P.sb([128, 64], F32, "U0%d" % s_id)
        S.U0b = P.sb([128, 64], BF16, "U0b%d" % s_id)
        S.GT = P.sb([64, 64], BF16, "GT%d" % s_id)
        S.DE = P.sb([64, 64], F32, "DE%d" % s_id)
        S.Ub = P.sb([128, 64], BF16, "Ub%d" % s_id)
        S.b_W, S.b_WT, S.b_X, S.b_U0, S.b_U0b, S.b_GT, S.b_DE, S.b_Ub = [Buf() for _ in range(8)]
        S.H = [P.sb([64, 64], BF16, "H%d_%d" % (s_id, i)) for i in range(2)]
        S.b_H = [Buf(), Buf()]
        S.ys = [P.sb([64, 128], F32, "ys%d_%d" % (s_id, i)) for i in range(2)]
        S.b_ys = [Buf(), Buf()]
        S.slot = 0
        S.ei = s_id
        S.mLQ, S.mQ, S.mM = (0, 1, 4) if S.d == 0 else (1, 2, 3)
        P.op("pool", lambda e, S=S: e.memset(S.H[0][:], 0.0), writes=[S.b_H[0]])
        streams.append(S)

    def next_slot(S):
        i = 2 * S.id + (S.slot % 2)
        S.slot += 1
        return slot_ap[i], slot_b[i]

    def ev_eng(S):
        S.ei += 1
        return ("act", "dve")[S.ei % 2]

    def load(S, step):
        c = CHUNK_ORDER[S.d][step]
        fb = step % 2
        P.dma(lambda e: e.dma_start(out=S.F[fb][0:64, :, :], in_=rwt[S.id].rearrange("f d t -> d f t")[:, :, c * 128:(c + 1) * 128]),
              writes=[S.b_F[fb]])

    def stage_prep(S, step):
        fb = step % 2
        F, bF = S.F[fb], S.b_F[fb]
        P.op("pool", lambda e: e.tensor_copy(out=S.Fb[:], in_=F[0:64, :, :]), reads=[bF], writes=[S.b_Fb])
        ps, pb = next_slot(S)
        for i, fidx in enumerate((0, 1, 2, 4)):
            P.op("pe", lambda e, i=i, fidx=fidx: e.matmul(ps[:, i * 64:(i + 1) * 64], lhsT=F[:, fidx, :], rhs=ident[:, 0:64],
                                                          start=True, stop=True),
                 reads=[bF, g["b_ident"]], writes=[pb])
        yield
        P.op("act", lambda e: e.activation(out=S.Atok[:], in_=ps[:, 0:64], func=AF.Copy), reads=[pb], writes=[S.b_Atok])
        P.op("dve", lambda e: e.tensor_copy(out=S.BKV[:], in_=ps[:, 64:256].rearrange("p (a d) -> p a d", a=3)), reads=[pb], writes=[S.b_BKV])
        yield

    def stage_scores(S, step):
        fb = step % 2
        F, bF = S.F[fb], S.b_F[fb]
        ps, pb = next_slot(S)
        P.op("pe", lambda e: e.matmul(ps[:, 0:128], lhsT=F[:, 0, :], rhs=F[:, 1, :], start=True, stop=True), reads=[bF], writes=[pb])
        P.op("pe", lambda e: e.matmul(ps[:, 128:256], lhsT=F[:, 1, :], rhs=F[:, 0, :], start=True, stop=True), reads=[bF], writes=[pb])
        yield
        P.op("dve", lambda e: e.tensor_tensor(out=S.LQ[0][:].rearrange("p (a f) -> p a f", a=2), in0=ps[:, 0:256].rearrange("p (a f) -> p a f", a=2),
                                              in1=masks[:, S.mLQ:S.mLQ + 2, :], op=ALU.mult), reads=[pb, b_masks], writes=[S.b_LQ[0]])
        P.op("pool", lambda e: e.tensor_tensor(out=S.Z[0][:], in0=S.LQ[0][:, 128:256], in1=ident[:], op=ALU.add),
             reads=[S.b_LQ[0], g["b_ident"]], writes=[S.b_Z[0]])
        yield
        ps2, pb2 = next_slot(S)
        P.op("pe", lambda e: e.matmul(ps2[:, 0:128], lhsT=S.Fb[:, 2, :], rhs=S.Fb[:, 0, :], start=True, stop=True), reads=[S.b_Fb], writes=[pb2])
        P.op("pe", lambda e: e.matmul(ps2[:, 128:256], lhsT=S.Fb[:, 1, :], rhs=S.Fb[:, 3, :], start=True, stop=True), reads=[S.b_Fb], writes=[pb2])
        yield
        P.op("dve", lambda e: e.tensor_tensor(out=S.LakT[:], in0=ps2[:, 0:128], in1=masks[:, S.mQ, :], op=ALU.mult),
             reads=[pb2, b_masks], writes=[S.b_LakT])
        P.op("dve", lambda e: e.tensor_tensor(out=S.MrbT[:], in0=ps2[:, 128:256], in1=masks[:, S.mM, :], op=ALU.mult),
             reads=[pb2, b_masks], writes=[S.b_MrbT])
        yield
        ps3, pb3 = next_slot(S)
        P.op("pe", lambda e: e.matmul(ps3[:, 0:128], lhsT=S.Fb[:, 2, :], rhs=S.Fb[:, 3, :], start=True, stop=True), reads=[S.b_Fb], writes=[pb3])
        yield
        P.op("dve", lambda e: e.tensor_tensor(out=S.MrkT[:], in0=ps3[:, 0:128], in1=masks[:, S.mM, :], op=ALU.mult),
             reads=[pb3, b_masks], writes=[S.b_MrkT])
        yield

    def stage_double(S, lvl):
        a, b = (lvl - 1) % 2, lvl % 2
        last = lvl == 6
        ps, pb = next_slot(S)
        La, Qa = S.LQ[a][:, 0:128], S.LQ[a][:, 128:256]
        P.op("pe", lambda e: e.matmul(ps[:, 0:128], lhsT=Qa, rhs=La, start=True, stop=True), reads=[S.b_LQ[a]], writes=[pb])
        if not last:
            P.op("pe", lambda e: e.matmul(ps[:, 128:256], lhsT=La, rhs=Qa, start=True, stop=True), reads=[S.b_LQ[a]], writes=[pb])
        yield
        w = 128 if last else 256
        eng = ev_eng(S)
        if eng == "act":
            P.op("act", lambda e: e.activation(out=S.LQ[b][:, 0:w], in_=ps[:, 0:w], func=AF.Copy), reads=[pb], writes=[S.b_LQ[b]])
        else:
            P.op("dve", lambda e: e.tensor_copy(out=S.LQ[b][:, 0:w], in_=ps[:, 0:w]), reads=[pb], writes=[S.b_LQ[b]])
        yield
        ps2, pb2 = next_slot(S)
        P.op("pe", lambda e: e.matmul(ps2[:, 0:128], lhsT=S.LQ[b][:, 0:128], rhs=S.Z[a][:], start=True, stop=True),
             reads=[S.b_LQ[b], S.b_Z[a]], writes=[pb2])
        yield
        P.op("dve", lambda e: e.tensor_tensor(out=S.Z[b][:], in0=ps2[:, 0:128], in1=S.Z[a][:], op=ALU.add),
             reads=[pb2, S.b_Z[a]], writes=[S.b_Z[b]])
        yield

    def stage_wux(S):
        Z, bZ = S.Z[0], S.b_Z[0]
        ps, pb = next_slot(S)
        P.op("pe", lambda e: e.matmul(ps[:, 0:64], lhsT=Z[:], rhs=S.Atok[:], start=True, stop=True), reads=[bZ, S.b_Atok], writes=[pb])
        P.op("pe", lambda e: e.matmul(ps[0:64, 64:192], lhsT=S.Atok[:], rhs=Z[:], start=True, stop=True), reads=[bZ, S.b_Atok], writes=[pb])
        P.op("pe", lambda e: e.matmul(ps[:, 192:256], lhsT=S.LakT[:], rhs=S.BKV[:, 2, :], start=True, stop=True),
             reads=[S.b_LakT, S.b_BKV], writes=[pb])
        yield
        P.op("dve", lambda e: e.tensor_copy(out=S.X[:], in_=ps[:, 192:256]), reads=[pb], writes=[S.b_X])
        P.op("act", lambda e: e.activation(out=S.W[:], in_=ps[:, 0:64], func=AF.Copy), reads=[pb], writes=[S.b_W])
        P.op("act", lambda e: e.activation(out=S.WT[:], in_=ps[0:64, 64:192], func=AF.Copy), reads=[pb], writes=[S.b_WT])
        yield
        ps2, pb2 = next_slot(S)
        P.op("pe", lambda e: e.matmul(ps2[:, 0:64], lhsT=Z[:], rhs=S.X[:], start=True, stop=True), reads=[bZ, S.b_X], writes=[pb2])
        yield
        P.op("act", lambda e: e.activation(out=S.U0[:], in_=ps2[:, 0:64], func=AF.Copy), reads=[pb2], writes=[S.b_U0])
        P.op("act", lambda e: e.activation(out=S.U0b[:], in_=ps2[:, 0:64], func=AF.Copy), reads=[pb2], writes=[S.b_U0b])
        yield

    def stage_gd(S, step):
        c = CHUNK_ORDER[S.d][step]
        ps, pb = next_slot(S)
        P.op("pe", lambda e: e.matmul(ps[0:64, 0:64], lhsT=S.W[:], rhs=S.BKV[:, 0, :], start=True, stop=False),
             reads=[S.b_W, S.b_BKV], writes=[pb])
        P.op("pe", lambda e: e.matmul(ps[0:64, 0:64], lhsT=identb[:], rhs=identb[:], start=False, stop=True), reads=[b_identb], writes=[pb])
        P.op("pe", lambda e: e.matmul(ps[0:64, 64:128], lhsT=S.BKV[:, 0, :], rhs=S.U0b[:], start=True, stop=False),
             reads=[S.b_BKV, S.b_U0b], writes=[pb])
        P.op("pe", lambda e: e.matmul(ps[0:64, 64:128], lhsT=S.BKV[:, 1, :], rhs=S.BKV[:, 2, :], start=False, stop=True),
             reads=[S.b_BKV], writes=[pb])
        yield
        P.op("act", lambda e: e.activation(out=S.GT[:], in_=ps[0:64, 0:64], func=AF.Copy), reads=[pb], writes=[S.b_GT])
        P.op("act", lambda e: e.activation(out=S.DE[:], in_=ps[0:64, 64:128], func=AF.Copy, scale=etot[:, S.id, c:c + 1]),
             reads=[pb, b_etot], writes=[S.b_DE])
        yield

    def stage_state(S, step):
        c = CHUNK_ORDER[S.d][step]
        hi, ho = step % 2, (step + 1) % 2
        H, bH = S.H[hi], S.b_H[hi]
        ps, pb = next_slot(S)
        P.op("pe", lambda e: e.matmul(ps[:, 0:64], lhsT=S.WT[:], rhs=H[:], start=True, stop=True), reads=[S.b_WT, bH], writes=[pb])
        yield
        P.op("dve", lambda e: e.tensor_tensor(out=S.Ub[:], in0=ps[:, 0:64], in1=S.U0[:], op=ALU.add), reads=[pb, S.b_U0], writes=[S.b_Ub])
        yield
        ps2, pb2 = next_slot(S)
        P.op("pe", lambda e: e.matmul(ps2[0:64, 0:128], lhsT=H[:], rhs=S.Fb[:, 3, :], start=True, stop=False), reads=[bH, S.b_Fb], writes=[pb2])
        P.op("pe", lambda e: e.matmul(ps2[0:64, 0:128], lhsT=S.Ub[:], rhs=S.MrbT[:], start=False, stop=False),
             reads=[S.b_Ub, S.b_MrbT], writes=[pb2])
        P.op("pe", lambda e: e.matmul(ps2[0:64, 0:128], lhsT=S.BKV[:, 2, :], rhs=S.MrkT[:], start=False, stop=True),
             reads=[S.b_BKV, S.b_MrkT], writes=[pb2])
        P.op("pe", lambda e: e.matmul(ps2[0:64, 128:192], lhsT=S.GT[:], rhs=H[:], start=True, stop=True), reads=[S.b_GT, bH], writes=[pb2])
        yield
        yb = step % 2
        P.op("dve", lambda e: e.scalar_tensor_tensor(out=S.H[ho][:], in0=ps2[0:64, 128:192], scalar=etot[:, S.id, c:c + 1], in1=S.DE[:],
                                                     op0=ALU.mult, op1=ALU.add), reads=[pb2, b_etot, S.b_DE], writes=[S.b_H[ho]])
        P.op("dve", lambda e: e.tensor_copy(out=S.ys[yb][:], in_=ps2[0:64, 0:128]), reads=[pb2], writes=[S.b_ys[yb]])
        P.dma(lambda e: e.dma_start(out=yT[S.id, :, c * 128:(c + 1) * 128], in_=S.ys[yb][:]), reads=[S.b_ys[yb]])
        yield

    def lockstep(gens):
        gens = list(gens)
        while gens:
            for gg in list(gens):
                try:
                    next(gg)
                except StopIteration:
                    gens.remove(gg)

    for S in streams:
        load(S, 0)
    for step in range(nsteps):
        if step + 1 < nsteps:
            for S in streams:
                load(S, step + 1)
        lockstep(stage_prep(S, step) for S in streams)
        lockstep(stage_scores(S, step) for S in streams)
        for lvl in range(1, 7):
            lockstep(stage_double(S, lvl) for S in streams)
        lockstep(stage_wux(S) for S in streams)
        lockstep(stage_gd(S, step) for S in streams)
        lockstep(stage_state(S, step) for S in streams)
    P.barrier()
    P.release()


def phase_rwkv_fin(P, l):
    nc, io, g = P.nc, P.io, P.g
    psum, pbuf = P.psum, P.pbuf
    rot = Rot(P)
    P.mark()
    gn_g = col_param(P, io["gn_g"].ap()[l, :], "gn_g")
    gn_b = col_param(P, io["gn_b"].ap()[l, :], "gn_b")
    od = P.sb([64, 64], F32, "onesdiv")
    b_od = Buf()
    P.op("dve", lambda e: e.memset(od[:], 1.0 / 64), writes=[b_od])
    y2 = [P.sb([64, 2, 512], F32, "y2_%d" % i) for i in range(2)]
    ax = [P.sb([64, 2, 512], F32, "ax_%d" % i) for i in range(2)]
    b_y2, b_ax = [Buf(), Buf()], [Buf(), Buf()]
    y = P.sb([64, 512], F32, "y")
    yc = P.sb([64, 512], F32, "yc")
    sq = P.sb([64, 512], F32, "sq")
    sd = P.sb([64, 512], F32, "sd")
    b_y, b_yc, b_sq, b_sd = Buf(), Buf(), Buf(), Buf()
    ost = [P.sb([64, 512], BF16, "rost%d" % i) for i in range(2)]
    b_ost = [Buf(), Buf()]
    yT = io["yT"].ap()
    cat = io["cat"].ap()

    def blk(h, t0, nb, n):
        k = n % 2
        hs = slice(h, h + 1)
        P.dma(lambda e: e.dma_start(out=y2[k][:, :, 0:nb], in_=yT[2 * h:2 * h + 2, :, t0:t0 + nb].rearrange("s d t -> d s t")), writes=[b_y2[k]])
        P.dma(lambda e: e.dma_start(out=ax[k][:, :, 0:nb], in_=io["rwaux"].ap()[:, h, :, t0:t0 + nb].rearrange("a d t -> d a t")), writes=[b_ax[k]])
        P.op("dve", lambda e: e.tensor_tensor(out=y[:, 0:nb], in0=y2[k][:, 0, 0:nb], in1=y2[k][:, 1, 0:nb], op=ALU.add), reads=[b_y2[k]], writes=[b_y])
        pb = rot.bank()
        P.op("pe", lambda e: e.matmul(psum[pb][0:64, 0:nb], lhsT=od[:], rhs=y[:, 0:nb], start=True, stop=True), reads=[b_od, b_y], writes=[pbuf[pb]])
        P.op("dve", lambda e: e.tensor_tensor(out=yc[:, 0:nb], in0=y[:, 0:nb], in1=psum[pb][0:64, 0:nb], op=ALU.subtract),
             reads=[b_y, pbuf[pb]], writes=[b_yc])
        P.op("pool", lambda e: e.tensor_tensor(out=sq[:, 0:nb], in0=yc[:, 0:nb], in1=yc[:, 0:nb], op=ALU.mult), reads=[b_yc], writes=[b_sq])
        pb2 = rot.bank()
        P.op("pe", lambda e: e.matmul(psum[pb2][0:64, 0:nb], lhsT=od[:], rhs=sq[:, 0:nb], start=True, stop=True), reads=[b_od, b_sq], writes=[pbuf[pb2]])
        P.op("dve", lambda e: e.tensor_scalar(out=sd[:, 0:nb], in0=psum[pb2][0:64, 0:nb], scalar1=float(GN_EPS), scalar2=None, op0=ALU.add),
             reads=[pbuf[pb2]], writes=[b_sd])
        P.op("act", lambda e: e.activation(out=sd[:, 0:nb], in_=sd[:, 0:nb], func=AF.Sqrt), reads=[b_sd], writes=[b_sd])
        P.op("dve", lambda e: e.reciprocal(out=sd[:, 0:nb], in_=sd[:, 0:nb]), reads=[b_sd], writes=[b_sd])
        P.op("dve", lambda e: e.tensor_tensor(out=yc[:, 0:nb], in0=yc[:, 0:nb], in1=sd[:, 0:nb], op=ALU.mult), reads=[b_yc, b_sd], writes=[b_yc])
        P.op("pool", lambda e: e.tensor_scalar(out=yc[:, 0:nb], in0=yc[:, 0:nb], scalar1=gn_g[0][:, hs], scalar2=gn_b[0][:, hs],
                                               op0=ALU.mult, op1=ALU.add), reads=[b_yc, gn_g[1], gn_b[1]], writes=[b_yc])
        P.op("dve", lambda e: e.tensor_tensor(out=yc[:, 0:nb], in0=yc[:, 0:nb], in1=ax[k][:, 1, 0:nb], op=ALU.add), reads=[b_yc, b_ax[k]], writes=[b_yc])
        P.op("pool", lambda e: e.tensor_tensor(out=ost[k][:, 0:nb], in0=yc[:, 0:nb], in1=ax[k][:, 0, 0:nb], op=ALU.mult),
             reads=[b_yc, b_ax[k]], writes=[b_ost[k]])
        P.dma(lambda e: e.dma_start(out=cat[256 + h * 64:256 + (h + 1) * 64, t0:t0 + nb], in_=ost[k][:, 0:nb]), reads=[b_ost[k]])

    n = 0
    for h in range(4):
        for (t0, nb, first, last) in RW_SEQS:
            blk(h, t0, nb, n)
            n += 1
    P.barrier()
    P.release()


_NC_CACHE = {}


def _get_nc(dbg=None):
    if dbg not in _NC_CACHE:
        _NC_CACHE[dbg] = build_program(dbg)
    return _NC_CACHE[dbg]


_CONST = {}


def _constants():
    if _CONST:
        return _CONST
    bf = ml_dtypes.bfloat16
    t = np.arange(SEQ)
    rows = (t // 64).astype(np.float64)
    cols = (t % 64).astype(np.float64)
    inv = 10000.0 ** (-np.arange(16, dtype=np.float64) / 16)
    ang = np.concatenate([rows[:, None] * inv, cols[:, None] * inv], -1)
    _CONST["rope_cos"] = np.cos(ang).astype(np.float32)
    _CONST["rope_sin"] = np.sin(ang).astype(np.float32)
    c = np.arange(64)
    th = 2 * np.pi * np.outer(c, c) / 64
    z = np.zeros((64, 64))
    _CONST["dftc"] = np.block([[np.cos(th), z], [z, np.cos(th)]]).astype(bf)
    _CONST["dfts"] = np.block([[np.sin(th), z], [z, np.sin(th)]]).astype(bf)
    pi_, fi_ = np.arange(128)[:, None], np.arange(128)[None, :]
    _CONST["masks"] = np.stack([fi_ < pi_, fi_ > pi_, fi_ < pi_, fi_ <= pi_, fi_ >= pi_], 0).astype(np.float32)
    pm = np.zeros((4, 5, 128, 128))
    for gi, win in enumerate((2, 4, 8, 16)):
        T3 = 384
        tt = np.arange(T3)
        lo = np.clip(tt - win // 2, 0, T3)
        hi = np.clip(tt + (win - win // 2), 0, T3)
        ss_ = np.arange(T3)[:, None]
        M = ((ss_ >= lo[None, :]) & (ss_ < hi[None, :])) / (hi - lo)[None, :].astype(np.float64) - np.eye(T3)
        pm[gi, 0] = M[0:128, 128:256]
        pm[gi, 1] = M[128:256, 0:128]
        pm[gi, 2] = M[128:256, 128:256]
        pm[gi, 3] = M[0:128, 0:128]
        pm[gi, 4] = M[256:384, 256:384]
    _CONST["poolm"] = pm.reshape(20, 128, 128).astype(bf)
    for nm, T in (("dft_lat", SEQ), ("dft_ctx", CTX)):
        k = np.arange(T)
        kk = (np.outer(k, k) % T).astype(np.float64)
        ang = 2 * np.pi * kk / T
        sc = 1.0 / np.sqrt(T * 64.0)
        _CONST[nm] = np.stack([np.cos(ang) * sc, -np.sin(ang) * sc], 0).astype(bf)
    return _CONST


def make_in_maps(inputs):
    f32 = lambda a: np.ascontiguousarray(np.asarray(a, dtype=np.float32))
    shared = {k: f32(inputs[k]) for k in ("w_mod", "b_mod", "ln_g", "ln_b", "w_ffn_in", "w_ffn_out", "w_in", "w_out",
                                          "q_norm_g", "k_norm_g", "pool_w", "pool_scale", "fourier_w", "rwkv_mu", "decay_w0", "decay_w1", "decay_w2",
                                          "icl_a0", "icl_a1", "icl_a2", "gate_g1", "gate_g2", "k_k", "k_a", "r_k", "gn_g", "gn_b")}
    shared["ident"] = np.eye(128, dtype=np.float32)
    shared.update(_constants())
    maps = []
    for core in range(8):
        b = core // 2
        m = dict(shared)
        m["x"] = f32(inputs["x"][b])
        m["ctx"] = f32(inputs["ctx"][b])
        m["c2"] = f32(np.stack([np.asarray(inputs["c"])[b], np.asarray(inputs["c_ctx"])], 0))
        maps.append(m)
    return maps


def kernel(**inputs):
    nc = _get_nc(None)
    res = run_bass_kernel_spmd(nc, make_in_maps(inputs), core_ids=list(range(8)))
    out = np.zeros((4, SEQ, D), np.float32)
    hf = SEQ // 2
    for core in range(8):
        b, j = core // 2, core % 2
        out[b, j * hf:(j + 1) * hf] = np.asarray(res.results[core]["y"])[j * hf:(j + 1) * hf]
    return out
```

```python
import contextlib
import numpy as np
import ml_dtypes
import concourse.bass as bass
import concourse.mybir as mybir
from concourse.bass_utils import run_bass_kernel_spmd

F32 = mybir.dt.float32
BF16 = mybir.dt.bfloat16
AF = mybir.ActivationFunctionType
ALU = mybir.AluOpType
AX = mybir.AxisListType

D = 1024
SEQ = 4096
CTX = 256
NTOK = SEQ + CTX
DEPTH = 4
DFF = 2816
DIN = 2048
ALPHA = (2 * DEPTH) ** 0.25
LN_EPS = 1e-5

ENGS = ("pe", "dve", "act", "pool", "sp")


class Buf:
    __slots__ = ("w", "r", "name", "psum")

    def __init__(self, name="", psum=False):
        self.w = None
        self.r = {}
        self.name = name
        self.psum = psum


class Prog:
    def __init__(self, nc, stack, n_dma_sems=16):
        self.nc = nc
        self.streams = {e: [] for e in ENGS}
        self.count = {e: 0 for e in ENGS}
        self.seen = {e: {} for e in ENGS}
        self.sems = {}
        for e in ENGS:
            self.sems["E_" + e] = stack.enter_context(nc.semaphore("sem_" + e))
        self.dma_keys = []
        self.dma_tot = {}
        for i in range(n_dma_sems):
            k = "D_%d" % i
            self.sems[k] = stack.enter_context(nc.semaphore("semd_%d" % i))
            self.dma_keys.append(k)
            self.dma_tot[k] = 0
        self.dma_rr = 0
        self.sb_off = 16640
        self.sb_marks = []
        self.n_alloc = 0
        self.n_ops = 0

    def sb(self, shape, dtype, name=None):
        esz = 4 if dtype == F32 else 2
        per_part = esz
        for s in shape[1:]:
            per_part *= s
        per_part = (per_part + 63) // 64 * 64
        off = self.sb_off
        self.sb_off += per_part
        assert self.sb_off <= 229376, "SBUF overflow %d" % self.sb_off
        self.n_alloc += 1
        t = self.nc.alloc_sbuf_tensor_at("sb%d_%s" % (self.n_alloc, name or "t"), list(shape), dtype, offset=off)
        return t

    def mark(self):
        self.sb_marks.append(self.sb_off)

    def release(self):
        self.sb_off = self.sb_marks.pop()

    def _wait(self, eng, toks):
        own = "E_" + eng
        seen = self.seen[eng]
        for key, val in toks:
            if key == own and eng in ("pe", "sp"):
                continue
            if seen.get(key, 0) >= val:
                continue
            seen[key] = val
            self.streams[eng].append(("wait", key, val))

    @staticmethod
    def _deps(reads, writes, own=None):
        toks = []
        for b in reads:
            if b.w is not None:
                toks.append(b.w)
            if b.psum:
                toks.extend((k, v) for k, v in b.r.items() if k != own)
        for b in writes:
            if b.w is not None:
                toks.append(b.w)
            toks.extend(b.r.items())
        return toks

    @staticmethod
    def _update(tok, reads, writes):
        key, val = tok
        for b in reads:
            if b.r.get(key, 0) < val:
                b.r[key] = val
        for b in writes:
            b.w = tok
            b.r = {}

    def op(self, eng, fn, reads=(), writes=()):
        self._wait(eng, self._deps(reads, writes, "E_" + eng))
        self.count[eng] += 1
        key = "E_" + eng
        self.streams[eng].append(("op", fn, key))
        tok = (key, self.count[eng])
        self._update(tok, reads, writes)
        self.n_ops += 1
        return tok

    def dma(self, fn, reads=(), writes=(), queue="sp"):
        key = self.dma_keys[self.dma_rr % len(self.dma_keys)]
        self.dma_rr += 1
        toks = self._deps(reads, writes)
        if self.dma_tot[key] > 0:
            toks.append((key, self.dma_tot[key]))
        self._wait(queue, toks)
        self.dma_tot[key] += 16
        self.streams[queue].append(("dma", fn, key))
        tok = (key, self.dma_tot[key])
        self._update(tok, reads, writes)
        self.n_ops += 1
        return tok

    def barrier(self):
        toks = [("E_" + e, self.count[e]) for e in ENGS if self.count[e] > 0]
        toks += [(k, v) for k, v in self.dma_tot.items() if v > 0]
        for e in ENGS:
            self._wait(e, toks)

    def replay(self, name, e):
        sems = self.sems
        for item in self.streams[name]:
            if item[0] == "wait":
                e.wait_ge(sems[item[1]], item[2])
            elif item[0] == "op":
                item[1](e).then_inc(sems[item[2]], 1)
            else:
                item[1](e).then_inc(sems[item[2]], 16)


def build_program(dbg=None):
    nc = bass.Bass("TRN2", target_bir_lowering=False)
    _so = bool(dbg) and dbg.startswith("so")

    def dt(name, shape, dtype=F32, kind="ExternalInput"):
        if _so and kind == "ExternalInput" and name not in ("ident", "masks", "c2", "rwt", "etot"):
            shape = [1, 2]
        return nc.dram_tensor(name, list(shape), dtype, kind=kind)
    io = {}
    io["x"] = dt("x", [SEQ, D])
    io["ctx"] = dt("ctx", [CTX, D])
    io["c2"] = dt("c2", [2, D])
    io["w_mod"] = dt("w_mod", [DEPTH, D, 9 * D])
    io["b_mod"] = dt("b_mod", [DEPTH, 9 * D])
    io["ln_g"] = dt("ln_g", [DEPTH, 3, D])
    io["ln_b"] = dt("ln_b", [DEPTH, 3, D])
    io["w_ffn_in"] = dt("w_ffn_in", [DEPTH, 2, D, 2 * DFF])
    io["w_ffn_out"] = dt("w_ffn_out", [DEPTH, 2, DFF, D])
    io["ident"] = dt("ident", [128, 128])
    io["w_in"] = dt("w_in", [DEPTH, D, DIN])
    io["w_out"] = dt("w_out", [DEPTH, D, D])
    io["q_norm_g"] = dt("q_norm_g", [DEPTH, 64])
    io["k_norm_g"] = dt("k_norm_g", [DEPTH, 64])
    io["rope_cos"] = dt("rope_cos", [SEQ, 32])
    io["rope_sin"] = dt("rope_sin", [SEQ, 32])
    io["dftc"] = dt("dftc", [128, 128], BF16)
    io["dfts"] = dt("dfts", [128, 128], BF16)
    io["poolm"] = dt("poolm", [20, 128, 128], BF16)
    io["pool_w"] = dt("pool_w", [DEPTH, 4, 64, 64])
    io["pool_scale"] = dt("pool_scale", [DEPTH, 256])
    io["fourier_w"] = dt("fourier_w", [DEPTH, 256, 256])
    io["dft_lat"] = dt("dft_lat", [2, SEQ, SEQ], BF16)
    io["dft_ctx"] = dt("dft_ctx", [2, CTX, CTX], BF16)
    io["rwkv_mu"] = dt("rwkv_mu", [DEPTH, 6, 256])
    io["decay_w0"] = dt("decay_w0", [DEPTH, 2, 256])
    io["decay_w1"] = dt("decay_w1", [DEPTH, 2, 256, 32])
    io["decay_w2"] = dt("decay_w2", [DEPTH, 2, 32, 256])
    io["icl_a0"] = dt("icl_a0", [DEPTH, 2, 256])
    io["icl_a1"] = dt("icl_a1", [DEPTH, 2, 256, 32])
    io["icl_a2"] = dt("icl_a2", [DEPTH, 2, 32, 256])
    io["gate_g1"] = dt("gate_g1", [DEPTH, 256, 64])
    io["gate_g2"] = dt("gate_g2", [DEPTH, 64, 256])
    io["k_k"] = dt("k_k", [DEPTH, 256])
    io["k_a"] = dt("k_a", [DEPTH, 256])
    io["r_k"] = dt("r_k", [DEPTH, 4, 64])
    io["gn_g"] = dt("gn_g", [DEPTH, 256])
    io["gn_b"] = dt("gn_b", [DEPTH, 256])
    io["masks"] = dt("masks", [5, 128, 128])
    scratch_kind = "ExternalOutput" if dbg else "Internal"
    io["xs"] = dt("xs", [NTOK, D], F32, kind=scratch_kind)
    io["modv"] = dt("modv", [2, 3 * D], F32, kind=scratch_kind)
    io["cat"] = dt("cat", [D, NTOK], BF16, kind=scratch_kind)
    io["zrw"] = dt("zrw", [4, 64, 4, NTOK], F32, kind=scratch_kind)
    io["zpool"] = dt("zpool", [NTOK, 256], BF16, kind=scratch_kind)
    io["ab"] = dt("ab", [NTOK, 512], BF16, kind=scratch_kind)
    so = bool(dbg) and dbg.startswith("so")
    io["rwt"] = dt("rwt", [8, 5, 64, NTOK], F32, kind=("ExternalInput" if so else scratch_kind))
    io["etot"] = dt("etot", [64, 8, 34], F32, kind=("ExternalInput" if so else scratch_kind))
    io["yT"] = dt("yT", [8, 64, NTOK], F32, kind=scratch_kind)
    io["rwaux"] = dt("rwaux", [2, 4, 64, NTOK], F32, kind=scratch_kind)
    io["y"] = dt("y", [SEQ, D], F32, kind="ExternalOutput")

    with contextlib.ExitStack() as stack:
        P = Prog(nc, stack)
        psum = [nc.alloc_psum_tensor("ps%d" % i, [128, 512], F32) for i in range(8)]
        pbuf = [Buf("ps%d" % i, psum=True) for i in range(8)]
        P.psum = psum
        P.pbuf = pbuf
        P.io = io
        emit_all(P, dbg)
        P.barrier()
        with nc.Block() as block:
            @block.tensor
            def _(e):
                P.replay("pe", e)

            @block.vector
            def _(e):
                P.replay("dve", e)

            @block.scalar
            def _(e):
                P.replay("act", e)

            @block.gpsimd
            def _(e):
                P.replay("pool", e)

            @block.sync
            def _(e):
                P.replay("sp", e)
    return nc


def emit_all(P, dbg):
    nc = P.nc
    io = P.io
    ident = P.sb([128, 128], F32, "ident")
    b_ident = Buf("ident")
    P.dma(lambda e: e.dma_start(out=ident[:], in_=io["ident"].ap()), writes=[b_ident])
    ones = P.sb([128, 128], F32, "ones")
    b_ones = Buf("ones")
    P.op("dve", lambda e: e.memset(ones[:], 1.0), writes=[b_ones])
    craw = P.sb([128, 2, 8], F32, "craw")
    b_craw = Buf()
    for r in range(2):
        P.dma(lambda e, r=r: e.dma_start(out=craw[:, r, :], in_=io["c2"].ap()[r, :].rearrange("(k p) -> p k", p=128),
                                         allow_slow_non_contiguous=True), writes=[b_craw])
    scT = P.sb([128, 8, 2], F32, "scT")
    b_scT = Buf()
    P.op("act", lambda e: e.activation(out=scT[:].rearrange("p k r -> p r k"), in_=craw[:], func=AF.Silu),
         reads=[b_craw], writes=[b_scT])
    P.g = dict(ident=ident, b_ident=b_ident, ones=ones, b_ones=b_ones, scT=scT, b_scT=b_scT)
    P.barrier()

    if dbg and dbg.startswith("so"):
        phase_rwkv_scan(P, 0, nsteps=int(dbg[2:]))
        return
    for l in range(DEPTH):
        last = l == DEPTH - 1
        src_lat = io["x"].ap() if l == 0 else io["xs"].ap()[0:SEQ, :]
        src_ctx = io["ctx"].ap() if l == 0 else io["xs"].ap()[SEQ:NTOK, :]
        phase_ffn(P, l, 0, 0, src_lat, src_ctx, io["xs"].ap()[0:SEQ, :], io["xs"].ap()[SEQ:NTOK, :], 0.5)
        if dbg == "ffn0":
            return
        phase_mixer(P, l, dbg)
        if dbg in ("att", "mixA", "pf", "wout", "rw1", "rw2", "rw3") or (dbg and dbg.startswith("rws")):
            return
        phase_ffn(P, l, 1, 2, io["xs"].ap()[0:SEQ, :], io["xs"].ap()[SEQ:NTOK, :],
                  io["y"].ap() if last else io["xs"].ap()[0:SEQ, :], io["xs"].ap()[SEQ:NTOK, :], 0.5, skip_ctx=last)
        if dbg == "l0":
            return


def phase_mod(P, l, sub, resid_w):
    nc, io, g = P.nc, P.io, P.g
    psum, pbuf = P.psum, P.pbuf
    P.mark()
    brow = P.sb([1, 3 * D], F32, "brow")
    b_brow = Buf()
    P.dma(lambda e: e.dma_start(out=brow[:], in_=io["b_mod"].ap()[l:l + 1, sub * 3 * D:(sub + 1) * 3 * D]), writes=[b_brow])
    mrow = P.sb([2, 3 * D], F32, "mrow")
    b_mrow = Buf()
    wst = [P.sb([128, 8, 512], F32, "wmst%d" % i) for i in range(2)]
    b_wst = [Buf(), Buf()]
    for cb in range(6):
        s = cb % 2
        c0 = sub * 3 * D + cb * 512
        P.dma(lambda e, s=s, c0=c0: e.dma_start(
            out=wst[s][:], in_=io["w_mod"].ap()[l, :, c0:c0 + 512].rearrange("(k p) n -> p k n", p=128)),
            writes=[b_wst[s]])
        pb = cb % 2
        for kc in range(8):
            P.op("pe", lambda e, s=s, kc=kc, pb=pb: e.matmul(psum[pb][0:2, :], lhsT=g["scT"][:, kc, :], rhs=wst[s][:, kc, :],
                                                             start=(kc == 0), stop=False),
                 reads=[g["b_scT"], b_wst[s]], writes=[pbuf[pb]])
        P.op("pe", lambda e, cb=cb, pb=pb: e.matmul(psum[pb][0:2, :], lhsT=g["ones"][0:1, 0:2], rhs=brow[0:1, cb * 512:(cb + 1) * 512],
                                                    start=False, stop=True),
             reads=[g["b_ones"], b_brow], writes=[pbuf[pb]])
        P.op("dve", lambda e, cb=cb, pb=pb: e.tensor_copy(out=mrow[:, cb * 512:(cb + 1) * 512], in_=psum[pb][0:2, :]),
             reads=[pbuf[pb]], writes=[b_mrow])
    b_modv = Buf()
    P.dma(lambda e: e.dma_start(out=io["modv"].ap(), in_=mrow[:]), reads=[b_mrow], writes=[b_modv])
    P.release()
    shT = P.sb([128, 2, 8], F32, "shT")
    scl = P.sb([128, 2, 8], F32, "scl")
    gbc = [P.sb([128, D], F32, "gbc%d" % r) for r in range(2)]
    b_shT, b_scl, b_g = Buf(), Buf(), [Buf(), Buf()]
    for r in range(2):
        P.dma(lambda e, r=r: e.dma_start(out=shT[:, r, :], in_=io["modv"].ap()[r, 0:D].rearrange("(k p) -> p k", p=128),
                                         allow_slow_non_contiguous=True), reads=[b_modv], writes=[b_shT])
        P.dma(lambda e, r=r: e.dma_start(out=scl[:, r, :], in_=io["modv"].ap()[r, D:2 * D].rearrange("(k p) -> p k", p=128),
                                         allow_slow_non_contiguous=True), reads=[b_modv], writes=[b_scl])
        P.dma(lambda e, r=r: e.dma_start(out=gbc[r][:], in_=io["modv"].ap()[r:r + 1, 2 * D:3 * D].partition_broadcast(128)),
              reads=[b_modv], writes=[b_g[r]])
    P.op("dve", lambda e: e.tensor_scalar(out=scl[:], in0=scl[:], scalar1=1.0, scalar2=None, op0=ALU.add),
         reads=[b_scl], writes=[b_scl])
    for r in range(2):
        if resid_w != 1.0:
            P.op("pool", lambda e, r=r: e.tensor_scalar(out=gbc[r][:], in0=gbc[r][:], scalar1=float(resid_w), scalar2=None, op0=ALU.mult),
                 reads=[b_g[r]], writes=[b_g[r]])
    lg = P.sb([128, D], F32, "lng")
    lb = P.sb([128, D], F32, "lnb")
    b_lg, b_lb = Buf(), Buf()
    P.dma(lambda e: e.dma_start(out=lg[:], in_=io["ln_g"].ap()[l, sub:sub + 1, :].partition_broadcast(128)), writes=[b_lg])
    P.dma(lambda e: e.dma_start(out=lb[:], in_=io["ln_b"].ap()[l, sub:sub + 1, :].partition_broadcast(128)), writes=[b_lb])
    return dict(shT=shT, scl=scl, gbc=gbc, b_shT=b_shT, b_scl=b_scl, b_g=b_g, lg=lg, lb=lb, b_lg=b_lg, b_lb=b_lb)


def emit_postnorm(P, m, r, xt_ap, b_x, y_halves, b_ys, out_ap, b_out, work):
    u, b_u, st, b_st = work["u"], work["b_u"], work["st"], work["b_st"]
    for h in range(2):
        sl = slice(h * 512, (h + 1) * 512)
        P.op("dve", lambda e, h=h, sl=sl: e.tensor_tensor(out=u[:, sl], in0=y_halves[h], in1=m["gbc"][r][:, sl], op=ALU.mult),
             reads=[b_ys[h], m["b_g"][r]], writes=[b_u])
        P.op("dve", lambda e, sl=sl: e.scalar_tensor_tensor(out=u[:, sl], in0=xt_ap[:, sl], scalar=float(ALPHA), in1=u[:, sl],
                                                            op0=ALU.mult, op1=ALU.add),
             reads=[b_x, b_u], writes=[b_u])
        P.op("dve", lambda e, h=h, sl=sl: e.bn_stats(out=st[:, h * 6:(h + 1) * 6], in_=u[:, sl]), reads=[b_u], writes=[b_st])
    P.op("dve", lambda e: e.bn_aggr(out=st[:, 12:14], in_=st[:, 0:12]), reads=[b_st], writes=[b_st])
    P.op("dve", lambda e: e.tensor_scalar(out=st[:, 14:15], in0=st[:, 13:14], scalar1=float(LN_EPS), scalar2=None, op0=ALU.add),
         reads=[b_st], writes=[b_st])
    P.op("act", lambda e: e.activation(out=st[:, 14:15], in_=st[:, 14:15], func=AF.Sqrt), reads=[b_st], writes=[b_st])
    P.op("dve", lambda e: e.reciprocal(out=st[:, 14:15], in_=st[:, 14:15]), reads=[b_st], writes=[b_st])
    P.op("dve", lambda e: e.scalar_tensor_tensor(out=st[:, 15:16], in0=st[:, 12:13], scalar=-1.0, in1=st[:, 14:15],
                                                 op0=ALU.mult, op1=ALU.mult), reads=[b_st], writes=[b_st])
    P.op("act", lambda e: e.activation(out=u[:], in_=u[:], func=AF.Identity, scale=st[:, 14:15], bias=st[:, 15:16]),
         reads=[b_u, b_st], writes=[b_u])
    P.op("pool", lambda e: e.tensor_tensor(out=u[:], in0=u[:], in1=m["lg"][:], op=ALU.mult), reads=[b_u, m["b_lg"]], writes=[b_u])
    P.op("pool", lambda e: e.tensor_tensor(out=out_ap, in0=u[:], in1=m["lb"][:], op=ALU.add), reads=[b_u, m["b_lb"]], writes=[b_out])


def phase_ffn(P, l, f, sub, src_lat, src_ctx, dst_lat, dst_ctx, resid_w, skip_ctx=False):
    nc, io, g = P.nc, P.io, P.g
    psum, pbuf = P.psum, P.pbuf
    P.mark()
    m = phase_mod(P, l, sub, resid_w)
    wgu = P.sb([128, 8, 2 * DFF], BF16, "wgu")
    wdn = P.sb([128, 22, D], BF16, "wdn")
    b_wgu, b_wdn = Buf(), Buf()
    P.mark()
    stg = [P.sb([128, DFF], F32, "stg%d" % i) for i in range(2)]
    b_stg = [Buf(), Buf()]
    cast_engs = ["pool", "dve", "act"]
    n = 0
    for kc in range(8):
        for hf in range(2):
            s = n % 2
            P.dma(lambda e, s=s, kc=kc, hf=hf: e.dma_start(
                out=stg[s][:], in_=io["w_ffn_in"].ap()[l, f, kc * 128:(kc + 1) * 128, hf * DFF:(hf + 1) * DFF]),
                writes=[b_stg[s]])
            ce = cast_engs[n % 3]
            if ce == "act":
                P.op("act", lambda e, s=s, kc=kc, hf=hf: e.activation(out=wgu[:, kc, hf * DFF:(hf + 1) * DFF], in_=stg[s][:], func=AF.Copy),
                     reads=[b_stg[s]], writes=[b_wgu])
            else:
                P.op(ce, lambda e, s=s, kc=kc, hf=hf: e.tensor_copy(out=wgu[:, kc, hf * DFF:(hf + 1) * DFF], in_=stg[s][:]),
                     reads=[b_stg[s]], writes=[b_wgu])
            n += 1
    for pc in range(11):
        s = n % 2
        P.dma(lambda e, s=s, pc=pc: e.dma_start(
            out=stg[s][:, 0:2048].rearrange("p (c n) -> p c n", c=2),
            in_=io["w_ffn_out"].ap()[l, f, pc * 256:(pc + 1) * 256, :].rearrange("(c p) n -> p c n", p=128)),
            writes=[b_stg[s]])
        ce = cast_engs[n % 3]
        if ce == "act":
            P.op("act", lambda e, s=s, pc=pc: e.activation(out=wdn[:, 2 * pc:2 * pc + 2, :].rearrange("p c n -> p (c n)"),
                                                           in_=stg[s][:, 0:2048], func=AF.Copy),
                 reads=[b_stg[s]], writes=[b_wdn])
        else:
            P.op(ce, lambda e, s=s, pc=pc: e.tensor_copy(out=wdn[:, 2 * pc:2 * pc + 2, :].rearrange("p c n -> p (c n)"),
                                                         in_=stg[s][:, 0:2048]),
                 reads=[b_stg[s]], writes=[b_wdn])
        n += 1
    P.barrier()
    P.release()
    NB = 256
    xt = [P.sb([128, 2, D], F32, "xt%d" % i) for i in range(2)]
    b_xt = [[Buf(), Buf()], [Buf(), Buf()]]
    hT = P.sb([128, 8, NB], BF16, "hT")
    b_hT = Buf()
    aT = P.sb([128, 22, NB], BF16, "aT")
    b_aT = [Buf() for _ in range(22)]
    sg = [P.sb([128, NB], F32, "sg%d" % i) for i in range(2)]
    b_sg = [Buf(), Buf()]
    ot = [P.sb([128, D], F32, "ot%d" % i) for i in range(2)]
    b_ot = [Buf(), Buf()]
    work = dict(u=P.sb([128, D], F32, "u"), b_u=Buf(), st=P.sb([128, 16], F32, "st"), b_st=Buf())
    blocks = [(src_lat, dst_lat, i * NB, 0) for i in range(SEQ // NB)] + ([] if skip_ctx else [(src_ctx, dst_ctx, 0, 1)])
    pi = 0
    for bi, (src, dst, t0, r) in enumerate(blocks):
        xs_ = bi % 2
        for t in range(2):
            P.dma(lambda e, xs_=xs_, t=t, src=src, t0=t0: e.dma_start(out=xt[xs_][:, t, :], in_=src[t0 + t * 128:t0 + (t + 1) * 128, :]),
                  writes=[b_xt[xs_][t]])
        for kp in range(4):
            pb = pi % 8
            pi += 1
            for kk in range(2):
                kc = kp * 2 + kk
                for t in range(2):
                    P.op("pe", lambda e, pb=pb, kk=kk, t=t, kc=kc, xs_=xs_: e.transpose(
                        out=psum[pb][:, kk * 256 + t * 128: kk * 256 + (t + 1) * 128], in_=xt[xs_][:, t, kc * 128:(kc + 1) * 128],
                        identity=g["ident"][:]),
                        reads=[b_xt[xs_][t], g["b_ident"]], writes=[pbuf[pb]])
            for kk in range(2):
                kc = kp * 2 + kk
                P.op("act", lambda e, pb=pb, kk=kk, kc=kc, r=r: e.activation(
                    out=hT[:, kc, :], in_=psum[pb][:, kk * 256:(kk + 1) * 256], func=AF.Identity,
                    scale=m["scl"][:, r, kc:kc + 1], bias=m["shT"][:, r, kc:kc + 1]),
                    reads=[pbuf[pb], m["b_scl"], m["b_shT"]], writes=[b_hT])
        for i in range(22):
            pb = pi % 8
            pi += 1
            for hf in range(2):
                for kc in range(8):
                    P.op("pe", lambda e, pb=pb, hf=hf, kc=kc, i=i: e.matmul(
                        psum[pb][:, hf * 256:(hf + 1) * 256], lhsT=wgu[:, kc, hf * DFF + i * 128: hf * DFF + (i + 1) * 128],
                        rhs=hT[:, kc, :], start=(kc == 0), stop=(kc == 7)),
                        reads=[b_hT, b_wgu], writes=[pbuf[pb]])
            s = i % 2
            P.op("act", lambda e, pb=pb, s=s: e.activation(out=sg[s][:], in_=psum[pb][:, 0:256], func=AF.Silu),
                 reads=[pbuf[pb]], writes=[b_sg[s]])
            P.op("dve", lambda e, pb=pb, s=s, i=i: e.tensor_tensor(out=aT[:, i, :], in0=psum[pb][:, 256:512], in1=sg[s][:], op=ALU.mult),
                 reads=[pbuf[pb], b_sg[s]], writes=[b_aT[i]])
        for t in range(2):
            pbs = []
            for hn in range(2):
                pb = pi % 8
                pi += 1
                pbs.append(pb)
                for i in range(22):
                    P.op("pe", lambda e, pb=pb, hn=hn, i=i, t=t: e.matmul(
                        psum[pb][:, :], lhsT=aT[:, i, t * 128:(t + 1) * 128], rhs=wdn[:, i, hn * 512:(hn + 1) * 512],
                        start=(i == 0), stop=(i == 21)),
                        reads=[b_aT[i], b_wdn], writes=[pbuf[pb]])
            o = (bi * 2 + t) % 2
            emit_postnorm(P, m, r, xt[xs_][:, t, :], b_xt[xs_][t], [psum[pbs[0]][:, :], psum[pbs[1]][:, :]],
                          [pbuf[pbs[0]], pbuf[pbs[1]]], ot[o][:], b_ot[o], work)
            P.dma(lambda e, o=o, dst=dst, t0=t0, t=t: e.dma_start(out=dst[t0 + t * 128:t0 + (t + 1) * 128, :], in_=ot[o][:]),
                  reads=[b_ot[o]])
    P.barrier()
    P.release()


class Rot:
    def __init__(self, P):
        self.P = P
        self.pi = 0
        self.ei = 0

    def bank(self):
        b = self.pi % 8
        self.pi += 1
        return b

    def evac(self, out_ap, in_ap, reads, writes, eng=None):
        P = self.P
        if eng is None:
            eng = ("act", "dve")[self.ei % 2]
            self.ei += 1
        if eng == "act":
            return P.op("act", lambda e: e.activation(out=out_ap, in_=in_ap, func=AF.Copy), reads=reads, writes=writes)
        return P.op(eng, lambda e: e.tensor_copy(out=out_ap, in_=in_ap), reads=reads, writes=writes)


def emit_hT(P, rot, m, r, xt, b_xts, ntile, hT, b_hT):
    g, psum, pbuf = P.g, P.psum, P.pbuf
    for kc in range(8):
        pb = rot.bank()
        for t in range(ntile):
            P.op("pe", lambda e, pb=pb, t=t, kc=kc: e.transpose(
                out=psum[pb][:, t * 128:(t + 1) * 128], in_=xt[:, t, kc * 128:(kc + 1) * 128], identity=g["ident"][:]),
                reads=[b_xts[t], g["b_ident"]], writes=[pbuf[pb]])
        P.op("act", lambda e, pb=pb, kc=kc: e.activation(
            out=hT[:, kc, 0:ntile * 128], in_=psum[pb][:, 0:ntile * 128], func=AF.Identity,
            scale=m["scl"][:, r, kc:kc + 1], bias=m["shT"][:, r, kc:kc + 1]),
            reads=[pbuf[pb], m["b_scl"], m["b_shT"]], writes=[b_hT])


def load_cast_rows(P, dst, b_dst, src_rows_fn, nchunk, width, stg, b_stg, n0=0):
    cast_engs = ["pool", "dve", "act"]
    n = n0
    for kc in range(nchunk):
        s = n % 2
        P.dma(lambda e, s=s, kc=kc: e.dma_start(out=stg[s][:, 0:width], in_=src_rows_fn(kc)), writes=[b_stg[s]])
        ce = cast_engs[n % 3]
        if ce == "act":
            P.op("act", lambda e, s=s, kc=kc: e.activation(out=dst[:, kc, :], in_=stg[s][:, 0:width], func=AF.Copy),
                 reads=[b_stg[s]], writes=[b_dst])
        else:
            P.op(ce, lambda e, s=s, kc=kc: e.tensor_copy(out=dst[:, kc, :], in_=stg[s][:, 0:width]),
                 reads=[b_stg[s]], writes=[b_dst])
        n += 1
    return n


def phase_mixer(P, l, dbg):
    nc, io, g = P.nc, P.io, P.g
    psum, pbuf = P.psum, P.pbuf
    rot = Rot(P)
    P.mark()
    m = phase_mod(P, l, 1, 1.0)
    P.mark()
    qT = P.sb([128, 4, NTOK], BF16, "qT")
    kT = P.sb([128, 2, NTOK], BF16, "kT")
    vx = P.sb([128, 34, 2, 66], BF16, "vx")
    b_qT = [Buf() for _ in range(9)]
    b_kT, b_vx = Buf(), Buf()
    P.op("pool", lambda e: e.memset(vx[:], 1.0), writes=[b_vx])
    for hh in range(4):
        for c0 in range(0, NTOK, 1088):
            P.op("dve", lambda e, hh=hh, c0=c0: e.memset(qT[64:128, hh, c0:c0 + 1088], 0.0), writes=list(b_qT))
    for hh in range(2):
        for c0 in range(0, NTOK, 1088):
            P.op("dve", lambda e, hh=hh, c0=c0: e.memset(kT[64:128, hh, c0:c0 + 1088], 0.0), writes=[b_kT])
    P.mark()
    win = P.sb([128, 8, DIN], BF16, "win")
    b_win = Buf()
    P.mark()
    stg = [P.sb([128, DIN], F32, "stg%d" % i) for i in range(2)]
    b_stg = [Buf(), Buf()]
    load_cast_rows(P, win, b_win, lambda kc: io["w_in"].ap()[l, kc * 128:(kc + 1) * 128, :], 8, DIN, stg, b_stg)
    P.barrier()
    P.release()
    gq = P.sb([128, 384], F32, "gq")
    b_gq = Buf()
    for h in range(6):
        src = io["q_norm_g"] if h < 4 else io["k_norm_g"]
        P.dma(lambda e, h=h, src=src: e.dma_start(out=gq[:, h * 64:(h + 1) * 64], in_=src.ap()[l:l + 1, :].partition_broadcast(128)),
              writes=[b_gq])
    cos_t = P.sb([128, 32, 32], F32, "cos")
    sin_t = P.sb([128, 32, 32], F32, "sin")
    b_cs = Buf()
    P.dma(lambda e: e.dma_start(out=cos_t[:], in_=io["rope_cos"].ap().rearrange("(t p) i -> p t i", p=128)), writes=[b_cs])
    P.dma(lambda e: e.dma_start(out=sin_t[:], in_=io["rope_sin"].ap().rearrange("(t p) i -> p t i", p=128)), writes=[b_cs])
    dftc = P.sb([128, 128], BF16, "dftc")
    dfts = P.sb([128, 128], BF16, "dfts")
    b_dft = Buf()
    P.dma(lambda e: e.dma_start(out=dftc[:], in_=io["dftc"].ap()), writes=[b_dft])
    P.dma(lambda e: e.dma_start(out=dfts[:], in_=io["dfts"].ap()), writes=[b_dft])

    xt = [P.sb([128, 4, D], F32, "xt%d" % i) for i in range(2)]
    b_xt = [[Buf() for _ in range(4)] for _ in range(2)]
    hT = P.sb([128, 8, 512], BF16, "hT")
    b_hT = Buf()
    zat = [P.sb([128, 512], F32, "zat%d" % i) for i in range(2)]
    b_zat = [Buf(), Buf()]
    sq = P.sb([128, 384], F32, "sq")
    qk = P.sb([128, 384], F32, "qk")
    qkr = P.sb([128, 384], F32, "qkr")
    tA = P.sb([128, 192], F32, "tA")
    tB = P.sb([128, 192], F32, "tB")
    ss = P.sb([128, 8], F32, "ss")
    b_sq, b_qk, b_qkr, b_tA, b_tB, b_ss = Buf(), Buf(), Buf(), Buf(), Buf(), Buf()
    zst = [P.sb([64, 4, 512], F32, "zst%d" % i) for i in range(2)]
    b_zst = [Buf(), Buf()]
    zp = P.sb([128, 4, 256], BF16, "zp")
    b_zp = Buf()
    zfT = P.sb([128, 2, 512], BF16, "zfT")
    b_zfT = Buf()
    abst = P.sb([128, 4, 512], BF16, "abst")
    b_abst = Buf()

    xs_ap = io["xs"].ap()
    blocks = [(i * 512, 4, 0) for i in range(8)] + [(SEQ, 2, 1)]
    for bi, (t0, ntile, r) in enumerate(blocks):
        nb = ntile * 128
        xb = bi % 2
        for t in range(ntile):
            P.dma(lambda e, xb=xb, t=t, t0=t0: e.dma_start(out=xt[xb][:, t, :], in_=xs_ap[t0 + t * 128:t0 + (t + 1) * 128, :]),
                  writes=[b_xt[xb][t]])
        emit_hT(P, rot, m, r, xt[xb], b_xt[xb], ntile, hT, b_hT)
        for t in range(ntile):
            tile_idx = (t0 // 128) + t
            tok0 = t0 + t * 128
            pb = rot.bank()
            for kc in range(8):
                P.op("pe", lambda e, pb=pb, kc=kc, t=t: e.matmul(psum[pb][:, :], lhsT=hT[:, kc, t * 128:(t + 1) * 128], rhs=win[:, kc, 0:512],
                                                                 start=(kc == 0), stop=(kc == 7)),
                     reads=[b_hT, b_win], writes=[pbuf[pb]])
            z = (bi * 4 + t) % 2
            P.op("act", lambda e, pb=pb, z=z: e.activation(out=zat[z][:], in_=psum[pb][:, :], func=AF.Copy),
                 reads=[pbuf[pb]], writes=[b_zat[z]])
            P.op("dve", lambda e, z=z: e.tensor_tensor(out=sq[:], in0=zat[z][:, 0:384], in1=zat[z][:, 0:384], op=ALU.mult),
                 reads=[b_zat[z]], writes=[b_sq])
            P.op("dve", lambda e: e.tensor_reduce(out=ss[:, 0:6], in_=sq[:].rearrange("p (h d) -> p h d", d=64), axis=AX.X, op=ALU.add),
                 reads=[b_sq], writes=[b_ss])
            P.op("dve", lambda e: e.tensor_scalar(out=ss[:, 0:6], in0=ss[:, 0:6], scalar1=1.0 / 64, scalar2=1e-6, op0=ALU.mult, op1=ALU.add),
                 reads=[b_ss], writes=[b_ss])
            P.op("act", lambda e: e.activation(out=ss[:, 0:6], in_=ss[:, 0:6], func=AF.Sqrt), reads=[b_ss], writes=[b_ss])
            P.op("dve", lambda e: e.reciprocal(out=ss[:, 0:6], in_=ss[:, 0:6]), reads=[b_ss], writes=[b_ss])
            P.op("dve", lambda e, z=z: e.tensor_tensor(out=qk[:].rearrange("p (h d) -> p h d", d=64),
                                                       in0=zat[z][:, 0:384].rearrange("p (h d) -> p h d", d=64),
                                                       in1=ss[:, 0:6].unsqueeze(2).broadcast_to([128, 6, 64]), op=ALU.mult),
                 reads=[b_zat[z], b_ss], writes=[b_qk])
            if r == 0:
                P.op("pool", lambda e: e.tensor_tensor(out=qk[:], in0=qk[:], in1=gq[:], op=ALU.mult), reads=[b_qk, b_gq], writes=[b_qk])
                v4 = lambda tl: tl[:].rearrange("p (h i two) -> p h i two", h=6, i=32, two=2)
                x0, x1 = v4(qk)[:, :, :, 0], v4(qk)[:, :, :, 1]
                o0, o1 = v4(qkr)[:, :, :, 0], v4(qkr)[:, :, :, 1]
                cb = cos_t[:, tile_idx:tile_idx + 1, :].broadcast_to([128, 6, 32])
                sb_ = sin_t[:, tile_idx:tile_idx + 1, :].broadcast_to([128, 6, 32])
                a3 = lambda tl: tl[:].rearrange("p (h i) -> p h i", h=6)
                P.op("dve", lambda e, x0=x0, cb=cb: e.tensor_tensor(out=a3(tA), in0=x0, in1=cb, op=ALU.mult), reads=[b_qk, b_cs], writes=[b_tA])
                P.op("pool", lambda e, x1=x1, sb_=sb_: e.tensor_tensor(out=a3(tB), in0=x1, in1=sb_, op=ALU.mult), reads=[b_qk, b_cs], writes=[b_tB])
                P.op("dve", lambda e, o0=o0: e.tensor_tensor(out=o0, in0=a3(tA), in1=a3(tB), op=ALU.subtract), reads=[b_tA, b_tB], writes=[b_qkr])
                P.op("pool", lambda e, x0=x0, sb_=sb_: e.tensor_tensor(out=a3(tA), in0=x0, in1=sb_, op=ALU.mult), reads=[b_qk, b_cs], writes=[b_tA])
                P.op("dve", lambda e, x1=x1, cb=cb: e.tensor_tensor(out=a3(tB), in0=x1, in1=cb, op=ALU.mult), reads=[b_qk, b_cs], writes=[b_tB])
                P.op("pool", lambda e, o1=o1: e.tensor_tensor(out=o1, in0=a3(tA), in1=a3(tB), op=ALU.add), reads=[b_tA, b_tB], writes=[b_qkr])
            else:
                P.op("pool", lambda e: e.tensor_tensor(out=qkr[:], in0=qk[:], in1=gq[:], op=ALU.mult), reads=[b_qk, b_gq], writes=[b_qkr])
            pq = rot.bank()
            for h in range(4):
                P.op("pe", lambda e, pq=pq, h=h: e.transpose(out=psum[pq][0:64, h * 128:(h + 1) * 128], in_=qkr[:, h * 64:(h + 1) * 64],
                                                             identity=g["ident"][:]),
                     reads=[b_qkr, g["b_ident"]], writes=[pbuf[pq]])
            rot.evac(qT[0:64, :, tok0:tok0 + 128], psum[pq][0:64, :].rearrange("p (h t) -> p h t", h=4), [pbuf[pq]], [b_qT[bi]])
            pk = rot.bank()
            for h in range(2):
                P.op("pe", lambda e, pk=pk, h=h: e.transpose(out=psum[pk][0:64, h * 128:(h + 1) * 128], in_=qkr[:, (4 + h) * 64:(5 + h) * 64],
                                                             identity=g["ident"][:]),
                     reads=[b_qkr, g["b_ident"]], writes=[pbuf[pk]])
            rot.evac(kT[0:64, :, tok0:tok0 + 128], psum[pk][0:64, 0:256].rearrange("p (h t) -> p h t", h=2), [pbuf[pk]], [b_kT])
            P.op("pool", lambda e, z=z, tile_idx=tile_idx: e.tensor_copy(out=vx[:, tile_idx, :, 0:64],
                                                                         in_=zat[z][:, 384:512].rearrange("p (h d) -> p h d", h=2)),
                 reads=[b_zat[z]], writes=[b_vx])
        for kind in range(4):
            zs = kind % 2
            for h in range(4):
                pb = rot.bank()
                c0 = 512 + kind * 256 + h * 64
                for kc in range(8):
                    P.op("pe", lambda e, pb=pb, kc=kc, c0=c0, nb=nb: e.matmul(psum[pb][0:64, 0:nb], lhsT=win[:, kc, c0:c0 + 64], rhs=hT[:, kc, 0:nb],
                                                                            start=(kc == 0), stop=(kc == 7)),
                         reads=[b_hT, b_win], writes=[pbuf[pb]])
                rot.evac(zst[zs][:, h, 0:nb], psum[pb][0:64, 0:nb], [pbuf[pb]], [b_zst[zs]])
            P.dma(lambda e, zs=zs, kind=kind, t0=t0, nb=nb: e.dma_start(out=io["zrw"].ap()[kind, :, :, t0:t0 + nb], in_=zst[zs][:, :, 0:nb]),
                  reads=[b_zst[zs]])
        for t in range(ntile):
            pb = rot.bank()
            for kc in range(8):
                P.op("pe", lambda e, pb=pb, kc=kc, t=t: e.matmul(psum[pb][:, 0:256], lhsT=hT[:, kc, t * 128:(t + 1) * 128], rhs=win[:, kc, 1536:1792],
                                                                 start=(kc == 0), stop=(kc == 7)),
                     reads=[b_hT, b_win], writes=[pbuf[pb]])
            rot.evac(zp[:, t, :], psum[pb][:, 0:256], [pbuf[pb]], [b_zp])
        P.dma(lambda e, t0=t0, ntile=ntile, nb=nb: e.dma_start(out=io["zpool"].ap()[t0:t0 + nb, :].rearrange("(t p) c -> p t c", p=128),
                                                               in_=zp[:, 0:ntile, :]), reads=[b_zp])
        for c in range(2):
            pb = rot.bank()
            c0 = 1792 + c * 128
            for kc in range(8):
                P.op("pe", lambda e, pb=pb, kc=kc, c0=c0, nb=nb: e.matmul(psum[pb][:, 0:nb], lhsT=win[:, kc, c0:c0 + 128], rhs=hT[:, kc, 0:nb],
                                                                        start=(kc == 0), stop=(kc == 7)),
                     reads=[b_hT, b_win], writes=[pbuf[pb]])
            rot.evac(zfT[:, c, 0:nb], psum[pb][:, 0:nb], [pbuf[pb]], [b_zfT])
        for t in range(ntile):
            pb = rot.bank()
            for ab_i, mat in enumerate((dftc, dfts)):
                for c in range(2):
                    P.op("pe", lambda e, pb=pb, ab_i=ab_i, c=c, t=t, mat=mat: e.matmul(
                        psum[pb][:, ab_i * 256 + c * 128: ab_i * 256 + (c + 1) * 128], lhsT=zfT[:, c, t * 128:(t + 1) * 128], rhs=mat[:],
                        start=True, stop=True), reads=[b_zfT, b_dft], writes=[pbuf[pb]])
            rot.evac(abst[:, t, :], psum[pb][:, :], [pbuf[pb]], [b_abst])
        P.dma(lambda e, t0=t0, ntile=ntile, nb=nb: e.dma_start(out=io["ab"].ap()[t0:t0 + nb, :].rearrange("(t p) c -> p t c", p=128),
                                                               in_=abst[:, 0:ntile, :]), reads=[b_abst])
    P.barrier()
    P.release()
    if dbg == "mixA":
        P.release()
        P.release()
        return
    P.mark()
    pT = [P.sb([128, 512], BF16, "pT%d" % i) for i in range(3)]
    b_pT = [Buf(), Buf(), Buf()]
    rden = P.sb([128, 512], F32, "rden")
    b_rden = Buf()
    osb = P.sb([64, 512], F32, "osb")
    b_osb = Buf()
    ost = [P.sb([64, 512], BF16, "ost%d" % i) for i in range(2)]
    b_ost = [Buf(), Buf()]
    n_p = 0
    n_o = 0
    cat = io["cat"].ap()
    busy = set()
    pending_tail = [None]
    for bi, (t0, ntile, r) in enumerate(blocks):
        nb = ntile * 128
        key_tiles = list(range(34)) if r == 0 else [32, 33]
        for h in range(4):
            kvh = h // 2
            pacc = rot.bank()
            while pacc in busy:
                pacc = rot.bank()
            nk = len(key_tiles)
            pend = {}

            def qk_exp(ki, pacc=pacc, kvh=kvh, h=h, t0=t0, nb=nb, bi=bi):
                nonlocal n_p
                kt = key_tiles[ki]
                ps_ = rot.bank()
                while ps_ == pacc or ps_ in busy:
                    ps_ = rot.bank()
                P.op("pe", lambda e: e.matmul(psum[ps_][:, 0:nb], lhsT=kT[:, kvh, kt * 128:(kt + 1) * 128], rhs=qT[:, h, t0:t0 + nb],
                                              start=True, stop=True), reads=[b_kT, b_qT[bi]], writes=[pbuf[ps_]])
                pp = n_p % 3
                n_p += 1
                P.op("act", lambda e: e.activation(out=pT[pp][:, 0:nb], in_=psum[ps_][:, 0:nb], func=AF.Exp, scale=0.125),
                     reads=[pbuf[ps_]], writes=[b_pT[pp]])
                pend[ki] = pp

            def pv(ki, pacc=pacc, kvh=kvh, nb=nb, nk=nk):
                kt = key_tiles[ki]
                pp = pend.pop(ki)
                P.op("pe", lambda e: e.matmul(psum[pacc][0:65, 0:nb], lhsT=vx[:, kt, kvh, 0:65], rhs=pT[pp][:, 0:nb],
                                              start=(ki == 0), stop=(ki == nk - 1)), reads=[b_vx, b_pT[pp]], writes=[pbuf[pacc]])

            def tail(pacc=pacc, nb=nb, h=h, t0=t0):
                nonlocal n_o
                P.op("dve", lambda e: e.reciprocal(out=rden[64:65, 0:nb], in_=psum[pacc][64:65, 0:nb]), reads=[pbuf[pacc]], writes=[b_rden])
                P.op("act", lambda e: e.activation(out=osb[:, 0:nb], in_=psum[pacc][0:64, 0:nb], func=AF.Copy), reads=[pbuf[pacc]], writes=[b_osb])
                pbc = rot.bank()
                while pbc == pacc or pbc in busy:
                    pbc = rot.bank()
                P.op("pe", lambda e: e.matmul(psum[pbc][0:64, 0:nb], lhsT=g["ones"][64:65, 0:64], rhs=rden[64:65, 0:nb], start=True, stop=True),
                     reads=[g["b_ones"], b_rden], writes=[pbuf[pbc]])
                oo = n_o % 2
                n_o += 1
                P.op("dve", lambda e: e.tensor_tensor(out=ost[oo][:, 0:nb], in0=psum[pbc][0:64, 0:nb], in1=osb[:, 0:nb], op=ALU.mult),
                     reads=[pbuf[pbc], b_osb], writes=[b_ost[oo]])
                P.dma(lambda e: e.dma_start(out=cat[h * 64:(h + 1) * 64, t0:t0 + nb], in_=ost[oo][:, 0:nb]), reads=[b_ost[oo]])

            LOOK = 2
            for ki in range(min(LOOK, nk)):
                qk_exp(ki)
            if pending_tail[0] is not None:
                pending_tail[0]()
                busy.clear()
            for ki in range(nk):
                pv(ki)
                if ki + LOOK < nk:
                    qk_exp(ki + LOOK)
            pending_tail[0] = tail
            busy.add(pacc)
    pending_tail[0]()
    P.barrier()
    P.release()
    P.release()
    if dbg == "att":
        P.release()
        return
    phase_rwkv_prep(P, l)
    if dbg == "rw1":
        P.release()
        return
    phase_rwkv_scan(P, l, nsteps=(int(dbg[3:]) if dbg and dbg.startswith("rws") else 34))
    if dbg and dbg.startswith("rws"):
        P.release()
        return
    phase_rwkv_fin(P, l)
    if dbg == "rw3":
        P.release()
        return
    phase_pool(P, l)
    phase_fourier(P, l)
    if dbg == "pf":
        P.release()
        return
    phase_wout(P, l, m)
    P.release()


def phase_pool(P, l):
    nc, io, g = P.nc, P.io, P.g
    psum, pbuf = P.psum, P.pbuf
    rot = Rot(P)
    P.mark()
    zp = P.sb([128, 34, 256], BF16, "zp_all")
    b_zp = Buf()
    P.dma(lambda e: e.dma_start(out=zp[:], in_=io["zpool"].ap().rearrange("(t p) c -> p t c", p=128)), writes=[b_zp])
    pm = P.sb([128, 20, 128], BF16, "poolm")
    b_pm = Buf()
    P.dma(lambda e: e.dma_start(out=pm[:], in_=io["poolm"].ap().rearrange("k s t -> s k t")), writes=[b_pm])
    pwf = P.sb([64, 4, 64], F32, "pwf")
    pw = P.sb([64, 4, 64], BF16, "pw")
    b_pwf, b_pw = Buf(), Buf()
    P.dma(lambda e: e.dma_start(out=pwf[:], in_=io["pool_w"].ap()[l].rearrange("g c d -> c g d")), writes=[b_pwf])
    P.op("dve", lambda e: e.tensor_copy(out=pw[:], in_=pwf[:]), reads=[b_pwf], writes=[b_pw])
    psc = P.sb([64, 4], F32, "psc")
    b_psc = Buf()
    P.dma(lambda e: e.dma_start(out=psc[:], in_=io["pool_scale"].ap()[l, :].rearrange("(g d) -> d g", d=64),
                                allow_slow_non_contiguous=True), writes=[b_psc])
    pooled = [P.sb([64, 512], BF16, "pooled%d" % i) for i in range(2)]
    b_pooled = [Buf(), Buf()]
    ost = [P.sb([64, 512], BF16, "post%d" % i) for i in range(2)]
    b_ost = [Buf(), Buf()]
    cat = io["cat"].ap()
    n = 0
    seqs = [(0, 32), (32, 2)]
    for (tile0, nt) in seqs:
        for j0 in range(0, nt, 4):
            ntile = min(4, nt - j0)
            nb = ntile * 128
            for gi in range(4):
                pb = rot.bank()
                for jj in range(ntile):
                    j = j0 + jj
                    terms = []
                    if j > 0:
                        terms.append((tile0 + j - 1, 0))
                    terms.append((tile0 + j, 3 if j == 0 else (4 if j == nt - 1 else 2)))
                    if j < nt - 1:
                        terms.append((tile0 + j + 1, 1))
                    for ti, (st, kind) in enumerate(terms):
                        P.op("pe", lambda e, pb=pb, jj=jj, st=st, kind=kind, gi=gi, ti=ti, nterm=len(terms): e.matmul(
                            psum[pb][0:64, jj * 128:(jj + 1) * 128], lhsT=zp[:, st, gi * 64:(gi + 1) * 64], rhs=pm[:, gi * 5 + kind, :],
                            start=(ti == 0), stop=(ti == nterm - 1)), reads=[b_zp, b_pm], writes=[pbuf[pb]])
                k = n % 2
                n += 1
                rot.evac(pooled[k][:, 0:nb], psum[pb][0:64, 0:nb], [pbuf[pb]], [b_pooled[k]])
                pb2 = rot.bank()
                P.op("pe", lambda e, pb2=pb2, gi=gi, k=k, nb=nb: e.matmul(psum[pb2][0:64, 0:nb], lhsT=pw[:, gi, :], rhs=pooled[k][:, 0:nb],
                                                                        start=True, stop=True), reads=[b_pw, b_pooled[k]], writes=[pbuf[pb2]])
                P.op("act", lambda e, pb2=pb2, gi=gi, k=k, nb=nb: e.activation(out=ost[k][:, 0:nb], in_=psum[pb2][0:64, 0:nb], func=AF.Copy,
                                                                              scale=psc[:, gi:gi + 1]),
                     reads=[pbuf[pb2], b_psc], writes=[b_ost[k]])
                tok0 = (tile0 + j0) * 128
                P.dma(lambda e, k=k, gi=gi, tok0=tok0, nb=nb: e.dma_start(out=cat[512 + gi * 64:512 + (gi + 1) * 64, tok0:tok0 + nb],
                                                                       in_=ost[k][:, 0:nb]), reads=[b_ost[k]])
    P.barrier()
    P.release()


def phase_fourier(P, l):
    nc, io, g = P.nc, P.io, P.g
    psum, pbuf = P.psum, P.pbuf
    rot = Rot(P)
    P.mark()
    ab = P.sb([128, 34, 512], BF16, "ab_all")
    b_ab = Buf()
    for q in range(2):
        P.dma(lambda e, q=q: e.dma_start(out=ab[:, q * 17:(q + 1) * 17, :],
                                         in_=io["ab"].ap()[q * 17 * 128:(q + 1) * 17 * 128, :].rearrange("(t p) c -> p t c", p=128)),
              writes=[b_ab])
    fwf = P.sb([128, 2, 256], F32, "fwf")
    fw = P.sb([128, 2, 256], BF16, "fw")
    b_fwf, b_fw = Buf(), Buf()
    P.dma(lambda e: e.dma_start(out=fwf[:], in_=io["fourier_w"].ap()[l].rearrange("(c p) d -> p c d", p=128)), writes=[b_fwf])
    P.op("dve", lambda e: e.tensor_copy(out=fw[:], in_=fwf[:]), reads=[b_fwf], writes=[b_fw])
    dm = [[P.sb([128, 32, 512], BF16, "dm%d_%d" % (i, j)) for j in range(2)] for i in range(2)]
    b_dm = [[Buf(), Buf()], [Buf(), Buf()]]
    fT = [P.sb([128, 2, 512], BF16, "fT%d" % i) for i in range(2)]
    b_fT = [[Buf(), Buf()], [Buf(), Buf()]]
    ost = [P.sb([128, 512], BF16, "fost%d" % i) for i in range(2)]
    b_ost = [Buf(), Buf()]
    cat = io["cat"].ap()
    jobs = [(0, 32, tb * 512, 512, "dft_lat") for tb in range(8)] + [(32, 2, 0, 256, "dft_ctx")]
    n_o = 0
    for ji, (tile0, nt, c0, wd, mname) in enumerate(jobs):
        bsel = ji % 2
        for cs in range(2):
            P.dma(lambda e, bsel=bsel, cs=cs, nt=nt, c0=c0, wd=wd, mname=mname: e.dma_start(
                out=dm[bsel][cs][:, 0:nt, 0:wd], in_=io[mname].ap()[cs, :, c0:c0 + wd].rearrange("(t p) n -> p t n", p=128)),
                writes=[b_dm[bsel][cs]])
        for c in range(2):
            pb = rot.bank()
            for cs in range(2):
                for t in range(nt):
                    P.op("pe", lambda e, pb=pb, cs=cs, t=t, c=c, bsel=bsel, wd=wd, tile0=tile0, nt=nt: e.matmul(
                        psum[pb][:, 0:wd], lhsT=ab[:, tile0 + t, cs * 256 + c * 128: cs * 256 + (c + 1) * 128], rhs=dm[bsel][cs][:, t, 0:wd],
                        start=(cs == 0 and t == 0), stop=(cs == 1 and t == nt - 1)),
                        reads=[b_ab, b_dm[bsel][cs]], writes=[pbuf[pb]])
            rot.evac(fT[bsel][:, c, 0:wd], psum[pb][:, 0:wd], [pbuf[pb]], [b_fT[bsel][c]])
        for dc in range(2):
            pb = rot.bank()
            for c in range(2):
                P.op("pe", lambda e, pb=pb, c=c, dc=dc, bsel=bsel, wd=wd: e.matmul(
                    psum[pb][:, 0:wd], lhsT=fw[:, c, dc * 128:(dc + 1) * 128], rhs=fT[bsel][:, c, 0:wd], start=(c == 0), stop=(c == 1)),
                    reads=[b_fw, b_fT[bsel][c]], writes=[pbuf[pb]])
            k = n_o % 2
            n_o += 1
            rot.evac(ost[k][:, 0:wd], psum[pb][:, 0:wd], [pbuf[pb]], [b_ost[k]])
            tok0 = tile0 * 128 + c0
            P.dma(lambda e, k=k, dc=dc, tok0=tok0, wd=wd: e.dma_start(out=cat[768 + dc * 128:768 + (dc + 1) * 128, tok0:tok0 + wd],
                                                                   in_=ost[k][:, 0:wd]), reads=[b_ost[k]])
    P.barrier()
    P.release()


def phase_wout(P, l, m):
    nc, io, g = P.nc, P.io, P.g
    psum, pbuf = P.psum, P.pbuf
    rot = Rot(P)
    P.mark()
    wo = P.sb([128, 8, D], BF16, "wo")
    b_wo = Buf()
    P.mark()
    stg = [P.sb([128, D], F32, "stg%d" % i) for i in range(2)]
    b_stg = [Buf(), Buf()]
    load_cast_rows(P, wo, b_wo, lambda kc: io["w_out"].ap()[l, kc * 128:(kc + 1) * 128, :], 8, D, stg, b_stg)
    P.barrier()
    P.release()
    ct = [P.sb([128, 8, 512], BF16, "ct%d" % i) for i in range(2)]
    b_ct = [Buf(), Buf()]
    xt = [P.sb([128, D], F32, "xt%d" % i) for i in range(2)]
    b_xt = [Buf(), Buf()]
    ot = [P.sb([128, D], F32, "ot%d" % i) for i in range(2)]
    b_ot = [Buf(), Buf()]
    work = dict(u=P.sb([128, D], F32, "u"), b_u=Buf(), st=P.sb([128, 16], F32, "st"), b_st=Buf())
    cat = io["cat"].ap()
    xs_ap = io["xs"].ap()
    blocks = [(i * 512, 4, 0) for i in range(8)] + [(SEQ, 2, 1)]
    n = 0
    for bi, (t0, ntile, r) in enumerate(blocks):
        nb = ntile * 128
        cb = bi % 2
        P.dma(lambda e, cb=cb, t0=t0, nb=nb: e.dma_start(out=ct[cb][:, :, 0:nb], in_=cat[:, t0:t0 + nb].rearrange("(c p) t -> p c t", p=128)),
              writes=[b_ct[cb]])
        for t in range(ntile):
            k = n % 2
            n += 1
            tok0 = t0 + t * 128
            P.dma(lambda e, k=k, tok0=tok0: e.dma_start(out=xt[k][:], in_=xs_ap[tok0:tok0 + 128, :]), writes=[b_xt[k]])
            pbs = []
            for hn in range(2):
                pb = rot.bank()
                pbs.append(pb)
                for c in range(8):
                    P.op("pe", lambda e, pb=pb, c=c, hn=hn, cb=cb, t=t: e.matmul(
                        psum[pb][:, :], lhsT=ct[cb][:, c, t * 128:(t + 1) * 128], rhs=wo[:, c, hn * 512:(hn + 1) * 512],
                        start=(c == 0), stop=(c == 7)), reads=[b_ct[cb], b_wo], writes=[pbuf[pb]])
            emit_postnorm(P, m, r, xt[k][:], b_xt[k], [psum[pbs[0]][:, :], psum[pbs[1]][:, :]], [pbuf[pbs[0]], pbuf[pbs[1]]],
                          ot[k][:], b_ot[k], work)
            P.dma(lambda e, k=k, tok0=tok0: e.dma_start(out=xs_ap[tok0:tok0 + 128, :], in_=ot[k][:]), reads=[b_ot[k]])
    P.barrier()
    P.release()


LOGDECAY_SCALE = -0.6065306597126334
GN_EPS = 64e-5
CHUNK_ORDER = {0: [32, 33] + list(range(32)), 1: [33, 32] + list(range(31, -1, -1))}
RW_SEQS = [(i * 512, 512, i == 0, i == 7) for i in range(8)] + [(SEQ, 256, True, True)]


def col_param(P, src_ap_1d, name, n=4):
    t = P.sb([64, n], F32, name)
    b = Buf()
    P.dma(lambda e: e.dma_start(out=t[:], in_=src_ap_1d.rearrange("(h d) -> d h", d=64), allow_slow_non_contiguous=True), writes=[b])
    return t, b


def load_halo(P, buf_ap_fn, b_buf, src_fn, t0, nb, first, last):
    lo = 0 if not first else 1
    hi = nb + 2 if not last else nb + 1
    if first:
        P.op("pool", lambda e: e.memset(buf_ap_fn(0, 1), 0.0), writes=[b_buf])
    if last:
        P.op("pool", lambda e: e.memset(buf_ap_fn(nb + 1, nb + 2), 0.0), writes=[b_buf])
    P.dma(lambda e: e.dma_start(out=buf_ap_fn(lo, hi), in_=src_fn(t0 - 1 + lo, t0 - 1 + hi)), writes=[b_buf])


def phase_rwkv_prep(P, l):
    nc, io, g = P.nc, P.io, P.g
    psum, pbuf = P.psum, P.pbuf
    rot = Rot(P)
    P.mark()
    mu = [col_param(P, io["rwkv_mu"].ap()[l, i, :], "mu%d" % i) for i in range(6)]
    w0 = [col_param(P, io["decay_w0"].ap()[l, d, :], "w0%d" % d) for d in range(2)]
    a0 = [col_param(P, io["icl_a0"].ap()[l, d, :], "a0%d" % d) for d in range(2)]
    k_k = col_param(P, io["k_k"].ap()[l, :], "k_k")
    k_a = col_param(P, io["k_a"].ap()[l, :], "k_a")
    r_k = col_param(P, io["r_k"].ap()[l].rearrange("h d -> (h d)"), "r_k")
    hm, om = [], []
    for i in range(3):
        t1 = P.sb([64, 4], F32, "hm%d" % i)
        t2 = P.sb([64, 4], F32, "om%d" % i)
        b1, b2 = Buf(), Buf()
        P.op("dve", lambda e, i=i, t1=t1: e.tensor_scalar(out=t1[:], in0=mu[i][0][:], scalar1=0.5, scalar2=None, op0=ALU.mult),
             reads=[mu[i][1]], writes=[b1])
        P.op("dve", lambda e, i=i, t2=t2: e.tensor_scalar(out=t2[:], in0=mu[i][0][:], scalar1=-1.0, scalar2=1.0, op0=ALU.mult, op1=ALU.add),
             reads=[mu[i][1]], writes=[b2])
        hm.append((t1, b1))
        om.append((t2, b2))
    omka = P.sb([64, 4], F32, "omka")
    b_omka = Buf()
    P.op("dve", lambda e: e.tensor_scalar(out=omka[:], in0=k_a[0][:], scalar1=-1.0, scalar2=1.0, op0=ALU.mult, op1=ALU.add),
         reads=[k_a[1]], writes=[b_omka])

    def lora_in(src, name, rank):
        tf = P.sb([64, 4, rank], F32, name + "f")
        tb = P.sb([64, 4, rank], BF16, name)
        bf_, bb = Buf(), Buf()
        P.dma(lambda e: e.dma_start(out=tf[:], in_=src.rearrange("(h d) r -> d h r", d=64)), writes=[bf_])
        P.op("dve", lambda e: e.tensor_copy(out=tb[:], in_=tf[:]), reads=[bf_], writes=[bb])
        return tb, bb

    def lora_out(src, name, rank):
        tf = P.sb([rank, 256], F32, name + "f")
        tb = P.sb([rank, 256], BF16, name)
        bf_, bb = Buf(), Buf()
        P.dma(lambda e: e.dma_start(out=tf[:], in_=src), writes=[bf_])
        P.op("dve", lambda e: e.tensor_copy(out=tb[:], in_=tf[:]), reads=[bf_], writes=[bb])
        return tb, bb

    W1 = [lora_in(io["decay_w1"].ap()[l, d], "W1_%d" % d, 32) for d in range(2)]
    A1 = [lora_in(io["icl_a1"].ap()[l, d], "A1_%d" % d, 32) for d in range(2)]
    G1 = lora_in(io["gate_g1"].ap()[l], "G1", 64)
    W2 = [lora_out(io["decay_w2"].ap()[l, d], "W2_%d" % d, 32) for d in range(2)]
    A2 = [lora_out(io["icl_a2"].ap()[l, d], "A2_%d" % d, 32) for d in range(2)]
    G2 = lora_out(io["gate_g2"].ap()[l], "G2", 64)
    tw = P.sb([32, 2, NTOK], BF16, "tw")
    ta = P.sb([32, 2, NTOK], BF16, "ta")
    tg = P.sb([64, NTOK], BF16, "tg")
    b_tw, b_ta, b_tg = [Buf(), Buf()], [Buf(), Buf()], Buf()
    etot = P.sb([64, 8, 34], F32, "etot")
    b_etot = Buf()
    zrw = io["zrw"].ap()
    P.mark()
    zu = [P.sb([64, 4, 514], F32, "zu%d" % i) for i in range(2)]
    b_zu = [Buf(), Buf()]
    ssum = P.sb([64, 4, 512], F32, "ssum")
    du = P.sb([64, 4, 512], F32, "du")
    b_ssum, b_du = Buf(), Buf()
    xq = [P.sb([64, 4, 512], BF16, "xq%d" % i) for i in range(3)]
    b_xq = [Buf(), Buf(), Buf()]
    for bi, (t0, nb, first, last) in enumerate(RW_SEQS):
        z = bi % 2
        load_halo(P, lambda a, b, z=z: zu[z][:, :, a:b], b_zu[z], lambda a, b: zrw[3, :, :, a:b], t0, nb, first, last)
        P.op("dve", lambda e, z=z, nb=nb: e.tensor_tensor(out=ssum[:, :, 0:nb], in0=zu[z][:, :, 0:nb], in1=zu[z][:, :, 2:nb + 2], op=ALU.add),
             reads=[b_zu[z]], writes=[b_ssum])
        P.op("dve", lambda e, z=z, nb=nb: e.scalar_tensor_tensor(out=du[:, :, 0:nb], in0=ssum[:, :, 0:nb], scalar=0.5, in1=zu[z][:, :, 1:nb + 1],
                                                                 op0=ALU.mult, op1=ALU.subtract), reads=[b_ssum, b_zu[z]], writes=[b_du])
        for j in range(3):
            for h in range(4):
                eng = "dve" if (j * 4 + h) % 2 == 0 else "pool"
                if eng == "dve":
                    P.op("dve", lambda e, j=j, h=h, z=z, nb=nb: e.scalar_tensor_tensor(
                        out=xq[j][:, h, 0:nb], in0=du[:, h, 0:nb], scalar=mu[3 + j][0][:, h:h + 1], in1=zu[z][:, h, 1:nb + 1],
                        op0=ALU.mult, op1=ALU.add), reads=[b_du, b_zu[z], mu[3 + j][1]], writes=[b_xq[j]])
                else:
                    P.op("pool", lambda e, j=j, h=h, nb=nb: e.tensor_scalar(
                        out=xq[j][:, h, 0:nb], in0=du[:, h, 0:nb], scalar1=mu[3 + j][0][:, h:h + 1], scalar2=None, op0=ALU.mult),
                        reads=[b_du, mu[3 + j][1]], writes=[b_xq[j]])
                    P.op("pool", lambda e, j=j, h=h, z=z, nb=nb: e.tensor_tensor(
                        out=xq[j][:, h, 0:nb], in0=xq[j][:, h, 0:nb], in1=zu[z][:, h, 1:nb + 1], op=ALU.add),
                        reads=[b_xq[j], b_zu[z]], writes=[b_xq[j]])
        jobs = [(0, W1[0], 32, tw[:, 0, t0:t0 + nb], b_tw[0], AF.Tanh), (0, W1[1], 32, tw[:, 1, t0:t0 + nb], b_tw[1], AF.Tanh),
                (1, A1[0], 32, ta[:, 0, t0:t0 + nb], b_ta[0], AF.Copy), (1, A1[1], 32, ta[:, 1, t0:t0 + nb], b_ta[1], AF.Copy),
                (2, G1, 64, tg[:, t0:t0 + nb], b_tg, AF.Sigmoid)]
        for (j, wt, rank, dst, b_dst, fn) in jobs:
            pb = rot.bank()
            for h in range(4):
                P.op("pe", lambda e, pb=pb, h=h, j=j, wt=wt, rank=rank, nb=nb: e.matmul(
                    psum[pb][0:rank, 0:nb], lhsT=wt[0][:, h, :], rhs=xq[j][:, h, 0:nb], start=(h == 0), stop=(h == 3)),
                    reads=[wt[1], b_xq[j]], writes=[pbuf[pb]])
            P.op("act", lambda e, pb=pb, rank=rank, nb=nb, dst=dst, fn=fn: e.activation(out=dst, in_=psum[pb][0:rank, 0:nb], func=fn),
                 reads=[pbuf[pb]], writes=[b_dst])
    P.barrier()
    P.release()
    P.mark()
    z3 = [P.sb([64, 3, 514], F32, "z3_%d" % i) for i in range(2)]
    b_z3 = [Buf(), Buf()]
    s3 = P.sb([64, 3, 512], F32, "s3")
    b_s3 = Buf()
    rk = P.sb([64, 2, 512], F32, "rk")
    b_r, b_k = Buf(), Buf()
    Fs = [[P.sb([64, 5, 512], F32, "F%d_%d" % (i, d)) for d in range(2)] for i in range(2)]
    b_F = [[Buf(), Buf()], [Buf(), Buf()]]
    kkr = P.sb([64, 512], F32, "kkr")
    kk = P.sb([64, 512], F32, "kk")
    sqt = P.sb([64, 512], F32, "sqt")
    nrm = P.sb([64, 512], F32, "nrm")
    b_kkr, b_kk, b_sqt, b_nrm = Buf(), Buf(), Buf(), Buf()
    TD = []
    for d_ in range(2):
        td = {}
        for nm in ("lw", "cl", "ci", "cml", "einc", "eexc", "einv", "av", "tt", "kd", "tmpb"):
            td[nm] = P.sb([64, 512], F32, "%s%d" % (nm, d_))
            td["b_" + nm] = Buf()
        td["tot"] = P.sb([64, 4], F32, "tot%d" % d_)
        td["b_tot"] = Buf()
        TD.append(td)
    kds = P.sb([64, 512], F32, "kds")
    tmpb = P.sb([64, 512], F32, "tmpb")
    b_kds, b_tmpb = Buf(), Buf()
    aux = [P.sb([64, 2, 512], F32, "aux%d" % i) for i in range(2)]
    b_aux = [Buf(), Buf()]
    rwt = io["rwt"].ap()
    def head_block(h, t0, nb, first, last, n_it):
        if True:
            hs = slice(h, h + 1)
            nch = nb // 128
            z = n_it % 2
            fi = n_it % 2
            n_it += 1
            for i in range(3):
                load_halo(P, lambda a, b, z=z, i=i: z3[z][:, i, a:b], b_z3[z], lambda a, b, i=i: zrw[i, :, h, a:b], t0, nb, first, last)
            P.op("dve", lambda e, z=z, nb=nb: e.tensor_tensor(out=s3[:, :, 0:nb], in0=z3[z][:, :, 0:nb], in1=z3[z][:, :, 2:nb + 2], op=ALU.add),
                 reads=[b_z3[z]], writes=[b_s3])
            for i in range(3):
                P.op("act", lambda e, i=i, nb=nb: e.activation(out=s3[:, i, 0:nb], in_=s3[:, i, 0:nb], func=AF.Copy, scale=hm[i][0][:, hs]),
                     reads=[b_s3, hm[i][1]], writes=[b_s3])
            dsts = [(rk[:, 0, 0:nb], b_r), (rk[:, 1, 0:nb], b_k), (Fs[fi][0][:, 4, 0:nb], b_F[fi][0])]
            for i in range(3):
                P.op("dve", lambda e, i=i, z=z, nb=nb, dst=dsts[i][0]: e.scalar_tensor_tensor(
                    out=dst, in0=z3[z][:, i, 1:nb + 1], scalar=om[i][0][:, hs], in1=s3[:, i, 0:nb], op0=ALU.mult, op1=ALU.add),
                    reads=[b_z3[z], b_s3, om[i][1]], writes=[dsts[i][1]])
            r_ap, k_ap, v_ap = rk[:, 0, 0:nb], rk[:, 1, 0:nb], Fs[fi][0][:, 4, 0:nb]
            b_v = b_F[fi][0]
            P.op("act", lambda e, nb=nb, fi=fi, v_ap=v_ap: e.activation(out=Fs[fi][1][:, 4, 0:nb], in_=v_ap, func=AF.Copy), reads=[b_v], writes=[b_F[fi][1]])
            P.op("dve", lambda e, nb=nb, k_ap=k_ap: e.tensor_scalar(out=kkr[:, 0:nb], in0=k_ap, scalar1=k_k[0][:, hs], scalar2=None, op0=ALU.mult),
                 reads=[b_k, k_k[1]], writes=[b_kkr])
            P.op("act", lambda e, nb=nb: e.activation(out=sqt[:, 0:nb], in_=kkr[:, 0:nb], func=AF.Square), reads=[b_kkr], writes=[b_sqt])
            pb = rot.bank()
            P.op("pe", lambda e, pb=pb, nb=nb: e.matmul(psum[pb][0:64, 0:nb], lhsT=g["ones"][0:64, 0:64], rhs=sqt[:, 0:nb], start=True, stop=True),
                 reads=[g["b_ones"], b_sqt], writes=[pbuf[pb]])
            P.op("act", lambda e, pb=pb, nb=nb: e.activation(out=nrm[:, 0:nb], in_=psum[pb][0:64, 0:nb], func=AF.Sqrt), reads=[pbuf[pb]], writes=[b_nrm])
            P.op("dve", lambda e, nb=nb: e.tensor_scalar(out=nrm[:, 0:nb], in0=nrm[:, 0:nb], scalar1=1e-12, scalar2=None, op0=ALU.max),
                 reads=[b_nrm], writes=[b_nrm])
            P.op("dve", lambda e, nb=nb: e.reciprocal(out=nrm[:, 0:nb], in_=nrm[:, 0:nb]), reads=[b_nrm], writes=[b_nrm])
            P.op("dve", lambda e, nb=nb: e.tensor_tensor(out=kk[:, 0:nb], in0=kkr[:, 0:nb], in1=nrm[:, 0:nb], op=ALU.mult),
                 reads=[b_kkr, b_nrm], writes=[b_kk])
            def dir_part(d):
                s_id = h * 2 + d
                F = Fs[fi][d]
                bF = b_F[fi][d]
                T = TD[d]
                lw, cl, ci, cml, einc, eexc, einv, av, tt, kd, tmpd, tot = (T[k] for k in ("lw", "cl", "ci", "cml", "einc", "eexc", "einv", "av", "tt", "kd", "tmpb", "tot"))
                b_lw, b_cl, b_ci, b_cml, b_einc, b_eexc, b_einv, b_av, b_tt, b_kd, b_tmpd, b_tot = (
                    T["b_" + k] for k in ("lw", "cl", "ci", "cml", "einc", "eexc", "einv", "av", "tt", "kd", "tmpb", "tot"))
                pb = rot.bank()
                P.op("pe", lambda e: e.matmul(psum[pb][0:64, 0:nb], lhsT=W2[d][0][:, h * 64:(h + 1) * 64], rhs=tw[:, d, t0:t0 + nb],
                                              start=True, stop=True), reads=[W2[d][1], b_tw[d]], writes=[pbuf[pb]])
                pb2 = rot.bank()
                P.op("pe", lambda e: e.matmul(psum[pb2][0:64, 0:nb], lhsT=A2[d][0][:, h * 64:(h + 1) * 64], rhs=ta[:, d, t0:t0 + nb],
                                              start=True, stop=True), reads=[A2[d][1], b_ta[d]], writes=[pbuf[pb2]])
                yield
                P.op("act", lambda e: e.activation(out=lw[:, 0:nb], in_=psum[pb][0:64, 0:nb], func=AF.Sigmoid, bias=w0[d][0][:, hs]),
                     reads=[pbuf[pb], w0[d][1]], writes=[b_lw])
                P.op("act", lambda e: e.activation(out=av[:, 0:nb], in_=psum[pb2][0:64, 0:nb], func=AF.Sigmoid, bias=a0[d][0][:, hs]),
                     reads=[pbuf[pb2], a0[d][1]], writes=[b_av])
                yield
                P.op("act", lambda e: e.activation(out=lw[:, 0:nb], in_=lw[:, 0:nb], func=AF.Copy, scale=LOGDECAY_SCALE),
                     reads=[b_lw], writes=[b_lw])
                P.op("pool", lambda e: e.tensor_tensor(out=tmpd[:, 0:nb], in0=kk[:, 0:nb], in1=av[:, 0:nb], op=ALU.mult),
                     reads=[b_kk, b_av], writes=[b_tmpd])
                yield
                for j in range(nch):
                    P.op("dve", lambda e, j=j: e.tensor_tensor_scan(out=cl[:, j * 128:(j + 1) * 128], data0=g["ones"][0:64, 0:128],
                                                                   data1=lw[:, j * 128:(j + 1) * 128], initial=0.0, op0=ALU.mult, op1=ALU.add),
                         reads=[b_lw, g["b_ones"]], writes=[b_cl])
                P.op("pool", lambda e: e.tensor_scalar(out=tt[:, 0:nb], in0=av[:, 0:nb], scalar1=k_a[0][:, hs], scalar2=omka[:, hs],
                                                       op0=ALU.mult, op1=ALU.add), reads=[b_av, k_a[1], b_omka], writes=[b_tt])
                yield
                clv = cl[:, 0:nb].rearrange("p (c j) -> p c j", j=128)
                P.op("dve", lambda e: e.tensor_copy(out=tot[:, 0:nch], in_=clv[:, :, 127]), reads=[b_cl], writes=[b_tot])
                P.op("pool", lambda e: e.tensor_tensor(out=kd[:, 0:nb], in0=k_ap, in1=tt[:, 0:nb], op=ALU.mult),
                     reads=[b_k, b_tt], writes=[b_kd])
                yield
                c0 = t0 // 128
                P.op("act", lambda e: e.activation(out=etot[:, s_id, c0:c0 + nch], in_=tot[:, 0:nch], func=AF.Exp),
                     reads=[b_tot], writes=[b_etot])
                if d == 0:
                    ci_ap, b_cix = cl, b_cl
                else:
                    P.op("dve", lambda e: e.tensor_tensor(
                        out=ci[:, 0:nb].rearrange("p (c j) -> p c j", j=128), in0=tot[:, 0:nch].unsqueeze(2).broadcast_to([64, nch, 128]),
                        in1=clv, op=ALU.subtract), reads=[b_tot, b_cl], writes=[b_ci])
                    P.op("dve", lambda e: e.tensor_tensor(out=ci[:, 0:nb], in0=ci[:, 0:nb], in1=lw[:, 0:nb], op=ALU.add),
                         reads=[b_ci, b_lw], writes=[b_ci])
                    ci_ap, b_cix = ci, b_ci
                yield
                P.op("dve", lambda e: e.tensor_tensor(out=cml[:, 0:nb], in0=ci_ap[:, 0:nb], in1=lw[:, 0:nb], op=ALU.subtract),
                     reads=[b_cix, b_lw], writes=[b_cml])
                P.op("act", lambda e: e.activation(out=einc[:, 0:nb], in_=ci_ap[:, 0:nb], func=AF.Exp), reads=[b_cix], writes=[b_einc])
                yield
                P.op("act", lambda e: e.activation(out=einv[:, 0:nb], in_=ci_ap[:, 0:nb], func=AF.Exp, scale=-1.0),
                     reads=[b_cix], writes=[b_einv])
                P.op("dve", lambda e: e.tensor_tensor(out=F[:, 3, 0:nb], in0=r_ap, in1=einc[:, 0:nb], op=ALU.mult),
                     reads=[b_r, b_einc], writes=[bF])
                yield
                P.op("act", lambda e: e.activation(out=eexc[:, 0:nb], in_=cml[:, 0:nb], func=AF.Exp), reads=[b_cml], writes=[b_eexc])
                P.op("dve", lambda e: e.tensor_tensor(out=F[:, 1, 0:nb], in0=tmpd[:, 0:nb], in1=einv[:, 0:nb], op=ALU.mult),
                     reads=[b_tmpd, b_einv], writes=[bF])
                yield
                P.op("pool", lambda e: e.tensor_tensor(out=F[:, 2, 0:nb], in0=kd[:, 0:nb], in1=einv[:, 0:nb], op=ALU.mult),
                     reads=[b_kd, b_einv], writes=[bF])
                P.op("dve", lambda e: e.scalar_tensor_tensor(out=F[:, 0, 0:nb], in0=kk[:, 0:nb], scalar=-1.0, in1=eexc[:, 0:nb],
                                                             op0=ALU.mult, op1=ALU.mult), reads=[b_kk, b_eexc], writes=[bF])
                yield
                P.dma(lambda e: e.dma_start(out=rwt[s_id].rearrange("f d t -> d f t")[:, :, t0:t0 + nb], in_=F[:, :, 0:nb]), reads=[bF])

            lockstep([dir_part(0), dir_part(1)])
            P.op("pool", lambda e: e.tensor_tensor(out=kds[:, 0:nb], in0=TD[0]["kd"][:, 0:nb], in1=TD[1]["kd"][:, 0:nb], op=ALU.add),
                 reads=[TD[0]["b_kd"], TD[1]["b_kd"]], writes=[b_kds])
            ax = aux[n_it % 2]
            b_ax = b_aux[n_it % 2]
            pb = rot.bank()
            P.op("pe", lambda e, pb=pb, nb=nb: e.matmul(psum[pb][0:64, 0:nb], lhsT=G2[0][:, h * 64:(h + 1) * 64], rhs=tg[:, t0:t0 + nb],
                                                        start=True, stop=True), reads=[G2[1], b_tg], writes=[pbuf[pb]])
            P.op("act", lambda e, pb=pb, nb=nb, ax=ax: e.activation(out=ax[:, 0, 0:nb], in_=psum[pb][0:64, 0:nb], func=AF.Copy),
                 reads=[pbuf[pb]], writes=[b_ax])
            P.op("dve", lambda e, nb=nb, r_ap=r_ap: e.scalar_tensor_tensor(out=tmpb[:, 0:nb], in0=r_ap, scalar=r_k[0][:, hs], in1=kds[:, 0:nb],
                                                                           op0=ALU.mult, op1=ALU.mult), reads=[b_r, b_kds, r_k[1]], writes=[b_tmpb])
            pb = rot.bank()
            P.op("pe", lambda e, pb=pb, nb=nb: e.matmul(psum[pb][0:64, 0:nb], lhsT=g["ones"][0:64, 0:64], rhs=tmpb[:, 0:nb], start=True, stop=True),
                 reads=[g["b_ones"], b_tmpb], writes=[pbuf[pb]])
            P.op("dve", lambda e, pb=pb, nb=nb, ax=ax, v_ap=v_ap: e.tensor_tensor(out=ax[:, 1, 0:nb], in0=psum[pb][0:64, 0:nb], in1=v_ap, op=ALU.mult),
                 reads=[pbuf[pb], b_v], writes=[b_ax])
            P.dma(lambda e, ax=ax, nb=nb: e.dma_start(out=io["rwaux"].ap()[:, h, :, t0:t0 + nb].rearrange("a d t -> d a t"), in_=ax[:, :, 0:nb]),
                  reads=[b_ax])
    n_it = 0
    for h in range(4):
        for (t0, nb, first, last) in RW_SEQS:
            head_block(h, t0, nb, first, last, n_it)
            n_it += 1
    P.dma(lambda e: e.dma_start(out=io["etot"].ap(), in_=etot[:]), reads=[b_etot])
    P.barrier()
    P.release()
    P.release()


class RwStream:
    pass


def lockstep(gens):
    gens = list(gens)
    while gens:
        for gg in list(gens):
            try:
                next(gg)
            except StopIteration:
                gens.remove(gg)


def phase_rwkv_scan(P, l, nsteps=34):
    nc, io, g = P.nc, P.io, P.g
    psum = P.psum
    P.mark()
    slot_ap = [psum[i // 2][:, (i % 2) * 256:(i % 2 + 1) * 256] for i in range(16)]
    bank_b = [Buf(psum=True) for _ in range(8)]
    slot_b = [bank_b[i // 2] for i in range(16)]
    masks = P.sb([128, 5, 128], F32, "masks")
    b_masks = Buf()
    P.dma(lambda e: e.dma_start(out=masks[:], in_=io["masks"].ap().rearrange("m p f -> p m f")), writes=[b_masks])
    identb = P.sb([64, 64], BF16, "identb")
    b_identb = Buf()
    P.op("dve", lambda e: e.tensor_copy(out=identb[:], in_=g["ident"][0:64, 0:64]), reads=[g["b_ident"]], writes=[b_identb])
    etot = P.sb([64, 8, 34], F32, "etot2")
    b_etot = Buf()
    P.dma(lambda e: e.dma_start(out=etot[:], in_=io["etot"].ap()), writes=[b_etot])
    rwt = io["rwt"].ap()
    yT = io["yT"].ap()
    ident = g["ident"]

    streams = []
    for s_id in range(8):
        S = RwStream()
        S.id = s_id
        S.d = s_id % 2
        S.F = [P.sb([128, 5, 128], F32, "F%d_%d" % (s_id, i)) for i in range(2)]
        S.b_F = [Buf(), Buf()]
        for i in range(2):
            P.op("pool", lambda e, S=S, i=i: e.memset(S.F[i][64:128, :, :], 0.0), writes=[S.b_F[i]])
        S.Fb = P.sb([64, 5, 128], BF16, "Fb%d" % s_id)
        S.b_Fb = Buf()
        S.Atok = P.sb([128, 64], F32, "Atok%d" % s_id)
        S.b_Atok = Buf()
        S.BKV = P.sb([128, 3, 64], BF16, "BKV%d" % s_id)
        S.b_BKV = Buf()
        S.LQ = [P.sb([128, 256], F32, "LQ%d_%d" % (s_id, i)) for i in range(2)]
        S.b_LQ = [Buf(), Buf()]
        S.Z = [P.sb([128, 128], F32, "Z%d_%d" % (s_id, i)) for i in range(2)]
        S.b_Z = [Buf(), Buf()]
        S.LakT = P.sb([128, 128], BF16, "LakT%d" % s_id)
        S.MrbT = P.sb([128, 128], BF16, "MrbT%d" % s_id)
        S.MrkT = P.sb([128, 128], BF16, "MrkT%d" % s_id)
        S.b_LakT, S.b_MrbT, S.b_MrkT = Buf(), Buf(), Buf()
        S.W = P.sb([128, 64], BF16, "W%d" % s_id)
        S.WT = P.sb([64, 128], BF16, "WT%d" % s_id)
        S.X = P.sb([128, 64], F32, "X%d" % s_id)
        S.U0 = P.sb([128, 64], F32, "U0%d" % s_id)
        S.U0b = P.sb([128, 64], BF16, "U0b%d" % s_id)
        S.GT = P.sb([64, 64], BF16, "GT%d" % s_id)
        S.DE = P.sb([64, 64], F32, "DE%d" % s_id)
        S.Ub = P.sb([128, 64], BF16, "Ub%d" % s_id)
        S.b_W, S.b_WT, S.b_X, S.b_U0, S.b_U0b, S.b_GT, S.b_DE, S.b_Ub = [Buf() for _ in range(8)]
        S.H = [P.sb([64, 64], BF16, "H%d_%d" % (s_id, i)) for i in range(2)]
        S.b_H = [Buf(), Buf()]
        S.ys = [P.sb([64, 128], F32, "ys%d_%d" % (s_id, i)) for i in range(2)]
        S.b_ys = [Buf(), Buf()]
        S.slot = 0
        S.ei = s_id
        S.mLQ, S.mQ, S.mM = (0, 1, 4) if S.d == 0 else (1, 2, 3)
        P.op("pool", lambda e, S=S: e.memset(S.H[0][:], 0.0), writes=[S.b_H[0]])
        streams.append(S)

    def next_slot(S):
        i = 2 * S.id + (S.slot % 2)
        S.slot += 1
        return slot_ap[i], slot_b[i]

    def ev_eng(S):
        S.ei += 1
        return ("act", "dve")[S.ei % 2]

    def load(S, step):
        c = CHUNK_ORDER[S.d][step]
        fb = step % 2
        P.dma(lambda e: e.dma_start(out=S.F[fb][0:64, :, :], in_=rwt[S.id].rearrange("f d t -> d f t")[:, :, c * 128:(c + 1) * 128]),
              writes=[S.b_F[fb]])

    def stage_prep(S, step):
        fb = step % 2
        F, bF = S.F[fb], S.b_F[fb]
        P.op("pool", lambda e: e.tensor_copy(out=S.Fb[:], in_=F[0:64, :, :]), reads=[bF], writes=[S.b_Fb])
        ps, pb = next_slot(S)
        for i, fidx in enumerate((0, 1, 2, 4)):
            P.op("pe", lambda e, i=i, fidx=fidx: e.matmul(ps[:, i * 64:(i + 1) * 64], lhsT=F[:, fidx, :], rhs=ident[:, 0:64],
                                                          start=True, stop=True),
                 reads=[bF, g["b_ident"]], writes=[pb])
        yield
        P.op("act", lambda e: e.activation(out=S.Atok[:], in_=ps[:, 0:64], func=AF.Copy), reads=[pb], writes=[S.b_Atok])
        P.op("dve", lambda e: e.tensor_copy(out=S.BKV[:], in_=ps[:, 64:256].rearrange("p (a d) -> p a d", a=3)), reads=[pb], writes=[S.b_BKV])
        yield

    def stage_scores(S, step):
        fb = step % 2
        F, bF = S.F[fb], S.b_F[fb]
        ps, pb = next_slot(S)
        P.op("pe", lambda e: e.matmul(ps[:, 0:128], lhsT=F[:, 0, :], rhs=F[:, 1, :], start=True, stop=True), reads=[bF], writes=[pb])
        P.op("pe", lambda e: e.matmul(ps[:, 128:256], lhsT=F[:, 1, :], rhs=F[:, 0, :], start=True, stop=True), reads=[bF], writes=[pb])
        yield
        P.op("dve", lambda e: e.tensor_tensor(out=S.LQ[0][:].rearrange("p (a f) -> p a f", a=2), in0=ps[:, 0:256].rearrange("p (a f) -> p a f", a=2),
                                              in1=masks[:, S.mLQ:S.mLQ + 2, :], op=ALU.mult), reads=[pb, b_masks], writes=[S.b_LQ[0]])
        P.op("pool", lambda e: e.tensor_tensor(out=S.Z[0][:], in0=S.LQ[0][:, 128:256], in1=ident[:], op=ALU.add),
             reads=[S.b_LQ[0], g["b_ident"]], writes=[S.b_Z[0]])
        yield
        ps2, pb2 = next_slot(S)
        P.op("pe", lambda e: e.matmul(ps2[:, 0:128], lhsT=S.Fb[:, 2, :], rhs=S.Fb[:, 0, :], start=True, stop=True), reads=[S.b_Fb], writes=[pb2])
        P.op("pe", lambda e: e.matmul(ps2[:, 128:256], lhsT=S.Fb[:, 1, :], rhs=S.Fb[:, 3, :], start=True, stop=True), reads=[S.b_Fb], writes=[pb2])
        yield
        P.op("dve", lambda e: e.tensor_tensor(out=S.LakT[:], in0=ps2[:, 0:128], in1=masks[:, S.mQ, :], op=ALU.mult),
             reads=[pb2, b_masks], writes=[S.b_LakT])
        P.op("dve", lambda e: e.tensor_tensor(out=S.MrbT[:], in0=ps2[:, 128:256], in1=masks[:, S.mM, :], op=ALU.mult),
             reads=[pb2, b_masks], writes=[S.b_MrbT])
        yield
        ps3, pb3 = next_slot(S)
        P.op("pe", lambda e: e.matmul(ps3[:, 0:128], lhsT=S.Fb[:, 2, :], rhs=S.Fb[:, 3, :], start=True, stop=True), reads=[S.b_Fb], writes=[pb3])
        yield
        P.op("dve", lambda e: e.tensor_tensor(out=S.MrkT[:], in0=ps3[:, 0:128], in1=masks[:, S.mM, :], op=ALU.mult),
             reads=[pb3, b_masks], writes=[S.b_MrkT])
        yield

    def stage_double(S, lvl):
        a, b = (lvl - 1) % 2, lvl % 2
        last = lvl == 6
        ps, pb = next_slot(S)
        La, Qa = S.LQ[a][:, 0:128], S.LQ[a][:, 128:256]
        P.op("pe", lambda e: e.matmul(ps[:, 0:128], lhsT=Qa, rhs=La, start=True, stop=True), reads=[S.b_LQ[a]], writes=[pb])
        if not last:
            P.op("pe", lambda e: e.matmul(ps[:, 128:256], lhsT=La, rhs=Qa, start=True, stop=True), reads=[S.b_LQ[a]], writes=[pb])
        yield
        w = 128 if last else 256
        eng = ev_eng(S)
        if eng == "act":
            P.op("act", lambda e: e.activation(out=S.LQ[b][:, 0:w], in_=ps[:, 0:w], func=AF.Copy), reads=[pb], writes=[S.b_LQ[b]])
        else:
            P.op("dve", lambda e: e.tensor_copy(out=S.LQ[b][:, 0:w], in_=ps[:, 0:w]), reads=[pb], writes=[S.b_LQ[b]])
        yield
        ps2, pb2 = next_slot(S)
        P.op("pe", lambda e: e.matmul(ps2[:, 0:128], lhsT=S.LQ[b][:, 0:128], rhs=S.Z[a][:], start=True, stop=True),
             reads=[S.b_LQ[b], S.b_Z[a]], writes=[pb2])
        yield
        P.op("dve", lambda e: e.tensor_tensor(out=S.Z[b][:], in0=ps2[:, 0:128], in1=S.Z[a][:], op=ALU.add),
             reads=[pb2, S.b_Z[a]], writes=[S.b_Z[b]])
        yield

    def stage_wux(S):
        Z, bZ = S.Z[0], S.b_Z[0]
        ps, pb = next_slot(S)
        P.op("pe", lambda e: e.matmul(ps[:, 0:64], lhsT=Z[:], rhs=S.Atok[:], start=True, stop=True), reads=[bZ, S.b_Atok], writes=[pb])
        P.op("pe", lambda e: e.matmul(ps[0:64, 64:192], lhsT=S.Atok[:], rhs=Z[:], start=True, stop=True), reads=[bZ, S.b_Atok], writes=[pb])
        P.op("pe", lambda e: e.matmul(ps[:, 192:256], lhsT=S.LakT[:], rhs=S.BKV[:, 2, :], start=True, stop=True),
             reads=[S.b_LakT, S.b_BKV], writes=[pb])
        yield
        P.op("dve", lambda e: e.tensor_copy(out=S.X[:], in_=ps[:, 192:256]), reads=[pb], writes=[S.b_X])
        P.op("act", lambda e: e.activation(out=S.W[:], in_=ps[:, 0:64], func=AF.Copy), reads=[pb], writes=[S.b_W])
        P.op("act", lambda e: e.activation(out=S.WT[:], in_=ps[0:64, 64:192], func=AF.Copy), reads=[pb], writes=[S.b_WT])
        yield
        ps2, pb2 = next_slot(S)
        P.op("pe", lambda e: e.matmul(ps2[:, 0:64], lhsT=Z[:], rhs=S.X[:], start=True, stop=True), reads=[bZ, S.b_X], writes=[pb2])
        yield
        P.op("act", lambda e: e.activation(out=S.U0[:], in_=ps2[:, 0:64], func=AF.Copy), reads=[pb2], writes=[S.b_U0])
        P.op("act", lambda e: e.activation(out=S.U0b[:], in_=ps2[:, 0:64], func=AF.Copy), reads=[pb2], writes=[S.b_U0b])
        yield

    def stage_gd(S, step):
        c = CHUNK_ORDER[S.d][step]
        ps, pb = next_slot(S)
        P.op("pe", lambda e: e.matmul(ps[0:64, 0:64], lhsT=S.W[:], rhs=S.BKV[:, 0, :], start=True, stop=False),
             reads=[S.b_W, S.b_BKV], writes=[pb])
        P.op("pe", lambda e: e.matmul(ps[0:64, 0:64], lhsT=identb[:], rhs=identb[:], start=False, stop=True), reads=[b_identb], writes=[pb])
        P.op("pe", lambda e: e.matmul(ps[0:64, 64:128], lhsT=S.BKV[:, 0, :], rhs=S.U0b[:], start=True, stop=False),
             reads=[S.b_BKV, S.b_U0b], writes=[pb])
        P.op("pe", lambda e: e.matmul(ps[0:64, 64:128], lhsT=S.BKV[:, 1, :], rhs=S.BKV[:, 2, :], start=False, stop=True),
             reads=[S.b_BKV], writes=[pb])
        yield
        P.op("act", lambda e: e.activation(out=S.GT[:], in_=ps[0:64, 0:64], func=AF.Copy), reads=[pb], writes=[S.b_GT])
        P.op("act", lambda e: e.activation(out=S.DE[:], in_=ps[0:64, 64:128], func=AF.Copy, scale=etot[:, S.id, c:c + 1]),
             reads=[pb, b_etot], writes=[S.b_DE])
        yield

    def stage_state(S, step):
        c = CHUNK_ORDER[S.d][step]
        hi, ho = step % 2, (step + 1) % 2
        H, bH = S.H[hi], S.b_H[hi]
        ps, pb = next_slot(S)
        P.op("pe", lambda e: e.matmul(ps[:, 0:64], lhsT=S.WT[:], rhs=H[:], start=True, stop=True), reads=[S.b_WT, bH], writes=[pb])
        yield
        P.op("dve", lambda e: e.tensor_tensor(out=S.Ub[:], in0=ps[:, 0:64], in1=S.U0[:], op=ALU.add), reads=[pb, S.b_U0], writes=[S.b_Ub])
        yield
        ps2, pb2 = next_slot(S)
        P.op("pe", lambda e: e.matmul(ps2[0:64, 0:128], lhsT=H[:], rhs=S.Fb[:, 3, :], start=True, stop=False), reads=[bH, S.b_Fb], writes=[pb2])
        P.op("pe", lambda e: e.matmul(ps2[0:64, 0:128], lhsT=S.Ub[:], rhs=S.MrbT[:], start=False, stop=False),
             reads=[S.b_Ub, S.b_MrbT], writes=[pb2])
        P.op("pe", lambda e: e.matmul(ps2[0:64, 0:128], lhsT=S.BKV[:, 2, :], rhs=S.MrkT[:], start=False, stop=True),
             reads=[S.b_BKV, S.b_MrkT], writes=[pb2])
        P.op("pe", lambda e: e.matmul(ps2[0:64, 128:192], lhsT=S.GT[:], rhs=H[:], start=True, stop=True), reads=[S.b_GT, bH], writes=[pb2])
        yield
        yb = step % 2
        P.op("dve", lambda e: e.scalar_tensor_tensor(out=S.H[ho][:], in0=ps2[0:64, 128:192], scalar=etot[:, S.id, c:c + 1], in1=S.DE[:],
                                                     op0=ALU.mult, op1=ALU.add), reads=[pb2, b_etot, S.b_DE], writes=[S.b_H[ho]])
        P.op("dve", lambda e: e.tensor_copy(out=S.ys[yb][:], in_=ps2[0:64, 0:128]), reads=[pb2], writes=[S.b_ys[yb]])
        P.dma(lambda e: e.dma_start(out=yT[S.id, :, c * 128:(c + 1) * 128], in_=S.ys[yb][:]), reads=[S.b_ys[yb]])
        yield

    def lockstep(gens):
        gens = list(gens)
        while gens:
            for gg in list(gens):
                try:
                    next(gg)
                except StopIteration:
                    gens.remove(gg)

    for S in streams:
        load(S, 0)
    for step in range(nsteps):
        if step + 1 < nsteps:
            for S in streams:
                load(S, step + 1)
        lockstep(stage_prep(S, step) for S in streams)
        lockstep(stage_scores(S, step) for S in streams)
        for lvl in range(1, 7):
            lockstep(stage_double(S, lvl) for S in streams)
        lockstep(stage_wux(S) for S in streams)
        lockstep(stage_gd(S, step) for S in streams)
        lockstep(stage_state(S, step) for S in streams)
    P.barrier()
    P.release()


def phase_rwkv_fin(P, l):
    nc, io, g = P.nc, P.io, P.g
    psum, pbuf = P.psum, P.pbuf
    rot = Rot(P)
    P.mark()
    gn_g = col_param(P, io["gn_g"].ap()[l, :], "gn_g")
    gn_b = col_param(P, io["gn_b"].ap()[l, :], "gn_b")
    od = P.sb([64, 64], F32, "onesdiv")
    b_od = Buf()
    P.op("dve", lambda e: e.memset(od[:], 1.0 / 64), writes=[b_od])
    y2 = [P.sb([64, 2, 512], F32, "y2_%d" % i) for i in range(2)]
    ax = [P.sb([64, 2, 512], F32, "ax_%d" % i) for i in range(2)]
    b_y2, b_ax = [Buf(), Buf()], [Buf(), Buf()]
    y = P.sb([64, 512], F32, "y")
    yc = P.sb([64, 512], F32, "yc")
    sq = P.sb([64, 512], F32, "sq")
    sd = P.sb([64, 512], F32, "sd")
    b_y, b_yc, b_sq, b_sd = Buf(), Buf(), Buf(), Buf()
    ost = [P.sb([64, 512], BF16, "rost%d" % i) for i in range(2)]
    b_ost = [Buf(), Buf()]
    yT = io["yT"].ap()
    cat = io["cat"].ap()

    def blk(h, t0, nb, n):
        k = n % 2
        hs = slice(h, h + 1)
        P.dma(lambda e: e.dma_start(out=y2[k][:, :, 0:nb], in_=yT[2 * h:2 * h + 2, :, t0:t0 + nb].rearrange("s d t -> d s t")), writes=[b_y2[k]])
        P.dma(lambda e: e.dma_start(out=ax[k][:, :, 0:nb], in_=io["rwaux"].ap()[:, h, :, t0:t0 + nb].rearrange("a d t -> d a t")), writes=[b_ax[k]])
        P.op("dve", lambda e: e.tensor_tensor(out=y[:, 0:nb], in0=y2[k][:, 0, 0:nb], in1=y2[k][:, 1, 0:nb], op=ALU.add), reads=[b_y2[k]], writes=[b_y])
        pb = rot.bank()
        P.op("pe", lambda e: e.matmul(psum[pb][0:64, 0:nb], lhsT=od[:], rhs=y[:, 0:nb], start=True, stop=True), reads=[b_od, b_y], writes=[pbuf[pb]])
        P.op("dve", lambda e: e.tensor_tensor(out=yc[:, 0:nb], in0=y[:, 0:nb], in1=psum[pb][0:64, 0:nb], op=ALU.subtract),
             reads=[b_y, pbuf[pb]], writes=[b_yc])
        P.op("pool", lambda e: e.tensor_tensor(out=sq[:, 0:nb], in0=yc[:, 0:nb], in1=yc[:, 0:nb], op=ALU.mult), reads=[b_yc], writes=[b_sq])
        pb2 = rot.bank()
        P.op("pe", lambda e: e.matmul(psum[pb2][0:64, 0:nb], lhsT=od[:], rhs=sq[:, 0:nb], start=True, stop=True), reads=[b_od, b_sq], writes=[pbuf[pb2]])
        P.op("dve", lambda e: e.tensor_scalar(out=sd[:, 0:nb], in0=psum[pb2][0:64, 0:nb], scalar1=float(GN_EPS), scalar2=None, op0=ALU.add),
             reads=[pbuf[pb2]], writes=[b_sd])
        P.op("act", lambda e: e.activation(out=sd[:, 0:nb], in_=sd[:, 0:nb], func=AF.Sqrt), reads=[b_sd], writes=[b_sd])
        P.op("dve", lambda e: e.reciprocal(out=sd[:, 0:nb], in_=sd[:, 0:nb]), reads=[b_sd], writes=[b_sd])
        P.op("dve", lambda e: e.tensor_tensor(out=yc[:, 0:nb], in0=yc[:, 0:nb], in1=sd[:, 0:nb], op=ALU.mult), reads=[b_yc, b_sd], writes=[b_yc])
        P.op("pool", lambda e: e.tensor_scalar(out=yc[:, 0:nb], in0=yc[:, 0:nb], scalar1=gn_g[0][:, hs], scalar2=gn_b[0][:, hs],
                                               op0=ALU.mult, op1=ALU.add), reads=[b_yc, gn_g[1], gn_b[1]], writes=[b_yc])
        P.op("dve", lambda e: e.tensor_tensor(out=yc[:, 0:nb], in0=yc[:, 0:nb], in1=ax[k][:, 1, 0:nb], op=ALU.add), reads=[b_yc, b_ax[k]], writes=[b_yc])
        P.op("pool", lambda e: e.tensor_tensor(out=ost[k][:, 0:nb], in0=yc[:, 0:nb], in1=ax[k][:, 0, 0:nb], op=ALU.mult),
             reads=[b_yc, b_ax[k]], writes=[b_ost[k]])
        P.dma(lambda e: e.dma_start(out=cat[256 + h * 64:256 + (h + 1) * 64, t0:t0 + nb], in_=ost[k][:, 0:nb]), reads=[b_ost[k]])

    n = 0
    for h in range(4):
        for (t0, nb, first, last) in RW_SEQS:
            blk(h, t0, nb, n)
            n += 1
    P.barrier()
    P.release()


_NC_CACHE = {}


def _get_nc(dbg=None):
    if dbg not in _NC_CACHE:
        _NC_CACHE[dbg] = build_program(dbg)
    return _NC_CACHE[dbg]


_CONST = {}


def _constants():
    if _CONST:
        return _CONST
    bf = ml_dtypes.bfloat16
    t = np.arange(SEQ)
    rows = (t // 64).astype(np.float64)
    cols = (t % 64).astype(np.float64)
    inv = 10000.0 ** (-np.arange(16, dtype=np.float64) / 16)
    ang = np.concatenate([rows[:, None] * inv, cols[:, None] * inv], -1)
    _CONST["rope_cos"] = np.cos(ang).astype(np.float32)
    _CONST["rope_sin"] = np.sin(ang).astype(np.float32)
    c = np.arange(64)
    th = 2 * np.pi * np.outer(c, c) / 64
    z = np.zeros((64, 64))
    _CONST["dftc"] = np.block([[np.cos(th), z], [z, np.cos(th)]]).astype(bf)
    _CONST["dfts"] = np.block([[np.sin(th), z], [z, np.sin(th)]]).astype(bf)
    pi_, fi_ = np.arange(128)[:, None], np.arange(128)[None, :]
    _CONST["masks"] = np.stack([fi_ < pi_, fi_ > pi_, fi_ < pi_, fi_ <= pi_, fi_ >= pi_], 0).astype(np.float32)
    pm = np.zeros((4, 5, 128, 128))
    for gi, win in enumerate((2, 4, 8, 16)):
        T3 = 384
        tt = np.arange(T3)
        lo = np.clip(tt - win // 2, 0, T3)
        hi = np.clip(tt + (win - win // 2), 0, T3)
        ss_ = np.arange(T3)[:, None]
        M = ((ss_ >= lo[None, :]) & (ss_ < hi[None, :])) / (hi - lo)[None, :].astype(np.float64) - np.eye(T3)
        pm[gi, 0] = M[0:128, 128:256]
        pm[gi, 1] = M[128:256, 0:128]
        pm[gi, 2] = M[128:256, 128:256]
        pm[gi, 3] = M[0:128, 0:128]
        pm[gi, 4] = M[256:384, 256:384]
    _CONST["poolm"] = pm.reshape(20, 128, 128).astype(bf)
    for nm, T in (("dft_lat", SEQ), ("dft_ctx", CTX)):
        k = np.arange(T)
        kk = (np.outer(k, k) % T).astype(np.float64)
        ang = 2 * np.pi * kk / T
        sc = 1.0 / np.sqrt(T * 64.0)
        _CONST[nm] = np.stack([np.cos(ang) * sc, -np.sin(ang) * sc], 0).astype(bf)
    return _CONST


def make_in_maps(inputs):
    f32 = lambda a: np.ascontiguousarray(np.asarray(a, dtype=np.float32))
    shared = {k: f32(inputs[k]) for k in ("w_mod", "b_mod", "ln_g", "ln_b", "w_ffn_in", "w_ffn_out", "w_in", "w_out",
                                          "q_norm_g", "k_norm_g", "pool_w", "pool_scale", "fourier_w", "rwkv_mu", "decay_w0", "decay_w1", "decay_w2",
                                          "icl_a0", "icl_a1", "icl_a2", "gate_g1", "gate_g2", "k_k", "k_a", "r_k", "gn_g", "gn_b")}
    shared["ident"] = np.eye(128, dtype=np.float32)
    shared.update(_constants())
    maps = []
    for core in range(8):
        b = core // 2
        m = dict(shared)
        m["x"] = f32(inputs["x"][b])
        m["ctx"] = f32(inputs["ctx"][b])
        m["c2"] = f32(np.stack([np.asarray(inputs["c"])[b], np.asarray(inputs["c_ctx"])], 0))
        maps.append(m)
    return maps


def kernel(**inputs):
    nc = _get_nc(None)
    res = run_bass_kernel_spmd(nc, make_in_maps(inputs), core_ids=list(range(8)))
    out = np.zeros((4, SEQ, D), np.float32)
    hf = SEQ // 2
    for core in range(8):
        b, j = core // 2, core % 2
        out[b, j * hf:(j + 1) * hf] = np.asarray(res.results[core]["y"])[j * hf:(j + 1) * hf]
    return out
```

```python
import contextlib
import numpy as np
import ml_dtypes
import concourse.bass as bass
import concourse.mybir as mybir
from concourse.bass_utils import run_bass_kernel_spmd

F32 = mybir.dt.float32
BF16 = mybir.dt.bfloat16
AF = mybir.ActivationFunctionType
ALU = mybir.AluOpType
AX = mybir.AxisListType

D = 1024
SEQ = 4096
CTX = 256
NTOK = SEQ + CTX
DEPTH = 4
DFF = 2816
DIN = 2048
ALPHA = (2 * DEPTH) ** 0.25
LN_EPS = 1e-5

ENGS = ("pe", "dve", "act", "pool", "sp")


class Buf:
    __slots__ = ("w", "r", "name", "psum")

    def __init__(self, name="", psum=False):
        self.w = None
        self.r = {}
        self.name = name
        self.psum = psum


class Prog:
    def __init__(self, nc, stack, n_dma_sems=16):
        self.nc = nc
        self.streams = {e: [] for e in ENGS}
        self.count = {e: 0 for e in ENGS}
        self.seen = {e: {} for e in ENGS}
        self.sems = {}
        for e in ENGS:
            self.sems["E_" + e] = stack.enter_context(nc.semaphore("sem_" + e))
        self.dma_keys = []
        self.dma_tot = {}
        for i in range(n_dma_sems):
            k = "D_%d" % i
            self.sems[k] = stack.enter_context(nc.semaphore("semd_%d" % i))
            self.dma_keys.append(k)
            self.dma_tot[k] = 0
        self.dma_rr = 0
        self.sb_off = 16640
        self.sb_marks = []
        self.n_alloc = 0
        self.n_ops = 0

    def sb(self, shape, dtype, name=None):
        esz = 4 if dtype == F32 else 2
        per_part = esz
        for s in shape[1:]:
            per_part *= s
        per_part = (per_part + 63) // 64 * 64
        off = self.sb_off
        self.sb_off += per_part
        assert self.sb_off <= 229376, "SBUF overflow %d" % self.sb_off
        self.n_alloc += 1
        t = self.nc.alloc_sbuf_tensor_at("sb%d_%s" % (self.n_alloc, name or "t"), list(shape), dtype, offset=off)
        return t

    def mark(self):
        self.sb_marks.append(self.sb_off)

    def release(self):
        self.sb_off = self.sb_marks.pop()

    def _wait(self, eng, toks):
        own = "E_" + eng
        seen = self.seen[eng]
        for key, val in toks:
            if key == own and eng in ("pe", "sp"):
                continue
            if seen.get(key, 0) >= val:
                continue
            seen[key] = val
            self.streams[eng].append(("wait", key, val))

    @staticmethod
    def _deps(reads, writes, own=None):
        toks = []
        for b in reads:
            if b.w is not None:
                toks.append(b.w)
            if b.psum:
                toks.extend((k, v) for k, v in b.r.items() if k != own)
        for b in writes:
            if b.w is not None:
                toks.append(b.w)
            toks.extend(b.r.items())
        return toks

    @staticmethod
    def _update(tok, reads, writes):
        key, val = tok
        for b in reads:
            if b.r.get(key, 0) < val:
                b.r[key] = val
        for b in writes:
            b.w = tok
            b.r = {}

    def op(self, eng, fn, reads=(), writes=()):
        self._wait(eng, self._deps(reads, writes, "E_" + eng))
        self.count[eng] += 1
        key = "E_" + eng
        self.streams[eng].append(("op", fn, key))
        tok = (key, self.count[eng])
        self._update(tok, reads, writes)
        self.n_ops += 1
        return tok

    def dma(self, fn, reads=(), writes=(), queue="sp"):
        key = self.dma_keys[self.dma_rr % len(self.dma_keys)]
        self.dma_rr += 1
        toks = self._deps(reads, writes)
        if self.dma_tot[key] > 0:
            toks.append((key, self.dma_tot[key]))
        self._wait(queue, toks)
        self.dma_tot[key] += 16
        self.streams[queue].append(("dma", fn, key))
        tok = (key, self.dma_tot[key])
        self._update(tok, reads, writes)
        self.n_ops += 1
        return tok

    def barrier(self):
        toks = [("E_" + e, self.count[e]) for e in ENGS if self.count[e] > 0]
        toks += [(k, v) for k, v in self.dma_tot.items() if v > 0]
        for e in ENGS:
            self._wait(e, toks)

    def replay(self, name, e):
        sems = self.sems
        for item in self.streams[name]:
            if item[0] == "wait":
                e.wait_ge(sems[item[1]], item[2])
            elif item[0] == "op":
                item[1](e).then_inc(sems[item[2]], 1)
            else:
                item[1](e).then_inc(sems[item[2]], 16)


def build_program(dbg=None):
    nc = bass.Bass("TRN2", target_bir_lowering=False)
    _so = bool(dbg) and dbg.startswith("so")

    def dt(name, shape, dtype=F32, kind="ExternalInput"):
        if _so and kind == "ExternalInput" and name not in ("ident", "masks", "c2", "rwt", "etot"):
            shape = [1, 2]
        return nc.dram_tensor(name, list(shape), dtype, kind=kind)
    io = {}
    io["x"] = dt("x", [SEQ, D])
    io["ctx"] = dt("ctx", [CTX, D])
    io["c2"] = dt("c2", [2, D])
    io["w_mod"] = dt("w_mod", [DEPTH, D, 9 * D])
    io["b_mod"] = dt("b_mod", [DEPTH, 9 * D])
    io["ln_g"] = dt("ln_g", [DEPTH, 3, D])
    io["ln_b"] = dt("ln_b", [DEPTH, 3, D])
    io["w_ffn_in"] = dt("w_ffn_in", [DEPTH, 2, D, 2 * DFF])
    io["w_ffn_out"] = dt("w_ffn_out", [DEPTH, 2, DFF, D])
    io["ident"] = dt("ident", [128, 128])
    io["w_in"] = dt("w_in", [DEPTH, D, DIN])
    io["w_out"] = dt("w_out", [DEPTH, D, D])
    io["q_norm_g"] = dt("q_norm_g", [DEPTH, 64])
    io["k_norm_g"] = dt("k_norm_g", [DEPTH, 64])
    io["rope_cos"] = dt("rope_cos", [SEQ, 32])
    io["rope_sin"] = dt("rope_sin", [SEQ, 32])
    io["dftc"] = dt("dftc", [128, 128], BF16)
    io["dfts"] = dt("dfts", [128, 128], BF16)
    io["poolm"] = dt("poolm", [20, 128, 128], BF16)
    io["pool_w"] = dt("pool_w", [DEPTH, 4, 64, 64])
    io["pool_scale"] = dt("pool_scale", [DEPTH, 256])
    io["fourier_w"] = dt("fourier_w", [DEPTH, 256, 256])
    io["dft_lat"] = dt("dft_lat", [2, SEQ, SEQ], BF16)
    io["dft_ctx"] = dt("dft_ctx", [2, CTX, CTX], BF16)
    io["rwkv_mu"] = dt("rwkv_mu", [DEPTH, 6, 256])
    io["decay_w0"] = dt("decay_w0", [DEPTH, 2, 256])
    io["decay_w1"] = dt("decay_w1", [DEPTH, 2, 256, 32])
    io["decay_w2"] = dt("decay_w2", [DEPTH, 2, 32, 256])
    io["icl_a0"] = dt("icl_a0", [DEPTH, 2, 256])
    io["icl_a1"] = dt("icl_a1", [DEPTH, 2, 256, 32])
    io["icl_a2"] = dt("icl_a2", [DEPTH, 2, 32, 256])
    io["gate_g1"] = dt("gate_g1", [DEPTH, 256, 64])
    io["gate_g2"] = dt("gate_g2", [DEPTH, 64, 256])
    io["k_k"] = dt("k_k", [DEPTH, 256])
    io["k_a"] = dt("k_a", [DEPTH, 256])
    io["r_k"] = dt("r_k", [DEPTH, 4, 64])
    io["gn_g"] = dt("gn_g", [DEPTH, 256])
    io["gn_b"] = dt("gn_b", [DEPTH, 256])
    io["masks"] = dt("masks", [5, 128, 128])
    scratch_kind = "ExternalOutput" if dbg else "Internal"
    io["xs"] = dt("xs", [NTOK, D], F32, kind=scratch_kind)
    io["modv"] = dt("modv", [2, 3 * D], F32, kind=scratch_kind)
    io["cat"] = dt("cat", [D, NTOK], BF16, kind=scratch_kind)
    io["zrw"] = dt("zrw", [4, 64, 4, NTOK], F32, kind=scratch_kind)
    io["zpool"] = dt("zpool", [NTOK, 256], BF16, kind=scratch_kind)
    io["ab"] = dt("ab", [NTOK, 512], BF16, kind=scratch_kind)
    so = bool(dbg) and dbg.startswith("so")
    io["rwt"] = dt("rwt", [8, 5, 64, NTOK], F32, kind=("ExternalInput" if so else scratch_kind))
    io["etot"] = dt("etot", [64, 8, 34], F32, kind=("ExternalInput" if so else scratch_kind))
    io["yT"] = dt("yT", [8, 64, NTOK], F32, kind=scratch_kind)
    io["rwaux"] = dt("rwaux", [2, 4, 64, NTOK], F32, kind=scratch_kind)
    io["y"] = dt("y", [SEQ, D], F32, kind="ExternalOutput")

    with contextlib.ExitStack() as stack:
        P = Prog(nc, stack)
        psum = [nc.alloc_psum_tensor("ps%d" % i, [128, 512], F32) for i in range(8)]
        pbuf = [Buf("ps%d" % i, psum=True) for i in range(8)]
        P.psum = psum
        P.pbuf = pbuf
        P.io = io
        emit_all(P, dbg)
        P.barrier()
        with nc.Block() as block:
            @block.tensor
            def _(e):
                P.replay("pe", e)

            @block.vector
            def _(e):
                P.replay("dve", e)

            @block.scalar
            def _(e):
                P.replay("act", e)

            @block.gpsimd
            def _(e):
                P.replay("pool", e)

            @block.sync
            def _(e):
                P.replay("sp", e)
    return nc


def emit_all(P, dbg):
    nc = P.nc
    io = P.io
    ident = P.sb([128, 128], F32, "ident")
    b_ident = Buf("ident")
    P.dma(lambda e: e.dma_start(out=ident[:], in_=io["ident"].ap()), writes=[b_ident])
    ones = P.sb([128, 128], F32, "ones")
    b_ones = Buf("ones")
    P.op("dve", lambda e: e.memset(ones[:], 1.0), writes=[b_ones])
    craw = P.sb([128, 2, 8], F32, "craw")
    b_craw = Buf()
    for r in range(2):
        P.dma(lambda e, r=r: e.dma_start(out=craw[:, r, :], in_=io["c2"].ap()[r, :].rearrange("(k p) -> p k", p=128),
                                         allow_slow_non_contiguous=True), writes=[b_craw])
    scT = P.sb([128, 8, 2], F32, "scT")
    b_scT = Buf()
    P.op("act", lambda e: e.activation(out=scT[:].rearrange("p k r -> p r k"), in_=craw[:], func=AF.Silu),
         reads=[b_craw], writes=[b_scT])
    P.g = dict(ident=ident, b_ident=b_ident, ones=ones, b_ones=b_ones, scT=scT, b_scT=b_scT)
    P.barrier()

    if dbg and dbg.startswith("so"):
        phase_rwkv_scan(P, 0, nsteps=int(dbg[2:]))
        return
    for l in range(DEPTH):
        last = l == DEPTH - 1
        src_lat = io["x"].ap() if l == 0 else io["xs"].ap()[0:SEQ, :]
        src_ctx = io["ctx"].ap() if l == 0 else io["xs"].ap()[SEQ:NTOK, :]
        phase_ffn(P, l, 0, 0, src_lat, src_ctx, io["xs"].ap()[0:SEQ, :], io["xs"].ap()[SEQ:NTOK, :], 0.5)
        if dbg == "ffn0":
            return
        phase_mixer(P, l, dbg)
        if dbg in ("att", "mixA", "pf", "wout", "rw1", "rw2", "rw3") or (dbg and dbg.startswith("rws")):
            return
        phase_ffn(P, l, 1, 2, io["xs"].ap()[0:SEQ, :], io["xs"].ap()[SEQ:NTOK, :],
                  io["y"].ap() if last else io["xs"].ap()[0:SEQ, :], io["xs"].ap()[SEQ:NTOK, :], 0.5, skip_ctx=last)
        if dbg == "l0":
            return


def phase_mod(P, l, sub, resid_w):
    nc, io, g = P.nc, P.io, P.g
    psum, pbuf = P.psum, P.pbuf
    P.mark()
    brow = P.sb([1, 3 * D], F32, "brow")
    b_brow = Buf()
    P.dma(lambda e: e.dma_start(out=brow[:], in_=io["b_mod"].ap()[l:l + 1, sub * 3 * D:(sub + 1) * 3 * D]), writes=[b_brow])
    mrow = P.sb([2, 3 * D], F32, "mrow")
    b_mrow = Buf()
    wst = [P.sb([128, 8, 512], F32, "wmst%d" % i) for i in range(2)]
    b_wst = [Buf(), Buf()]
    for cb in range(6):
        s = cb % 2
        c0 = sub * 3 * D + cb * 512
        P.dma(lambda e, s=s, c0=c0: e.dma_start(
            out=wst[s][:], in_=io["w_mod"].ap()[l, :, c0:c0 + 512].rearrange("(k p) n -> p k n", p=128)),
            writes=[b_wst[s]])
        pb = cb % 2
        for kc in range(8):
            P.op("pe", lambda e, s=s, kc=kc, pb=pb: e.matmul(psum[pb][0:2, :], lhsT=g["scT"][:, kc, :], rhs=wst[s][:, kc, :],
                                                             start=(kc == 0), stop=False),
                 reads=[g["b_scT"], b_wst[s]], writes=[pbuf[pb]])
        P.op("pe", lambda e, cb=cb, pb=pb: e.matmul(psum[pb][0:2, :], lhsT=g["ones"][0:1, 0:2], rhs=brow[0:1, cb * 512:(cb + 1) * 512],
                                                    start=False, stop=True),
             reads=[g["b_ones"], b_brow], writes=[pbuf[pb]])
        P.op("dve", lambda e, cb=cb, pb=pb: e.tensor_copy(out=mrow[:, cb * 512:(cb + 1) * 512], in_=psum[pb][0:2, :]),
             reads=[pbuf[pb]], writes=[b_mrow])
    b_modv = Buf()
    P.dma(lambda e: e.dma_start(out=io["modv"].ap(), in_=mrow[:]), reads=[b_mrow], writes=[b_modv])
    P.release()
    shT = P.sb([128, 2, 8], F32, "shT")
    scl = P.sb([128, 2, 8], F32, "scl")
    gbc = [P.sb([128, D], F32, "gbc%d" % r) for r in range(2)]
    b_shT, b_scl, b_g = Buf(), Buf(), [Buf(), Buf()]
    for r in range(2):
        P.dma(lambda e, r=r: e.dma_start(out=shT[:, r, :], in_=io["modv"].ap()[r, 0:D].rearrange("(k p) -> p k", p=128),
                                         allow_slow_non_contiguous=True), reads=[b_modv], writes=[b_shT])
        P.dma(lambda e, r=r: e.dma_start(out=scl[:, r, :], in_=io["modv"].ap()[r, D:2 * D].rearrange("(k p) -> p k", p=128),
                                         allow_slow_non_contiguous=True), reads=[b_modv], writes=[b_scl])
        P.dma(lambda e, r=r: e.dma_start(out=gbc[r][:], in_=io["modv"].ap()[r:r + 1, 2 * D:3 * D].partition_broadcast(128)),
              reads=[b_modv], writes=[b_g[r]])
    P.op("dve", lambda e: e.tensor_scalar(out=scl[:], in0=scl[:], scalar1=1.0, scalar2=None, op0=ALU.add),
         reads=[b_scl], writes=[b_scl])
    for r in range(2):
        if resid_w != 1.0:
            P.op("pool", lambda e, r=r: e.tensor_scalar(out=gbc[r][:], in0=gbc[r][:], scalar1=float(resid_w), scalar2=None, op0=ALU.mult),
                 reads=[b_g[r]], writes=[b_g[r]])
    lg = P.sb([128, D], F32, "lng")
    lb = P.sb([128, D], F32, "lnb")
    b_lg, b_lb = Buf(), Buf()
    P.dma(lambda e: e.dma_start(out=lg[:], in_=io["ln_g"].ap()[l, sub:sub + 1, :].partition_broadcast(128)), writes=[b_lg])
    P.dma(lambda e: e.dma_start(out=lb[:], in_=io["ln_b"].ap()[l, sub:sub + 1, :].partition_broadcast(128)), writes=[b_lb])
    return dict(shT=shT, scl=scl, gbc=gbc, b_shT=b_shT, b_scl=b_scl, b_g=b_g, lg=lg, lb=lb, b_lg=b_lg, b_lb=b_lb)


def emit_postnorm(P, m, r, xt_ap, b_x, y_halves, b_ys, out_ap, b_out, work):
    u, b_u, st, b_st = work["u"], work["b_u"], work["st"], work["b_st"]
    for h in range(2):
        sl = slice(h * 512, (h + 1) * 512)
        P.op("dve", lambda e, h=h, sl=sl: e.tensor_tensor(out=u[:, sl], in0=y_halves[h], in1=m["gbc"][r][:, sl], op=ALU.mult),
             reads=[b_ys[h], m["b_g"][r]], writes=[b_u])
        P.op("dve", lambda e, sl=sl: e.scalar_tensor_tensor(out=u[:, sl], in0=xt_ap[:, sl], scalar=float(ALPHA), in1=u[:, sl],
                                                            op0=ALU.mult, op1=ALU.add),
             reads=[b_x, b_u], writes=[b_u])
        P.op("dve", lambda e, h=h, sl=sl: e.bn_stats(out=st[:, h * 6:(h + 1) * 6], in_=u[:, sl]), reads=[b_u], writes=[b_st])
    P.op("dve", lambda e: e.bn_aggr(out=st[:, 12:14], in_=st[:, 0:12]), reads=[b_st], writes=[b_st])
    P.op("dve", lambda e: e.tensor_scalar(out=st[:, 14:15], in0=st[:, 13:14], scalar1=float(LN_EPS), scalar2=None, op0=ALU.add),
         reads=[b_st], writes=[b_st])
    P.op("act", lambda e: e.activation(out=st[:, 14:15], in_=st[:, 14:15], func=AF.Sqrt), reads=[b_st], writes=[b_st])
    P.op("dve", lambda e: e.reciprocal(out=st[:, 14:15], in_=st[:, 14:15]), reads=[b_st], writes=[b_st])
    P.op("dve", lambda e: e.scalar_tensor_tensor(out=st[:, 15:16], in0=st[:, 12:13], scalar=-1.0, in1=st[:, 14:15],
                                                 op0=ALU.mult, op1=ALU.mult), reads=[b_st], writes=[b_st])
    P.op("act", lambda e: e.activation(out=u[:], in_=u[:], func=AF.Identity, scale=st[:, 14:15], bias=st[:, 15:16]),
         reads=[b_u, b_st], writes=[b_u])
    P.op("pool", lambda e: e.tensor_tensor(out=u[:], in0=u[:], in1=m["lg"][:], op=ALU.mult), reads=[b_u, m["b_lg"]], writes=[b_u])
    P.op("pool", lambda e: e.tensor_tensor(out=out_ap, in0=u[:], in1=m["lb"][:], op=ALU.add), reads=[b_u, m["b_lb"]], writes=[b_out])


def phase_ffn(P, l, f, sub, src_lat, src_ctx, dst_lat, dst_ctx, resid_w, skip_ctx=False):
    nc, io, g = P.nc, P.io, P.g
    psum, pbuf = P.psum, P.pbuf
    P.mark()
    m = phase_mod(P, l, sub, resid_w)
    wgu = P.sb([128, 8, 2 * DFF], BF16, "wgu")
    wdn = P.sb([128, 22, D], BF16, "wdn")
    b_wgu, b_wdn = Buf(), Buf()
    P.mark()
    stg = [P.sb([128, DFF], F32, "stg%d" % i) for i in range(2)]
    b_stg = [Buf(), Buf()]
    cast_engs = ["pool", "dve", "act"]
    n = 0
    for kc in range(8):
        for hf in range(2):
            s = n % 2
            P.dma(lambda e, s=s, kc=kc, hf=hf: e.dma_start(
                out=stg[s][:], in_=io["w_ffn_in"].ap()[l, f, kc * 128:(kc + 1) * 128, hf * DFF:(hf + 1) * DFF]),
                writes=[b_stg[s]])
            ce = cast_engs[n % 3]
            if ce == "act":
                P.op("act", lambda e, s=s, kc=kc, hf=hf: e.activation(out=wgu[:, kc, hf * DFF:(hf + 1) * DFF], in_=stg[s][:], func=AF.Copy),
                     reads=[b_stg[s]], writes=[b_wgu])
            else:
                P.op(ce, lambda e, s=s, kc=kc, hf=hf: e.tensor_copy(out=wgu[:, kc, hf * DFF:(hf + 1) * DFF], in_=stg[s][:]),
                     reads=[b_stg[s]], writes=[b_wgu])
            n += 1
    for pc in range(11):
        s = n % 2
        P.dma(lambda e, s=s, pc=pc: e.dma_start(
            out=stg[s][:, 0:2048].rearrange("p (c n) -> p c n", c=2),
            in_=io["w_ffn_out"].ap()[l, f, pc * 256:(pc + 1) * 256, :].rearrange("(c p) n -> p c n", p=128)),
            writes=[b_stg[s]])
        ce = cast_engs[n % 3]
        if ce == "act":
            P.op("act", lambda e, s=s, pc=pc: e.activation(out=wdn[:, 2 * pc:2 * pc + 2, :].rearrange("p c n -> p (c n)"),
                                                           in_=stg[s][:, 0:2048], func=AF.Copy),
                 reads=[b_stg[s]], writes=[b_wdn])
        else:
            P.op(ce, lambda e, s=s, pc=pc: e.tensor_copy(out=wdn[:, 2 * pc:2 * pc + 2, :].rearrange("p c n -> p (c n)"),
                                                         in_=stg[s][:, 0:2048]),
                 reads=[b_stg[s]], writes=[b_wdn])
        n += 1
    P.barrier()
    P.release()
    NB = 256
    xt = [P.sb([128, 2, D], F32, "xt%d" % i) for i in range(2)]
    b_xt = [[Buf(), Buf()], [Buf(), Buf()]]
    hT = P.sb([128, 8, NB], BF16, "hT")
    b_hT = Buf()
    aT = P.sb([128, 22, NB], BF16, "aT")
    b_aT = [Buf() for _ in range(22)]
    sg = [P.sb([128, NB], F32, "sg%d" % i) for i in range(2)]
    b_sg = [Buf(), Buf()]
    ot = [P.sb([128, D], F32, "ot%d" % i) for i in range(2)]
    b_ot = [Buf(), Buf()]
    work = dict(u=P.sb([128, D], F32, "u"), b_u=Buf(), st=P.sb([128, 16], F32, "st"), b_st=Buf())
    blocks = [(src_lat, dst_lat, i * NB, 0) for i in range(SEQ // NB)] + ([] if skip_ctx else [(src_ctx, dst_ctx, 0, 1)])
    pi = 0
    for bi, (src, dst, t0, r) in enumerate(blocks):
        xs_ = bi % 2
        for t in range(2):
            P.dma(lambda e, xs_=xs_, t=t, src=src, t0=t0: e.dma_start(out=xt[xs_][:, t, :], in_=src[t0 + t * 128:t0 + (t + 1) * 128, :]),
                  writes=[b_xt[xs_][t]])
        for kp in range(4):
            pb = pi % 8
            pi += 1
            for kk in range(2):
                kc = kp * 2 + kk
                for t in range(2):
                    P.op("pe", lambda e, pb=pb, kk=kk, t=t, kc=kc, xs_=xs_: e.transpose(
                        out=psum[pb][:, kk * 256 + t * 128: kk * 256 + (t + 1) * 128], in_=xt[xs_][:, t, kc * 128:(kc + 1) * 128],
                        identity=g["ident"][:]),
                        reads=[b_xt[xs_][t], g["b_ident"]], writes=[pbuf[pb]])
            for kk in range(2):
                kc = kp * 2 + kk
                P.op("act", lambda e, pb=pb, kk=kk, kc=kc, r=r: e.activation(
                    out=hT[:, kc, :], in_=psum[pb][:, kk * 256:(kk + 1) * 256], func=AF.Identity,
                    scale=m["scl"][:, r, kc:kc + 1], bias=m["shT"][:, r, kc:kc + 1]),
                    reads=[pbuf[pb], m["b_scl"], m["b_shT"]], writes=[b_hT])
        for i in range(22):
            pb = pi % 8
            pi += 1
            for hf in range(2):
                for kc in range(8):
                    P.op("pe", lambda e, pb=pb, hf=hf, kc=kc, i=i: e.matmul(
                        psum[pb][:, hf * 256:(hf + 1) * 256], lhsT=wgu[:, kc, hf * DFF + i * 128: hf * DFF + (i + 1) * 128],
                        rhs=hT[:, kc, :], start=(kc == 0), stop=(kc == 7)),
                        reads=[b_hT, b_wgu], writes=[pbuf[pb]])
            s = i % 2
            P.op("act", lambda e, pb=pb, s=s: e.activation(out=sg[s][:], in_=psum[pb][:, 0:256], func=AF.Silu),
                 reads=[pbuf[pb]], writes=[b_sg[s]])
            P.op("dve", lambda e, pb=pb, s=s, i=i: e.tensor_tensor(out=aT[:, i, :], in0=psum[pb][:, 256:512], in1=sg[s][:], op=ALU.mult),
                 reads=[pbuf[pb], b_sg[s]], writes=[b_aT[i]])
        for t in range(2):
            pbs = []
            for hn in range(2):
                pb = pi % 8
                pi += 1
                pbs.append(pb)
                for i in range(22):
                    P.op("pe", lambda e, pb=pb, hn=hn, i=i, t=t: e.matmul(
                        psum[pb][:, :], lhsT=aT[:, i, t * 128:(t + 1) * 128], rhs=wdn[:, i, hn * 512:(hn + 1) * 512],
                        start=(i == 0), stop=(i == 21)),
                        reads=[b_aT[i], b_wdn], writes=[pbuf[pb]])
            o = (bi * 2 + t) % 2
            emit_postnorm(P, m, r, xt[xs_][:, t, :], b_xt[xs_][t], [psum[pbs[0]][:, :], psum[pbs[1]][:, :]],
                          [pbuf[pbs[0]], pbuf[pbs[1]]], ot[o][:], b_ot[o], work)
            P.dma(lambda e, o=o, dst=dst, t0=t0, t=t: e.dma_start(out=dst[t0 + t * 128:t0 + (t + 1) * 128, :], in_=ot[o][:]),
                  reads=[b_ot[o]])
    P.barrier()
    P.release()


class Rot:
    def __init__(self, P):
        self.P = P
        self.pi = 0
        self.ei = 0

    def bank(self):
        b = self.pi % 8
        self.pi += 1
        return b

    def evac(self, out_ap, in_ap, reads, writes, eng=None):
        P = self.P
        if eng is None:
            eng = ("act", "dve")[self.ei % 2]
            self.ei += 1
        if eng == "act":
            return P.op("act", lambda e: e.activation(out=out_ap, in_=in_ap, func=AF.Copy), reads=reads, writes=writes)
        return P.op(eng, lambda e: e.tensor_copy(out=out_ap, in_=in_ap), reads=reads, writes=writes)


def emit_hT(P, rot, m, r, xt, b_xts, ntile, hT, b_hT):
    g, psum, pbuf = P.g, P.psum, P.pbuf
    for kc in range(8):
        pb = rot.bank()
        for t in range(ntile):
            P.op("pe", lambda e, pb=pb, t=t, kc=kc: e.transpose(
                out=psum[pb][:, t * 128:(t + 1) * 128], in_=xt[:, t, kc * 128:(kc + 1) * 128], identity=g["ident"][:]),
                reads=[b_xts[t], g["b_ident"]], writes=[pbuf[pb]])
        P.op("act", lambda e, pb=pb, kc=kc: e.activation(
            out=hT[:, kc, 0:ntile * 128], in_=psum[pb][:, 0:ntile * 128], func=AF.Identity,
            scale=m["scl"][:, r, kc:kc + 1], bias=m["shT"][:, r, kc:kc + 1]),
            reads=[pbuf[pb], m["b_scl"], m["b_shT"]], writes=[b_hT])


def load_cast_rows(P, dst, b_dst, src_rows_fn, nchunk, width, stg, b_stg, n0=0):
    cast_engs = ["pool", "dve", "act"]
    n = n0
    for kc in range(nchunk):
        s = n % 2
        P.dma(lambda e, s=s, kc=kc: e.dma_start(out=stg[s][:, 0:width], in_=src_rows_fn(kc)), writes=[b_stg[s]])
        ce = cast_engs[n % 3]
        if ce == "act":
            P.op("act", lambda e, s=s, kc=kc: e.activation(out=dst[:, kc, :], in_=stg[s][:, 0:width], func=AF.Copy),
                 reads=[b_stg[s]], writes=[b_dst])
        else:
            P.op(ce, lambda e, s=s, kc=kc: e.tensor_copy(out=dst[:, kc, :], in_=stg[s][:, 0:width]),
                 reads=[b_stg[s]], writes=[b_dst])
        n += 1
    return n


def phase_mixer(P, l, dbg):
    nc, io, g = P.nc, P.io, P.g
    psum, pbuf = P.psum, P.pbuf
    rot = Rot(P)
    P.mark()
    m = phase_mod(P, l, 1, 1.0)
    P.mark()
    qT = P.sb([128, 4, NTOK], BF16, "qT")
    kT = P.sb([128, 2, NTOK], BF16, "kT")
    vx = P.sb([128, 34, 2, 66], BF16, "vx")
    b_qT = [Buf() for _ in range(9)]
    b_kT, b_vx = Buf(), Buf()
    P.op("pool", lambda e: e.memset(vx[:], 1.0), writes=[b_vx])
    for hh in range(4):
        for c0 in range(0, NTOK, 1088):
            P.op("dve", lambda e, hh=hh, c0=c0: e.memset(qT[64:128, hh, c0:c0 + 1088], 0.0), writes=list(b_qT))
    for hh in range(2):
        for c0 in range(0, NTOK, 1088):
            P.op("dve", lambda e, hh=hh, c0=c0: e.memset(kT[64:128, hh, c0:c0 + 1088], 0.0), writes=[b_kT])
    P.mark()
    win = P.sb([128, 8, DIN], BF16, "win")
    b_win = Buf()
    P.mark()
    stg = [P.sb([128, DIN], F32, "stg%d" % i) for i in range(2)]
    b_stg = [Buf(), Buf()]
    load_cast_rows(P, win, b_win, lambda kc: io["w_in"].ap()[l, kc * 128:(kc + 1) * 128, :], 8, DIN, stg, b_stg)
    P.barrier()
    P.release()
    gq = P.sb([128, 384], F32, "gq")
    b_gq = Buf()
    for h in range(6):
        src = io["q_norm_g"] if h < 4 else io["k_norm_g"]
        P.dma(lambda e, h=h, src=src: e.dma_start(out=gq[:, h * 64:(h + 1) * 64], in_=src.ap()[l:l + 1, :].partition_broadcast(128)),
              writes=[b_gq])
    cos_t = P.sb([128, 32, 32], F32, "cos")
    sin_t = P.sb([128, 32, 32], F32, "sin")
    b_cs = Buf()
    P.dma(lambda e: e.dma_start(out=cos_t[:], in_=io["rope_cos"].ap().rearrange("(t p) i -> p t i", p=128)), writes=[b_cs])
    P.dma(lambda e: e.dma_start(out=sin_t[:], in_=io["rope_sin"].ap().rearrange("(t p) i -> p t i", p=128)), writes=[b_cs])
    dftc = P.sb([128, 128], BF16, "dftc")
    dfts = P.sb([128, 128], BF16, "dfts")
    b_dft = Buf()
    P.dma(lambda e: e.dma_start(out=dftc[:], in_=io["dftc"].ap()), writes=[b_dft])
    P.dma(lambda e: e.dma_start(out=dfts[:], in_=io["dfts"].ap()), writes=[b_dft])

    xt = [P.sb([128, 4, D], F32, "xt%d" % i) for i in range(2)]
    b_xt = [[Buf() for _ in range(4)] for _ in range(2)]
    hT = P.sb([128, 8, 512], BF16, "hT")
    b_hT = Buf()
    zat = [P.sb([128, 512], F32, "zat%d" % i) for i in range(2)]
    b_zat = [Buf(), Buf()]
    sq = P.sb([128, 384], F32, "sq")
    qk = P.sb([128, 384], F32, "qk")
    qkr = P.sb([128, 384], F32, "qkr")
    tA = P.sb([128, 192], F32, "tA")
    tB = P.sb([128, 192], F32, "tB")
    ss = P.sb([128, 8], F32, "ss")
    b_sq, b_qk, b_qkr, b_tA, b_tB, b_ss = Buf(), Buf(), Buf(), Buf(), Buf(), Buf()
    zst = [P.sb([64, 4, 512], F32, "zst%d" % i) for i in range(2)]
    b_zst = [Buf(), Buf()]
    zp = P.sb([128, 4, 256], BF16, "zp")
    b_zp = Buf()
    zfT = P.sb([128, 2, 512], BF16, "zfT")
    b_zfT = Buf()
    abst = P.sb([128, 4, 512], BF16, "abst")
    b_abst = Buf()

    xs_ap = io["xs"].ap()
    blocks = [(i * 512, 4, 0) for i in range(8)] + [(SEQ, 2, 1)]
    for bi, (t0, ntile, r) in enumerate(blocks):
        nb = ntile * 128
        xb = bi % 2
        for t in range(ntile):
            P.dma(lambda e, xb=xb, t=t, t0=t0: e.dma_start(out=xt[xb][:, t, :], in_=xs_ap[t0 + t * 128:t0 + (t + 1) * 128, :]),
                  writes=[b_xt[xb][t]])
        emit_hT(P, rot, m, r, xt[xb], b_xt[xb], ntile, hT, b_hT)
        for t in range(ntile):
            tile_idx = (t0 // 128) + t
            tok0 = t0 + t * 128
            pb = rot.bank()
            for kc in range(8):
                P.op("pe", lambda e, pb=pb, kc=kc, t=t: e.matmul(psum[pb][:, :], lhsT=hT[:, kc, t * 128:(t + 1) * 128], rhs=win[:, kc, 0:512],
                                                                 start=(kc == 0), stop=(kc == 7)),
                     reads=[b_hT, b_win], writes=[pbuf[pb]])
            z = (bi * 4 + t) % 2
            P.op("act", lambda e, pb=pb, z=z: e.activation(out=zat[z][:], in_=psum[pb][:, :], func=AF.Copy),
                 reads=[pbuf[pb]], writes=[b_zat[z]])
            P.op("dve", lambda e, z=z: e.tensor_tensor(out=sq[:], in0=zat[z][:, 0:384], in1=zat[z][:, 0:384], op=ALU.mult),
                 reads=[b_zat[z]], writes=[b_sq])
            P.op("dve", lambda e: e.tensor_reduce(out=ss[:, 0:6], in_=sq[:].rearrange("p (h d) -> p h d", d=64), axis=AX.X, op=ALU.add),
                 reads=[b_sq], writes=[b_ss])
            P.op("dve", lambda e: e.tensor_scalar(out=ss[:, 0:6], in0=ss[:, 0:6], scalar1=1.0 / 64, scalar2=1e-6, op0=ALU.mult, op1=ALU.add),
                 reads=[b_ss], writes=[b_ss])
            P.op("act", lambda e: e.activation(out=ss[:, 0:6], in_=ss[:, 0:6], func=AF.Sqrt), reads=[b_ss], writes=[b_ss])
            P.op("dve", lambda e: e.reciprocal(out=ss[:, 0:6], in_=ss[:, 0:6]), reads=[b_ss], writes=[b_ss])
            P.op("dve", lambda e, z=z: e.tensor_tensor(out=qk[:].rearrange("p (h d) -> p h d", d=64),
                                                       in0=zat[z][:, 0:384].rearrange("p (h d) -> p h d", d=64),
                                                       in1=ss[:, 0:6].unsqueeze(2).broadcast_to([128, 6, 64]), op=ALU.mult),
                 reads=[b_zat[z], b_ss], writes=[b_qk])
            if r == 0:
                P.op("pool", lambda e: e.tensor_tensor(out=qk[:], in0=qk[:], in1=gq[:], op=ALU.mult), reads=[b_qk, b_gq], writes=[b_qk])
                v4 = lambda tl: tl[:].rearrange("p (h i two) -> p h i two", h=6, i=32, two=2)
                x0, x1 = v4(qk)[:, :, :, 0], v4(qk)[:, :, :, 1]
                o0, o1 = v4(qkr)[:, :, :, 0], v4(qkr)[:, :, :, 1]
                cb = cos_t[:, tile_idx:tile_idx + 1, :].broadcast_to([128, 6, 32])
                sb_ = sin_t[:, tile_idx:tile_idx + 1, :].broadcast_to([128, 6, 32])
                a3 = lambda tl: tl[:].rearrange("p (h i) -> p h i", h=6)
                P.op("dve", lambda e, x0=x0, cb=cb: e.tensor_tensor(out=a3(tA), in0=x0, in1=cb, op=ALU.mult), reads=[b_qk, b_cs], writes=[b_tA])
                P.op("pool", lambda e, x1=x1, sb_=sb_: e.tensor_tensor(out=a3(tB), in0=x1, in1=sb_, op=ALU.mult), reads=[b_qk, b_cs], writes=[b_tB])
                P.op("dve", lambda e, o0=o0: e.tensor_tensor(out=o0, in0=a3(tA), in1=a3(tB), op=ALU.subtract), reads=[b_tA, b_tB], writes=[b_qkr])
                P.op("pool", lambda e, x0=x0, sb_=sb_: e.tensor_tensor(out=a3(tA), in0=x0, in1=sb_, op=ALU.mult), reads=[b_qk, b_cs], writes=[b_tA])
                P.op("dve", lambda e, x1=x1, cb=cb: e.tensor_tensor(out=a3(tB), in0=x1, in1=cb, op=ALU.mult), reads=[b_qk, b_cs], writes=[b_tB])
                P.op("pool", lambda e, o1=o1: e.tensor_tensor(out=o1, in0=a3(tA), in1=a3(tB), op=ALU.add), reads=[b_tA, b_tB], writes=[b_qkr])
            else:
                P.op("pool", lambda e: e.tensor_tensor(out=qkr[:], in0=qk[:], in1=gq[:], op=ALU.mult), reads=[b_qk, b_gq], writes=[b_qkr])
            pq = rot.bank()
            for h in range(4):
                P.op("pe", lambda e, pq=pq, h=h: e.transpose(out=psum[pq][0:64, h * 128:(h + 1) * 128], in_=qkr[:, h * 64:(h + 1) * 64],
                                                             identity=g["ident"][:]),
                     reads=[b_qkr, g["b_ident"]], writes=[pbuf[pq]])
            rot.evac(qT[0:64, :, tok0:tok0 + 128], psum[pq][0:64, :].rearrange("p (h t) -> p h t", h=4), [pbuf[pq]], [b_qT[bi]])
            pk = rot.bank()
            for h in range(2):
                P.op("pe", lambda e, pk=pk, h=h: e.transpose(out=psum[pk][0:64, h * 128:(h + 1) * 128], in_=qkr[:, (4 + h) * 64:(5 + h) * 64],
                                                             identity=g["ident"][:]),
                     reads=[b_qkr, g["b_ident"]], writes=[pbuf[pk]])
            rot.evac(kT[0:64, :, tok0:tok0 + 128], psum[pk][0:64, 0:256].rearrange("p (h t) -> p h t", h=2), [pbuf[pk]], [b_kT])
            P.op("pool", lambda e, z=z, tile_idx=tile_idx: e.tensor_copy(out=vx[:, tile_idx, :, 0:64],
                                                                         in_=zat[z][:, 384:512].rearrange("p (h d) -> p h d", h=2)),
                 reads=[b_zat[z]], writes=[b_vx])
        for kind in range(4):
            zs = kind % 2
            for h in range(4):
                pb = rot.bank()
                c0 = 512 + kind * 256 + h * 64
                for kc in range(8):
                    P.op("pe", lambda e, pb=pb, kc=kc, c0=c0, nb=nb: e.matmul(psum[pb][0:64, 0:nb], lhsT=win[:, kc, c0:c0 + 64], rhs=hT[:, kc, 0:nb],
                                                                            start=(kc == 0), stop=(kc == 7)),
                         reads=[b_hT, b_win], writes=[pbuf[pb]])
                rot.evac(zst[zs][:, h, 0:nb], psum[pb][0:64, 0:nb], [pbuf[pb]], [b_zst[zs]])
            P.dma(lambda e, zs=zs, kind=kind, t0=t0, nb=nb: e.dma_start(out=io["zrw"].ap()[kind, :, :, t0:t0 + nb], in_=zst[zs][:, :, 0:nb]),
                  reads=[b_zst[zs]])
        for t in range(ntile):
            pb = rot.bank()
            for kc in range(8):
                P.op("pe", lambda e, pb=pb, kc=kc, t=t: e.matmul(psum[pb][:, 0:256], lhsT=hT[:, kc, t * 128:(t + 1) * 128], rhs=win[:, kc, 1536:1792],
                                                                 start=(kc == 0), stop=(kc == 7)),
                     reads=[b_hT, b_win], writes=[pbuf[pb]])
            rot.evac(zp[:, t, :], psum[pb][:, 0:256], [pbuf[pb]], [b_zp])
        P.dma(lambda e, t0=t0, ntile=ntile, nb=nb: e.dma_start(out=io["zpool"].ap()[t0:t0 + nb, :].rearrange("(t p) c -> p t c", p=128),
                                                               in_=zp[:, 0:ntile, :]), reads=[b_zp])
        for c in range(2):
            pb = rot.bank()
            c0 = 1792 + c * 128
            for kc in range(8):
                P.op("pe", lambda e, pb=pb, kc=kc, c0=c0, nb=nb: e.matmul(psum[pb][:, 0:nb], lhsT=win[:, kc, c0:c0 + 128], rhs=hT[:, kc, 0:nb],
                                                                        start=(kc == 0), stop=(kc == 7)),
                     reads=[b_hT, b_win], writes=[pbuf[pb]])
            rot.evac(zfT[:, c, 0:nb], psum[pb][:, 0:nb], [pbuf[pb]], [b_zfT])
        for t in range(ntile):
            pb = rot.bank()
            for ab_i, mat in enumerate((dftc, dfts)):
                for c in range(2):
                    P.op("pe", lambda e, pb=pb, ab_i=ab_i, c=c, t=t, mat=mat: e.matmul(
                        psum[pb][:, ab_i * 256 + c * 128: ab_i * 256 + (c + 1) * 128], lhsT=zfT[:, c, t * 128:(t + 1) * 128], rhs=mat[:],
                        start=True, stop=True), reads=[b_zfT, b_dft], writes=[pbuf[pb]])
            rot.evac(abst[:, t, :], psum[pb][:, :], [pbuf[pb]], [b_abst])
        P.dma(lambda e, t0=t0, ntile=ntile, nb=nb: e.dma_start(out=io["ab"].ap()[t0:t0 + nb, :].rearrange("(t p) c -> p t c", p=128),
                                                               in_=abst[:, 0:ntile, :]), reads=[b_abst])
    P.barrier()
    P.release()
    if dbg == "mixA":
        P.release()
        P.release()
        return
    P.mark()
    pT = [P.sb([128, 512], BF16, "pT%d" % i) for i in range(4)]
    b_pT = [Buf(), Buf(), Buf(), Buf()]
    rden = P.sb([128, 512], F32, "rden")
    b_rden = Buf()
    osb = P.sb([64, 512], F32, "osb")
    b_osb = Buf()
    ost = [P.sb([64, 512], BF16, "ost%d" % i) for i in range(2)]
    b_ost = [Buf(), Buf()]
    n_p = 0
    n_o = 0
    cat = io["cat"].ap()
    busy = set()
    pending_tail = [None]
    for bi, (t0, ntile, r) in enumerate(blocks):
        nb = ntile * 128
        key_tiles = list(range(34)) if r == 0 else [32, 33]
        for h in range(4):
            kvh = h // 2
            pacc = rot.bank()
            while pacc in busy:
                pacc = rot.bank()
            nk = len(key_tiles)
            pend = {}

            def qk_exp(ki, pacc=pacc, kvh=kvh, h=h, t0=t0, nb=nb, bi=bi):
                nonlocal n_p
                kt = key_tiles[ki]
                ps_ = rot.bank()
                while ps_ == pacc or ps_ in busy:
                    ps_ = rot.bank()
                P.op("pe", lambda e: e.matmul(psum[ps_][:, 0:nb], lhsT=kT[:, kvh, kt * 128:(kt + 1) * 128], rhs=qT[:, h, t0:t0 + nb],
                                              start=True, stop=True), reads=[b_kT, b_qT[bi]], writes=[pbuf[ps_]])
                pp = n_p % 4
                n_p += 1
                P.op("act", lambda e: e.activation(out=pT[pp][:, 0:nb], in_=psum[ps_][:, 0:nb], func=AF.Exp, scale=0.125),
                     reads=[pbuf[ps_]], writes=[b_pT[pp]])
                pend[ki] = pp

            def pv(ki, pacc=pacc, kvh=kvh, nb=nb, nk=nk):
                kt = key_tiles[ki]
                pp = pend.pop(ki)
                P.op("pe", lambda e: e.matmul(psum[pacc][0:65, 0:nb], lhsT=vx[:, kt, kvh, 0:65], rhs=pT[pp][:, 0:nb],
                                              start=(ki == 0), stop=(ki == nk - 1)), reads=[b_vx, b_pT[pp]], writes=[pbuf[pacc]])

            def tail(pacc=pacc, nb=nb, h=h, t0=t0):
                nonlocal n_o
                P.op("dve", lambda e: e.reciprocal(out=rden[64:65, 0:nb], in_=psum[pacc][64:65, 0:nb]), reads=[pbuf[pacc]], writes=[b_rden])
                P.op("act", lambda e: e.activation(out=osb[:, 0:nb], in_=psum[pacc][0:64, 0:nb], func=AF.Copy), reads=[pbuf[pacc]], writes=[b_osb])
                pbc = rot.bank()
                while pbc == pacc or pbc in busy:
                    pbc = rot.bank()
                P.op("pe", lambda e: e.matmul(psum[pbc][0:64, 0:nb], lhsT=g["ones"][64:65, 0:64], rhs=rden[64:65, 0:nb], start=True, stop=True),
                     reads=[g["b_ones"], b_rden], writes=[pbuf[pbc]])
                oo = n_o % 2
                n_o += 1
                P.op("dve", lambda e: e.tensor_tensor(out=ost[oo][:, 0:nb], in0=psum[pbc][0:64, 0:nb], in1=osb[:, 0:nb], op=ALU.mult),
                     reads=[pbuf[pbc], b_osb], writes=[b_ost[oo]])
                P.dma(lambda e: e.dma_start(out=cat[h * 64:(h + 1) * 64, t0:t0 + nb], in_=ost[oo][:, 0:nb]), reads=[b_ost[oo]])

            LOOK = 3
            for ki in range(min(LOOK, nk)):
                qk_exp(ki)
            if pending_tail[0] is not None:
                pending_tail[0]()
                busy.clear()
            for ki in range(nk):
                pv(ki)
                if ki + LOOK < nk:
                    qk_exp(ki + LOOK)
            pending_tail[0] = tail
            busy.add(pacc)
    pending_tail[0]()
    P.barrier()
    P.release()
    P.release()
    if dbg == "att":
        P.release()
        return
    phase_rwkv_prep(P, l)
    if dbg == "rw1":
        P.release()
        return
    phase_rwkv_scan(P, l, nsteps=(int(dbg[3:]) if dbg and dbg.startswith("rws") else 34))
    if dbg and dbg.startswith("rws"):
        P.release()
        return
    phase_rwkv_fin(P, l)
    if dbg == "rw3":
        P.release()
        return
    phase_pool(P, l)
    phase_fourier(P, l)
    if dbg == "pf":
        P.release()
        return
    phase_wout(P, l, m)
    P.release()


def phase_pool(P, l):
    nc, io, g = P.nc, P.io, P.g
    psum, pbuf = P.psum, P.pbuf
    rot = Rot(P)
    P.mark()
    zp = P.sb([128, 34, 256], BF16, "zp_all")
    b_zp = Buf()
    P.dma(lambda e: e.dma_start(out=zp[:], in_=io["zpool"].ap().rearrange("(t p) c -> p t c", p=128)), writes=[b_zp])
    pm = P.sb([128, 20, 128], BF16, "poolm")
    b_pm = Buf()
    P.dma(lambda e: e.dma_start(out=pm[:], in_=io["poolm"].ap().rearrange("k s t -> s k t")), writes=[b_pm])
    pwf = P.sb([64, 4, 64], F32, "pwf")
    pw = P.sb([64, 4, 64], BF16, "pw")
    b_pwf, b_pw = Buf(), Buf()
    P.dma(lambda e: e.dma_start(out=pwf[:], in_=io["pool_w"].ap()[l].rearrange("g c d -> c g d")), writes=[b_pwf])
    P.op("dve", lambda e: e.tensor_copy(out=pw[:], in_=pwf[:]), reads=[b_pwf], writes=[b_pw])
    psc = P.sb([64, 4], F32, "psc")
    b_psc = Buf()
    P.dma(lambda e: e.dma_start(out=psc[:], in_=io["pool_scale"].ap()[l, :].rearrange("(g d) -> d g", d=64),
                                allow_slow_non_contiguous=True), writes=[b_psc])
    pooled = [P.sb([64, 512], BF16, "pooled%d" % i) for i in range(2)]
    b_pooled = [Buf(), Buf()]
    ost = [P.sb([64, 512], BF16, "post%d" % i) for i in range(2)]
    b_ost = [Buf(), Buf()]
    cat = io["cat"].ap()
    n = 0
    seqs = [(0, 32), (32, 2)]
    for (tile0, nt) in seqs:
        for j0 in range(0, nt, 4):
            ntile = min(4, nt - j0)
            nb = ntile * 128
            for gi in range(4):
                pb = rot.bank()
                for jj in range(ntile):
                    j = j0 + jj
                    terms = []
                    if j > 0:
                        terms.append((tile0 + j - 1, 0))
                    terms.append((tile0 + j, 3 if j == 0 else (4 if j == nt - 1 else 2)))
                    if j < nt - 1:
                        terms.append((tile0 + j + 1, 1))
                    for ti, (st, kind) in enumerate(terms):
                        P.op("pe", lambda e, pb=pb, jj=jj, st=st, kind=kind, gi=gi, ti=ti, nterm=len(terms): e.matmul(
                            psum[pb][0:64, jj * 128:(jj + 1) * 128], lhsT=zp[:, st, gi * 64:(gi + 1) * 64], rhs=pm[:, gi * 5 + kind, :],
                            start=(ti == 0), stop=(ti == nterm - 1)), reads=[b_zp, b_pm], writes=[pbuf[pb]])
                k = n % 2
                n += 1
                rot.evac(pooled[k][:, 0:nb], psum[pb][0:64, 0:nb], [pbuf[pb]], [b_pooled[k]])
                pb2 = rot.bank()
                P.op("pe", lambda e, pb2=pb2, gi=gi, k=k, nb=nb: e.matmul(psum[pb2][0:64, 0:nb], lhsT=pw[:, gi, :], rhs=pooled[k][:, 0:nb],
                                                                        start=True, stop=True), reads=[b_pw, b_pooled[k]], writes=[pbuf[pb2]])
                P.op("act", lambda e, pb2=pb2, gi=gi, k=k, nb=nb: e.activation(out=ost[k][:, 0:nb], in_=psum[pb2][0:64, 0:nb], func=AF.Copy,
                                                                              scale=psc[:, gi:gi + 1]),
                     reads=[pbuf[pb2], b_psc], writes=[b_ost[k]])
                tok0 = (tile0 + j0) * 128
                P.dma(lambda e, k=k, gi=gi, tok0=tok0, nb=nb: e.dma_start(out=cat[512 + gi * 64:512 + (gi + 1) * 64, tok0:tok0 + nb],
                                                                       in_=ost[k][:, 0:nb]), reads=[b_ost[k]])
    P.barrier()
    P.release()


def phase_fourier(P, l):
    nc, io, g = P.nc, P.io, P.g
    psum, pbuf = P.psum, P.pbuf
    rot = Rot(P)
    P.mark()
    ab = P.sb([128, 34, 512], BF16, "ab_all")
    b_ab = Buf()
    for q in range(2):
        P.dma(lambda e, q=q: e.dma_start(out=ab[:, q * 17:(q + 1) * 17, :],
                                         in_=io["ab"].ap()[q * 17 * 128:(q + 1) * 17 * 128, :].rearrange("(t p) c -> p t c", p=128)),
              writes=[b_ab])
    fwf = P.sb([128, 2, 256], F32, "fwf")
    fw = P.sb([128, 2, 256], BF16, "fw")
    b_fwf, b_fw = Buf(), Buf()
    P.dma(lambda e: e.dma_start(out=fwf[:], in_=io["fourier_w"].ap()[l].rearrange("(c p) d -> p c d", p=128)), writes=[b_fwf])
    P.op("dve", lambda e: e.tensor_copy(out=fw[:], in_=fwf[:]), reads=[b_fwf], writes=[b_fw])
    dm = [[P.sb([128, 32, 512], BF16, "dm%d_%d" % (i, j)) for j in range(2)] for i in range(2)]
    b_dm = [[Buf(), Buf()], [Buf(), Buf()]]
    fT = [P.sb([128, 2, 512], BF16, "fT%d" % i) for i in range(2)]
    b_fT = [[Buf(), Buf()], [Buf(), Buf()]]
    ost = [P.sb([128, 512], BF16, "fost%d" % i) for i in range(2)]
    b_ost = [Buf(), Buf()]
    cat = io["cat"].ap()
    jobs = [(0, 32, tb * 512, 512, "dft_lat") for tb in range(8)] + [(32, 2, 0, 256, "dft_ctx")]
    n_o = 0
    for ji, (tile0, nt, c0, wd, mname) in enumerate(jobs):
        bsel = ji % 2
        for cs in range(2):
            P.dma(lambda e, bsel=bsel, cs=cs, nt=nt, c0=c0, wd=wd, mname=mname: e.dma_start(
                out=dm[bsel][cs][:, 0:nt, 0:wd], in_=io[mname].ap()[cs, :, c0:c0 + wd].rearrange("(t p) n -> p t n", p=128)),
                writes=[b_dm[bsel][cs]])
        for c in range(2):
            pb = rot.bank()
            for cs in range(2):
                for t in range(nt):
                    P.op("pe", lambda e, pb=pb, cs=cs, t=t, c=c, bsel=bsel, wd=wd, tile0=tile0, nt=nt: e.matmul(
                        psum[pb][:, 0:wd], lhsT=ab[:, tile0 + t, cs * 256 + c * 128: cs * 256 + (c + 1) * 128], rhs=dm[bsel][cs][:, t, 0:wd],
                        start=(cs == 0 and t == 0), stop=(cs == 1 and t == nt - 1)),
                        reads=[b_ab, b_dm[bsel][cs]], writes=[pbuf[pb]])
            rot.evac(fT[bsel][:, c, 0:wd], psum[pb][:, 0:wd], [pbuf[pb]], [b_fT[bsel][c]])
        for dc in range(2):
            pb = rot.bank()
            for c in range(2):
                P.op("pe", lambda e, pb=pb, c=c, dc=dc, bsel=bsel, wd=wd: e.matmul(
                    psum[pb][:, 0:wd], lhsT=fw[:, c, dc * 128:(dc + 1) * 128], rhs=fT[bsel][:, c, 0:wd], start=(c == 0), stop=(c == 1)),
                    reads=[b_fw, b_fT[bsel][c]], writes=[pbuf[pb]])
            k = n_o % 2
            n_o += 1
            rot.evac(ost[k][:, 0:wd], psum[pb][:, 0:wd], [pbuf[pb]], [b_ost[k]])
            tok0 = tile0 * 128 + c0
            P.dma(lambda e, k=k, dc=dc, tok0=tok0, wd=wd: e.dma_start(out=cat[768 + dc * 128:768 + (dc + 1) * 128, tok0:tok0 + wd],
                                                                   in_=ost[k][:, 0:wd]), reads=[b_ost[k]])
    P.barrier()
    P.release()


def phase_wout(P, l, m):
    nc, io, g = P.nc, P.io, P.g
    psum, pbuf = P.psum, P.pbuf
    rot = Rot(P)
    P.mark()
    wo = P.sb([128, 8, D], BF16, "wo")
    b_wo = Buf()
    P.mark()
    stg = [P.sb([128, D], F32, "stg%d" % i) for i in range(2)]
    b_stg = [Buf(), Buf()]
    load_cast_rows(P, wo, b_wo, lambda kc: io["w_out"].ap()[l, kc * 128:(kc + 1) * 128, :], 8, D, stg, b_stg)
    P.barrier()
    P.release()
    ct = [P.sb([128, 8, 512], BF16, "ct%d" % i) for i in range(2)]
    b_ct = [Buf(), Buf()]
    xt = [P.sb([128, D], F32, "xt%d" % i) for i in range(2)]
    b_xt = [Buf(), Buf()]
    ot = [P.sb([128, D], F32, "ot%d" % i) for i in range(2)]
    b_ot = [Buf(), Buf()]
    work = dict(u=P.sb([128, D], F32, "u"), b_u=Buf(), st=P.sb([128, 16], F32, "st"), b_st=Buf())
    cat = io["cat"].ap()
    xs_ap = io["xs"].ap()
    blocks = [(i * 512, 4, 0) for i in range(8)] + [(SEQ, 2, 1)]
    n = 0
    for bi, (t0, ntile, r) in enumerate(blocks):
        nb = ntile * 128
        cb = bi % 2
        P.dma(lambda e, cb=cb, t0=t0, nb=nb: e.dma_start(out=ct[cb][:, :, 0:nb], in_=cat[:, t0:t0 + nb].rearrange("(c p) t -> p c t", p=128)),
              writes=[b_ct[cb]])
        for t in range(ntile):
            k = n % 2
            n += 1
            tok0 = t0 + t * 128
            P.dma(lambda e, k=k, tok0=tok0: e.dma_start(out=xt[k][:], in_=xs_ap[tok0:tok0 + 128, :]), writes=[b_xt[k]])
            pbs = []
            for hn in range(2):
                pb = rot.bank()
                pbs.append(pb)
                for c in range(8):
                    P.op("pe", lambda e, pb=pb, c=c, hn=hn, cb=cb, t=t: e.matmul(
                        psum[pb][:, :], lhsT=ct[cb][:, c, t * 128:(t + 1) * 128], rhs=wo[:, c, hn * 512:(hn + 1) * 512],
                        start=(c == 0), stop=(c == 7)), reads=[b_ct[cb], b_wo], writes=[pbuf[pb]])
            emit_postnorm(P, m, r, xt[k][:], b_xt[k], [psum[pbs[0]][:, :], psum[pbs[1]][:, :]], [pbuf[pbs[0]], pbuf[pbs[1]]],
                          ot[k][:], b_ot[k], work)
            P.dma(lambda e, k=k, tok0=tok0: e.dma_start(out=xs_ap[tok0:tok0 + 128, :], in_=ot[k][:]), reads=[b_ot[k]])
    P.barrier()
    P.release()


LOGDECAY_SCALE = -0.6065306597126334
GN_EPS = 64e-5
CHUNK_ORDER = {0: [32, 33] + list(range(32)), 1: [33, 32] + list(range(31, -1, -1))}
RW_SEQS = [(i * 512, 512, i == 0, i == 7) for i in range(8)] + [(SEQ, 256, True, True)]


def col_param(P, src_ap_1d, name, n=4):
    t = P.sb([64, n], F32, name)
    b = Buf()
    P.dma(lambda e: e.dma_start(out=t[:], in_=src_ap_1d.rearrange("(h d) -> d h", d=64), allow_slow_non_contiguous=True), writes=[b])
    return t, b


def load_halo(P, buf_ap_fn, b_buf, src_fn, t0, nb, first, last):
    lo = 0 if not first else 1
    hi = nb + 2 if not last else nb + 1
    if first:
        P.op("pool", lambda e: e.memset(buf_ap_fn(0, 1), 0.0), writes=[b_buf])
    if last:
        P.op("pool", lambda e: e.memset(buf_ap_fn(nb + 1, nb + 2), 0.0), writes=[b_buf])
    P.dma(lambda e: e.dma_start(out=buf_ap_fn(lo, hi), in_=src_fn(t0 - 1 + lo, t0 - 1 + hi)), writes=[b_buf])


def phase_rwkv_prep(P, l):
    nc, io, g = P.nc, P.io, P.g
    psum, pbuf = P.psum, P.pbuf
    rot = Rot(P)
    P.mark()
    mu = [col_param(P, io["rwkv_mu"].ap()[l, i, :], "mu%d" % i) for i in range(6)]
    w0 = [col_param(P, io["decay_w0"].ap()[l, d, :], "w0%d" % d) for d in range(2)]
    a0 = [col_param(P, io["icl_a0"].ap()[l, d, :], "a0%d" % d) for d in range(2)]
    k_k = col_param(P, io["k_k"].ap()[l, :], "k_k")
    k_a = col_param(P, io["k_a"].ap()[l, :], "k_a")
    r_k = col_param(P, io["r_k"].ap()[l].rearrange("h d -> (h d)"), "r_k")
    hm, om = [], []
    for i in range(3):
        t1 = P.sb([64, 4], F32, "hm%d" % i)
        t2 = P.sb([64, 4], F32, "om%d" % i)
        b1, b2 = Buf(), Buf()
        P.op("dve", lambda e, i=i, t1=t1: e.tensor_scalar(out=t1[:], in0=mu[i][0][:], scalar1=0.5, scalar2=None, op0=ALU.mult),
             reads=[mu[i][1]], writes=[b1])
        P.op("dve", lambda e, i=i, t2=t2: e.tensor_scalar(out=t2[:], in0=mu[i][0][:], scalar1=-1.0, scalar2=1.0, op0=ALU.mult, op1=ALU.add),
             reads=[mu[i][1]], writes=[b2])
        hm.append((t1, b1))
        om.append((t2, b2))
    omka = P.sb([64, 4], F32, "omka")
    b_omka = Buf()
    P.op("dve", lambda e: e.tensor_scalar(out=omka[:], in0=k_a[0][:], scalar1=-1.0, scalar2=1.0, op0=ALU.mult, op1=ALU.add),
         reads=[k_a[1]], writes=[b_omka])

    def lora_in(src, name, rank):
        tf = P.sb([64, 4, rank], F32, name + "f")
        tb = P.sb([64, 4, rank], BF16, name)
        bf_, bb = Buf(), Buf()
        P.dma(lambda e: e.dma_start(out=tf[:], in_=src.rearrange("(h d) r -> d h r", d=64)), writes=[bf_])
        P.op("dve", lambda e: e.tensor_copy(out=tb[:], in_=tf[:]), reads=[bf_], writes=[bb])
        return tb, bb

    def lora_out(src, name, rank):
        tf = P.sb([rank, 256], F32, name + "f")
        tb = P.sb([rank, 256], BF16, name)
        bf_, bb = Buf(), Buf()
        P.dma(lambda e: e.dma_start(out=tf[:], in_=src), writes=[bf_])
        P.op("dve", lambda e: e.tensor_copy(out=tb[:], in_=tf[:]), reads=[bf_], writes=[bb])
        return tb, bb

    W1 = [lora_in(io["decay_w1"].ap()[l, d], "W1_%d" % d, 32) for d in range(2)]
    A1 = [lora_in(io["icl_a1"].ap()[l, d], "A1_%d" % d, 32) for d in range(2)]
    G1 = lora_in(io["gate_g1"].ap()[l], "G1", 64)
    W2 = [lora_out(io["decay_w2"].ap()[l, d], "W2_%d" % d, 32) for d in range(2)]
    A2 = [lora_out(io["icl_a2"].ap()[l, d], "A2_%d" % d, 32) for d in range(2)]
    G2 = lora_out(io["gate_g2"].ap()[l], "G2", 64)
    tw = P.sb([32, 2, NTOK], BF16, "tw")
    ta = P.sb([32, 2, NTOK], BF16, "ta")
    tg = P.sb([64, NTOK], BF16, "tg")
    b_tw, b_ta, b_tg = [Buf(), Buf()], [Buf(), Buf()], Buf()
    etot = P.sb([64, 8, 34], F32, "etot")
    b_etot = Buf()
    zrw = io["zrw"].ap()
    P.mark()
    zu = [P.sb([64, 4, 514], F32, "zu%d" % i) for i in range(2)]
    b_zu = [Buf(), Buf()]
    ssum = P.sb([64, 4, 512], F32, "ssum")
    du = P.sb([64, 4, 512], F32, "du")
    b_ssum, b_du = Buf(), Buf()
    xq = [P.sb([64, 4, 512], BF16, "xq%d" % i) for i in range(3)]
    b_xq = [Buf(), Buf(), Buf()]
    for bi, (t0, nb, first, last) in enumerate(RW_SEQS):
        z = bi % 2
        load_halo(P, lambda a, b, z=z: zu[z][:, :, a:b], b_zu[z], lambda a, b: zrw[3, :, :, a:b], t0, nb, first, last)
        P.op("dve", lambda e, z=z, nb=nb: e.tensor_tensor(out=ssum[:, :, 0:nb], in0=zu[z][:, :, 0:nb], in1=zu[z][:, :, 2:nb + 2], op=ALU.add),
             reads=[b_zu[z]], writes=[b_ssum])
        P.op("dve", lambda e, z=z, nb=nb: e.scalar_tensor_tensor(out=du[:, :, 0:nb], in0=ssum[:, :, 0:nb], scalar=0.5, in1=zu[z][:, :, 1:nb + 1],
                                                                 op0=ALU.mult, op1=ALU.subtract), reads=[b_ssum, b_zu[z]], writes=[b_du])
        for j in range(3):
            for h in range(4):
                eng = "dve" if (j * 4 + h) % 2 == 0 else "pool"
                if eng == "dve":
                    P.op("dve", lambda e, j=j, h=h, z=z, nb=nb: e.scalar_tensor_tensor(
                        out=xq[j][:, h, 0:nb], in0=du[:, h, 0:nb], scalar=mu[3 + j][0][:, h:h + 1], in1=zu[z][:, h, 1:nb + 1],
                        op0=ALU.mult, op1=ALU.add), reads=[b_du, b_zu[z], mu[3 + j][1]], writes=[b_xq[j]])
                else:
                    P.op("pool", lambda e, j=j, h=h, nb=nb: e.tensor_scalar(
                        out=xq[j][:, h, 0:nb], in0=du[:, h, 0:nb], scalar1=mu[3 + j][0][:, h:h + 1], scalar2=None, op0=ALU.mult),
                        reads=[b_du, mu[3 + j][1]], writes=[b_xq[j]])
                    P.op("pool", lambda e, j=j, h=h, z=z, nb=nb: e.tensor_tensor(
                        out=xq[j][:, h, 0:nb], in0=xq[j][:, h, 0:nb], in1=zu[z][:, h, 1:nb + 1], op=ALU.add),
                        reads=[b_xq[j], b_zu[z]], writes=[b_xq[j]])
        jobs = [(0, W1[0], 32, tw[:, 0, t0:t0 + nb], b_tw[0], AF.Tanh), (0, W1[1], 32, tw[:, 1, t0:t0 + nb], b_tw[1], AF.Tanh),
                (1, A1[0], 32, ta[:, 0, t0:t0 + nb], b_ta[0], AF.Copy), (1, A1[1], 32, ta[:, 1, t0:t0 + nb], b_ta[1], AF.Copy),
                (2, G1, 64, tg[:, t0:t0 + nb], b_tg, AF.Sigmoid)]
        for (j, wt, rank, dst, b_dst, fn) in jobs:
            pb = rot.bank()
            for h in range(4):
                P.op("pe", lambda e, pb=pb, h=h, j=j, wt=wt, rank=rank, nb=nb: e.matmul(
                    psum[pb][0:rank, 0:nb], lhsT=wt[0][:, h, :], rhs=xq[j][:, h, 0:nb], start=(h == 0), stop=(h == 3)),
                    reads=[wt[1], b_xq[j]], writes=[pbuf[pb]])
            P.op("act", lambda e, pb=pb, rank=rank, nb=nb, dst=dst, fn=fn: e.activation(out=dst, in_=psum[pb][0:rank, 0:nb], func=fn),
                 reads=[pbuf[pb]], writes=[b_dst])
    P.barrier()
    P.release()
    P.mark()
    z3 = [P.sb([64, 3, 514], F32, "z3_%d" % i) for i in range(2)]
    b_z3 = [Buf(), Buf()]
    s3 = P.sb([64, 3, 512], F32, "s3")
    b_s3 = Buf()
    rk = P.sb([64, 2, 512], F32, "rk")
    b_r, b_k = Buf(), Buf()
    Fs = [[P.sb([64, 5, 512], F32, "F%d_%d" % (i, d)) for d in range(2)] for i in range(2)]
    b_F = [[Buf(), Buf()], [Buf(), Buf()]]
    kkr = P.sb([64, 512], F32, "kkr")
    kk = P.sb([64, 512], F32, "kk")
    sqt = P.sb([64, 512], F32, "sqt")
    nrm = P.sb([64, 512], F32, "nrm")
    b_kkr, b_kk, b_sqt, b_nrm = Buf(), Buf(), Buf(), Buf()
    TD = []
    for d_ in range(2):
        td = {}
        for nm in ("lw", "cl", "ci", "cml", "einc", "eexc", "einv", "av", "tt", "kd", "tmpb"):
            td[nm] = P.sb([64, 512], F32, "%s%d" % (nm, d_))
            td["b_" + nm] = Buf()
        td["tot"] = P.sb([64, 4], F32, "tot%d" % d_)
        td["b_tot"] = Buf()
        TD.append(td)
    kds = P.sb([64, 512], F32, "kds")
    tmpb = P.sb([64, 512], F32, "tmpb")
    b_kds, b_tmpb = Buf(), Buf()
    aux = [P.sb([64, 2, 512], F32, "aux%d" % i) for i in range(2)]
    b_aux = [Buf(), Buf()]
    rwt = io["rwt"].ap()
    def head_block(h, t0, nb, first, last, n_it):
        if True:
            hs = slice(h, h + 1)
            nch = nb // 128
            z = n_it % 2
            fi = n_it % 2
            n_it += 1
            for i in range(3):
                load_halo(P, lambda a, b, z=z, i=i: z3[z][:, i, a:b], b_z3[z], lambda a, b, i=i: zrw[i, :, h, a:b], t0, nb, first, last)
            P.op("dve", lambda e, z=z, nb=nb: e.tensor_tensor(out=s3[:, :, 0:nb], in0=z3[z][:, :, 0:nb], in1=z3[z][:, :, 2:nb + 2], op=ALU.add),
                 reads=[b_z3[z]], writes=[b_s3])
            for i in range(3):
                P.op("act", lambda e, i=i, nb=nb: e.activation(out=s3[:, i, 0:nb], in_=s3[:, i, 0:nb], func=AF.Copy, scale=hm[i][0][:, hs]),
                     reads=[b_s3, hm[i][1]], writes=[b_s3])
            dsts = [(rk[:, 0, 0:nb], b_r), (rk[:, 1, 0:nb], b_k), (Fs[fi][0][:, 4, 0:nb], b_F[fi][0])]
            for i in range(3):
                P.op("dve", lambda e, i=i, z=z, nb=nb, dst=dsts[i][0]: e.scalar_tensor_tensor(
                    out=dst, in0=z3[z][:, i, 1:nb + 1], scalar=om[i][0][:, hs], in1=s3[:, i, 0:nb], op0=ALU.mult, op1=ALU.add),
                    reads=[b_z3[z], b_s3, om[i][1]], writes=[dsts[i][1]])
            r_ap, k_ap, v_ap = rk[:, 0, 0:nb], rk[:, 1, 0:nb], Fs[fi][0][:, 4, 0:nb]
            b_v = b_F[fi][0]
            P.op("act", lambda e, nb=nb, fi=fi, v_ap=v_ap: e.activation(out=Fs[fi][1][:, 4, 0:nb], in_=v_ap, func=AF.Copy), reads=[b_v], writes=[b_F[fi][1]])
            P.op("dve", lambda e, nb=nb, k_ap=k_ap: e.tensor_scalar(out=kkr[:, 0:nb], in0=k_ap, scalar1=k_k[0][:, hs], scalar2=None, op0=ALU.mult),
                 reads=[b_k, k_k[1]], writes=[b_kkr])
            P.op("act", lambda e, nb=nb: e.activation(out=sqt[:, 0:nb], in_=kkr[:, 0:nb], func=AF.Square), reads=[b_kkr], writes=[b_sqt])
            pb = rot.bank()
            P.op("pe", lambda e, pb=pb, nb=nb: e.matmul(psum[pb][0:64, 0:nb], lhsT=g["ones"][0:64, 0:64], rhs=sqt[:, 0:nb], start=True, stop=True),
                 reads=[g["b_ones"], b_sqt], writes=[pbuf[pb]])
            P.op("act", lambda e, pb=pb, nb=nb: e.activation(out=nrm[:, 0:nb], in_=psum[pb][0:64, 0:nb], func=AF.Sqrt), reads=[pbuf[pb]], writes=[b_nrm])
            P.op("dve", lambda e, nb=nb: e.tensor_scalar(out=nrm[:, 0:nb], in0=nrm[:, 0:nb], scalar1=1e-12, scalar2=None, op0=ALU.max),
                 reads=[b_nrm], writes=[b_nrm])
            P.op("dve", lambda e, nb=nb: e.reciprocal(out=nrm[:, 0:nb], in_=nrm[:, 0:nb]), reads=[b_nrm], writes=[b_nrm])
            P.op("dve", lambda e, nb=nb: e.tensor_tensor(out=kk[:, 0:nb], in0=kkr[:, 0:nb], in1=nrm[:, 0:nb], op=ALU.mult),
                 reads=[b_kkr, b_nrm], writes=[b_kk])
            def dir_part(d):
                s_id = h * 2 + d
                F = Fs[fi][d]
                bF = b_F[fi][d]
                T = TD[d]
                lw, cl, ci, cml, einc, eexc, einv, av, tt, kd, tmpd, tot = (T[k] for k in ("lw", "cl", "ci", "cml", "einc", "eexc", "einv", "av", "tt", "kd", "tmpb", "tot"))
                b_lw, b_cl, b_ci, b_cml, b_einc, b_eexc, b_einv, b_av, b_tt, b_kd, b_tmpd, b_tot = (
                    T["b_" + k] for k in ("lw", "cl", "ci", "cml", "einc", "eexc", "einv", "av", "tt", "kd", "tmpb", "tot"))
                pb = rot.bank()
                P.op("pe", lambda e: e.matmul(psum[pb][0:64, 0:nb], lhsT=W2[d][0][:, h * 64:(h + 1) * 64], rhs=tw[:, d, t0:t0 + nb],
                                              start=True, stop=True), reads=[W2[d][1], b_tw[d]], writes=[pbuf[pb]])
                pb2 = rot.bank()
                P.op("pe", lambda e: e.matmul(psum[pb2][0:64, 0:nb], lhsT=A2[d][0][:, h * 64:(h + 1) * 64], rhs=ta[:, d, t0:t0 + nb],
                                              start=True, stop=True), reads=[A2[d][1], b_ta[d]], writes=[pbuf[pb2]])
                yield
                P.op("act", lambda e: e.activation(out=lw[:, 0:nb], in_=psum[pb][0:64, 0:nb], func=AF.Sigmoid, bias=w0[d][0][:, hs]),
                     reads=[pbuf[pb], w0[d][1]], writes=[b_lw])
                P.op("act", lambda e: e.activation(out=av[:, 0:nb], in_=psum[pb2][0:64, 0:nb], func=AF.Sigmoid, bias=a0[d][0][:, hs]),
                     reads=[pbuf[pb2], a0[d][1]], writes=[b_av])
                yield
                P.op("act", lambda e: e.activation(out=lw[:, 0:nb], in_=lw[:, 0:nb], func=AF.Copy, scale=LOGDECAY_SCALE),
                     reads=[b_lw], writes=[b_lw])
                P.op("pool", lambda e: e.tensor_tensor(out=tmpd[:, 0:nb], in0=kk[:, 0:nb], in1=av[:, 0:nb], op=ALU.mult),
                     reads=[b_kk, b_av], writes=[b_tmpd])
                yield
                for j in range(nch):
                    P.op("dve", lambda e, j=j: e.tensor_tensor_scan(out=cl[:, j * 128:(j + 1) * 128], data0=g["ones"][0:64, 0:128],
                                                                   data1=lw[:, j * 128:(j + 1) * 128], initial=0.0, op0=ALU.mult, op1=ALU.add),
                         reads=[b_lw, g["b_ones"]], writes=[b_cl])
                P.op("pool", lambda e: e.tensor_scalar(out=tt[:, 0:nb], in0=av[:, 0:nb], scalar1=k_a[0][:, hs], scalar2=omka[:, hs],
                                                       op0=ALU.mult, op1=ALU.add), reads=[b_av, k_a[1], b_omka], writes=[b_tt])
                yield
                clv = cl[:, 0:nb].rearrange("p (c j) -> p c j", j=128)
                P.op("dve", lambda e: e.tensor_copy(out=tot[:, 0:nch], in_=clv[:, :, 127]), reads=[b_cl], writes=[b_tot])
                P.op("pool", lambda e: e.tensor_tensor(out=kd[:, 0:nb], in0=k_ap, in1=tt[:, 0:nb], op=ALU.mult),
                     reads=[b_k, b_tt], writes=[b_kd])
                yield
                c0 = t0 // 128
                P.op("act", lambda e: e.activation(out=etot[:, s_id, c0:c0 + nch], in_=tot[:, 0:nch], func=AF.Exp),
                     reads=[b_tot], writes=[b_etot])
                if d == 0:
                    ci_ap, b_cix = cl, b_cl
                else:
                    P.op("dve", lambda e: e.tensor_tensor(
                        out=ci[:, 0:nb].rearrange("p (c j) -> p c j", j=128), in0=tot[:, 0:nch].unsqueeze(2).broadcast_to([64, nch, 128]),
                        in1=clv, op=ALU.subtract), reads=[b_tot, b_cl], writes=[b_ci])
                    P.op("dve", lambda e: e.tensor_tensor(out=ci[:, 0:nb], in0=ci[:, 0:nb], in1=lw[:, 0:nb], op=ALU.add),
                         reads=[b_ci, b_lw], writes=[b_ci])
                    ci_ap, b_cix = ci, b_ci
                yield
                P.op("dve", lambda e: e.tensor_tensor(out=cml[:, 0:nb], in0=ci_ap[:, 0:nb], in1=lw[:, 0:nb], op=ALU.subtract),
                     reads=[b_cix, b_lw], writes=[b_cml])
                P.op("act", lambda e: e.activation(out=einc[:, 0:nb], in_=ci_ap[:, 0:nb], func=AF.Exp), reads=[b_cix], writes=[b_einc])
                yield
                P.op("act", lambda e: e.activation(out=einv[:, 0:nb], in_=ci_ap[:, 0:nb], func=AF.Exp, scale=-1.0),
                     reads=[b_cix], writes=[b_einv])
                P.op("dve", lambda e: e.tensor_tensor(out=F[:, 3, 0:nb], in0=r_ap, in1=einc[:, 0:nb], op=ALU.mult),
                     reads=[b_r, b_einc], writes=[bF])
                yield
                P.op("act", lambda e: e.activation(out=eexc[:, 0:nb], in_=cml[:, 0:nb], func=AF.Exp), reads=[b_cml], writes=[b_eexc])
                P.op("dve", lambda e: e.tensor_tensor(out=F[:, 1, 0:nb], in0=tmpd[:, 0:nb], in1=einv[:, 0:nb], op=ALU.mult),
                     reads=[b_tmpd, b_einv], writes=[bF])
                yield
                P.op("pool", lambda e: e.tensor_tensor(out=F[:, 2, 0:nb], in0=kd[:, 0:nb], in1=einv[:, 0:nb], op=ALU.mult),
                     reads=[b_kd, b_einv], writes=[bF])
                P.op("dve", lambda e: e.scalar_tensor_tensor(out=F[:, 0, 0:nb], in0=kk[:, 0:nb], scalar=-1.0, in1=eexc[:, 0:nb],
                                                             op0=ALU.mult, op1=ALU.mult), reads=[b_kk, b_eexc], writes=[bF])
                yield
                P.dma(lambda e: e.dma_start(out=rwt[s_id].rearrange("f d t -> d f t")[:, :, t0:t0 + nb], in_=F[:, :, 0:nb]), reads=[bF])

            lockstep([dir_part(0), dir_part(1)])
            P.op("pool", lambda e: e.tensor_tensor(out=kds[:, 0:nb], in0=TD[0]["kd"][:, 0:nb], in1=TD[1]["kd"][:, 0:nb], op=ALU.add),
                 reads=[TD[0]["b_kd"], TD[1]["b_kd"]], writes=[b_kds])
            ax = aux[n_it % 2]
            b_ax = b_aux[n_it % 2]
            pb = rot.bank()
            P.op("pe", lambda e, pb=pb, nb=nb: e.matmul(psum[pb][0:64, 0:nb], lhsT=G2[0][:, h * 64:(h + 1) * 64], rhs=tg[:, t0:t0 + nb],
                                                        start=True, stop=True), reads=[G2[1], b_tg], writes=[pbuf[pb]])
            P.op("act", lambda e, pb=pb, nb=nb, ax=ax: e.activation(out=ax[:, 0, 0:nb], in_=psum[pb][0:64, 0:nb], func=AF.Copy),
                 reads=[pbuf[pb]], writes=[b_ax])
            P.op("dve", lambda e, nb=nb, r_ap=r_ap: e.scalar_tensor_tensor(out=tmpb[:, 0:nb], in0=r_ap, scalar=r_k[0][:, hs], in1=kds[:, 0:nb],
                                                                           op0=ALU.mult, op1=ALU.mult), reads=[b_r, b_kds, r_k[1]], writes=[b_tmpb])
            pb = rot.bank()
            P.op("pe", lambda e, pb=pb, nb=nb: e.matmul(psum[pb][0:64, 0:nb], lhsT=g["ones"][0:64, 0:64], rhs=tmpb[:, 0:nb], start=True, stop=True),
                 reads=[g["b_ones"], b_tmpb], writes=[pbuf[pb]])
            P.op("dve", lambda e, pb=pb, nb=nb, ax=ax, v_ap=v_ap: e.tensor_tensor(out=ax[:, 1, 0:nb], in0=psum[pb][0:64, 0:nb], in1=v_ap, op=ALU.mult),
                 reads=[pbuf[pb], b_v], writes=[b_ax])
            P.dma(lambda e, ax=ax, nb=nb: e.dma_start(out=io["rwaux"].ap()[:, h, :, t0:t0 + nb].rearrange("a d t -> d a t"), in_=ax[:, :, 0:nb]),
                  reads=[b_ax])
    n_it = 0
    for h in range(4):
        for (t0, nb, first, last) in RW_SEQS:
            head_block(h, t0, nb, first, last, n_it)
            n_it += 1
    P.dma(lambda e: e.dma_start(out=io["etot"].ap(), in_=etot[:]), reads=[b_etot])
    P.barrier()
    P.release()
    P.release()


class RwStream:
    pass


def lockstep(gens):
    gens = list(gens)
    while gens:
        for gg in list(gens):
            try:
                next(gg)
            except StopIteration:
                gens.remove(gg)


def phase_rwkv_scan(P, l, nsteps=34):
    nc, io, g = P.nc, P.io, P.g
    psum = P.psum
    P.mark()
    slot_ap = [psum[i // 2][:, (i % 2) * 256:(i % 2 + 1) * 256] for i in range(16)]
    bank_b = [Buf(psum=True) for _ in range(8)]
    slot_b = [bank_b[i // 2] for i in range(16)]
    masks = P.sb([128, 5, 128], F32, "masks")
    b_masks = Buf()
    P.dma(lambda e: e.dma_start(out=masks[:], in_=io["masks"].ap().rearrange("m p f -> p m f")), writes=[b_masks])
    identb = P.sb([64, 64], BF16, "identb")
    b_identb = Buf()
    P.op("dve", lambda e: e.tensor_copy(out=identb[:], in_=g["ident"][0:64, 0:64]), reads=[g["b_ident"]], writes=[b_identb])
    etot = P.sb([64, 8, 34], F32, "etot2")
    b_etot = Buf()
    P.dma(lambda e: e.dma_start(out=etot[:], in_=io["etot"].ap()), writes=[b_etot])
    rwt = io["rwt"].ap()
    yT = io["yT"].ap()
    ident = g["ident"]

    streams = []
    for s_id in range(8):
        S = RwStream()
        S.id = s_id
        S.d = s_id % 2
        S.F = [P.sb([128, 5, 128], F32, "F%d_%d" % (s_id, i)) for i in range(2)]
        S.b_F = [Buf(), Buf()]
        for i in range(2):
            P.op("pool", lambda e, S=S, i=i: e.memset(S.F[i][64:128, :, :], 0.0), writes=[S.b_F[i]])
        S.Fb = P.sb([64, 5, 128], BF16, "Fb%d" % s_id)
        S.b_Fb = Buf()
        S.Atok = P.sb([128, 64], F32, "Atok%d" % s_id)
        S.b_Atok = Buf()
        S.BKV = P.sb([128, 3, 64], BF16, "BKV%d" % s_id)
        S.b_BKV = Buf()
        S.LQ = [P.sb([128, 256], F32, "LQ%d_%d" % (s_id, i)) for i in range(2)]
        S.b_LQ = [Buf(), Buf()]
        S.Z = [P.sb([128, 128], F32, "Z%d_%d" % (s_id, i)) for i in range(2)]
        S.b_Z = [Buf(), Buf()]
        S.LakT = P.sb([128, 128], BF16, "LakT%d" % s_id)
        S.MrbT = P.sb([128, 128], BF16, "MrbT%d" % s_id)
        S.MrkT = P.sb([128, 128], BF16, "MrkT%d" % s_id)
        S.b_LakT, S.b_MrbT, S.b_MrkT = Buf(), Buf(), Buf()
        S.W = P.sb([128, 64], BF16, "W%d" % s_id)
        S.WT = P.sb([64, 128], BF16, "WT%d" % s_id)
        S.X = P.sb([128, 64], F32, "X%d" % s_id)
        S.U0 = P.sb([128, 64], F32, "U0%d" % s_id)
        S.U0b = P.sb([128, 64], BF16, "U0b%d" % s_id)
        S.GT = P.sb([64, 64], BF16, "GT%d" % s_id)
        S.DE = P.sb([64, 64], F32, "DE%d" % s_id)
        S.Ub = P.sb([128, 64], BF16, "Ub%d" % s_id)
        S.b_W, S.b_WT, S.b_X, S.b_U0, S.b_U0b, S.b_GT, S.b_DE, S.b_Ub = [Buf() for _ in range(8)]
        S.H = [P.sb([64, 64], BF16, "H%d_%d" % (s_id, i)) for i in range(2)]
        S.b_H = [Buf(), Buf()]
        S.ys = [P.sb([64, 128], F32, "ys%d_%d" % (s_id, i)) for i in range(2)]
        S.b_ys = [Buf(), Buf()]
        S.slot = 0
        S.ei = s_id
        S.mLQ, S.mQ, S.mM = (0, 1, 4) if S.d == 0 else (1, 2, 3)
        P.op("pool", lambda e, S=S: e.memset(S.H[0][:], 0.0), writes=[S.b_H[0]])
        streams.append(S)

    def next_slot(S):
        i = 2 * S.id + (S.slot % 2)
        S.slot += 1
        return slot_ap[i], slot_b[i]

    def ev_eng(S):
        S.ei += 1
        return ("act", "dve")[S.ei % 2]

    def load(S, step):
        c = CHUNK_ORDER[S.d][step]
        fb = step % 2
        P.dma(lambda e: e.dma_start(out=S.F[fb][0:64, :, :], in_=rwt[S.id].rearrange("f d t -> d f t")[:, :, c * 128:(c + 1) * 128]),
              writes=[S.b_F[fb]])

    def stage_prep(S, step):
        fb = step % 2
        F, bF = S.F[fb], S.b_F[fb]
        P.op("pool", lambda e: e.tensor_copy(out=S.Fb[:], in_=F[0:64, :, :]), reads=[bF], writes=[S.b_Fb])
        ps, pb = next_slot(S)
        for i, fidx in enumerate((0, 1, 2, 4)):
            P.op("pe", lambda e, i=i, fidx=fidx: e.matmul(ps[:, i * 64:(i + 1) * 64], lhsT=F[:, fidx, :], rhs=ident[:, 0:64],
                                                          start=True, stop=True),
                 reads=[bF, g["b_ident"]], writes=[pb])
        yield
        P.op("act", lambda e: e.activation(out=S.Atok[:], in_=ps[:, 0:64], func=AF.Copy), reads=[pb], writes=[S.b_Atok])
        P.op("dve", lambda e: e.tensor_copy(out=S.BKV[:], in_=ps[:, 64:256].rearrange("p (a d) -> p a d", a=3)), reads=[pb], writes=[S.b_BKV])
        yield

    def stage_scores(S, step):
        fb = step % 2
        F, bF = S.F[fb], S.b_F[fb]
        ps, pb = next_slot(S)
        P.op("pe", lambda e: e.matmul(ps[:, 0:128], lhsT=F[:, 0, :], rhs=F[:, 1, :], start=True, stop=True), reads=[bF], writes=[pb])
        P.op("pe", lambda e: e.matmul(ps[:, 128:256], lhsT=F[:, 1, :], rhs=F[:, 0, :], start=True, stop=True), reads=[bF], writes=[pb])
        yield
        P.op("dve", lambda e: e.tensor_tensor(out=S.LQ[0][:].rearrange("p (a f) -> p a f", a=2), in0=ps[:, 0:256].rearrange("p (a f) -> p a f", a=2),
                                              in1=masks[:, S.mLQ:S.mLQ + 2, :], op=ALU.mult), reads=[pb, b_masks], writes=[S.b_LQ[0]])
        P.op("pool", lambda e: e.tensor_tensor(out=S.Z[0][:], in0=S.LQ[0][:, 128:256], in1=ident[:], op=ALU.add),
             reads=[S.b_LQ[0], g["b_ident"]], writes=[S.b_Z[0]])
        yield
        ps2, pb2 = next_slot(S)
        P.op("pe", lambda e: e.matmul(ps2[:, 0:128], lhsT=S.Fb[:, 2, :], rhs=S.Fb[:, 0, :], start=True, stop=True), reads=[S.b_Fb], writes=[pb2])
        P.op("pe", lambda e: e.matmul(ps2[:, 128:256], lhsT=S.Fb[:, 1, :], rhs=S.Fb[:, 3, :], start=True, stop=True), reads=[S.b_Fb], writes=[pb2])
        yield
        P.op("dve", lambda e: e.tensor_tensor(out=S.LakT[:], in0=ps2[:, 0:128], in1=masks[:, S.mQ, :], op=ALU.mult),
             reads=[pb2, b_masks], writes=[S.b_LakT])
        P.op("dve", lambda e: e.tensor_tensor(out=S.MrbT[:], in0=ps2[:, 128:256], in1=masks[:, S.mM, :], op=ALU.mult),
             reads=[pb2, b_masks], writes=[S.b_MrbT])
        yield
        ps3, pb3 = next_slot(S)
        P.op("pe", lambda e: e.matmul(ps3[:, 0:128], lhsT=S.Fb[:, 2, :], rhs=S.Fb[:, 3, :], start=True, stop=True), reads=[S.b_Fb], writes=[pb3])
        yield
        P.op("dve", lambda e: e.tensor_tensor(out=S.MrkT[:], in0=ps3[:, 0:128], in1=masks[:, S.mM, :], op=ALU.mult),
             reads=[pb3, b_masks], writes=[S.b_MrkT])
        yield

    def stage_double(S, lvl):
        a, b = (lvl - 1) % 2, lvl % 2
        last = lvl == 6
        ps, pb = next_slot(S)
        La, Qa = S.LQ[a][:, 0:128], S.LQ[a][:, 128:256]
        P.op("pe", lambda e: e.matmul(ps[:, 0:128], lhsT=Qa, rhs=La, start=True, stop=True), reads=[S.b_LQ[a]], writes=[pb])
        if not last:
            P.op("pe", lambda e: e.matmul(ps[:, 128:256], lhsT=La, rhs=Qa, start=True, stop=True), reads=[S.b_LQ[a]], writes=[pb])
        yield
        w = 128 if last else 256
        eng = ev_eng(S)
        if eng == "act":
            P.op("act", lambda e: e.activation(out=S.LQ[b][:, 0:w], in_=ps[:, 0:w], func=AF.Copy), reads=[pb], writes=[S.b_LQ[b]])
        else:
            P.op("dve", lambda e: e.tensor_copy(out=S.LQ[b][:, 0:w], in_=ps[:, 0:w]), reads=[pb], writes=[S.b_LQ[b]])
        yield
        ps2, pb2 = next_slot(S)
        P.op("pe", lambda e: e.matmul(ps2[:, 0:128], lhsT=S.LQ[b][:, 0:128], rhs=S.Z[a][:], start=True, stop=True),
             reads=[S.b_LQ[b], S.b_Z[a]], writes=[pb2])
        yield
        P.op("dve", lambda e: e.tensor_tensor(out=S.Z[b][:], in0=ps2[:, 0:128], in1=S.Z[a][:], op=ALU.add),
             reads=[pb2, S.b_Z[a]], writes=[S.b_Z[b]])
        yield

    def stage_wux(S):
        Z, bZ = S.Z[0], S.b_Z[0]
        ps, pb = next_slot(S)
        P.op("pe", lambda e: e.matmul(ps[:, 0:64], lhsT=Z[:], rhs=S.Atok[:], start=True, stop=True), reads=[bZ, S.b_Atok], writes=[pb])
        P.op("pe", lambda e: e.matmul(ps[0:64, 64:192], lhsT=S.Atok[:], rhs=Z[:], start=True, stop=True), reads=[bZ, S.b_Atok], writes=[pb])
        P.op("pe", lambda e: e.matmul(ps[:, 192:256], lhsT=S.LakT[:], rhs=S.BKV[:, 2, :], start=True, stop=True),
             reads=[S.b_LakT, S.b_BKV], writes=[pb])
        yield
        P.op("dve", lambda e: e.tensor_copy(out=S.X[:], in_=ps[:, 192:256]), reads=[pb], writes=[S.b_X])
        P.op("act", lambda e: e.activation(out=S.W[:], in_=ps[:, 0:64], func=AF.Copy), reads=[pb], writes=[S.b_W])
        P.op("act", lambda e: e.activation(out=S.WT[:], in_=ps[0:64, 64:192], func=AF.Copy), reads=[pb], writes=[S.b_WT])
        yield
        ps2, pb2 = next_slot(S)
        P.op("pe", lambda e: e.matmul(ps2[:, 0:64], lhsT=Z[:], rhs=S.X[:], start=True, stop=True), reads=[bZ, S.b_X], writes=[pb2])
        yield
        P.op("act", lambda e: e.activation(out=S.U0[:], in_=ps2[:, 0:64], func=AF.Copy), reads=[pb2], writes=[S.b_U0])
        P.op("act", lambda e: e.activation(out=S.U0b[:], in_=ps2[:, 0:64], func=AF.Copy), reads=[pb2], writes=[S.b_U0b])
        yield

    def stage_gd(S, step):
        c = CHUNK_ORDER[S.d][step]
        ps, pb = next_slot(S)
        P.op("pe", lambda e: e.matmul(ps[0:64, 0:64], lhsT=S.W[:], rhs=S.BKV[:, 0, :], start=True, stop=False),
             reads=[S.b_W, S.b_BKV], writes=[pb])
        P.op("pe", lambda e: e.matmul(ps[0:64, 0:64], lhsT=identb[:], rhs=identb[:], start=False, stop=True), reads=[b_identb], writes=[pb])
        P.op("pe", lambda e: e.matmul(ps[0:64, 64:128], lhsT=S.BKV[:, 0, :], rhs=S.U0b[:], start=True, stop=False),
             reads=[S.b_BKV, S.b_U0b], writes=[pb])
        P.op("pe", lambda e: e.matmul(ps[0:64, 64:128], lhsT=S.BKV[:, 1, :], rhs=S.BKV[:, 2, :], start=False, stop=True),
             reads=[S.b_BKV], writes=[pb])
        yield
        P.op("act", lambda e: e.activation(out=S.GT[:], in_=ps[0:64, 0:64], func=AF.Copy), reads=[pb], writes=[S.b_GT])
        P.op("act", lambda e: e.activation(out=S.DE[:], in_=ps[0:64, 64:128], func=AF.Copy, scale=etot[:, S.id, c:c + 1]),
             reads=[pb, b_etot], writes=[S.b_DE])
        yield

    def stage_state(S, step):
        c = CHUNK_ORDER[S.d][step]
        hi, ho = step % 2, (step + 1) % 2
        H, bH = S.H[hi], S.b_H[hi]
        ps, pb = next_slot(S)
        P.op("pe", lambda e: e.matmul(ps[:, 0:64], lhsT=S.WT[:], rhs=H[:], start=True, stop=True), reads=[S.b_WT, bH], writes=[pb])
        yield
        P.op("dve", lambda e: e.tensor_tensor(out=S.Ub[:], in0=ps[:, 0:64], in1=S.U0[:], op=ALU.add), reads=[pb, S.b_U0], writes=[S.b_Ub])
        yield
        ps2, pb2 = next_slot(S)
        P.op("pe", lambda e: e.matmul(ps2[0:64, 0:128], lhsT=H[:], rhs=S.Fb[:, 3, :], start=True, stop=False), reads=[bH, S.b_Fb], writes=[pb2])
        P.op("pe", lambda e: e.matmul(ps2[0:64, 0:128], lhsT=S.Ub[:], rhs=S.MrbT[:], start=False, stop=False),
             reads=[S.b_Ub, S.b_MrbT], writes=[pb2])
        P.op("pe", lambda e: e.matmul(ps2[0:64, 0:128], lhsT=S.BKV[:, 2, :], rhs=S.MrkT[:], start=False, stop=True),
             reads=[S.b_BKV, S.b_MrkT], writes=[pb2])
        P.op("pe", lambda e: e.matmul(ps2[0:64, 128:192], lhsT=S.GT[:], rhs=H[:], start=True, stop=True), reads=[S.b_GT, bH], writes=[pb2])
        yield
        yb = step % 2
        P.op("dve", lambda e: e.scalar_tensor_tensor(out=S.H[ho][:], in0=ps2[0:64, 128:192], scalar=etot[:, S.id, c:c + 1], in1=S.DE[:],
                                                     op0=ALU.mult, op1=ALU.add), reads=[pb2, b_etot, S.b_DE], writes=[S.b_H[ho]])
        P.op("dve", lambda e: e.tensor_copy(out=S.ys[yb][:], in_=ps2[0:64, 0:128]), reads=[pb2], writes=[S.b_ys[yb]])
        P.dma(lambda e: e.dma_start(out=yT[S.id, :, c * 128:(c + 1) * 128], in_=S.ys[yb][:]), reads=[S.b_ys[yb]])
        yield

    def lockstep(gens):
        gens = list(gens)
        while gens:
            for gg in list(gens):
                try:
                    next(gg)
                except StopIteration:
                    gens.remove(gg)

    for S in streams:
        load(S, 0)
    for step in range(nsteps):
        if step + 1 < nsteps:
            for S in streams:
                load(S, step + 1)
        lockstep(stage_prep(S, step) for S in streams)
        lockstep(stage_scores(S, step) for S in streams)
        for lvl in range(1, 7):
            lockstep(stage_double(S, lvl) for S in streams)
        lockstep(stage_wux(S) for S in streams)
        lockstep(stage_gd(S, step) for S in streams)
        lockstep(stage_state(S, step) for S in streams)
    P.barrier()
    P.release()


def phase_rwkv_fin(P, l):
    nc, io, g = P.nc, P.io, P.g
    psum, pbuf = P.psum, P.pbuf
    rot = Rot(P)
    P.mark()
    gn_g = col_param(P, io["gn_g"].ap()[l, :], "gn_g")
    gn_b = col_param(P, io["gn_b"].ap()[l, :], "gn_b")
    od = P.sb([64, 64], F32, "onesdiv")
    b_od = Buf()
    P.op("dve", lambda e: e.memset(od[:], 1.0 / 64), writes=[b_od])
    y2 = [P.sb([64, 2, 512], F32, "y2_%d" % i) for i in range(2)]
    ax = [P.sb([64, 2, 512], F32, "ax_%d" % i) for i in range(2)]
    b_y2, b_ax = [Buf(), Buf()], [Buf(), Buf()]
    y = P.sb([64, 512], F32, "y")
    yc = P.sb([64, 512], F32, "yc")
    sq = P.sb([64, 512], F32, "sq")
    sd = P.sb([64, 512], F32, "sd")
    b_y, b_yc, b_sq, b_sd = Buf(), Buf(), Buf(), Buf()
    ost = [P.sb([64, 512], BF16, "rost%d" % i) for i in range(2)]
    b_ost = [Buf(), Buf()]
    yT = io["yT"].ap()
    cat = io["cat"].ap()

    def blk(h, t0, nb, n):
        k = n % 2
        hs = slice(h, h + 1)
        P.dma(lambda e: e.dma_start(out=y2[k][:, :, 0:nb], in_=yT[2 * h:2 * h + 2, :, t0:t0 + nb].rearrange("s d t -> d s t")), writes=[b_y2[k]])
        P.dma(lambda e: e.dma_start(out=ax[k][:, :, 0:nb], in_=io["rwaux"].ap()[:, h, :, t0:t0 + nb].rearrange("a d t -> d a t")), writes=[b_ax[k]])
        P.op("dve", lambda e: e.tensor_tensor(out=y[:, 0:nb], in0=y2[k][:, 0, 0:nb], in1=y2[k][:, 1, 0:nb], op=ALU.add), reads=[b_y2[k]], writes=[b_y])
        pb = rot.bank()
        P.op("pe", lambda e: e.matmul(psum[pb][0:64, 0:nb], lhsT=od[:], rhs=y[:, 0:nb], start=True, stop=True), reads=[b_od, b_y], writes=[pbuf[pb]])
        P.op("dve", lambda e: e.tensor_tensor(out=yc[:, 0:nb], in0=y[:, 0:nb], in1=psum[pb][0:64, 0:nb], op=ALU.subtract),
             reads=[b_y, pbuf[pb]], writes=[b_yc])
        P.op("pool", lambda e: e.tensor_tensor(out=sq[:, 0:nb], in0=yc[:, 0:nb], in1=yc[:, 0:nb], op=ALU.mult), reads=[b_yc], writes=[b_sq])
        pb2 = rot.bank()
        P.op("pe", lambda e: e.matmul(psum[pb2][0:64, 0:nb], lhsT=od[:], rhs=sq[:, 0:nb], start=True, stop=True), reads=[b_od, b_sq], writes=[pbuf[pb2]])
        P.op("dve", lambda e: e.tensor_scalar(out=sd[:, 0:nb], in0=psum[pb2][0:64, 0:nb], scalar1=float(GN_EPS), scalar2=None, op0=ALU.add),
             reads=[pbuf[pb2]], writes=[b_sd])
        P.op("act", lambda e: e.activation(out=sd[:, 0:nb], in_=sd[:, 0:nb], func=AF.Sqrt), reads=[b_sd], writes=[b_sd])
        P.op("dve", lambda e: e.reciprocal(out=sd[:, 0:nb], in_=sd[:, 0:nb]), reads=[b_sd], writes=[b_sd])
        P.op("dve", lambda e: e.tensor_tensor(out=yc[:, 0:nb], in0=yc[:, 0:nb], in1=sd[:, 0:nb], op=ALU.mult), reads=[b_yc, b_sd], writes=[b_yc])
        P.op("pool", lambda e: e.tensor_scalar(out=yc[:, 0:nb], in0=yc[:, 0:nb], scalar1=gn_g[0][:, hs], scalar2=gn_b[0][:, hs],
                                               op0=ALU.mult, op1=ALU.add), reads=[b_yc, gn_g[1], gn_b[1]], writes=[b_yc])
        P.op("dve", lambda e: e.tensor_tensor(out=yc[:, 0:nb], in0=yc[:, 0:nb], in1=ax[k][:, 1, 0:nb], op=ALU.add), reads=[b_yc, b_ax[k]], writes=[b_yc])
        P.op("pool", lambda e: e.tensor_tensor(out=ost[k][:, 0:nb], in0=yc[:, 0:nb], in1=ax[k][:, 0, 0:nb], op=ALU.mult),
             reads=[b_yc, b_ax[k]], writes=[b_ost[k]])
        P.dma(lambda e: e.dma_start(out=cat[256 + h * 64:256 + (h + 1) * 64, t0:t0 + nb], in_=ost[k][:, 0:nb]), reads=[b_ost[k]])

    n = 0
    for h in range(4):
        for (t0, nb, first, last) in RW_SEQS:
            blk(h, t0, nb, n)
            n += 1
    P.barrier()
    P.release()


_NC_CACHE = {}


def _get_nc(dbg=None):
    if dbg not in _NC_CACHE:
        _NC_CACHE[dbg] = build_program(dbg)
    return _NC_CACHE[dbg]


_CONST = {}


def _constants():
    if _CONST:
        return _CONST
    bf = ml_dtypes.bfloat16
    t = np.arange(SEQ)
    rows = (t // 64).astype(np.float64)
    cols = (t % 64).astype(np.float64)
    inv = 10000.0 ** (-np.arange(16, dtype=np.float64) / 16)
    ang = np.concatenate([rows[:, None] * inv, cols[:, None] * inv], -1)
    _CONST["rope_cos"] = np.cos(ang).astype(np.float32)
    _CONST["rope_sin"] = np.sin(ang).astype(np.float32)
    c = np.arange(64)
    th = 2 * np.pi * np.outer(c, c) / 64
    z = np.zeros((64, 64))
    _CONST["dftc"] = np.block([[np.cos(th), z], [z, np.cos(th)]]).astype(bf)
    _CONST["dfts"] = np.block([[np.sin(th), z], [z, np.sin(th)]]).astype(bf)
    pi_, fi_ = np.arange(128)[:, None], np.arange(128)[None, :]
    _CONST["masks"] = np.stack([fi_ < pi_, fi_ > pi_, fi_ < pi_, fi_ <= pi_, fi_ >= pi_], 0).astype(np.float32)
    pm = np.zeros((4, 5, 128, 128))
    for gi, win in enumerate((2, 4, 8, 16)):
        T3 = 384
        tt = np.arange(T3)
        lo = np.clip(tt - win // 2, 0, T3)
        hi = np.clip(tt + (win - win // 2), 0, T3)
        ss_ = np.arange(T3)[:, None]
        M = ((ss_ >= lo[None, :]) & (ss_ < hi[None, :])) / (hi - lo)[None, :].astype(np.float64) - np.eye(T3)
        pm[gi, 0] = M[0:128, 128:256]
        pm[gi, 1] = M[128:256, 0:128]
        pm[gi, 2] = M[128:256, 128:256]
        pm[gi, 3] = M[0:128, 0:128]
        pm[gi, 4] = M[256:384, 256:384]
    _CONST["poolm"] = pm.reshape(20, 128, 128).astype(bf)
    for nm, T in (("dft_lat", SEQ), ("dft_ctx", CTX)):
        k = np.arange(T)
        kk = (np.outer(k, k) % T).astype(np.float64)
        ang = 2 * np.pi * kk / T
        sc = 1.0 / np.sqrt(T * 64.0)
        _CONST[nm] = np.stack([np.cos(ang) * sc, -np.sin(ang) * sc], 0).astype(bf)
    return _CONST


def make_in_maps(inputs):
    f32 = lambda a: np.ascontiguousarray(np.asarray(a, dtype=np.float32))
    shared = {k: f32(inputs[k]) for k in ("w_mod", "b_mod", "ln_g", "ln_b", "w_ffn_in", "w_ffn_out", "w_in", "w_out",
                                          "q_norm_g", "k_norm_g", "pool_w", "pool_scale", "fourier_w", "rwkv_mu", "decay_w0", "decay_w1", "decay_w2",
                                          "icl_a0", "icl_a1", "icl_a2", "gate_g1", "gate_g2", "k_k", "k_a", "r_k", "gn_g", "gn_b")}
    shared["ident"] = np.eye(128, dtype=np.float32)
    shared.update(_constants())
    maps = []
    for core in range(8):
        b = core // 2
        m = dict(shared)
        m["x"] = f32(inputs["x"][b])
        m["ctx"] = f32(inputs["ctx"][b])
        m["c2"] = f32(np.stack([np.asarray(inputs["c"])[b], np.asarray(inputs["c_ctx"])], 0))
        maps.append(m)
    return maps


def kernel(**inputs):
    nc = _get_nc(None)
    res = run_bass_kernel_spmd(nc, make_in_maps(inputs), core_ids=list(range(8)))
    out = np.zeros((4, SEQ, D), np.float32)
    hf = SEQ // 2
    for core in range(8):
        b, j = core // 2, core % 2
        out[b, j * hf:(j + 1) * hf] = np.asarray(res.results[core]["y"])[j * hf:(j + 1) * hf]
    return out
```
